# Optimizing a Trainium2 kernel written in Bass

```python
import math
import jax, jax.numpy as jnp
from jax import lax
import numpy as np

D_MODEL = 1024
BATCH = 4
SEQ = 4096
DEPTH = 4
DEC_BATCH = 128
DEC_SEQ = 4
PAST_LEN = 2048
PAGE_SIZE = 128

N_A_LAYERS = DEPTH // 2
N_B_LAYERS = DEPTH - N_A_LAYERS
NORM_EPS = 1e-6
DN_HEADS = 8
DN_HEAD_K = D_MODEL // DN_HEADS
DN_HEAD_V = D_MODEL // DN_HEADS
DN_KEY_DIM = DN_HEADS * DN_HEAD_K
DN_VAL_DIM = DN_HEADS * DN_HEAD_V
DN_QKV_DIM = 2 * DN_KEY_DIM + DN_VAL_DIM
DN_IN_DIM = DN_QKV_DIM + DN_VAL_DIM + 2 * DN_HEADS
DN_CONV = 4
DN_CHUNK = 64
NSA_HEADS = 16
NSA_HEAD_DIM = D_MODEL // NSA_HEADS
NSA_DIM = NSA_HEADS * NSA_HEAD_DIM
NSA_KV_HEADS = 4
NSA_GROUP = NSA_HEADS // NSA_KV_HEADS
NSA_IN_DIM = NSA_DIM + 3 * NSA_HEADS
NSA_KV_DIM = 6 * NSA_KV_HEADS * NSA_HEAD_DIM
CMP_STRIDE = 16
CMP_BLOCK = 2 * CMP_STRIDE
SEL_BLOCK = 64
N_SEL = 16
WINDOW = 512
Q_BLOCK = 128
REL_BUCKETS = 32
REL_MAX_EXACT = REL_BUCKETS // 2
REL_MAX_DIST = 1024
D_FF = ((8 * D_MODEL + 3 * 256 - 1) // (3 * 256)) * 256

kernel_name = 'yoco_gdn_nsa_decoder_step'


def rmsnorm(x, w):
    xf = x.astype(jnp.float32)
    y = xf * lax.rsqrt(jnp.mean(xf * xf, axis=-1, keepdims=True) + NORM_EPS)
    return y.astype(x.dtype) * w


def l2norm(x):
    xf = x.astype(jnp.float32)
    return xf * lax.rsqrt(jnp.sum(xf * xf, axis=-1, keepdims=True) + NORM_EPS)


def swiglu_ffn(h, w_in, w_out):
    gate, up = jnp.split(h @ w_in, 2, axis=-1)
    return (jax.nn.silu(gate) * up) @ w_out


def causal_short_conv(x, buf, w):
    t = x.shape[1]
    xp = jnp.concatenate([buf.astype(x.dtype), x], axis=1)
    y = xp[:, 0:t] * w[0]
    for i in range(1, DN_CONV):
        y = y + xp[:, i:i + t] * w[i]
    return jax.nn.silu(y), xp[:, xp.shape[1] - (DN_CONV - 1):]


def gated_delta_rule(q, k, v, g, beta, S0):
    B_, T, H, DK = q.shape
    DV = v.shape[-1]
    C = math.gcd(T, DN_CHUNK)
    N = T // C

    def to_chunks(a):
        return jnp.swapaxes(a.reshape((B_, N, C) + a.shape[2:]), 2, 3)

    qc, kc, vc, gc, bc = to_chunks(q), to_chunks(k), to_chunks(v), to_chunks(g), to_chunks(beta)
    G = jnp.cumsum(gc, axis=-1)
    ii = jnp.arange(C)
    strict = ii[:, None] > ii[None, :]
    incl = ii[:, None] >= ii[None, :]
    diff = G[..., :, None] - G[..., None, :]
    dec = jnp.where(incl, jnp.exp(jnp.where(incl, diff, 0.0)), 0.0)
    kk = jnp.einsum('bnhid,bnhjd->bnhij', kc, kc)
    A = jnp.where(strict, bc[..., :, None] * dec * kk, 0.0)
    L = A + jnp.eye(C, dtype=A.dtype)
    eG = jnp.exp(G)
    wv = lax.linalg.triangular_solve(L, bc[..., None] * vc, left_side=True, lower=True, unit_diagonal=True)
    wk = lax.linalg.triangular_solve(L, (bc * eG)[..., None] * kc, left_side=True, lower=True, unit_diagonal=True)
    aqk = dec * jnp.einsum('bnhid,bnhjd->bnhij', qc, kc)
    qg = eG[..., None] * qc
    kdec = jnp.exp(G[..., -1:] - G)[..., None] * kc
    glast = jnp.exp(G[..., -1])
    xs = tuple(jnp.swapaxes(a, 0, 1) for a in (wv, wk, aqk, qg, kdec, glast))

    def step(S, xc):
        wv_n, wk_n, aqk_n, qg_n, kdec_n, gl_n = xc
        U = wv_n - jnp.einsum('bhik,bhkv->bhiv', wk_n, S)
        O = jnp.einsum('bhik,bhkv->bhiv', qg_n, S) + jnp.einsum('bhij,bhjv->bhiv', aqk_n, U)
        S = gl_n[..., None, None] * S + jnp.einsum('bhik,bhiv->bhkv', kdec_n, U)
        return S, O

    S_fin, O = lax.scan(step, S0, xs)
    O = O.transpose(1, 0, 3, 2, 4).reshape(B_, T, H, DV)
    return O, S_fin


def deltanet_mixer(h, conv_buf, S0, w_in, conv_w, A_log, dt_bias, out_norm, w_out):
    f32 = jnp.float32
    B_, T, _ = h.shape
    qkv, z, a, b = jnp.split(h @ w_in, [DN_QKV_DIM, DN_QKV_DIM + DN_VAL_DIM, DN_QKV_DIM + DN_VAL_DIM + DN_HEADS], axis=-1)
    qkv, new_buf = causal_short_conv(qkv, conv_buf, conv_w)
    q, k, v = jnp.split(qkv, [DN_KEY_DIM, 2 * DN_KEY_DIM], axis=-1)
    q = l2norm(q.reshape(B_, T, DN_HEADS, DN_HEAD_K)) * (DN_HEAD_K ** -0.5)
    k = l2norm(k.reshape(B_, T, DN_HEADS, DN_HEAD_K))
    v = v.reshape(B_, T, DN_HEADS, DN_HEAD_V).astype(f32)
    beta = jax.nn.sigmoid(b.astype(f32))
    g = -jnp.exp(A_log.astype(f32)) * jax.nn.softplus(a.astype(f32) + dt_bias.astype(f32))
    o, S = gated_delta_rule(q, k, v, g, beta, S0.astype(f32))
    o = rmsnorm(o, out_norm) * jax.nn.silu(z.reshape(B_, T, DN_HEADS, DN_HEAD_V).astype(f32))
    return o.reshape(B_, T, DN_VAL_DIM).astype(h.dtype) @ w_out, new_buf, S


def shared_rows(x, norm_w, w_kv):
    B_, T, _ = x.shape
    return (rmsnorm(x, norm_w) @ w_kv).reshape(B_, T, 6, NSA_KV_HEADS, NSA_HEAD_DIM)


def nsa_context(hist, cmp_pos_w, w_cmp):
    B_, L = hist.shape[:2]
    L_pad = ((L + SEL_BLOCK - 1) // SEL_BLOCK) * SEL_BLOCK
    hist = jnp.pad(hist, ((0, 0), (0, L_pad - L), (0, 0), (0, 0), (0, 0)))
    n_sub = L_pad // CMP_STRIDE
    sub = hist[:, :, :2].reshape(B_, n_sub, CMP_STRIDE, 2, NSA_KV_HEADS, NSA_HEAD_DIM)
    lo = jnp.einsum('bmrckd,crkd->bmckd', sub, cmp_pos_w[:, :CMP_STRIDE])
    hi = jnp.einsum('bmrckd,crkd->bmckd', sub, cmp_pos_w[:, CMP_STRIDE:])
    blocks = lo[:, :-1] + hi[:, 1:]
    cmp = jnp.einsum('bnckd,ckde->bncke', blocks, w_cmp)
    cmp_end = jnp.arange(n_sub - 1, dtype=jnp.int32) * CMP_STRIDE + (CMP_BLOCK - 1)
    n_sel_blocks = L_pad // SEL_BLOCK
    sel = hist[:, :, 2:].reshape(B_, n_sel_blocks, SEL_BLOCK, 2, NSA_KV_HEADS, NSA_HEAD_DIM).transpose(3, 0, 4, 1, 2, 5)
    return (cmp[:, :, 0], cmp[:, :, 1], cmp_end, sel[0], sel[1])


def rel_bucket(dist):
    n = jnp.maximum(dist, 0)
    nf = jnp.maximum(n, 1).astype(jnp.float32)
    large = REL_MAX_EXACT + (jnp.log(nf / REL_MAX_EXACT) / math.log(REL_MAX_DIST / REL_MAX_EXACT) * (REL_BUCKETS - REL_MAX_EXACT)).astype(jnp.int32)
    large = jnp.minimum(large, REL_BUCKETS - 1)
    return jnp.where(n < REL_MAX_EXACT, n, large)


def masked_softmax(logits, mask):
    l = jnp.where(mask, logits.astype(jnp.float32), -1e30)
    m = jnp.max(l, axis=-1, keepdims=True)
    e = jnp.where(mask, jnp.exp(l - m), 0.0)
    return e / jnp.maximum(jnp.sum(e, axis=-1, keepdims=True), 1e-30)


def nsa_attend(q, gates, q_pos, k_cmp, v_cmp, cmp_end, k_sel, v_sel, k_win, v_win, w_pos, rel_bias):
    f32 = jnp.float32
    B_, Tq = q.shape[:2]
    table = rel_bias.astype(f32)
    qg = q.reshape(B_, Tq, NSA_KV_HEADS, NSA_GROUP, NSA_HEAD_DIM)
    d_c = q_pos[:, None] - cmp_end[None, :]
    bias_c = table[rel_bucket(d_c)].reshape(Tq, -1, NSA_KV_HEADS, NSA_GROUP).transpose(2, 3, 0, 1)
    p_c = masked_softmax(jnp.einsum('bqkgd,bckd->bkgqc', qg, k_cmp).astype(f32) + bias_c, d_c >= 0)
    o_cmp = jnp.einsum('bkgqc,bckd->bqkgd', p_c.astype(v_cmp.dtype), v_cmp)
    n_blocks = k_sel.shape[2]
    n_pick = min(N_SEL, n_blocks)
    imp = jnp.pad(jnp.sum(p_c, axis=2), ((0, 0), (0, 0), (0, 0), (0, 1)))
    cover = imp + jnp.pad(imp[..., :-1], ((0, 0), (0, 0), (0, 0), (1, 0)))
    p_slc = cover.reshape(B_, NSA_KV_HEADS, Tq, n_blocks, SEL_BLOCK // CMP_STRIDE).sum(-1)
    blk = jnp.arange(n_blocks, dtype=jnp.int32)[None, :]
    cur = (q_pos // SEL_BLOCK)[:, None]
    forced = (blk == 0) | (blk == cur) | (blk == cur - 1)
    valid = blk * SEL_BLOCK <= q_pos[:, None]
    score = jnp.where(forced, 1e4, jnp.where(valid, p_slc, -1.0))
    _, idx = lax.top_k(score, n_pick)
    b_ix = jnp.arange(B_)[:, None, None, None]
    h_ix = jnp.arange(NSA_KV_HEADS)[None, :, None, None]
    k_g = k_sel[b_ix, h_ix, idx]
    v_g = v_sel[b_ix, h_ix, idx]
    key_pos = idx[..., None] * SEL_BLOCK + jnp.arange(SEL_BLOCK, dtype=jnp.int32)
    d_s = q_pos[None, None, :, None, None] - key_pos
    bias_s = table.reshape(REL_BUCKETS, NSA_KV_HEADS, NSA_GROUP)[rel_bucket(d_s), h_ix[..., None]]
    bias_s = jnp.moveaxis(bias_s, -1, 3)
    l_s = jnp.einsum('bqkgd,bkqsrd->bkqgsr', qg, k_g).astype(f32) + bias_s
    n_keys = n_pick * SEL_BLOCK
    p_s = masked_softmax(l_s.reshape(B_, NSA_KV_HEADS, Tq, NSA_GROUP, n_keys),
                         (d_s >= 0).reshape(B_, NSA_KV_HEADS, Tq, 1, n_keys)).reshape(l_s.shape)
    o_sel = jnp.einsum('bkqgsr,bkqsrd->bqkgd', p_s.astype(v_g.dtype), v_g)
    d_w = q_pos[:, None] - w_pos[None, :]
    mask_w = (d_w >= 0) & (d_w < WINDOW) & (w_pos[None, :] >= 0)
    bias_w = table[rel_bucket(d_w)].reshape(Tq, -1, NSA_KV_HEADS, NSA_GROUP).transpose(2, 3, 0, 1)
    p_w = masked_softmax(jnp.einsum('bqkgd,bwkd->bkgqw', qg, k_win).astype(f32) + bias_w, mask_w)
    o_win = jnp.einsum('bkgqw,bwkd->bqkgd', p_w.astype(v_win.dtype), v_win)
    o = jnp.stack([o_cmp, o_sel, o_win], axis=-1).reshape(B_, Tq, NSA_HEADS, NSA_HEAD_DIM, 3)
    return jnp.einsum('bqhdr,bqhr->bqhd', o.astype(f32), gates)


def nsa_mixer(h, q_pos, ctx, kw, vw, w_pos, sweep, w_in, w_out, rel_bias):
    B_, T, _ = h.shape
    proj = h @ w_in
    q = proj[..., :NSA_DIM].reshape(B_, T, NSA_HEADS, NSA_HEAD_DIM) * (NSA_HEAD_DIM ** -0.5)
    gates = jax.nn.sigmoid(proj[..., NSA_DIM:].astype(jnp.float32)).reshape(B_, T, NSA_HEADS, 3)
    if sweep:
        qb = min(Q_BLOCK, T)
        nb = T // qb
        band = WINDOW + qb
        q_b = jnp.swapaxes(q.reshape(B_, nb, qb, NSA_HEADS, NSA_HEAD_DIM), 0, 1)
        g_b = jnp.swapaxes(gates.reshape(B_, nb, qb, NSA_HEADS, 3), 0, 1)
        starts = jnp.arange(nb, dtype=jnp.int32) * qb

        def one_block(args):
            qi, gi, s = args
            pos = s + jnp.arange(qb, dtype=jnp.int32)
            kwi = lax.dynamic_slice_in_dim(kw, s, band, axis=1)
            vwi = lax.dynamic_slice_in_dim(vw, s, band, axis=1)
            wpi = s - WINDOW + jnp.arange(band, dtype=jnp.int32)
            return nsa_attend(qi, gi, pos, ctx[0], ctx[1], ctx[2], ctx[3], ctx[4], kwi, vwi, wpi, rel_bias)

        o = jnp.swapaxes(lax.map(one_block, (q_b, g_b, starts)), 0, 1)
    else:
        o = nsa_attend(q, gates, q_pos, ctx[0], ctx[1], ctx[2], ctx[3], ctx[4], kw, vw, w_pos, rel_bias)
    return o.reshape(B_, T, NSA_DIM).astype(h.dtype) @ w_out


def run_trunk(x, dn_S0, dn_conv0, past_rows, win_prev, pos0, P):
    B_, T, _ = x.shape
    prompt = past_rows is None
    q_pos = pos0 + jnp.arange(T, dtype=jnp.int32)
    new_S, new_conv = [], []
    ctx = kw = vw = w_pos = kv_rows = new_win = None
    for l in range(DEPTH):
        h = rmsnorm(x, P['norm_mix'][l])
        if l < N_A_LAYERS:
            y, cb, S = deltanet_mixer(h, dn_conv0[l], dn_S0[l], P['dn_w_in'][l], P['dn_conv_w'][l], P['dn_A_log'][l],
                                      P['dn_dt_bias'][l], P['dn_out_norm'][l], P['dn_w_out'][l])
            new_S.append(S)
            new_conv.append(cb)
        else:
            j = l - N_A_LAYERS
            y = nsa_mixer(h, q_pos, ctx, kw, vw, w_pos, prompt, P['nsa_w_in'][j], P['nsa_w_out'][j], P['rel_bias'])
        x = x + y
        x = x + swiglu_ffn(rmsnorm(x, P['norm_ffn'][l]), P['ffn_w_in'][l], P['ffn_w_out'][l])
        if l == N_A_LAYERS - 1:
            rows = shared_rows(x, P['norm_kv'], P['nsa_w_kv'])
            kv_rows = rows[:, :, :4]
            if prompt:
                hist = kv_rows
                win_rows = rows[:, :, 4:]
                pad = ((0, 0), (WINDOW, 0), (0, 0), (0, 0))
                kw = jnp.pad(win_rows[:, :, 0], pad)
                vw = jnp.pad(win_rows[:, :, 1], pad)
                new_win = win_rows[:, T - min(WINDOW, T):]
            else:
                hist = jnp.concatenate([past_rows.astype(kv_rows.dtype), kv_rows], axis=1)
                buf = win_prev.shape[1]
                wb = jnp.concatenate([win_prev.astype(rows.dtype), rows[:, :, 4:]], axis=1)
                kw, vw = wb[:, :, 0], wb[:, :, 1]
                w_pos = pos0 - buf + jnp.arange(buf + T, dtype=jnp.int32)
                new_win = wb[:, T:]
            ctx = nsa_context(hist, P['nsa_cmp_pos_w'], P['nsa_w_cmp'])
    y = rmsnorm(x, P['norm_final'])
    return y, jnp.stack(new_S), jnp.stack(new_conv), kv_rows, new_win


def setup_inputs(seed: int = 0) -> dict:
    key = jax.random.key(seed)
    ks = jax.random.split(key, 26)
    f32 = jnp.float32

    def normal(k, shape, scale):
        return jax.random.normal(k, shape, f32) * scale

    n_pages = PAST_LEN // PAGE_SIZE
    n_used = DEC_BATCH * n_pages
    n_pool = n_used + (n_used + 3) // 4
    win_buf = min(WINDOW, PAST_LEN)
    page_table = jax.random.permutation(ks[6], n_pool)[:n_used].reshape(DEC_BATCH, n_pages).astype(jnp.int32)
    dt = jnp.exp(jax.random.uniform(ks[15], (N_A_LAYERS, DN_HEADS), f32, math.log(1e-3), math.log(1e-1)))
    return {
        'x_prompt': normal(ks[0], (BATCH, SEQ, D_MODEL), 1.0),
        'x_sample': normal(ks[1], (DEC_BATCH, DEC_SEQ, D_MODEL), 1.0),
        'state_dn_S': normal(ks[2], (N_A_LAYERS, DEC_BATCH, DN_HEADS, DN_HEAD_K, DN_HEAD_V), 0.1),
        'state_dn_conv': normal(ks[3], (N_A_LAYERS, DEC_BATCH, DN_CONV - 1, DN_QKV_DIM), 1.0),
        'cache_kv': normal(ks[4], (n_pool, PAGE_SIZE, 4, NSA_KV_HEADS, NSA_HEAD_DIM), 1.0),
        'state_win_kv': normal(ks[5], (DEC_BATCH, win_buf, 2, NSA_KV_HEADS, NSA_HEAD_DIM), 1.0),
        'page_table': page_table,
        'norm_mix': 1.0 + normal(ks[7], (DEPTH, D_MODEL), 0.02),
        'norm_ffn': 1.0 + normal(ks[8], (DEPTH, D_MODEL), 0.02),
        'norm_kv': 1.0 + normal(ks[9], (D_MODEL,), 0.02),
        'norm_final': 1.0 + normal(ks[10], (D_MODEL,), 0.02),
        'ffn_w_in': normal(ks[11], (DEPTH, D_MODEL, 2 * D_FF), D_MODEL ** -0.5),
        'ffn_w_out': normal(ks[12], (DEPTH, D_FF, D_MODEL), D_FF ** -0.5),
        'dn_w_in': normal(ks[13], (N_A_LAYERS, D_MODEL, DN_IN_DIM), D_MODEL ** -0.5),
        'dn_conv_w': normal(ks[14], (N_A_LAYERS, DN_CONV, DN_QKV_DIM), DN_CONV ** -0.5),
        'dn_A_log': jnp.log(jax.random.uniform(ks[16], (N_A_LAYERS, DN_HEADS), f32, 1.0, 16.0)),
        'dn_dt_bias': dt + jnp.log(-jnp.expm1(-dt)),
        'dn_out_norm': 1.0 + normal(ks[17], (N_A_LAYERS, DN_HEAD_V), 0.02),
        'dn_w_out': normal(ks[18], (N_A_LAYERS, DN_VAL_DIM, D_MODEL), DN_VAL_DIM ** -0.5),
        'nsa_w_kv': normal(ks[19], (D_MODEL, NSA_KV_DIM), D_MODEL ** -0.5),
        'nsa_cmp_pos_w': (1.0 + normal(ks[20], (2, CMP_BLOCK, NSA_KV_HEADS, NSA_HEAD_DIM), 0.1)) * CMP_BLOCK ** -0.5,
        'nsa_w_cmp': normal(ks[21], (2, NSA_KV_HEADS, NSA_HEAD_DIM, NSA_HEAD_DIM), NSA_HEAD_DIM ** -0.5),
        'nsa_w_in': normal(ks[22], (N_B_LAYERS, D_MODEL, NSA_IN_DIM), D_MODEL ** -0.5),
        'nsa_w_out': normal(ks[23], (N_B_LAYERS, NSA_DIM, D_MODEL), NSA_DIM ** -0.5),
        'rel_bias': normal(ks[24], (REL_BUCKETS, NSA_HEADS), 0.5),
    }


def reference(x_prompt, x_sample, state_dn_S, state_dn_conv, cache_kv, state_win_kv, page_table,
              norm_mix, norm_ffn, norm_kv, norm_final, ffn_w_in, ffn_w_out,
              dn_w_in, dn_conv_w, dn_A_log, dn_dt_bias, dn_out_norm, dn_w_out,
              nsa_w_kv, nsa_cmp_pos_w, nsa_w_cmp, nsa_w_in, nsa_w_out, rel_bias):
    P = {'norm_mix': norm_mix, 'norm_ffn': norm_ffn, 'norm_kv': norm_kv, 'norm_final': norm_final,
         'ffn_w_in': ffn_w_in, 'ffn_w_out': ffn_w_out, 'dn_w_in': dn_w_in, 'dn_conv_w': dn_conv_w,
         'dn_A_log': dn_A_log, 'dn_dt_bias': dn_dt_bias, 'dn_out_norm': dn_out_norm, 'dn_w_out': dn_w_out,
         'nsa_w_kv': nsa_w_kv, 'nsa_cmp_pos_w': nsa_cmp_pos_w, 'nsa_w_cmp': nsa_w_cmp,
         'nsa_w_in': nsa_w_in, 'nsa_w_out': nsa_w_out, 'rel_bias': rel_bias}
    bp = x_prompt.shape[0]
    p_S0 = jnp.zeros((N_A_LAYERS, bp, DN_HEADS, DN_HEAD_K, DN_HEAD_V), jnp.float32)
    p_conv0 = jnp.zeros((N_A_LAYERS, bp, DN_CONV - 1, DN_QKV_DIM), x_prompt.dtype)
    y_prompt, p_dn_S, p_dn_conv, p_kv_rows, p_win_kv = run_trunk(x_prompt, p_S0, p_conv0, None, None, 0, P)
    past = cache_kv[page_table]
    past = past.reshape(page_table.shape[0], page_table.shape[1] * PAGE_SIZE, 4, NSA_KV_HEADS, NSA_HEAD_DIM)
    y_sample, s_dn_S, s_dn_conv, s_kv_rows, s_win_kv = run_trunk(x_sample, state_dn_S, state_dn_conv, past, state_win_kv, PAST_LEN, P)
    return (y_prompt, y_sample, p_dn_S, p_dn_conv, p_kv_rows, p_win_kv, s_dn_S, s_dn_conv, s_kv_rows, s_win_kv)
```

```python
import math
import numpy as np
import concourse.bass as bass
import concourse.mybir as mybir
from concourse.bass_utils import run_bass_kernel_spmd

F32 = mybir.dt.float32
BF16 = mybir.dt.bfloat16
I32 = mybir.dt.int32
AF = mybir.ActivationFunctionType
ALU = mybir.AluOpType

EPOCH = 16000
NSLOT = 8
ENGS = ('pe', 'act', 'dve', 'pool', 'sp')
SAME_ENGINE_SYNC = {'pe': False, 'act': True, 'dve': True, 'pool': True, 'sp': True}

D = 1024
H = 8
DFF = 2816
NEG = -30000.0


class Buf:
    __slots__ = ('name', 'lw', 'rd')

    def __init__(self, name):
        self.name = name
        self.lw = None
        self.rd = []


class Tile:
    def __init__(self, h, name):
        self.h = h
        self.buf = Buf(name)

    def __getitem__(self, idx):
        return self.h[idx]

    def ap(self):
        return self.h.ap()


class View:
    def __init__(self, base, off, n):
        self.base, self.buf, self.off, self.n = base, base.buf, off, n

    def __getitem__(self, idx):
        idx = list(idx)
        a = idx[1]
        if isinstance(a, slice):
            st = (a.start or 0) + self.off
            en = (a.stop if a.stop is not None else self.n) + self.off
            idx[1] = slice(st, en)
        else:
            idx[1] = a + self.off
        return self.base.h[tuple(idx)]


class Prog:
    def __init__(self, nc):
        self.nc = nc
        self.ops = {e: [] for e in ENGS}
        self.cnt = {e: 0 for e in ENGS}
        self.dcnt = {e: 0 for e in ENGS}
        self.known = {e: {} for e in ENGS}
        self.nt = 0
        self.stack = None

    def sb(self, shape, dtype=F32, name=None):
        self.nt += 1
        name = name or "t"
        if self.stack is not None:
            h = self.stack.enter_context(self.nc.sbuf_tensor(f"{name}_{self.nt}", list(shape), dtype))
        else:
            h = self.nc.alloc_sbuf_tensor(f"{name}_{self.nt}", list(shape), dtype)
        return Tile(h, name)

    def barrier(self):
        evs = []
        for f in ENGS:
            if self.cnt[f] > 0:
                evs.append(('c', f, self.cnt[f]))
            n = self.dcnt[f]
            for slot in range(min(n, NSLOT)):
                evs.append(('d', f, slot, (n - 1 - slot) // NSLOT + 1))
        for e in ENGS:
            waits = []
            for ev in evs:
                if ev[0] == 'c' and ev[1] == e:
                    continue
                self._need(e, ev, waits)
            self.cnt[e] += 1
            self.ops[e].append((waits, (lambda en: en.nop()), ('c', e, self.cnt[e])))

    def ps(self, shape, dtype=F32, name=None):
        self.nt += 1
        name = name or "p"
        h = self.nc.alloc_psum_tensor(f"{name}_{self.nt}", list(shape), dtype)
        return Tile(h, name)

    def dram(self, name, shape, dtype=F32, kind="Internal"):
        h = self.nc.dram_tensor(name, list(shape), dtype, kind=kind)
        return Tile(h, name)

    def _need(self, eng, ev, waits):
        if ev is None:
            return
        if ev[0] == 'c':
            _, f, seq = ev
            if f == eng and not SAME_ENGINE_SYNC[eng]:
                return
            key = ('c', f)
            if self.known[eng].get(key, 0) >= seq:
                return
            self.known[eng][key] = seq
            waits.append(ev)
        else:
            _, q, slot, k = ev
            key = ('d', q, slot)
            if self.known[eng].get(key, 0) >= k:
                return
            self.known[eng][key] = k
            waits.append(ev)

    def add(self, eng, emit, R=(), W=(), dma=False):
        waits = []
        for t in R:
            self._need(eng, t.buf.lw, waits)
        for t in W:
            b = t.buf
            self._need(eng, b.lw, waits)
            for ev in b.rd:
                self._need(eng, ev, waits)
        if dma:
            i = self.dcnt[eng]
            self.dcnt[eng] += 1
            slot, k = i % NSLOT, i // NSLOT + 1
            if k > 1:
                self._need(eng, ('d', eng, slot, k - 1), waits)
            ev = ('d', eng, slot, k)
        else:
            self.cnt[eng] += 1
            ev = ('c', eng, self.cnt[eng])
        for t in R:
            t.buf.rd.append(ev)
        for t in W:
            t.buf.lw = ev
            t.buf.rd = []
        self.ops[eng].append((waits, emit, ev))
        return ev

    def dma(self, out_ap, in_ap, R=(), W=(), q='sp', **kw):
        return self.add(q, lambda e: e.dma_start(out=out_ap, in_=in_ap, **kw), R, W, dma=True)

    def emit(self):
        nc = self.nc
        csem = {}
        for e in ENGS:
            n = (self.cnt[e] + EPOCH - 1) // EPOCH
            csem[e] = [nc.alloc_semaphore(f"c_{e}_{j}") for j in range(n)]
        dsem = {}
        for e in ENGS:
            n = min(self.dcnt[e], NSLOT)
            dsem[e] = [nc.alloc_semaphore(f"d_{e}_{j}") for j in range(n)]

        def semval(ev):
            if ev[0] == 'c':
                _, f, seq = ev
                return csem[f][(seq - 1) // EPOCH], (seq - 1) % EPOCH + 1
            _, q, slot, k = ev
            return dsem[q][slot], 16 * k

        def run(eng, e):
            for waits, emit, ev in self.ops[eng]:
                for w in waits:
                    s, v = semval(w)
                    e.wait_ge(s, v)
                ins = emit(e)
                s, v = semval(ev)
                ins.then_inc(s, 16 if ev[0] == 'd' else 1)
            n = self.dcnt[eng]
            for slot in range(min(n, NSLOT)):
                k = (n - 1 - slot) // NSLOT + 1
                if self.known[eng].get(('d', eng, slot), 0) < k:
                    e.wait_ge(dsem[eng][slot], 16 * k)

        with nc.Block() as block:
            @block.tensor
            def _(e):
                run('pe', e)

            @block.scalar
            def _(e):
                run('act', e)

            @block.vector
            def _(e):
                run('dve', e)

            @block.gpsimd
            def _(e):
                run('pool', e)

            @block.sync
            def _(e):
                run('sp', e)
        return nc


class Ctx:
    def __init__(self, name, TP, NT, NSEQ, NS, nlev):
        self.name = name
        self.TP = TP
        self.NT = NT
        self.GT = TP * NT
        self.NSEQ = NSEQ
        self.TS = self.GT // NSEQ
        self.NCH = self.GT // 64
        self.NS = NS
        self.nlev = nlev


def host_consts(NS, TS):
    seg = np.arange(64) // TS if NS > 1 else np.zeros(64, np.int64)
    j = np.arange(64)[:, None]
    i = np.arange(64)[None, :]
    same = (seg[:, None] == seg[None, :])
    c = {}
    c['ucs'] = ((j <= i) & same).astype(np.float32)
    c['maskT'] = np.where((j <= i) & same, 0.0, NEG).astype(np.float32)
    c['noff'] = -np.where((j != i), 1.0, 0.0).astype(np.float32)
    c['same'] = same.astype(np.float32)
    si = np.zeros((64, NS), np.float32)
    si[np.arange(64), seg] = 1.0
    c['seqind'] = si
    cm = np.zeros((128, NS, 64), np.float32)
    cm[:, seg, np.arange(64)] = 1.0
    c['colmask'] = cm.reshape(128, NS * 64)
    return c


class _Stop(Exception):
    pass


class Builder:
    def __init__(self, cfg):
        self.cfg = cfg
        nc = bass.Bass("TRN2", target_bir_lowering=False)
        self.nc = nc
        self.P = Prog(nc)
        self.rr = {}

    def mm(self, out, lhsT, rhs, start=True, stop=True, R=(), W=()):
        self.P.add('pe', lambda e: e.matmul(out, lhsT=lhsT, rhs=rhs, start=start, stop=stop), R, W)

    def tr(self, out, in_, ident, R=(), W=()):
        self.P.add('pe', lambda e: e.transpose(out=out, in_=in_, identity=ident), R, W)

    def actf(self, out, in_, func, R=(), W=(), bias=None, scale=None, accum_out=None):
        kw = {}
        if bias is not None:
            kw['bias'] = bias
        if scale is not None:
            kw['scale'] = scale
        if accum_out is not None:
            kw['accum_out'] = accum_out
        self.P.add('act', lambda e: e.activation(out=out, in_=in_, func=func, **kw), R, W)

    def cp(self, eng, out, in_, R=(), W=()):
        if eng == 'act':
            self.P.add('act', lambda e: e.copy(out=out, in_=in_), R, W)
        else:
            self.P.add(eng, lambda e: e.tensor_copy(out=out, in_=in_), R, W)

    def tt(self, eng, out, in0, in1, op, R=(), W=()):
        self.P.add(eng, lambda e: e.tensor_tensor(out=out, in0=in0, in1=in1, op=op), R, W)

    def ts(self, eng, out, in0, s1, s2, op0, op1=None, R=(), W=()):
        if op1 is None:
            self.P.add(eng, lambda e: e.tensor_scalar(out=out, in0=in0, scalar1=s1, scalar2=None, op0=op0), R, W)
        else:
            self.P.add(eng, lambda e: e.tensor_scalar(out=out, in0=in0, scalar1=s1, scalar2=s2, op0=op0, op1=op1), R, W)

    def stt(self, eng, out, in0, scalar, in1, op0, op1, R=(), W=()):
        self.P.add(eng, lambda e: e.scalar_tensor_tensor(out=out, in0=in0, scalar=scalar, in1=in1, op0=op0, op1=op1), R, W)

    def memset(self, eng, ap, val, W=()):
        self.P.add(eng, lambda e: e.memset(ap, val), (), W)

    def ring(self, key, n, make):
        if key not in self.rr:
            self.rr[key] = [[make(i) for i in range(n)], 0]
        r = self.rr[key]
        t = r[0][r[1] % n]
        r[1] += 1
        return t

    def declare_io(self):
        P, cfg = self.P, self.cfg
        d = {}

        def inp(name, shape, dt=F32):
            d[name] = P.dram(name, shape, dt, kind="ExternalInput")

        def out(name, shape, dt=F32):
            d[name] = P.dram(name, shape, dt, kind="ExternalOutput")

        SEQ = cfg['SEQ']
        inp('x_prompt', [SEQ, D])
        inp('x_sample', [64, D])
        inp('state_dn_S', [2, 16, H, 128, 128])
        inp('state_dn_conv', [2, 16, 3, 3072])
        inp('state_win_kv', [16, 512, 512])
        inp('norm_mix', [4, D])
        inp('norm_ffn', [4, D])
        inp('norm_kv', [D])
        inp('norm_final', [D])
        inp('ffn_w_in', [4, D, 2 * DFF])
        inp('ffn_w_out', [4, DFF, D])
        inp('dn_w_in', [2, D, 4112])
        inp('dn_conv_w', [2, 4, 3072])
        inp('dn_A_log', [2, H])
        inp('dn_dt_bias', [2, H])
        inp('dn_out_norm', [2, 128])
        inp('dn_w_out', [2, D, D])
        inp('nsa_w_kv', [D, 1536])
        for pre, NS in (('cp_', 1), ('cs_', 16)):
            inp(pre + 'ucs', [64, 64])
            inp(pre + 'maskT', [64, 64])
            inp(pre + 'noff', [64, 64])
            inp(pre + 'same', [64, 64])
            inp(pre + 'seqind', [64, NS])
            inp(pre + 'colmask', [128, NS * 64])
        out('p_dn_S', [2, H, 128, 128])
        out('p_dn_conv', [2, 3, 3072])
        out('p_kv_rows', [SEQ, 1024])
        out('p_win_kv', [512, 512])
        out('s_dn_S', [2, 16, H, 128, 128])
        out('s_dn_conv', [2, 16, 3, 3072])
        out('s_kv_rows', [64, 1024])
        out('s_win_kv', [16, 512, 512])
        out('x2_p', [SEQ, D])
        out('x2_s', [64, D])
        d['wc_dn_in'] = [P.dram(f'wc_dn_in{l}', [32, 128, 8, 128], BF16) for l in range(2)]
        d['wc_ffn_in'] = [P.dram(f'wc_ffn_in{l}', [44, 128, 8, 128], BF16) for l in range(4)]
        d['wb_ffn_out'] = [P.dram(f'wb_ffn_out{l}', [DFF, D], BF16) for l in range(4)]
        d['wb_dn_out'] = [P.dram(f'wb_dn_out{l}', [D, D], BF16) for l in range(2)]
        d['wb_kv'] = P.dram('wb_kv', [D, 1536], BF16)
        self.d = d

    def consts(self):
        P, d = self.P, self.d
        c = {}
        identf = P.sb([128, 128], F32, "identf")
        self.memset('pool', identf[:, :], 1.0, W=[identf])
        P.add('pool', lambda e: e.affine_select(out=identf[:, :], in_=identf[:, :], pattern=[[-1, 128]],
                                                compare_op=ALU.is_equal, fill=0.0, base=0, channel_multiplier=1),
              R=[identf], W=[identf])
        identb = P.sb([128, 128], BF16, "identb")
        self.cp('dve', identb[:, :], identf[:, :], R=[identf], W=[identb])
        onesb = P.sb([128, 128], BF16, "onesb")
        self.memset('pool', onesb[:, :], 1.0, W=[onesb])
        onesf = P.sb([64, 128], F32, "onesf")
        self.memset('pool', onesf[:, :], 1.0, W=[onesf])
        c.update(identf=identf, identb=identb, onesb=onesb, onesf=onesf)
        for pre, NS in (('cp_', 1), ('cs_', 16)):
            for nm, shp in (('ucs', [64, 64]), ('maskT', [64, 64]), ('noff', [64, 64]), ('same', [64, 64]),
                            ('seqind', [64, NS])):
                t = P.sb(shp, F32, pre + nm)
                P.dma(t[:, :], d[pre + nm].ap(), W=[t])
                c[pre + nm] = t
            if NS > 1:
                t = P.sb([128, NS * 64], BF16, pre + 'colmask')
                P.dma(t[:, :], d[pre + 'colmask'].ap(), W=[t], q='pool')
                c[pre + 'colmask'] = t
        cw = P.sb([128, 2, 24, 4], F32, "cw")
        for l in range(2):
            for i in range(4):
                P.dma(cw[:, l, :, i], d['dn_conv_w'].ap()[l, i].rearrange("(c p) -> p c", p=128), W=[cw],
                      allow_slow_non_contiguous=True)
        c['cw'] = cw
        negA = P.sb([128, 2, H], F32, "negA")
        dtb = P.sb([128, 2, H], F32, "dtb")
        P.dma(negA[:, :, :], d['dn_A_log'].ap().rearrange("l h -> (l h)").partition_broadcast(128).rearrange("p (l h) -> p l h", l=2), W=[negA])
        P.dma(dtb[:, :, :], d['dn_dt_bias'].ap().rearrange("l h -> (l h)").partition_broadcast(128).rearrange("p (l h) -> p l h", l=2), W=[dtb])
        self.actf(negA[:, :, :], negA[:, :, :], AF.Exp, R=[negA], W=[negA])
        self.ts('dve', negA[:, :, :], negA[:, :, :], -1.0, None, ALU.mult, R=[negA], W=[negA])
        c.update(negA=negA, dtb=dtb)
        onw = P.sb([128, 2], F32, "onw")
        P.dma(onw[:, :], d['dn_out_norm'].ap().rearrange("l p -> p l"), W=[onw], allow_slow_non_contiguous=True)
        c['onw'] = onw
        wab = P.sb([128, 2, 8, 16], BF16, "wab")
        for l in range(2):
            P.dma(wab[:, l, :, :], d['dn_w_in'].ap()[l, :, 4096:4112].rearrange("(kc p) n -> p kc n", p=128), W=[wab], q='pool')
        c['wab'] = wab
        self.c = c
        self.pb = [P.ps([128, 512], F32, f"pb{i}") for i in range(6)]
        self.pt = [P.ps([128, 1024], BF16, f"pt{i}") for i in range(2)]

    def convert_weights(self):
        P, d = self.P, self.d
        k = [0]
        SW = 2048

        def stage():
            i = k[0]
            k[0] += 1
            f = self.ring('cvf', 2, lambda j: P.sb([128, SW], F32, f"cvf{j}"))
            b = self.ring('cvb', 2, lambda j: P.sb([128, SW], BF16, f"cvb{j}"))
            return f, b, ('act', 'dve', 'pool')[i % 3]

        def conv_chunked(W_ap, dst, nch):
            for g in range(nch // 2):
                f, b, e = stage()
                P.dma(f[:, :].rearrange("p (k n) -> p k n", k=8),
                      W_ap[:, g * 256:(g + 1) * 256].rearrange("(kc p) n -> p kc n", p=128), W=[f], q='sp')
                o = b[:, :].rearrange("p (c k n) -> p k c n", c=2, k=8)
                sv = f[:, :].rearrange("p (k c n) -> p k c n", k=8, c=2)
                self.cp(e, o, sv, R=[f], W=[b])
                P.dma(dst.ap()[g * 2:(g + 1) * 2].rearrange("c p k n -> p c (k n)"),
                      b[:, :].rearrange("p (c kn) -> p c kn", c=2), R=[b], W=[dst], q='act')

        def conv_natural(W_ap, dst, K, N):
            per = max(1, SW // N)
            nk = K // 128
            kc = 0
            while kc < nk:
                m = min(per, nk - kc)
                f, b, e = stage()
                P.dma(f[:, 0:m * N].rearrange("p (k n) -> p k n", k=m),
                      W_ap[kc * 128:(kc + m) * 128, :].rearrange("(k p) n -> p k n", p=128), W=[f], q='sp')
                self.cp(e, b[:, 0:m * N], f[:, 0:m * N], R=[f], W=[b])
                P.dma(dst.ap()[kc * 128:(kc + m) * 128, :].rearrange("(k p) n -> p k n", p=128),
                      b[:, 0:m * N].rearrange("p (k n) -> p k n", k=m), R=[b], W=[dst], q='act')
                kc += m

        for l in range(self.cfg['n_dn']):
            conv_chunked(d['dn_w_in'].ap()[l, :, 0:4096], d['wc_dn_in'][l], 32)
            conv_natural(d['dn_w_out'].ap()[l], d['wb_dn_out'][l], D, D)
            conv_chunked(d['ffn_w_in'].ap()[l], d['wc_ffn_in'][l], 44)
            conv_natural(d['ffn_w_out'].ap()[l], d['wb_ffn_out'][l], DFF, D)
        conv_natural(d['nsa_w_kv'].ap(), d['wb_kv'], D, 1536)
        if self.cfg.get('nsa', True):
            for j in range(2):
                conv_chunked(d['nsa_w_in'].ap()[j, :, 0:1024], d['wc_nsa_in'][j], 8)
                conv_natural(d['nsa_w_out'].ap()[j], d['wb_nsa_out'][j], D, D)
                conv_chunked(d['ffn_w_in'].ap()[2 + j], d['wc_ffn_in'][2 + j], 44)
                conv_natural(d['ffn_w_out'].ap()[2 + j], d['wb_ffn_out'][2 + j], DFF, D)

    def alloc_ctx(self, G):
        P = self.P
        n = G.name
        T = {}
        T['x'] = P.sb([G.TP, G.NT, D], F32, n + "x")
        T['xnT'] = P.sb([128, 8, G.GT], BF16, n + "xnT")
        big = P.sb([128, 24, G.GT], BF16, n + "big")
        T['qT'] = View(big, 0, 8)
        T['kT'] = View(big, 8, 8)
        T['vT'] = View(big, 16, 8)
        T['actT'] = View(big, 0, 22)
        T['zs'] = P.sb([128, H, G.GT], BF16, n + "zs")
        T['OT'] = P.sb([128, H, G.GT], F32, n + "OT")
        T['oT'] = P.sb([128, H, G.GT], BF16, n + "oT")
        T['carry'] = [P.sb([128, 24, G.NSEQ, 3], F32, n + f"carry{l}") for l in range(2)]
        T['ab'] = P.sb([64, G.NCH, 16], F32, n + "ab")
        T['g'] = P.sb([64, G.NCH, H], F32, n + "g")
        T['beta'] = P.sb([64, G.NCH, H], F32, n + "beta")
        G.T = T

    def norm_T(self, G, wrow_src):
        P, c, T = self.P, self.c, G.T
        TP = G.TP
        wrow = self.ring('wrow', 2, lambda j: P.sb([128, D], F32, f"wrow{j}"))
        P.dma(wrow[:, :], wrow_src.partition_broadcast(128), W=[wrow])
        x = T['x']
        for t in range(G.NT):
            junk = self.ring('junk', 1, lambda j: P.sb([128, D], BF16, f"junk{j}"))
            st = self.ring('nst', 4, lambda j: P.sb([128, 2], F32, f"nst{j}"))
            self.memset('pool', st[:, :], 0.0, W=[st])
            self.actf(junk[0:TP, :], x[:, t, :], AF.Square, R=[x], W=[junk, st], accum_out=st[0:TP, 0:1])
            self.ts('dve', st[0:TP, 1:2], st[0:TP, 0:1], 1.0 / D, 1e-6, ALU.mult, ALU.add, R=[st], W=[st])
            self.actf(st[0:TP, 1:2], st[0:TP, 1:2], AF.Sqrt, R=[st], W=[st])
            P.add('dve', lambda e, st=st: e.reciprocal(out=st[0:TP, 1:2], in_=st[0:TP, 1:2]), R=[st], W=[st])
            xn = self.ring('xn', 2, lambda j: P.sb([128, D], BF16, f"xn{j}"))
            self.stt('dve', xn[0:TP, :], x[:, t, :], st[0:TP, 1:2], wrow[0:TP, :], ALU.mult, ALU.mult,
                     R=[x, st, wrow], W=[xn])
            pt = self.pt[0]
            for cc in range(8):
                self.tr(pt[:, cc * 128:cc * 128 + TP], xn[0:TP, cc * 128:(cc + 1) * 128], c['identb'][0:TP, 0:TP],
                        R=[xn, c['identb']], W=[pt])
            self.cp('act', T['xnT'][:, :, t * TP:(t + 1) * TP],
                    pt[:, :].rearrange("p (c n) -> p c n", c=8)[:, :, 0:TP], R=[pt], W=[T['xnT']])

    def load_wchunk(self, src_ap):
        P = self.P
        wt = self.ring('wch', 4, lambda j: P.sb([128, 8, 128], BF16, f"wch{j}"))
        P.dma(wt[:, :, :], src_ap, W=[wt])
        return wt

    def next_pb(self, key, banks):
        r = self.rr.setdefault('pb_' + key, [0])
        b = banks[r[0] % len(banks)]
        r[0] += 1
        return self.pb[b]

    def dn_mixer(self, G, l, S_tiles, last_group):
        P, c, T, d = self.P, self.c, G.T, self.d
        GT, TP, NT, NSEQ, TS, NCH, NS = G.GT, G.TP, G.NT, G.NSEQ, G.TS, G.NCH, G.NS
        pre = 'cp_' if NS == 1 else 'cs_'
        self.norm_T(G, d['norm_mix'].ap()[l])
        xnT = T['xnT']
        pbs = self.pb[0]
        for n in range(NCH):
            for kc in range(8):
                self.mm(pbs[0:64, n * 16:(n + 1) * 16], xnT[:, kc, n * 64:(n + 1) * 64], c['wab'][:, l, kc, :],
                        start=(kc == 0), stop=(kc == 7), R=[xnT, c['wab']], W=[pbs])
        ab = T['ab']
        self.cp('act', ab[:, :, :], pbs[0:64, 0:NCH * 16].rearrange("p (n k) -> p n k", k=16), R=[pbs], W=[ab])
        gt = self.ring('gtmp', 2, lambda j: P.sb([64, 8, H], F32, f"gtmp{j}"))
        g2 = self.ring('gtmp', 2, lambda j: None)
        av = gt[:, 0:NCH, :]
        a2 = g2[:, 0:NCH, :]
        self.tt('dve', av, ab[:, :, 0:8], c['dtb'][0:64, l, :].unsqueeze(1).to_broadcast([64, NCH, H]), ALU.add,
                R=[ab, c['dtb']], W=[gt])
        self.actf(a2, av, AF.Abs, R=[gt], W=[g2])
        self.actf(a2, a2, AF.Exp, R=[g2], W=[g2], scale=-1.0)
        self.actf(a2, a2, AF.Ln, R=[g2], W=[g2], bias=1.0)
        self.stt('dve', a2, av, 0.0, a2, ALU.max, ALU.add, R=[gt, g2], W=[g2])
        self.tt('dve', T['g'][:, :, :], a2, c['negA'][0:64, l, :].unsqueeze(1).to_broadcast([64, NCH, H]), ALU.mult,
                R=[g2, c['negA']], W=[T['g']])
        self.actf(T['beta'][:, :, :], ab[:, :, 8:16], AF.Sigmoid, R=[ab], W=[T['beta']])

        carry = T['carry'][l]
        for ci in range(32):
            wt = self.load_wchunk(d['wc_dn_in'][l].ap()[ci])
            ps = self.next_pb('lin', [1, 2, 3])
            for kc in range(8):
                self.mm(ps[:, 0:GT], wt[:, kc, :], xnT[:, kc, :], start=(kc == 0), stop=(kc == 7), R=[wt, xnT], W=[ps])
            hh = ci % 8
            if ci >= 24:
                self.actf(T['zs'][:, hh, :], ps[:, 0:GT], AF.Silu, R=[ps], W=[T['zs']])
                continue
            xp = self.ring(G.name + 'xp', 2, lambda j: P.sb([128, NSEQ, TS + 3], F32, G.name + f"xp{j}"))
            self.cp('act', xp[:, :, 3:TS + 3], ps[:, 0:GT].rearrange("p (s t) -> p s t", s=NSEQ), R=[ps], W=[xp])
            self.cp('pool', xp[:, :, 0:3], carry[:, ci, :, :], R=[carry], W=[xp])
            self.cp('pool', carry[:, ci, :, :], xp[:, :, TS:TS + 3], R=[xp], W=[carry])
            acc = self.ring(G.name + 'acc', 2, lambda j: P.sb([128, NSEQ, TS], F32, G.name + f"acc{j}"))
            cwl = c['cw']
            self.ts('dve', acc[:, :, :], xp[:, :, 0:TS], cwl[:, l, ci, 0:1], None, ALU.mult, R=[xp, cwl], W=[acc])
            for i in range(1, 4):
                self.stt('dve', acc[:, :, :], xp[:, :, i:TS + i], cwl[:, l, ci, i:i + 1], acc[:, :, :], ALU.mult, ALU.add,
                         R=[xp, cwl, acc], W=[acc])
            accf = acc[:, :, :].rearrange("p s t -> p (s t)")
            if ci >= 16:
                self.actf(T['vT'][:, hh, :], accf, AF.Silu, R=[acc], W=[T['vT']])
                continue
            sl = self.ring(G.name + 'sl', 2, lambda j: P.sb([128, GT], F32, G.name + f"sl{j}"))
            self.actf(sl[:, :], accf, AF.Silu, R=[acc], W=[sl])
            sq = self.ring(G.name + 'sq', 2, lambda j: P.sb([128, GT], BF16, G.name + f"sq{j}"))
            self.actf(sq[:, :], sl[:, :], AF.Square, R=[sl], W=[sq])
            ps2 = self.next_pb('nrm', [4, 5])
            self.mm(ps2[:, 0:GT], c['onesb'][:, :], sq[:, :], R=[c['onesb'], sq], W=[ps2])
            rn = self.ring(G.name + 'rn', 2, lambda j: P.sb([128, GT], F32, G.name + f"rn{j}"))
            self.ts('dve', rn[:, :], ps2[:, 0:GT], 1e-6, None, ALU.add, R=[ps2], W=[rn])
            self.actf(rn[:, :], rn[:, :], AF.Sqrt, R=[rn], W=[rn])
            P.add('dve', lambda e, rn=rn: e.reciprocal(out=rn[:, :], in_=rn[:, :]), R=[rn], W=[rn])
            dst = T['qT'] if ci < 8 else T['kT']
            scl = (128 ** -0.5) if ci < 8 else 1.0
            self.stt('dve', dst[:, hh, :], sl[:, :], scl, rn[:, :], ALU.mult, ALU.mult, R=[sl, rn], W=[dst])

        for n in range(NCH):
            self.dn_chunk(G, l, n, S_tiles, pre)

        OT, oT, zs = T['OT'], T['oT'], T['zs']
        for hh in range(H):
            sq = self.ring(G.name + 'sq', 2, lambda j: None)
            self.actf(sq[:, :], OT[:, hh, :], AF.Square, R=[OT], W=[sq])
            ps2 = self.next_pb('nrm', [4, 5])
            self.mm(ps2[:, 0:GT], c['onesb'][:, :], sq[:, :], R=[c['onesb'], sq], W=[ps2])
            rn = self.ring(G.name + 'rn', 2, lambda j: None)
            self.ts('dve', rn[:, :], ps2[:, 0:GT], 1.0 / 128, 1e-6, ALU.mult, ALU.add, R=[ps2], W=[rn])
            self.actf(rn[:, :], rn[:, :], AF.Sqrt, R=[rn], W=[rn])
            P.add('dve', lambda e, rn=rn: e.reciprocal(out=rn[:, :], in_=rn[:, :]), R=[rn], W=[rn])
            self.stt('dve', rn[:, :], OT[:, hh, :], c['onw'][:, l:l + 1], rn[:, :], ALU.mult, ALU.mult,
                     R=[OT, c['onw'], rn], W=[rn])
            self.tt('dve', oT[:, hh, :], rn[:, :], zs[:, hh, :], ALU.mult, R=[rn, zs], W=[oT])

        self.tok_linear_add(G, oT, 8, d['wb_dn_out'][l])

        if last_group:
            self.conv_state_out(G, l)

    def tok_linear_add(self, G, aT, nk, wsrc):
        P, T = self.P, G.T
        TP, NT = G.TP, G.NT
        x = T['x']
        for half in range(2):
            banks = [self.pb[1 + t] for t in range(NT)]
            for kc in range(nk):
                wt = self.ring('wrh', 4, lambda j: P.sb([128, 512], BF16, f"wrh{j}"))
                P.dma(wt[:, :], wsrc.ap()[kc * 128:(kc + 1) * 128, half * 512:(half + 1) * 512], R=[wsrc], W=[wt])
                for t in range(NT):
                    self.mm(banks[t][0:TP, :], aT[:, kc, t * TP:(t + 1) * TP], wt[:, :], start=(kc == 0),
                            stop=(kc == nk - 1), R=[aT, wt], W=[banks[t]])
            for t in range(NT):
                self.tt('dve', x[:, t, half * 512:(half + 1) * 512], x[:, t, half * 512:(half + 1) * 512],
                        banks[t][0:TP, :], ALU.add, R=[x, banks[t]], W=[x])

    def conv_state_out(self, G, l):
        P, c, T, d = self.P, self.c, G.T, self.d
        NSEQ = G.NSEQ
        R3 = NSEQ * 3
        carry = T['carry'][l]
        for g4 in range(6):
            co = self.ring(G.name + 'co', 2, lambda j: P.sb([R3, 4, 128], F32, G.name + f"co{j}"))
            ps = self.next_pb('lin', [1, 2, 3])
            for j in range(4):
                ci = g4 * 4 + j
                self.tr(ps[0:R3, j * 128:(j + 1) * 128], carry[:, ci, :, :].rearrange("p s r -> p (s r)"),
                        c['identf'][:, :], R=[carry, c['identf']], W=[ps])
            self.cp('act', co[:, :, :], ps[0:R3, :].rearrange("p (j n) -> p j n", j=4), R=[ps], W=[co])
            if G.NS == 1:
                dst = d['p_dn_conv']
                P.dma(dst.ap()[l][:, g4 * 512:(g4 + 1) * 512].rearrange("r (c p) -> r c p", p=128), co[:, :, :],
                      R=[co], W=[dst], q='pool')
            else:
                dst = d['s_dn_conv']
                P.dma(dst.ap()[l][:, :, g4 * 512:(g4 + 1) * 512].rearrange("s r (c p) -> (s r) c p", p=128), co[:, :, :],
                      R=[co], W=[dst], q='pool')

    def conv_state_in(self, G, l):
        P, c, T, d = self.P, self.c, G.T, self.d
        R3 = G.NSEQ * 3
        carry = T['carry'][l]
        ci_t = self.ring(G.name + 'cin', 1, lambda j: P.sb([R3, 3072], F32, G.name + "cin"))
        P.dma(ci_t[:, :], d['state_dn_conv'].ap()[l].rearrange("s r n -> (s r) n"), W=[ci_t])
        for g4 in range(6):
            ps = self.next_pb('lin', [1, 2, 3])
            for j in range(4):
                ci = g4 * 4 + j
                self.tr(ps[:, j * R3:(j + 1) * R3], ci_t[:, ci * 128:(ci + 1) * 128], c['identf'][0:R3, 0:R3],
                        R=[ci_t, c['identf']], W=[ps])
            self.cp('act', carry[:, g4 * 4:(g4 + 1) * 4, :, :].rearrange("p c s r -> p c (s r)"),
                    ps[:, 0:4 * R3].rearrange("p (j n) -> p j n", j=4), R=[ps], W=[carry])

    def dn_chunk(self, G, l, n, S_tiles, pre):
        P, c, T = self.P, self.c, G.T
        NS = G.NS
        cs = slice(n * 64, (n + 1) * 64)
        qT, kT, vT = T['qT'], T['kT'], T['vT']
        ucs, maskT, noff, same, seqind = (c[pre + k] for k in ('ucs', 'maskT', 'noff', 'same', 'seqind'))
        gtok = T['g']
        beta = T['beta']
        pb = self.pb
        nm = G.name

        def sbt(key, shape, dt=F32, nbuf=2):
            pfx = nm if NS > 1 and key in ('SG', 'gl') else 'ck'
            return self.ring(pfx + key, nbuf, lambda j: P.sb(shape, dt, pfx + key + str(j)))

        self.mm(pb[0][0:64, 0:8], ucs[:, :], gtok[:, n, :], R=[ucs, gtok], W=[pb[0]])
        self.mm(pb[0][0:64, 8:16], same[:, :], gtok[:, n, :], R=[same, gtok], W=[pb[0]])
        Gt = sbt('Gt', [64, 16])
        self.cp('act', Gt[:, :], pb[0][0:64, 0:16], R=[pb[0]], W=[Gt])
        eGd = sbt('eGd', [64, 16])
        self.tt('dve', eGd[:, 8:16], Gt[:, 8:16], Gt[:, 0:8], ALU.subtract, R=[Gt], W=[eGd])
        self.cp('dve', eGd[:, 0:8], Gt[:, 0:8], R=[Gt], W=[eGd])
        self.actf(eGd[:, :], eGd[:, :], AF.Exp, R=[eGd], W=[eGd])
        SG = sbt('SG', [64, H, NS])
        self.tt('dve', SG[:, :, :], gtok[:, n, :].unsqueeze(2).to_broadcast([64, H, NS]),
                seqind[:, :].unsqueeze(1).to_broadcast([64, H, NS]), ALU.mult, R=[gtok, seqind], W=[SG])
        self.mm(pb[0][:, 16:16 + H * NS], c['onesf'][:, :], SG[:, :, :].rearrange("p h s -> p (h s)"),
                R=[c['onesf'], SG], W=[pb[0]])
        gl = sbt('gl', [128, H, NS])
        self.actf(gl[:, :, :].rearrange("p h s -> p (h s)"), pb[0][:, 16:16 + H * NS], AF.Exp, R=[pb[0]], W=[gl])
        UG = sbt('UG', nbuf=1, shape=[64, H, 64])
        self.tt('dve', UG[:, :, :], ucs[:, :].unsqueeze(1).to_broadcast([64, H, 64]),
                gtok[:, n, :].unsqueeze(2).to_broadcast([64, H, 64]), ALU.mult, R=[ucs, gtok], W=[UG])
        for hh in range(H):
            self.mm(pb[1][:, hh * 64:(hh + 1) * 64], c['onesf'][:, :], UG[:, hh, :], R=[c['onesf'], UG], W=[pb[1]])
        eGrow = sbt('eGrow', nbuf=1, shape=[128, H, 64])
        self.actf(eGrow[:, :, :].rearrange("p h i -> p (h i)"), pb[1][:, :], AF.Exp, R=[pb[1]], W=[eGrow])
        qgT = sbt('qgT', [128, H, 64], BF16)
        self.tt('dve', qgT[:, :, :], qT[:, :, cs], eGrow[:, :, :], ALU.mult, R=[qT, eGrow], W=[qgT])
        tmp = sbt('dtmp', nbuf=1, shape=[64, H, 64])
        self.tt('dve', tmp[:, :, :], pb[1][0:64, :].rearrange("p (h i) -> p h i", h=H),
                Gt[:, 0:8].unsqueeze(2).to_broadcast([64, H, 64]), ALU.subtract, R=[pb[1], Gt], W=[tmp])
        self.tt('pool', tmp[:, :, :], tmp[:, :, :], maskT[:, :].unsqueeze(1).to_broadcast([64, H, 64]), ALU.add,
                R=[tmp, maskT], W=[tmp])
        DT = sbt('DT', nbuf=1, shape=[64, H, 64])
        self.actf(DT[:, :, :], tmp[:, :, :], AF.Exp, R=[tmp], W=[DT])
        for hh in range(H):
            self.mm(pb[2][0:64, hh * 64:(hh + 1) * 64], kT[:, hh, cs], kT[:, hh, cs], R=[kT], W=[pb[2]])
        for hh in range(H):
            self.mm(pb[3][0:64, hh * 64:(hh + 1) * 64], kT[:, hh, cs], qT[:, hh, cs], R=[kT, qT], W=[pb[3]])
        aqkT = sbt('aqkT', [64, H, 64], BF16)
        self.tt('dve', aqkT[:, :, :], DT[:, :, :], pb[3][0:64, :].rearrange("p (h i) -> p h i", h=H), ALU.mult,
                R=[DT, pb[3]], W=[aqkT])
        nbo = sbt('nbo', nbuf=1, shape=[64, H, 64])
        self.tt('pool', nbo[:, :, :], beta[:, n, :].unsqueeze(2).to_broadcast([64, H, 64]),
                noff[:, :].unsqueeze(1).to_broadcast([64, H, 64]), ALU.mult, R=[beta, noff], W=[nbo])
        X = sbt('X', [64, H, 64])
        self.tt('dve', X[:, :, :], pb[2][0:64, :].rearrange("p (h i) -> p h i", h=H), nbo[:, :, :], ALU.mult,
                R=[pb[2], nbo], W=[X])
        self.tt('dve', X[:, :, :], X[:, :, :], DT[:, :, :], ALU.mult, R=[X, DT], W=[X])
        for hh in range(H):
            self.tr(pb[4][0:64, hh * 64:(hh + 1) * 64], X[:, hh, :], c['identf'][0:64, 0:64], R=[X, c['identf']], W=[pb[4]])
        Z = sbt('Z', [64, H, 64])
        self.cp('act', Z[:, :, :].rearrange("p h i -> p (h i)"), pb[4][0:64, :], R=[pb[4]], W=[Z])
        Pm = sbt('Pm', [64, H, 64])
        self.tt('pool', Pm[:, :, :], X[:, :, :], c['identf'][0:64, 0:64].unsqueeze(1).to_broadcast([64, H, 64]), ALU.add,
                R=[X, c['identf']], W=[Pm])
        Y = X
        for lv in range(G.nlev):
            last = (lv == G.nlev - 1)
            if not last:
                for hh in range(H):
                    self.mm(pb[5][0:64, hh * 64:(hh + 1) * 64], Z[:, hh, :], Y[:, hh, :], R=[Z, Y], W=[pb[5]])
            for hh in range(H):
                self.mm(pb[4][0:64, hh * 64:(hh + 1) * 64], Y[:, hh, :], Z[:, hh, :], R=[Z, Y], W=[pb[4]])
            Zn = sbt('Z', [64, H, 64])
            self.cp('act', Zn[:, :, :].rearrange("p h i -> p (h i)"), pb[4][0:64, :], R=[pb[4]], W=[Zn])
            if not last:
                Yn = sbt('X', [64, H, 64])
                self.cp('dve', Yn[:, :, :].rearrange("p h i -> p (h i)"), pb[5][0:64, :], R=[pb[5]], W=[Yn])
                Y = Yn
            Z = Zn
            for hh in range(H):
                self.mm(pb[2][0:64, hh * 64:(hh + 1) * 64], Z[:, hh, :], Pm[:, hh, :], R=[Z, Pm], W=[pb[2]])
            Pn = sbt('Pm', [64, H, 64])
            self.tt('dve', Pn[:, :, :].rearrange("p h i -> p (h i)"), Pm[:, :, :].rearrange("p h i -> p (h i)"),
                    pb[2][0:64, :], ALU.add, R=[Pm, pb[2]], W=[Pn])
            Pm = Pn
        vtok = sbt('vtok', [64, H, 128], BF16, 1)
        ktok = sbt('ktok', [64, H, 128], BF16, 1)
        for src, dst, pt in ((vT, vtok, self.pt[0]), (kT, ktok, self.pt[1])):
            for hh in range(H):
                self.tr(pt[0:64, hh * 128:(hh + 1) * 128], src[:, hh, cs], c['identb'][:, :], R=[src, c['identb']], W=[pt])
            self.cp('act', dst[:, :, :].rearrange("p h d -> p (h d)"), pt[0:64, :], R=[pt], W=[dst])
        kdec = sbt('kdec', [64, H, 128], BF16)
        self.tt('pool', kdec[:, :, :], ktok[:, :, :], eGd[:, 8:16].unsqueeze(2).to_broadcast([64, H, 128]), ALU.mult,
                R=[ktok, eGd], W=[kdec])

        OT = T['OT']
        for hh in range(H):
            Sf, Sb = S_tiles(hh)
            if NS > 1:
                cm = c[pre + 'colmask']
                kTm = sbt('kTm', [128, NS, 64], BF16)
                self.tt('pool', kTm[:, :, :], kT[:, hh, cs].unsqueeze(1).to_broadcast([128, NS, 64]),
                        cm[:, :].rearrange("p (s i) -> p s i", s=NS), ALU.mult, R=[kT, cm], W=[kTm])
                qgm = sbt('qgm', [128, NS, 64], BF16)
                self.tt('pool', qgm[:, :, :], qgT[:, hh, :].unsqueeze(1).to_broadcast([128, NS, 64]),
                        cm[:, :].rearrange("p (s i) -> p s i", s=NS), ALU.mult, R=[qgT, cm], W=[qgm])
                kdm = sbt('kdm', [64, NS, 128], BF16)
                self.tt('pool', kdm[:, :, :], kdec[:, hh, :].unsqueeze(1).to_broadcast([64, NS, 128]),
                        seqind[:, :].unsqueeze(2).to_broadcast([64, NS, 128]), ALU.mult, R=[kdec, seqind], W=[kdm])
            pks = pb[0]
            for s in range(NS):
                lhs = kTm[:, s, :] if NS > 1 else kT[:, hh, cs]
                self.mm(pks[0:64, 0:128], lhs, Sb[:, s, :], start=(s == 0), stop=(s == NS - 1),
                        R=[kTm if NS > 1 else kT, Sb], W=[pks])
            r = sbt('r', [64, 128])
            self.ts('dve', r[:, :], pks[0:64, 0:128], eGd[:, hh:hh + 1], None, ALU.mult, R=[pks, eGd], W=[r])
            self.tt('dve', r[:, :], vtok[:, hh, :], r[:, :], ALU.subtract, R=[vtok, r], W=[r])
            pu = pb[1]
            self.mm(pu[0:64, 0:128], Pm[:, hh, :], r[:, :], R=[Pm, r], W=[pu])
            U = sbt('U', [64, 128], BF16)
            self.ts('dve', U[:, :], pu[0:64, 0:128], beta[:, n, hh:hh + 1], None, ALU.mult, R=[pu, beta], W=[U])
            po = pb[3]
            for s in range(NS):
                rhs = qgm[:, s, :] if NS > 1 else qgT[:, hh, :]
                self.mm(po[:, 0:64], Sb[:, s, :], rhs, start=(s == 0), stop=False, R=[Sb, qgm if NS > 1 else qgT], W=[po])
            self.mm(po[:, 0:64], U[:, :], aqkT[:, hh, :], start=False, stop=True, R=[U, aqkT], W=[po])
            self.cp('act', OT[:, hh, cs], po[:, 0:64], R=[po], W=[OT])
            for s0 in range(0, NS, 4):
                psn = pb[5] if (s0 // 4) % 2 == 0 else pb[4]
                ns = min(4, NS - s0)
                for s in range(s0, s0 + ns):
                    lhs = kdm[:, s, :] if NS > 1 else kdec[:, hh, :]
                    self.mm(psn[:, (s - s0) * 128:(s - s0 + 1) * 128], lhs, U[:, :], R=[kdm if NS > 1 else kdec, U], W=[psn])
                self.tt('dve', Sf[:, s0:s0 + ns, :], Sf[:, s0:s0 + ns, :],
                        gl[:, hh, s0:s0 + ns].unsqueeze(2).to_broadcast([128, ns, 128]), ALU.mult, R=[Sf, gl], W=[Sf])
                self.tt('dve', Sf[:, s0:s0 + ns, :], Sf[:, s0:s0 + ns, :],
                        psn[:, 0:ns * 128].rearrange("p (s d) -> p s d", s=ns), ALU.add, R=[Sf, psn], W=[Sf])
                self.cp('act', Sb[:, s0:s0 + ns, :], Sf[:, s0:s0 + ns, :], R=[Sf], W=[Sb])

    def ffn(self, G, l):
        P, c, T, d = self.P, self.c, G.T, self.d
        GT = G.GT
        self.norm_T(G, d['norm_ffn'].ap()[l])
        xnT, actT = T['xnT'], T['actT']
        for ci in range(22):
            wg = self.load_wchunk(d['wc_ffn_in'][l].ap()[ci])
            wu = self.load_wchunk(d['wc_ffn_in'][l].ap()[22 + ci])
            pg = self.next_pb('ffg', [1, 2])
            pu = self.next_pb('ffu', [3, 4])
            for kc in range(8):
                self.mm(pg[:, 0:GT], wg[:, kc, :], xnT[:, kc, :], start=(kc == 0), stop=(kc == 7), R=[wg, xnT], W=[pg])
            for kc in range(8):
                self.mm(pu[:, 0:GT], wu[:, kc, :], xnT[:, kc, :], start=(kc == 0), stop=(kc == 7), R=[wu, xnT], W=[pu])
            sg = self.ring(G.name + 'sl', 2, lambda j: P.sb([128, GT], F32, G.name + f"sl{j}"))
            self.actf(sg[:, :], pg[:, 0:GT], AF.Silu, R=[pg], W=[sg])
            self.tt('dve', actT[:, ci, :], sg[:, :], pu[:, 0:GT], ALU.mult, R=[sg, pu], W=[actT])
        self.tok_linear_add(G, actT, 22, d['wb_ffn_out'][l])

    def shared_rows(self, G, dst_kv, row0, win_cb):
        P, c, T, d = self.P, self.c, G.T, self.d
        TP, NT = G.TP, G.NT
        self.norm_T(G, d['norm_kv'].ap())
        xnT = T['xnT']
        for third in range(3):
            wt = self.ring('kvw', 2, lambda j: P.sb([128, 8, 512], BF16, f"kvw{j}"))
            P.dma(wt[:, :, :], d['wb_kv'].ap()[:, third * 512:(third + 1) * 512].rearrange("(kc p) n -> p kc n", p=128),
                  R=[d['wb_kv']], W=[wt])
            for t in range(NT):
                ps = self.next_pb('lin', [1, 2, 3])
                for kc in range(8):
                    self.mm(ps[0:TP, :], xnT[:, kc, t * TP:(t + 1) * TP], wt[:, kc, :], start=(kc == 0), stop=(kc == 7),
                            R=[xnT, wt], W=[ps])
                rp = self.ring(G.name + 'rowp', 2, lambda j: P.sb([TP, 512], F32, G.name + f"rowp{j}"))
                self.cp('act', rp[:, :], ps[0:TP, :], R=[ps], W=[rp])
                if third < 2:
                    P.dma(dst_kv.ap()[row0 + t * TP: row0 + (t + 1) * TP, third * 512:(third + 1) * 512], rp[:, :],
                          R=[rp], W=[dst_kv], q='pool')
                else:
                    win_cb(t, rp)

    def phase(self):
        from contextlib import ExitStack
        b = self

        class _Ph:
            def __enter__(self_):
                self_.st = ExitStack()
                b.P.stack = self_.st
                b.rr = {}
                return self_

            def __exit__(self_, *a):
                b.P.barrier()
                b.P.stack = None
                b.rr = {}
                self_.st.close()
                return False
        return _Ph()

    def build(self):
        P, cfg = self.P, self.cfg
        self.declare_io()
        d = self.d
        nsa = cfg.get('nsa', True)
        if nsa:
            self.nsa_declare()
        self.consts()
        if nsa:
            self.nsa_consts()
        with self.phase():
            self.convert_weights()
        if nsa:
            with self.phase():
                self.nsa_tables()
        n_dn = cfg['n_dn']
        if cfg.get('prompt', True):
          with self.phase():
            G = Ctx('p', 128, 4, 1, 1, 5)
            self.alloc_ctx(G)
            T = G.T
            Sp = [[(P.sb([128, 1, 128], F32, f"Sp{l}_{h}"), P.sb([128, 1, 128], BF16, f"Sbp{l}_{h}")) for h in range(H)]
                  for l in range(2)]
            for l in range(2):
                for h in range(H):
                    self.memset('pool', Sp[l][h][0][:, :, :], 0.0, W=[Sp[l][h][0]])
                    self.memset('pool', Sp[l][h][1][:, :, :], 0.0, W=[Sp[l][h][1]])
                self.memset('pool', T['carry'][l][:, :, :, :], 0.0, W=[T['carry'][l]])
            ngrp = cfg['SEQ'] // G.GT
            for g in range(ngrp):
                P.dma(T['x'][:, :, :], d['x_prompt'].ap()[g * 512:(g + 1) * 512, :].rearrange("(t p) n -> p t n", p=128),
                      W=[T['x']])
                for l in range(n_dn):
                    self.dn_mixer(G, l, lambda hh, l=l: Sp[l][hh], last_group=(g == ngrp - 1))
                    self.ffn(G, l)
                P.dma(d['x2_p'].ap()[g * 512:(g + 1) * 512, :].rearrange("(t p) n -> p t n", p=128), T['x'][:, :, :],
                      R=[T['x']], W=[d['x2_p']], q='pool')

                def win_cb(t, rp, g=g):
                    r0 = g * 512 + t * 128 - (cfg['SEQ'] - 512)
                    if nsa:
                        P.dma(d['pwin_d'].ap()[g * 512 + t * 128:g * 512 + (t + 1) * 128, :], rp[:, :], R=[rp],
                              W=[d['pwin_d']], q='pool')
                    if r0 >= 0:
                        P.dma(d['p_win_kv'].ap()[r0:r0 + 128, :], rp[:, :], R=[rp], W=[d['p_win_kv']], q='pool')
                self.shared_rows(G, d['p_kv_rows'], g * 512, win_cb)
            for l in range(2):
                for h in range(H):
                    P.dma(d['p_dn_S'].ap()[l, h], Sp[l][h][0][:, 0, :], R=[Sp[l][h][0]], W=[d['p_dn_S']], q='pool')
        if nsa and cfg.get('prompt', True) and cfg.get('nsa_stop', 99) > 1:
            try:
                self.nsa_prompt()
            except _Stop:
                pass
        if cfg.get('sample', True):
          with self.phase():
            G = Ctx('s', 64, 1, 16, 16, 1)
            self.alloc_ctx(G)
            T = G.T
            P.dma(T['x'][:, 0, :], d['x_sample'].ap(), W=[T['x']])
            Ss = P.sb([128, 16, 128], F32, "Ss")
            Ssb = P.sb([128, 16, 128], BF16, "Ssb")
            for l in range(n_dn):
                self.conv_state_in(G, l)
                cur = [None]

                def S_tiles(hh, l=l, cur=cur):
                    if cur[0] != hh:
                        if cur[0] is not None:
                            P.dma(d['s_dn_S'].ap()[l, :, cur[0]].rearrange("s k v -> k s v"), Ss[:, :, :], R=[Ss],
                                  W=[d['s_dn_S']], q='pool')
                        P.dma(Ss[:, :, :], d['state_dn_S'].ap()[l, :, hh].rearrange("s k v -> k s v"), W=[Ss])
                        self.cp('act', Ssb[:, :, :], Ss[:, :, :], R=[Ss], W=[Ssb])
                        cur[0] = hh
                    return Ss, Ssb
                self.dn_mixer(G, l, S_tiles, last_group=True)
                P.dma(d['s_dn_S'].ap()[l, :, cur[0]].rearrange("s k v -> k s v"), Ss[:, :, :], R=[Ss], W=[d['s_dn_S']],
                      q='pool')
                self.ffn(G, l)
            P.dma(d['x2_s'].ap(), T['x'][:, 0, :], R=[T['x']], W=[d['x2_s']], q='pool')
            for s4 in range(4):
                P.dma(d['s_win_kv'].ap()[s4 * 4:(s4 + 1) * 4, 0:508, :], d['state_win_kv'].ap()[s4 * 4:(s4 + 1) * 4, 4:512, :],
                      W=[d['s_win_kv']], q='sp')

            def win_cb_s(t, rp):
                for sq in range(16):
                    P.dma(d['s_win_kv'].ap()[sq, 508:512, :], rp[4 * sq:4 * sq + 4, :], R=[rp],
                          W=[d['s_win_kv']], q='pool')
            self.shared_rows(G, d['s_kv_rows'], 0, win_cb_s)
        if nsa and cfg.get('sample', True):
            self.nsa_sample()
        P.emit()
        return self.nc


_CONST_CACHE = {}


def const_inputs():
    if not _CONST_CACHE:
        for pre, NS, TS in (('cp_', 1, 64), ('cs_', 16, 4)):
            for k, v in host_consts(NS, TS).items():
                _CONST_CACHE[pre + k] = v
    return _CONST_CACHE


def make_in_maps(inp, cfg, n_cores=8):
    SEQ = cfg['SEQ']
    cst = const_inputs()
    maps = []
    shared = {k: np.ascontiguousarray(inp[k]) for k in
              ('norm_mix', 'norm_ffn', 'norm_kv', 'norm_final', 'ffn_w_in', 'ffn_w_out', 'dn_w_in', 'dn_conv_w',
               'dn_A_log', 'dn_dt_bias', 'dn_out_norm', 'dn_w_out', 'nsa_w_kv')}
    nsa = cfg.get('nsa', True)
    if nsa:
        shared['nsa_w_in'] = np.ascontiguousarray(inp['nsa_w_in'])
        shared['nsa_w_out'] = np.ascontiguousarray(inp['nsa_w_out'])
        shared['nsa_cmp_pos_w'] = np.ascontiguousarray(inp['nsa_cmp_pos_w']).reshape(2, 32, 256)
        shared['nsa_w_cmp'] = np.ascontiguousarray(inp['nsa_w_cmp'])
        shared['rel_bias'] = np.ascontiguousarray(inp['rel_bias'])
        shared['cache_kv'] = np.ascontiguousarray(inp['cache_kv']).reshape(2560 * 128, 1024)
        nsc = [nsa_host_consts(0), nsa_host_consts(1)]
    for c in range(n_cores):
        b = c // 2
        m = dict(shared)
        m.update(cst)
        m['x_prompt'] = np.ascontiguousarray(inp['x_prompt'][b, :SEQ])
        sl = slice(16 * c, 16 * c + 16)
        m['x_sample'] = np.ascontiguousarray(inp['x_sample'][sl]).reshape(64, D)
        m['state_dn_S'] = np.ascontiguousarray(inp['state_dn_S'][:, sl])
        m['state_dn_conv'] = np.ascontiguousarray(inp['state_dn_conv'][:, sl])
        m['state_win_kv'] = np.ascontiguousarray(inp['state_win_kv'][sl]).reshape(16, 512, 512)
        if nsa:
            m.update(nsc[c % 2])
            m['page_table'] = np.ascontiguousarray(inp['page_table'][sl]).reshape(1, 256).astype(np.int32)
        maps.append(m)
    return maps


def kernel(**inp):
    cfg = dict(SEQ=4096, n_dn=2)
    b = Builder(cfg)
    nc = b.build()
    maps = make_in_maps(inp, cfg)
    res = run_bass_kernel_spmd(nc, maps, core_ids=list(range(8)))
    R = res.results
    f32 = np.float32
    y_prompt = np.zeros((4, 4096, D), f32)
    for c in range(8):
        yp = R[c]['y_p'].reshape(16, 128, D)
        y_prompt[c // 2].reshape(32, 128, D)[c % 2::2] = yp
    y_sample = np.concatenate([R[c]['y_s'].reshape(16, 4, D) for c in range(8)], axis=0).astype(f32)
    p_dn_S = np.stack([R[2 * b]['p_dn_S'] for b in range(4)], axis=1)
    p_dn_conv = np.stack([R[2 * b]['p_dn_conv'] for b in range(4)], axis=1)
    p_kv_rows = np.stack([R[2 * b]['p_kv_rows'] for b in range(4)], axis=0).reshape(4, 4096, 4, 4, 64)
    p_win_kv = np.stack([R[2 * b]['p_win_kv'] for b in range(4)], axis=0).reshape(4, 512, 2, 4, 64)
    s_dn_S = np.concatenate([R[c]['s_dn_S'] for c in range(8)], axis=1)
    s_dn_conv = np.concatenate([R[c]['s_dn_conv'] for c in range(8)], axis=1)
    s_kv_rows = np.concatenate([R[c]['s_kv_rows'].reshape(16, 4, 4, 4, 64) for c in range(8)], axis=0)
    s_win_kv = np.concatenate([R[c]['s_win_kv'].reshape(16, 512, 2, 4, 64) for c in range(8)], axis=0)
    return (y_prompt, y_sample, p_dn_S.astype(f32), p_dn_conv.astype(f32), p_kv_rows.astype(f32),
            p_win_kv.astype(f32), s_dn_S.astype(f32), s_dn_conv.astype(f32), s_kv_rows.astype(f32),
            s_win_kv.astype(f32))


def _bucket_np(d):
    n = np.maximum(d, 0)
    nf = np.maximum(n, 1).astype(np.float32)
    large = 16 + (np.log(nf / np.float32(16)) / np.float32(math.log(64.0)) * np.float32(16)).astype(np.int32)
    large = np.minimum(large, 31)
    return np.where(n < 16, n, large)


def _onehot(d, valid):
    b = np.where(valid, _bucket_np(d), 32)
    oh = np.zeros((33, d.shape[0]), np.float32)
    oh[b, np.arange(d.shape[0])] = 1.0
    return oh


def nsa_host_consts(par):
    c = {}
    t = np.arange(1280)
    d = t - 255 + 128 * par
    c['oh_sel'] = _onehot(d, d >= 0)
    t = np.arange(1024)
    d = t - 255 + 128 * par
    c['oh_win'] = _onehot(d, (d >= 0) & (d < 512))
    r = np.arange(16)[:, None]
    w = np.arange(512)[None, :]
    d = (16 * (247 - w + 8 * par) + r - 31).reshape(-1)
    c['oh_cmp'] = _onehot(d, d >= 0)
    tt = np.arange(4)[:, None]
    cc = np.arange(128)[None, :]
    d = (2017 + tt - 16 * cc).reshape(-1)
    c['oh_cs'] = _onehot(d, d >= 0)
    x = np.arange(2304)
    d = x - 127
    c['oh_ss'] = _onehot(d, d >= 0)
    x = np.arange(768)
    d = x - 127
    c['oh_ws'] = _onehot(d, (d >= 0) & (d < 512))
    blk = np.arange(64)[None, None, :]
    qpos = (128 * (2 * np.arange(16)[:, None, None] + par) + np.arange(128)[None, :, None])
    cur = qpos // 64
    forced = (blk == 0) | (blk == cur) | (blk == cur - 1)
    valid = blk * 64 <= qpos
    c['selmul_p'] = np.ascontiguousarray(np.where(valid & ~forced, 1.0, 0.0).astype(np.float32).transpose(1, 0, 2))
    c['seladd_p'] = np.ascontiguousarray(np.where(forced, 1e4, np.where(valid, 0.0, -1.0)).astype(np.float32).transpose(1, 0, 2))
    blk = np.arange(64)[None, :]
    qpos = 2048 + np.arange(4)[:, None]
    cur = qpos // 64
    exists = blk < 33
    forced = ((blk == 0) | (blk == cur) | (blk == cur - 1)) & exists
    valid = (blk * 64 <= qpos) & exists
    c['selmul_s'] = np.where(valid & ~forced, 1.0, 0.0).astype(np.float32)
    c['seladd_s'] = np.where(forced, 1e4, np.where(valid, 0.0, np.where(exists, -1.0, -2.0))).astype(np.float32)
    k = np.arange(4096)[None, :]
    c['expE'] = (k // 64 == np.arange(64)[:, None]).astype(np.float32)
    sel8 = np.zeros((128, 8), np.float32)
    sel8[np.arange(128), np.arange(128) // 16] = 1.0
    c['sel8'] = sel8
    c['antiI'] = np.ascontiguousarray(np.eye(128, dtype=np.float32)[::-1])
    c['parf'] = np.tile(np.array([[float(par), 1.0 - float(par)]], np.float32), (128, 1))
    c['iota'] = np.arange(128, dtype=np.float32).reshape(128, 1)
    return c


def _nsa_declare(self):
    P, cfg, d = self.P, self.cfg, self.d
    SEQ = cfg['SEQ']

    def inp(name, shape, dt=F32):
        d[name] = P.dram(name, shape, dt, kind="ExternalInput")

    inp('nsa_w_in', [2, D, 1072])
    inp('nsa_w_out', [2, D, D])
    inp('nsa_cmp_pos_w', [2, 32, 256])
    inp('nsa_w_cmp', [2, 4, 64, 64])
    inp('rel_bias', [32, 16])
    inp('cache_kv', [2560 * 128, 1024])
    inp('page_table', [1, 256], I32)
    for nm, shp in (('oh_sel', [33, 1280]), ('oh_win', [33, 1024]), ('oh_cmp', [33, 8192]), ('oh_cs', [33, 512]),
                    ('oh_ss', [33, 2304]), ('oh_ws', [33, 768]), ('selmul_p', [128, 16, 64]), ('seladd_p', [128, 16, 64]),
                    ('selmul_s', [4, 64]), ('seladd_s', [4, 64]), ('expE', [64, 4096]), ('sel8', [128, 8]),
                    ('antiI', [128, 128]), ('parf', [128, 2]), ('iota', [128, 1])):
        inp(nm, shp)
    d['y_p'] = P.dram('y_p', [SEQ // 2, D], F32, kind="ExternalOutput")
    d['y_s'] = P.dram('y_s', [64, D], F32, kind="ExternalOutput")
    d['wc_nsa_in'] = [P.dram(f'wc_nsa_in{j}', [8, 128, 8, 128], BF16) for j in range(2)]
    d['wb_nsa_out'] = [P.dram(f'wb_nsa_out{j}', [D, D], BF16) for j in range(2)]
    for nm, ln in (('t_sel', 1280), ('t_win', 1024), ('t_cmp', 8192), ('t_cs', 512), ('t_ss', 2304), ('t_ws', 768)):
        d[nm] = P.dram(nm, [16, ln], F32)
    d['BTd'] = P.dram('BTd', [4, 15, 128, 512], F32)
    NT = SEQ // 128
    d['pwin_d'] = P.dram('pwin_d', [SEQ, 512], F32)
    d['kselT_p'] = P.dram('kselT_p', [4, 128, SEQ], BF16)
    d['vsel_p'] = P.dram('vsel_p', [NT, 128, 4, 66], BF16)
    d['kwinT_p'] = P.dram('kwinT_p', [4, 128, SEQ], BF16)
    d['vwin_p'] = P.dram('vwin_p', [NT, 128, 4, 66], BF16)
    d['kselT_s'] = P.dram('kselT_s', [16, 4, 128, 17 * 128], BF16)
    d['vsel_s'] = P.dram('vsel_s', [16, 17, 128, 4, 66], BF16)
    d['kwinT_s'] = P.dram('kwinT_s', [16, 4, 128, 5 * 128], BF16)
    d['vwin_s'] = P.dram('vwin_s', [16, 5, 128, 4, 66], BF16)


def _nsa_consts(self):
    P, d, c = self.P, self.d, self.c
    antiI = P.sb([128, 128], F32, "antiI")
    P.dma(antiI[:, :], d['antiI'].ap(), W=[antiI])
    sel8 = P.sb([128, 8], BF16, "sel8")
    P.dma(sel8[:, :], d['sel8'].ap(), W=[sel8], q='pool')
    parf = P.sb([128, 2], F32, "parf")
    P.dma(parf[:, :], d['parf'].ap(), W=[parf])
    tabx = P.sb([33, 16], F32, "tabx")
    r31 = P.sb([32, 16], F32, "r31")
    P.dma(tabx[0:32, :], d['rel_bias'].ap(), W=[tabx])
    P.dma(r31[:, :], d['rel_bias'].ap()[31].partition_broadcast(32), W=[r31])
    self.tt('dve', tabx[0:32, :], tabx[0:32, :], r31[:, :], ALU.subtract, R=[tabx, r31], W=[tabx])
    self.memset('pool', tabx[32:33, :], NEG, W=[tabx])
    c.update(antiI=antiI, sel8=sel8, parf=parf, tabx=tabx)
    wg = P.sb([128, 2, 8, 48], BF16, "wg")
    for j in range(2):
        P.dma(wg[:, j, :, :], d['nsa_w_in'].ap()[j, :, 1024:1072].rearrange("(kc p) n -> p kc n", p=128), W=[wg], q='pool')
    c['wg'] = wg


def _nsa_late_consts(self):
    P, d, c = self.P, self.d, self.c
    if 'wlo' in c:
        return
    wlo = P.sb([128, 512], F32, "wlo")
    whi = P.sb([128, 512], F32, "whi")
    for a in range(8):
        P.dma(wlo[16 * a:16 * a + 16, :].rearrange("r (c n) -> r c n", c=2),
              d['nsa_cmp_pos_w'].ap()[:, 0:16, :].rearrange("c r n -> r c n"), W=[wlo])
        P.dma(whi[16 * a:16 * a + 16, :].rearrange("r (c n) -> r c n", c=2),
              d['nsa_cmp_pos_w'].ap()[:, 16:32, :].rearrange("c r n -> r c n"), W=[whi])
    wck = P.sb([128, 4, 128], BF16, "wck")
    wcv = P.sb([128, 4, 64], BF16, "wcv")
    for half in range(2):
        for dup in range(2):
            P.dma(wck[half * 64:(half + 1) * 64, :, dup * 64:(dup + 1) * 64],
                  d['nsa_w_cmp'].ap()[0].rearrange("k d e -> d k e"), W=[wck], q='pool')
        P.dma(wcv[half * 64:(half + 1) * 64, :, :], d['nsa_w_cmp'].ap()[1].rearrange("k d e -> d k e"), W=[wcv], q='pool')
    expE = P.sb([64, 4096], BF16, "expE")
    P.dma(expE[:, :], d['expE'].ap(), W=[expE], q='pool')
    c.update(wlo=wlo, whi=whi, wck=wck, wcv=wcv, expE=expE)


Builder.nsa_late_consts = _nsa_late_consts


def _nsa_tables(self):
    P, d, c = self.P, self.d, self.c
    for oh, dst, ln in (('oh_sel', 't_sel', 1280), ('oh_win', 't_win', 1024), ('oh_cmp', 't_cmp', 8192),
                        ('oh_cs', 't_cs', 512), ('oh_ss', 't_ss', 2304), ('oh_ws', 't_ws', 768)):
        for off in range(0, ln, 512):
            n = min(512, ln - off)
            ot = self.ring('oht', 2, lambda j: P.sb([33, 512], F32, f"oht{j}"))
            P.dma(ot[:, 0:n], d[oh].ap()[:, off:off + n], W=[ot])
            ps = self.next_pb('lin', [1, 2, 3])
            self.mm(ps[0:16, 0:n], c['tabx'][:, :], ot[:, 0:n], R=[c['tabx'], ot], W=[ps])
            tb = self.ring('tbo', 2, lambda j: P.sb([16, 512], F32, f"tbo{j}"))
            self.cp('act', tb[:, 0:n], ps[0:16, 0:n], R=[ps], W=[tb])
            P.dma(d[dst].ap()[:, off:off + n], tb[:, 0:n], R=[tb], W=[d[dst]], q='pool')
    if self.cfg.get('prompt', True):
        for kvh in range(4):
            for idx in range(15):
                tab, ln, e = ('t_sel', 1280, idx - 1) if idx < 9 else ('t_win', 1024, idx - 10)
                tr_ = self.ring('trv', 2, lambda j: P.sb([128, 4, 128], F32, f"trv{j}"))
                src = bass.AP(d[tab].h, 4 * kvh * ln + 128 * (e + 1), [[1, 128], [ln, 4], [1, 128]])
                P.dma(tr_[:, :, :], src, R=[d[tab]], W=[tr_])
                ps = self.next_pb('lin', [1, 2, 3])
                self.mm(ps[:, :], c['antiI'][:, :], tr_[:, :, :].rearrange("p g q -> p (g q)"), R=[c['antiI'], tr_], W=[ps])
                fl = self.ring('flp', 2, lambda j: P.sb([128, 512], F32, f"flp{j}"))
                self.cp('act', fl[:, :], ps[:, :], R=[ps], W=[fl])
                P.dma(d['BTd'].ap()[kvh, idx], fl[:, :], R=[fl], W=[d['BTd']], q='pool')


def _ctx_rows(self, rows, P_idx, lohi, kT_dst, v_dst, kw_dst, vw_dst, has_cmpsel=True, has_win=True, win_rows=None):
    P, c = self.P, self.c
    if has_cmpsel:
        if lohi is not None:
            alo = self.ring('alo', 2, lambda j: P.sb([128, 512], BF16, f"alo{j}"))
            ahi = self.ring('ahi', 2, lambda j: P.sb([128, 512], BF16, f"ahi{j}"))
            self.tt('pool', alo[:, :], rows[:, 0:512], c['wlo'][:, :], ALU.mult, R=[rows, c['wlo']], W=[alo])
            self.tt('dve', ahi[:, :], rows[:, 0:512], c['whi'][:, :], ALU.mult, R=[rows, c['whi']], W=[ahi])
            ps = self.next_pb('ctxp', [0])
            for lh, a in enumerate((alo, ahi)):
                for ch in range(4):
                    self.mm(ps[:, (lh * 4 + ch) * 8:(lh * 4 + ch + 1) * 8], a[:, ch * 128:(ch + 1) * 128], c['sel8'][:, :],
                            R=[a, c['sel8']], W=[ps])
            self.cp('act', lohi[:, :, :, 8 * P_idx:8 * P_idx + 8], ps[:, 0:64].rearrange("p (l c m) -> p l c m", l=2, c=4),
                    R=[ps], W=[lohi])
        for (col0, kdst, vdst) in ((512, kT_dst, v_dst),):
            _kv_tile(self, rows, col0, kdst, vdst)
    if has_win:
        wt, wc0 = win_rows
        _kv_tile(self, wt, wc0, kw_dst, vw_dst)


def _kv_tile(self, rows, col0, kdst, vdst):
    P, c = self.P, self.c
    kd = self.ring('kd', 2, lambda j: P.sb([128, 4, 2, 64], BF16, f"kd{j}"))
    self.cp('pool', kd[:, :, :, :], rows[:, col0:col0 + 256].rearrange("p (k d) -> p k d", k=4).unsqueeze(2).to_broadcast([128, 4, 2, 64]),
            R=[rows], W=[kd])
    pt = self.pt[1]
    for k in range(4):
        self.tr(pt[:, k * 128:(k + 1) * 128], kd[:, k, :, :].rearrange("p a d -> p (a d)"), c['identb'][:, :],
                R=[kd, c['identb']], W=[pt])
    ks = self.ring('ks', 2, lambda j: P.sb([128, 4, 128], BF16, f"ks{j}"))
    self.cp('act', ks[:, :, :].rearrange("p k n -> p (k n)"), pt[:, 0:512], R=[pt], W=[ks])
    kap, ktile = kdst
    P.dma(kap, ks[:, :, :], R=[ks], W=[ktile], q='pool')
    va = self.ring('va', 2, lambda j: P.sb([128, 4, 66], BF16, f"va{j}"))
    self.memset('pool', va[:, :, 64:65], 1.0, W=[va])
    self.memset('pool', va[:, :, 65:66], 0.0, W=[va])
    self.cp('dve', va[:, :, 0:64], rows[:, col0 + 256:col0 + 512].rearrange("p (k d) -> p k d", k=4), R=[rows], W=[va])
    vap, vtile = vdst
    P.dma(vap, va[:, :, :], R=[va], W=[vtile], q='pool')


def _cmp_finish(self, lohi, NCB, kcd_ap, vc_ap_fn, kcd_t, vc_t, ncw=256, njh=2):
    P, c = self.P, self.c
    bl = self.ring('blk', 1, lambda j: P.sb([128, 4, 256], BF16, "blk"))
    self.memset('pool', bl[:, :, :], 0.0, W=[bl])
    self.tt('dve', bl[:, :, 0:NCB], lohi[:, 0, :, 0:NCB], lohi[:, 1, :, 1:NCB + 1], ALU.add, R=[lohi], W=[bl])
    for kvh in range(4):
        hs = slice((kvh % 2) * 64, (kvh % 2) * 64 + 64)
        ps = self.next_pb('lin', [1, 2, 3])
        self.mm(ps[:, 0:256], c['wck'][hs, kvh, :], bl[hs, kvh // 2, :], R=[c['wck'], bl], W=[ps])
        self.cp('act', kcd_ap(kvh), ps[:, 0:ncw], R=[ps], W=[kcd_t])
        for jh in range(njh):
            ps2 = self.next_pb('lin', [1, 2, 3])
            self.mm(ps2[:, 0:64], bl[hs, 2 + kvh // 2, jh * 128:(jh + 1) * 128], c['wcv'][hs, kvh, :], R=[bl, c['wcv']], W=[ps2])
            self.cp('act', vc_ap_fn(jh, kvh), ps2[:, 0:64], R=[ps2], W=[vc_t])


Builder.nsa_declare = _nsa_declare
Builder.nsa_consts = _nsa_consts
Builder.nsa_tables = _nsa_tables
Builder.ctx_rows = _ctx_rows
Builder.cmp_finish = _cmp_finish


def _attend(self, A):
    P, c = self.P, self.c
    NQ, NC, kvh = A['NQ'], A['NC'], A['kvh']
    pb = self.pb
    qT, qTt = A['qT']
    qbd, qcols = A['qbd'], A['qcols']
    gate = A['gate']
    N4 = 4 * NQ

    def sbt(key, shape, dt=F32, nbuf=2):
        k = f"at{NQ}{key}"
        return self.ring(k, nbuf, lambda j: P.sb(shape, dt, k + str(j)))

    if A.get('stop', 99) == 0:
        raise _Stop()
    kc_ap, kc_t = A['kcmp']
    for g in range(4):
        ps = pb[g % 2]
        self.mm(ps[0:NQ, (g // 2) * 256:(g // 2) * 256 + NC], qT(g), kc_ap(g), R=[qTt, kc_t], W=[ps])
    if A.get('stop', 99) == 10:
        raise _Stop()
    bc_ap, bc_t = A['bias_c']
    sc = sbt('sc', [NQ, 4, 256], nbuf=1)
    for g in range(4):
        ps = pb[g % 2]
        self.tt('dve', sc[:, g, 0:NC], ps[0:NQ, (g // 2) * 256:(g // 2) * 256 + NC], bc_ap[:, g, 0:NC], ALU.add,
                R=[ps, bc_t], W=[sc])
    if A.get('stop', 99) == 11:
        raise _Stop()
    ssum = sbt('ssum', [NQ, 8])
    self.memset('pool', ssum[:, :], 0.0, W=[ssum])
    for g in range(4):
        self.actf(sc[:, g, 0:NC], sc[:, g, 0:NC], AF.Exp, R=[sc], W=[sc, ssum], accum_out=ssum[:, g:g + 1])
    if A.get('stop', 99) == 12:
        raise _Stop()
    self.ts('dve', ssum[:, 4:8], ssum[:, 0:4], 1e-30, None, ALU.max, R=[ssum], W=[ssum])
    P.add('dve', lambda e: e.reciprocal(out=ssum[:, 4:8], in_=ssum[:, 4:8]), R=[ssum], W=[ssum])
    if A.get('stop', 99) == 13:
        raise _Stop()
    pc = sbt('pc', [NQ, 4, 256], nbuf=1)
    if not A.get('pc_init'):
        pass
    self.memset('pool', pc[:, :, :], 0.0, W=[pc])
    self.tt('dve', pc[:, :, 0:NC], sc[:, :, 0:NC], ssum[:, 4:8].unsqueeze(2).to_broadcast([NQ, 4, NC]), ALU.mult,
            R=[sc, ssum], W=[pc])
    if A.get('stop', 99) == 1:
        raise _Stop()
    imp = sbt('imp', [NQ, 264], nbuf=1)
    self.memset('pool', imp[:, :], 0.0, W=[imp])
    P.add('dve', lambda e: e.tensor_reduce(out=imp[:, 1:257], in_=pc[:, :, :].rearrange("p g n -> p n g"),
                                           axis=mybir.AxisListType.X, op=ALU.add), R=[pc], W=[imp])
    cov = sbt('cov', [NQ, 256], nbuf=1)
    self.tt('dve', cov[:, :], imp[:, 1:257], imp[:, 0:256], ALU.add, R=[imp], W=[cov])
    psl = sbt('psl', [NQ, 64], nbuf=1)
    P.add('dve', lambda e: e.tensor_reduce(out=psl[:, :], in_=cov[:, :].rearrange("p (b r) -> p b r", r=4),
                                           axis=mybir.AxisListType.X, op=ALU.add), R=[cov], W=[psl])
    smul, sadd, s_t = A['selc']
    self.tt('dve', psl[:, :], psl[:, :], smul, ALU.mult, R=[psl, s_t], W=[psl])
    self.tt('dve', psl[:, :], psl[:, :], sadd, ALU.add, R=[psl, s_t], W=[psl])
    m16 = sbt('m16', [NQ, 16], nbuf=1)
    ps2 = sbt('psl2', [NQ, 64], nbuf=1)
    P.add('dve', lambda e: e.max(out=m16[:, 0:8], in_=psl[:, :]), R=[psl], W=[m16])
    P.add('dve', lambda e: e.match_replace(out=ps2[:, :], in_to_replace=m16[:, 0:8], in_values=psl[:, :], imm_value=-5.0),
          R=[psl, m16], W=[ps2])
    P.add('dve', lambda e: e.max(out=m16[:, 8:16], in_=ps2[:, :]), R=[ps2], W=[m16])
    nsel = sbt('nsel', [NQ, 64], nbuf=1)
    self.ts('dve', nsel[:, :], psl[:, :], m16[:, 15:16], None, ALU.is_ge, R=[psl, m16], W=[nsel])
    self.ts('dve', nsel[:, :], nsel[:, :], -NEG, NEG, ALU.mult, ALU.add, R=[nsel], W=[nsel])
    self.tr(pb[0][0:64, 0:NQ], nsel[:, :], c['identf'][0:NQ, 0:NQ], R=[nsel, c['identf']], W=[pb[0]])
    nsT = sbt('nsT', [64, 4, NQ], BF16)
    self.cp('act', nsT[:, :, :], pb[0][0:64, 0:NQ].unsqueeze(1).to_broadcast([64, 4, NQ]), R=[pb[0]], W=[nsT])
    if A.get('stop', 99) == 2:
        raise _Stop()
    pcb = sbt('pcb', [NQ, 4, 256], BF16, 1)
    self.cp('pool', pcb[:, :, :], pc[:, :, :], R=[pc], W=[pcb])
    pt = self.pt[0]
    NH = A['NH']
    for g in range(4):
        for hf in range(NH):
            self.tr(pt[:, (g * NH + hf) * NQ:(g * NH + hf + 1) * NQ], pcb[:, g, hf * 128:(hf + 1) * 128],
                    c['identb'][0:NQ, 0:NQ], R=[pcb, c['identb']], W=[pt])
    pcT = sbt('pcT', [128, 4 * NH, NQ], BF16, 1)
    self.cp('act', pcT[:, :, :].rearrange("p a q -> p (a q)"), pt[:, 0:4 * NH * NQ], R=[pt], W=[pcT])
    vc_ap, vc_t = A['vcmp']
    for g in range(4):
        for hf in range(NH):
            self.mm(pb[1][0:NQ, g * 64:(g + 1) * 64], pcT[:, g * NH + hf, :], vc_ap(hf), start=(hf == 0), stop=(hf == NH - 1),
                    R=[pcT, vc_t], W=[pb[1]])
    oacc = sbt('oacc', [NQ, 4, 64])
    g_ap, g_t = gate
    self.tt('dve', oacc[:, :, :], pb[1][0:NQ, 0:256].rearrange("p (g d) -> p g d", g=4),
            g_ap(0).unsqueeze(2).to_broadcast([NQ, 4, 64]), ALU.mult, R=[pb[1], g_t], W=[oacc])

    if A.get('stop', 99) == 3:
        raise _Stop()
    for br, (tiles, ksrc, vsrc, po) in enumerate((A['sel'], A['win'])):
        if A.get('stop', 99) == 4 and br == 1:
            raise _Stop()
        nt = len(tiles)
        for c0 in range(0, nt, 4):
            n = min(4, nt - c0)
            kt0 = tiles[c0][0]
            kc = sbt('kc', [128, 512], BF16, 3)
            kap, ktile = ksrc(kt0, n)
            P.dma(kc[:, 0:n * 128], kap, R=[ktile], W=[kc])
            vcx = sbt('vcx', [128, 4, 66], BF16, 3)
            vap, vtile = vsrc(kt0, n)
            P.dma(vcx[:, 0:n, :], vap, R=[vtile], W=[vcx])
            for t in range(n):
                kt, bias = tiles[c0 + t]
                ps = self.next_pb('sc', [2, 3])
                for pr in range(2):
                    self.mm(ps[:, pr * 2 * NQ:(pr + 1) * 2 * NQ], kc[:, t * 128:(t + 1) * 128], qbd[:, pr, :, qcols],
                            start=(pr == 0), stop=(br == 1 and pr == 1), R=[kc, A['qbd_t']], W=[ps])
                if br == 0:
                    self.mm(ps[:, 0:N4], c['expE'][:, kt * 128:(kt + 1) * 128], nsT[:, :, :].rearrange("p g q -> p (g q)"),
                            start=False, stop=True, R=[c['expE'], nsT], W=[ps])
                PT = sbt('PT', [128, N4], BF16, 3)
                if bias is not None:
                    b_ap, b_t = bias
                    sb_ = sbt('sbias', [128, N4], F32, 2)
                    self.tt('dve', sb_[:, :], ps[:, 0:N4], b_ap, ALU.add, R=[ps, b_t], W=[sb_])
                    self.actf(PT[:, :], sb_[:, :], AF.Exp, R=[sb_], W=[PT])
                else:
                    self.actf(PT[:, :], ps[:, 0:N4], AF.Exp, R=[ps], W=[PT])
                for g in range(4):
                    self.mm(po[0:NQ, g * 66:(g + 1) * 66], PT[:, g * NQ:(g + 1) * NQ], vcx[:, t, :],
                            start=(c0 + t == 0 and g == 0), stop=(c0 + t == nt - 1 and g == 3), R=[PT, vcx], W=[po])
        pov = po[0:NQ, 0:264].rearrange("p (g e) -> p g e", g=4)
        rs = sbt('rs', [NQ, 4])
        self.ts('dve', rs[:, :], pov[:, :, 64], 1e-30, None, ALU.max, R=[po], W=[rs])
        P.add('dve', lambda e, rs=rs: e.reciprocal(out=rs[:, :], in_=rs[:, :]), R=[rs], W=[rs])
        self.tt('dve', rs[:, :], rs[:, :], g_ap(br + 1), ALU.mult, R=[rs, g_t], W=[rs])
        tmpo = sbt('tmpo', [NQ, 4, 64])
        self.tt('dve', tmpo[:, :, :], pov[:, :, 0:64], rs[:, :].unsqueeze(2).to_broadcast([NQ, 4, 64]), ALU.mult,
                R=[po, rs], W=[tmpo])
        self.tt('pool', oacc[:, :, :], oacc[:, :, :], tmpo[:, :, :], ALU.add, R=[oacc, tmpo], W=[oacc])
    o_ap, o_t = A['out']
    self.cp('act', o_ap, oacc[:, :, :], R=[oacc], W=[o_t])


Builder.attend = _attend


def _nsa_qproj(self, G, j, qT_all, NT_):
    P, d, T = self.P, self.d, G.T
    GT = G.GT
    for ci in range(8):
        wt = self.load_wchunk(d['wc_nsa_in'][j].ap()[ci])
        ps = self.next_pb('lin', [1, 2, 3])
        for kc in range(8):
            self.mm(ps[:, 0:GT], wt[:, kc, :], T['xnT'][:, kc, :], start=(kc == 0), stop=(kc == 7), R=[wt, T['xnT']], W=[ps])
        self.actf(qT_all[:, ci, :], ps[:, 0:GT], AF.Copy, R=[ps], W=[qT_all], scale=0.125)


def _final_norm(self, G, dst, row0, ytile=None):
    P, d, T = self.P, self.d, G.T
    TP = G.TP
    wrow = self.ring('wrow', 2, lambda j: P.sb([128, D], F32, f"wrow{j}"))
    P.dma(wrow[:, :], d['norm_final'].ap().partition_broadcast(128), W=[wrow])
    x = T['x']
    for t in range(G.NT):
        junk = self.ring('junk', 1, lambda j: P.sb([128, D], BF16, f"junk{j}"))
        st = self.ring('nst', 4, lambda j: P.sb([128, 2], F32, f"nst{j}"))
        self.memset('pool', st[:, :], 0.0, W=[st])
        self.actf(junk[0:TP, :], x[:, t, :], AF.Square, R=[x], W=[junk, st], accum_out=st[0:TP, 0:1])
        self.ts('dve', st[0:TP, 1:2], st[0:TP, 0:1], 1.0 / D, 1e-6, ALU.mult, ALU.add, R=[st], W=[st])
        self.actf(st[0:TP, 1:2], st[0:TP, 1:2], AF.Sqrt, R=[st], W=[st])
        P.add('dve', lambda e, st=st: e.reciprocal(out=st[0:TP, 1:2], in_=st[0:TP, 1:2]), R=[st], W=[st])
        if ytile is None:
            yt = self.ring(G.name + 'yt', 2, lambda j: P.sb([TP, D], F32, G.name + f"yt{j}"))
            ya = yt[:, :]
        else:
            yt = ytile
            ya = ytile[:, t % 2, :]
        self.stt('dve', ya, x[:, t, :], st[0:TP, 1:2], wrow[0:TP, :], ALU.mult, ALU.mult, R=[x, st, wrow], W=[yt])
        P.dma(dst.ap()[row0 + t * TP:row0 + (t + 1) * TP, :], ya, R=[yt], W=[dst], q='pool')


def _nsa_prompt(self):
    P, d, c, cfg = self.P, self.d, self.c, self.cfg
    SEQ = cfg['SEQ']
    NTT = SEQ // 128
    NQT = NTT // 2
    NCB = NTT * 8 - 1
    self.nsa_late_consts()
    kcd = P.sb([128, 4, 256], BF16, "kcd")
    vcm = P.sb([128, 2, 4, 64], BF16, "vcm")
    self.memset('pool', vcm[:, :, :, :], 0.0, W=[vcm])
    with self.phase():
        lohi = P.sb([128, 2, 4, 264], F32, "lohi")
        self.memset('pool', lohi[:, :, :, :], 0.0, W=[lohi])
        for Pi in range(NTT):
            rows = self.ring('crow', 2, lambda j: P.sb([128, 1536], F32, f"crow{j}"))
            P.dma(rows[:, 0:1024], d['p_kv_rows'].ap()[Pi * 128:(Pi + 1) * 128, :], R=[d['p_kv_rows']], W=[rows])
            P.dma(rows[:, 1024:1536], d['pwin_d'].ap()[Pi * 128:(Pi + 1) * 128, :], R=[d['pwin_d']], W=[rows])
            cs = slice(Pi * 128, (Pi + 1) * 128)
            self.ctx_rows(rows, Pi, lohi,
                          (d['kselT_p'].ap()[:, :, cs].rearrange("k p n -> p k n"), d['kselT_p']),
                          (d['vsel_p'].ap()[Pi], d['vsel_p']),
                          (d['kwinT_p'].ap()[:, :, cs].rearrange("k p n -> p k n"), d['kwinT_p']),
                          (d['vwin_p'].ap()[Pi], d['vwin_p']), win_rows=(rows, 1024))
        self.cmp_finish(lohi, NCB, lambda kvh: kcd[:, kvh, :], lambda jh, kvh: vcm[:, jh, kvh, :], kcd, vcm)
    if cfg.get('nsa_stop', 99) == 2:
        raise _Stop()
    with self.phase():
        G = Ctx('n', 128, 4, 1, 1, 5)
        T = {}
        T['x'] = P.sb([128, 4, D], F32, "nx")
        T['xnT'] = P.sb([128, 8, 512], BF16, "nxnT")
        big = P.sb([128, 24, 512], BF16, "nbig")
        T['actT'] = View(big, 0, 22)
        qT_all = View(big, 0, 8)
        T['oT'] = P.sb([128, 8, 512], BF16, "noT")
        G.T = T
        o_tok = P.sb([128, 4, D], BF16, "o_tok")
        gates = P.sb([128, 4, 48], F32, "gates")
        qbd = P.sb([128, 2, 2, 512], BF16, "qbd")
        self.memset('pool', qbd[:, :, :, :], 0.0, W=[qbd])
        BT = P.sb([128, 15, 512], F32, "BT")
        for grp in range(NQT // 4):
            for tl in range(4):
                i = grp * 4 + tl
                xe = self.ring('xeo', 1, lambda j: P.sb([128, 2, D], F32, f"xeo{j}"))
                P.dma(xe[:, :, :], d['x2_p'].ap()[2 * i * 128:(2 * i + 2) * 128, :].rearrange("(e p) n -> p e n", p=128),
                      R=[d['x2_p']], W=[xe])
                self.ts('dve', T['x'][:, tl, :], xe[:, 0, :], c['parf'][:, 1:2], None, ALU.mult, R=[xe, c['parf']], W=[T['x']])
                self.stt('dve', T['x'][:, tl, :], xe[:, 1, :], c['parf'][:, 0:1], T['x'][:, tl, :], ALU.mult, ALU.add,
                         R=[xe, c['parf'], T['x']], W=[T['x']])
            for j in range(2):
                self.norm_T(G, d['norm_mix'].ap()[2 + j])
                _nsa_qproj(self, G, j, qT_all, 4)
                pg = self.pb[0]
                for tl in range(4):
                    for kc in range(8):
                        self.mm(pg[:, tl * 48:(tl + 1) * 48], T['xnT'][:, kc, tl * 128:(tl + 1) * 128], c['wg'][:, j, kc, :],
                                start=(kc == 0), stop=(kc == 7), R=[T['xnT'], c['wg']], W=[pg])
                self.actf(gates[:, :, :].rearrange("p t n -> p (t n)"), pg[:, 0:192], AF.Sigmoid, R=[pg], W=[gates])
                if cfg.get('nsa_stop', 99) == 3:
                    raise _Stop()
                for kvh in range(4):
                    for b3 in range(5):
                        P.dma(BT[:, 3 * b3:3 * b3 + 3, :], d['BTd'].ap()[kvh, 3 * b3:3 * b3 + 3].rearrange("i p n -> p i n"),
                              R=[d['BTd']], W=[BT])
                    for pr in range(2):
                        self.cp('pool', qbd[0:64, pr, 0, :], qT_all[0:64, 2 * kvh + pr, :], R=[qT_all], W=[qbd])
                        self.cp('pool', qbd[64:128, pr, 1, :], qT_all[64:128, 2 * kvh + pr, :], R=[qT_all], W=[qbd])
                    for tl in range(4):
                        i = grp * 4 + tl
                        qc = slice(tl * 128, (tl + 1) * 128)
                        bc = self.ring('bcp', 2, lambda jj: P.sb([128, 4, 256], F32, f"bcp{jj}"))
                        smc = self.ring('smc', 2, lambda jj: P.sb([128, 2, 64], F32, f"smc{jj}"))
                        P.dma(smc[:, 0, :], d['selmul_p'].ap()[:, i, :], W=[smc])
                        P.dma(smc[:, 1, :], d['seladd_p'].ap()[:, i, :], W=[smc])
                        for a in range(8):
                            src = bass.AP(d['t_cmp'].h, 4 * kvh * 8192 + 247 - 16 * i - a, [[512, 16], [8192, 4], [1, 256]])
                            P.dma(bc[16 * a:16 * a + 16, :, :], src, R=[d['t_cmp']], W=[bc])
                        nkt = 2 * i + 2
                        sel_tiles = []
                        for kt in range(nkt):
                            e = 2 * i - kt
                            sel_tiles.append((kt, (BT[:, e + 1, :], BT) if e <= 7 else None))
                        win_tiles = []
                        for e in range(4, -2, -1):
                            kt = 2 * i - e
                            if kt >= 0:
                                win_tiles.append((kt, (BT[:, 10 + e, :], BT)))
                        A = dict(
                            NQ=128, NC=NCB + 1, NH=2, kvh=kvh,
                            qT=(lambda g, qc=qc, kvh=kvh: qT_all[(g % 2) * 64:(g % 2) * 64 + 64, 2 * kvh + g // 2, qc], qT_all),
                            qbd=qbd, qbd_t=qbd, qcols=qc,
                            gate=(lambda br, tl=tl, kvh=kvh: gates[:, tl, :].rearrange("p (h r) -> p h r", r=3)[:, 4 * kvh:4 * kvh + 4, br], gates),
                            kcmp=(lambda g, kvh=kvh: kcd[(g % 2) * 64:(g % 2) * 64 + 64, kvh, 0:NCB + 1], kcd),
                            vcmp=(lambda hf, kvh=kvh: vcm[:, hf, kvh, :], vcm),
                            bias_c=(bc[:, :, :], bc),
                            selc=(smc[:, 0, :], smc[:, 1, :], smc),
                            sel=(sel_tiles,
                                 lambda kt0, n, kvh=kvh: (d['kselT_p'].ap()[kvh, :, kt0 * 128:(kt0 + n) * 128], d['kselT_p']),
                                 lambda kt0, n, kvh=kvh: (d['vsel_p'].ap()[kt0:kt0 + n, :, kvh, :].rearrange("t p e -> p t e"), d['vsel_p']),
                                 self.pb[4]),
                            win=(win_tiles,
                                 lambda kt0, n, kvh=kvh: (d['kwinT_p'].ap()[kvh, :, kt0 * 128:(kt0 + n) * 128], d['kwinT_p']),
                                 lambda kt0, n, kvh=kvh: (d['vwin_p'].ap()[kt0:kt0 + n, :, kvh, :].rearrange("t p e -> p t e"), d['vwin_p']),
                                 self.pb[5]),
                            out=(o_tok[:, tl, kvh * 256:(kvh + 1) * 256].rearrange("p (g e) -> p g e", g=4), o_tok),
                        )
                        self.attend(A)
                        if cfg.get('nsa_stop', 99) == 4:
                            raise _Stop()
                for tl in range(4):
                    pt = self.pt[0]
                    for cc in range(8):
                        self.tr(pt[:, cc * 128:(cc + 1) * 128], o_tok[:, tl, cc * 128:(cc + 1) * 128], c['identb'][:, :],
                                R=[o_tok, c['identb']], W=[pt])
                    self.cp('act', T['oT'][:, :, tl * 128:(tl + 1) * 128], pt[:, :].rearrange("p (c n) -> p c n", c=8),
                            R=[pt], W=[T['oT']])
                self.tok_linear_add(G, T['oT'], 8, d['wb_nsa_out'][j])
                self.ffn(G, 2 + j)
            _final_norm(self, G, d['y_p'], grp * 512, ytile=self.rr['xeo'][0][0])


Builder.nsa_prompt = _nsa_prompt


def _nsa_sample(self):
    P, d, c, cfg = self.P, self.d, self.c, self.cfg
    NSQ = 16
    self.nsa_late_consts()
    kcd = P.sb([128, NSQ, 4, 128], BF16, "kcds")
    vcm = P.sb([128, NSQ, 4, 64], BF16, "vcms")
    self.memset('pool', vcm[:, :, :, :], 0.0, W=[vcm])
    with self.phase():
        ptb = P.sb([128, 256], I32, "ptb")
        P.dma(ptb[:, :], d['page_table'].ap()[0].partition_broadcast(128), W=[ptb])
        ptf = P.sb([128, 256], F32, "ptf")
        self.cp('dve', ptf[:, :], ptb[:, :], R=[ptb], W=[ptf])
        iot = P.sb([128, 1], F32, "iot")
        P.dma(iot[:, :], d['iota'].ap(), W=[iot])
        self.stt('dve', ptf[:, :], ptf[:, :], 128.0, iot[:, 0:1].to_broadcast([128, 256]), ALU.mult, ALU.add,
                 R=[ptf, iot], W=[ptf])
        idx = P.sb([128, 256], I32, "idxall")
        self.cp('dve', idx[:, :], ptf[:, :], R=[ptf], W=[idx])
        for s in range(NSQ):
            lohi = self.ring('lohis', 2, lambda j: P.sb([128, 2, 4, 136], F32, f"lohis{j}"))
            self.memset('pool', lohi[:, :, :, :], 0.0, W=[lohi])
            for Pi in range(17):
                rows = self.ring('crow', 2, lambda j: P.sb([128, 1024], F32, f"crows{j}"))
                if Pi < 16:
                    k = s * 16 + Pi
                    P.add('pool', lambda e, rows=rows, k=k: e.indirect_dma_start(
                        out=rows[:, :], out_offset=None, in_=d['cache_kv'].ap(),
                        in_offset=bass.IndirectOffsetOnAxis(ap=idx[:, k:k + 1], axis=0)),
                        R=[idx, d['cache_kv']], W=[rows], dma=True)
                else:
                    self.memset('pool', rows[:, :], 0.0, W=[rows])
                    P.dma(rows[0:4, :], d['s_kv_rows'].ap()[4 * s:4 * s + 4, :], R=[d['s_kv_rows']], W=[rows])
                cs = slice(Pi * 128, (Pi + 1) * 128)
                self.ctx_rows(rows, Pi, lohi if Pi < 16 else None,
                              (d['kselT_s'].ap()[s][:, :, cs].rearrange("k p n -> p k n"), d['kselT_s']),
                              (d['vsel_s'].ap()[s, Pi], d['vsel_s']), None, None, has_win=False)
            for W_ in range(5):
                wr = self.ring('wrow_s', 2, lambda j: P.sb([128, 512], F32, f"wrows{j}"))
                if W_ < 4:
                    P.dma(wr[:, :], d['state_win_kv'].ap()[s, W_ * 128:(W_ + 1) * 128, :], W=[wr])
                else:
                    self.memset('pool', wr[:, :], 0.0, W=[wr])
                    P.dma(wr[0:4, :], d['s_win_kv'].ap()[s, 508:512, :], R=[d['s_win_kv']], W=[wr])
                cs = slice(W_ * 128, (W_ + 1) * 128)
                self.ctx_rows(None, 0, None, None, None,
                              (d['kwinT_s'].ap()[s][:, :, cs].rearrange("k p n -> p k n"), d['kwinT_s']),
                              (d['vwin_s'].ap()[s, W_], d['vwin_s']), has_cmpsel=False, win_rows=(wr, 0))
            self.cmp_finish(lohi, 127, lambda kvh, s=s: kcd[:, s, kvh, :], lambda jh, kvh, s=s: vcm[:, s, kvh, :], kcd, vcm,
                            ncw=128, njh=1)
    with self.phase():
        G = Ctx('m', 64, 1, 16, 16, 1)
        T = {}
        T['x'] = P.sb([64, 1, D], F32, "mx")
        T['xnT'] = P.sb([128, 8, 64], BF16, "mxnT")
        big = P.sb([128, 24, 64], BF16, "mbig")
        T['actT'] = View(big, 0, 22)
        qT_all = View(big, 0, 8)
        T['oT'] = P.sb([128, 8, 64], BF16, "moT")
        G.T = T
        P.dma(T['x'][:, 0, :], d['x2_s'].ap(), R=[d['x2_s']], W=[T['x']])
        qbd = P.sb([128, 4, 2, 2, 64], BF16, "qbds")
        self.memset('pool', qbd[:, :, :, :, :], 0.0, W=[qbd])
        smul = P.sb([4, 64], F32, "smuls")
        sadd = P.sb([4, 64], F32, "sadds")
        P.dma(smul[:, :], d['selmul_s'].ap(), W=[smul])
        P.dma(sadd[:, :], d['seladd_s'].ap(), W=[sadd])
        bcs = P.sb([4, 16, 128], F32, "bcs")
        P.dma(bcs[:, :, :], bass.AP(d['t_cs'].h, 0, [[128, 4], [512, 16], [1, 128]]), R=[d['t_cs']], W=[bcs])
        trv = P.sb([128, 22, 16, 4], F32, "trvs")
        for Pi in range(17):
            P.dma(trv[:, Pi, :, :], bass.AP(d['t_ss'].h, 2048 - 128 * Pi, [[1, 128], [2304, 16], [1, 4]]), R=[d['t_ss']], W=[trv])
        for W_ in range(5):
            P.dma(trv[:, 17 + W_, :, :], bass.AP(d['t_ws'].h, 512 - 128 * W_, [[1, 128], [768, 16], [1, 4]]), R=[d['t_ws']], W=[trv])
        BTs = P.sb([128, 22, 16, 4], F32, "BTs")
        tf = trv[:, :, :, :].rearrange("p a h t -> p (a h t)")
        bf_ = BTs[:, :, :, :].rearrange("p a h t -> p (a h t)")
        for off in range(0, 22 * 64, 512):
            n = min(512, 22 * 64 - off)
            ps = self.next_pb('lin', [1, 2, 3])
            self.mm(ps[:, 0:n], c['antiI'][:, :], tf[:, off:off + n], R=[c['antiI'], trv], W=[ps])
            self.cp('act', bf_[:, off:off + n], ps[:, 0:n], R=[ps], W=[BTs])
        o_seq = P.sb([4, D], BF16, "o_seq")
        for j in range(2):
            self.norm_T(G, d['norm_mix'].ap()[2 + j])
            _nsa_qproj(self, G, j, qT_all, 1)
            for kvh in range(4):
                for pr in range(2):
                    self.cp('pool', qbd[0:64, kvh, pr, 0, :], qT_all[0:64, 2 * kvh + pr, :], R=[qT_all], W=[qbd])
                    self.cp('pool', qbd[64:128, kvh, pr, 1, :], qT_all[64:128, 2 * kvh + pr, :], R=[qT_all], W=[qbd])
            for s in range(NSQ):
                qc = slice(4 * s, 4 * s + 4)
                pg = self.pb[0]
                for kc in range(8):
                    self.mm(pg[0:4, 0:48], T['xnT'][:, kc, qc], c['wg'][:, j, kc, :], start=(kc == 0), stop=(kc == 7),
                            R=[T['xnT'], c['wg']], W=[pg])
                gs = self.ring('gs', 2, lambda jj: P.sb([4, 48], F32, f"gs{jj}"))
                self.actf(gs[:, :], pg[0:4, 0:48], AF.Sigmoid, R=[pg], W=[gs])
                for kvh in range(4):
                    sel_tiles = [(Pi, (BTs[:, Pi, 4 * kvh:4 * kvh + 4, :].rearrange("p h t -> p (h t)"), BTs)) for Pi in range(17)]
                    win_tiles = [(W_, (BTs[:, 17 + W_, 4 * kvh:4 * kvh + 4, :].rearrange("p h t -> p (h t)"), BTs)) for W_ in range(5)]
                    A = dict(
                        NQ=4, NC=128, NH=1, kvh=kvh,
                        qT=(lambda g, qc=qc, kvh=kvh: qT_all[(g % 2) * 64:(g % 2) * 64 + 64, 2 * kvh + g // 2, qc], qT_all),
                        qbd=qbd.h[:, kvh], qbd_t=qbd, qcols=qc,
                        gate=(lambda br, kvh=kvh, gs=gs: gs[:, :].rearrange("p (h r) -> p h r", r=3)[:, 4 * kvh:4 * kvh + 4, br], gs),
                        kcmp=(lambda g, kvh=kvh, s=s: kcd[(g % 2) * 64:(g % 2) * 64 + 64, s, kvh, :], kcd),
                        vcmp=(lambda hf, kvh=kvh, s=s: vcm[:, s, kvh, :], vcm),
                        bias_c=(bcs[:, 4 * kvh:4 * kvh + 4, :], bcs),
                        selc=(smul[:, :], sadd[:, :], smul),
                        sel=(sel_tiles,
                             lambda kt0, n, kvh=kvh, s=s: (d['kselT_s'].ap()[s, kvh, :, kt0 * 128:(kt0 + n) * 128], d['kselT_s']),
                             lambda kt0, n, kvh=kvh, s=s: (d['vsel_s'].ap()[s, kt0:kt0 + n, :, kvh, :].rearrange("t p e -> p t e"), d['vsel_s']),
                             self.pb[4]),
                        win=(win_tiles,
                             lambda kt0, n, kvh=kvh, s=s: (d['kwinT_s'].ap()[s, kvh, :, kt0 * 128:(kt0 + n) * 128], d['kwinT_s']),
                             lambda kt0, n, kvh=kvh, s=s: (d['vwin_s'].ap()[s, kt0:kt0 + n, :, kvh, :].rearrange("t p e -> p t e"), d['vwin_s']),
                             self.pb[5]),
                        out=(o_seq[:, kvh * 256:(kvh + 1) * 256].rearrange("p (g e) -> p g e", g=4), o_seq),
                    )
                    self.attend(A)
                pt = self.pt[0]
                for cc in range(8):
                    self.tr(pt[:, cc * 4:(cc + 1) * 4], o_seq[:, cc * 128:(cc + 1) * 128], c['identb'][0:4, 0:4],
                            R=[o_seq, c['identb']], W=[pt])
                self.cp('act', T['oT'][:, :, qc], pt[:, 0:32].rearrange("p (c n) -> p c n", c=8), R=[pt], W=[T['oT']])
            self.tok_linear_add(G, T['oT'], 8, d['wb_nsa_out'][j])
            self.ffn(G, 2 + j)
        _final_norm(self, G, d['y_s'], 0)


Builder.nsa_sample = _nsa_sample
```

```python
import math
import numpy as np
import concourse.bass as bass
import concourse.mybir as mybir
from concourse.bass_utils import run_bass_kernel_spmd

F32 = mybir.dt.float32
BF16 = mybir.dt.bfloat16
I32 = mybir.dt.int32
AF = mybir.ActivationFunctionType
ALU = mybir.AluOpType

EPOCH = 16000
NSLOT = 8
ENGS = ('pe', 'act', 'dve', 'pool', 'sp')
SAME_ENGINE_SYNC = {'pe': False, 'act': True, 'dve': True, 'pool': True, 'sp': True}

D = 1024
H = 8
DFF = 2816
NEG = -30000.0


class Buf:
    __slots__ = ('name', 'lw', 'rd')

    def __init__(self, name):
        self.name = name
        self.lw = None
        self.rd = []


class Tile:
    def __init__(self, h, name):
        self.h = h
        self.buf = Buf(name)

    def __getitem__(self, idx):
        return self.h[idx]

    def ap(self):
        return self.h.ap()


class View:
    def __init__(self, base, off, n):
        self.base, self.buf, self.off, self.n = base, base.buf, off, n

    def __getitem__(self, idx):
        idx = list(idx)
        a = idx[1]
        if isinstance(a, slice):
            st = (a.start or 0) + self.off
            en = (a.stop if a.stop is not None else self.n) + self.off
            idx[1] = slice(st, en)
        else:
            idx[1] = a + self.off
        return self.base.h[tuple(idx)]


class Prog:
    def __init__(self, nc):
        self.nc = nc
        self.ops = {e: [] for e in ENGS}
        self.cnt = {e: 0 for e in ENGS}
        self.dcnt = {e: 0 for e in ENGS}
        self.known = {e: {} for e in ENGS}
        self.nt = 0
        self.stack = None

    def sb(self, shape, dtype=F32, name=None):
        self.nt += 1
        name = name or "t"
        if self.stack is not None:
            h = self.stack.enter_context(self.nc.sbuf_tensor(f"{name}_{self.nt}", list(shape), dtype))
        else:
            h = self.nc.alloc_sbuf_tensor(f"{name}_{self.nt}", list(shape), dtype)
        return Tile(h, name)

    def barrier(self):
        evs = []
        for f in ENGS:
            if self.cnt[f] > 0:
                evs.append(('c', f, self.cnt[f]))
            n = self.dcnt[f]
            for slot in range(min(n, NSLOT)):
                evs.append(('d', f, slot, (n - 1 - slot) // NSLOT + 1))
        for e in ENGS:
            waits = []
            for ev in evs:
                if ev[0] == 'c' and ev[1] == e:
                    continue
                self._need(e, ev, waits)
            self.cnt[e] += 1
            self.ops[e].append((waits, (lambda en: en.nop()), ('c', e, self.cnt[e])))

    def ps(self, shape, dtype=F32, name=None):
        self.nt += 1
        name = name or "p"
        h = self.nc.alloc_psum_tensor(f"{name}_{self.nt}", list(shape), dtype)
        return Tile(h, name)

    def dram(self, name, shape, dtype=F32, kind="Internal"):
        h = self.nc.dram_tensor(name, list(shape), dtype, kind=kind)
        return Tile(h, name)

    def _need(self, eng, ev, waits):
        if ev is None:
            return
        if ev[0] == 'c':
            _, f, seq = ev
            if f == eng and not SAME_ENGINE_SYNC[eng]:
                return
            key = ('c', f)
            if self.known[eng].get(key, 0) >= seq:
                return
            self.known[eng][key] = seq
            waits.append(ev)
        else:
            _, q, slot, k = ev
            key = ('d', q, slot)
            if self.known[eng].get(key, 0) >= k:
                return
            self.known[eng][key] = k
            waits.append(ev)

    def add(self, eng, emit, R=(), W=(), dma=False):
        waits = []
        for t in R:
            self._need(eng, t.buf.lw, waits)
        for t in W:
            b = t.buf
            self._need(eng, b.lw, waits)
            for ev in b.rd:
                self._need(eng, ev, waits)
        if dma:
            i = self.dcnt[eng]
            self.dcnt[eng] += 1
            slot, k = i % NSLOT, i // NSLOT + 1
            if k > 1:
                self._need(eng, ('d', eng, slot, k - 1), waits)
            ev = ('d', eng, slot, k)
        else:
            self.cnt[eng] += 1
            ev = ('c', eng, self.cnt[eng])
        for t in R:
            t.buf.rd.append(ev)
        for t in W:
            t.buf.lw = ev
            t.buf.rd = []
        self.ops[eng].append((waits, emit, ev))
        return ev

    def dma(self, out_ap, in_ap, R=(), W=(), q='sp', **kw):
        return self.add(q, lambda e: e.dma_start(out=out_ap, in_=in_ap, **kw), R, W, dma=True)

    def emit(self):
        nc = self.nc
        csem = {}
        for e in ENGS:
            n = (self.cnt[e] + EPOCH - 1) // EPOCH
            csem[e] = [nc.alloc_semaphore(f"c_{e}_{j}") for j in range(n)]
        dsem = {}
        for e in ENGS:
            n = min(self.dcnt[e], NSLOT)
            dsem[e] = [nc.alloc_semaphore(f"d_{e}_{j}") for j in range(n)]

        def semval(ev):
            if ev[0] == 'c':
                _, f, seq = ev
                return csem[f][(seq - 1) // EPOCH], (seq - 1) % EPOCH + 1
            _, q, slot, k = ev
            return dsem[q][slot], 16 * k

        def run(eng, e):
            for waits, emit, ev in self.ops[eng]:
                for w in waits:
                    s, v = semval(w)
                    e.wait_ge(s, v)
                ins = emit(e)
                s, v = semval(ev)
                ins.then_inc(s, 16 if ev[0] == 'd' else 1)
            n = self.dcnt[eng]
            for slot in range(min(n, NSLOT)):
                k = (n - 1 - slot) // NSLOT + 1
                if self.known[eng].get(('d', eng, slot), 0) < k:
                    e.wait_ge(dsem[eng][slot], 16 * k)

        with nc.Block() as block:
            @block.tensor
            def _(e):
                run('pe', e)

            @block.scalar
            def _(e):
                run('act', e)

            @block.vector
            def _(e):
                run('dve', e)

            @block.gpsimd
            def _(e):
                run('pool', e)

            @block.sync
            def _(e):
                run('sp', e)
        return nc


class Ctx:
    def __init__(self, name, TP, NT, NSEQ, NS, nlev):
        self.name = name
        self.TP = TP
        self.NT = NT
        self.GT = TP * NT
        self.NSEQ = NSEQ
        self.TS = self.GT // NSEQ
        self.NCH = self.GT // 64
        self.NS = NS
        self.nlev = nlev


def host_consts(NS, TS):
    seg = np.arange(64) // TS if NS > 1 else np.zeros(64, np.int64)
    j = np.arange(64)[:, None]
    i = np.arange(64)[None, :]
    same = (seg[:, None] == seg[None, :])
    c = {}
    c['ucs'] = ((j <= i) & same).astype(np.float32)
    c['maskT'] = np.where((j <= i) & same, 0.0, NEG).astype(np.float32)
    c['noff'] = -np.where((j != i), 1.0, 0.0).astype(np.float32)
    c['same'] = same.astype(np.float32)
    si = np.zeros((64, NS), np.float32)
    si[np.arange(64), seg] = 1.0
    c['seqind'] = si
    cm = np.zeros((128, NS, 64), np.float32)
    cm[:, seg, np.arange(64)] = 1.0
    c['colmask'] = cm.reshape(128, NS * 64)
    return c


class _Stop(Exception):
    pass


class Builder:
    def __init__(self, cfg):
        self.cfg = cfg
        nc = bass.Bass("TRN2", target_bir_lowering=False)
        self.nc = nc
        self.P = Prog(nc)
        self.rr = {}

    def mm(self, out, lhsT, rhs, start=True, stop=True, R=(), W=()):
        self.P.add('pe', lambda e: e.matmul(out, lhsT=lhsT, rhs=rhs, start=start, stop=stop), R, W)

    def tr(self, out, in_, ident, R=(), W=()):
        self.P.add('pe', lambda e: e.transpose(out=out, in_=in_, identity=ident), R, W)

    def actf(self, out, in_, func, R=(), W=(), bias=None, scale=None, accum_out=None):
        kw = {}
        if bias is not None:
            kw['bias'] = bias
        if scale is not None:
            kw['scale'] = scale
        if accum_out is not None:
            kw['accum_out'] = accum_out
        self.P.add('act', lambda e: e.activation(out=out, in_=in_, func=func, **kw), R, W)

    def cp(self, eng, out, in_, R=(), W=()):
        if eng == 'act':
            self.P.add('act', lambda e: e.copy(out=out, in_=in_), R, W)
        else:
            self.P.add(eng, lambda e: e.tensor_copy(out=out, in_=in_), R, W)

    def tt(self, eng, out, in0, in1, op, R=(), W=()):
        self.P.add(eng, lambda e: e.tensor_tensor(out=out, in0=in0, in1=in1, op=op), R, W)

    def ts(self, eng, out, in0, s1, s2, op0, op1=None, R=(), W=()):
        if op1 is None:
            self.P.add(eng, lambda e: e.tensor_scalar(out=out, in0=in0, scalar1=s1, scalar2=None, op0=op0), R, W)
        else:
            self.P.add(eng, lambda e: e.tensor_scalar(out=out, in0=in0, scalar1=s1, scalar2=s2, op0=op0, op1=op1), R, W)

    def stt(self, eng, out, in0, scalar, in1, op0, op1, R=(), W=()):
        self.P.add(eng, lambda e: e.scalar_tensor_tensor(out=out, in0=in0, scalar=scalar, in1=in1, op0=op0, op1=op1), R, W)

    def memset(self, eng, ap, val, W=()):
        self.P.add(eng, lambda e: e.memset(ap, val), (), W)

    def ring(self, key, n, make):
        if key not in self.rr:
            self.rr[key] = [[make(i) for i in range(n)], 0]
        r = self.rr[key]
        t = r[0][r[1] % n]
        r[1] += 1
        return t

    def declare_io(self):
        P, cfg = self.P, self.cfg
        d = {}

        def inp(name, shape, dt=F32):
            d[name] = P.dram(name, shape, dt, kind="ExternalInput")

        def out(name, shape, dt=F32):
            d[name] = P.dram(name, shape, dt, kind="ExternalOutput")

        SEQ = cfg['SEQ']
        inp('x_prompt', [SEQ, D])
        inp('x_sample', [64, D])
        inp('state_dn_S', [2, 16, H, 128, 128])
        inp('state_dn_conv', [2, 16, 3, 3072])
        inp('state_win_kv', [16, 512, 512])
        inp('norm_mix', [4, D])
        inp('norm_ffn', [4, D])
        inp('norm_kv', [D])
        inp('norm_final', [D])
        inp('ffn_w_in', [4, D, 2 * DFF])
        inp('ffn_w_out', [4, DFF, D])
        inp('dn_w_in', [2, D, 4112])
        inp('dn_conv_w', [2, 4, 3072])
        inp('dn_A_log', [2, H])
        inp('dn_dt_bias', [2, H])
        inp('dn_out_norm', [2, 128])
        inp('dn_w_out', [2, D, D])
        inp('nsa_w_kv', [D, 1536])
        for pre, NS in (('cp_', 1), ('cs_', 16)):
            inp(pre + 'ucs', [64, 64])
            inp(pre + 'maskT', [64, 64])
            inp(pre + 'noff', [64, 64])
            inp(pre + 'same', [64, 64])
            inp(pre + 'seqind', [64, NS])
            inp(pre + 'colmask', [128, NS * 64])
        out('p_dn_S', [2, H, 128, 128])
        out('p_dn_conv', [2, 3, 3072])
        out('p_kv_rows', [SEQ, 1024])
        out('p_win_kv', [512, 512])
        out('s_dn_S', [2, 16, H, 128, 128])
        out('s_dn_conv', [2, 16, 3, 3072])
        out('s_kv_rows', [64, 1024])
        out('s_win_kv', [16, 512, 512])
        out('x2_p', [SEQ, D])
        out('x2_s', [64, D])
        d['wc_dn_in'] = [P.dram(f'wc_dn_in{l}', [32, 128, 8, 128], BF16) for l in range(2)]
        d['wc_ffn_in'] = [P.dram(f'wc_ffn_in{l}', [44, 128, 8, 128], BF16) for l in range(4)]
        d['wb_ffn_out'] = [P.dram(f'wb_ffn_out{l}', [DFF, D], BF16) for l in range(4)]
        d['wb_dn_out'] = [P.dram(f'wb_dn_out{l}', [D, D], BF16) for l in range(2)]
        d['wb_kv'] = P.dram('wb_kv', [D, 1536], BF16)
        self.d = d

    def consts(self):
        P, d = self.P, self.d
        c = {}
        identf = P.sb([128, 128], F32, "identf")
        self.memset('pool', identf[:, :], 1.0, W=[identf])
        P.add('pool', lambda e: e.affine_select(out=identf[:, :], in_=identf[:, :], pattern=[[-1, 128]],
                                                compare_op=ALU.is_equal, fill=0.0, base=0, channel_multiplier=1),
              R=[identf], W=[identf])
        identb = P.sb([128, 128], BF16, "identb")
        self.cp('dve', identb[:, :], identf[:, :], R=[identf], W=[identb])
        onesb = P.sb([128, 128], BF16, "onesb")
        self.memset('pool', onesb[:, :], 1.0, W=[onesb])
        onesf = P.sb([64, 128], F32, "onesf")
        self.memset('pool', onesf[:, :], 1.0, W=[onesf])
        c.update(identf=identf, identb=identb, onesb=onesb, onesf=onesf)
        for pre, NS in (('cp_', 1), ('cs_', 16)):
            for nm, shp in (('ucs', [64, 64]), ('maskT', [64, 64]), ('noff', [64, 64]), ('same', [64, 64]),
                            ('seqind', [64, NS])):
                t = P.sb(shp, F32, pre + nm)
                P.dma(t[:, :], d[pre + nm].ap(), W=[t])
                c[pre + nm] = t
            if NS > 1:
                t = P.sb([128, NS * 64], BF16, pre + 'colmask')
                P.dma(t[:, :], d[pre + 'colmask'].ap(), W=[t], q='pool')
                c[pre + 'colmask'] = t
        cw = P.sb([128, 2, 24, 4], F32, "cw")
        for l in range(2):
            for i in range(4):
                P.dma(cw[:, l, :, i], d['dn_conv_w'].ap()[l, i].rearrange("(c p) -> p c", p=128), W=[cw],
                      allow_slow_non_contiguous=True)
        c['cw'] = cw
        negA = P.sb([128, 2, H], F32, "negA")
        dtb = P.sb([128, 2, H], F32, "dtb")
        P.dma(negA[:, :, :], d['dn_A_log'].ap().rearrange("l h -> (l h)").partition_broadcast(128).rearrange("p (l h) -> p l h", l=2), W=[negA])
        P.dma(dtb[:, :, :], d['dn_dt_bias'].ap().rearrange("l h -> (l h)").partition_broadcast(128).rearrange("p (l h) -> p l h", l=2), W=[dtb])
        self.actf(negA[:, :, :], negA[:, :, :], AF.Exp, R=[negA], W=[negA])
        self.ts('dve', negA[:, :, :], negA[:, :, :], -1.0, None, ALU.mult, R=[negA], W=[negA])
        c.update(negA=negA, dtb=dtb)
        onw = P.sb([128, 2], F32, "onw")
        P.dma(onw[:, :], d['dn_out_norm'].ap().rearrange("l p -> p l"), W=[onw], allow_slow_non_contiguous=True)
        c['onw'] = onw
        wab = P.sb([128, 2, 8, 16], BF16, "wab")
        for l in range(2):
            P.dma(wab[:, l, :, :], d['dn_w_in'].ap()[l, :, 4096:4112].rearrange("(kc p) n -> p kc n", p=128), W=[wab], q='pool')
        c['wab'] = wab
        self.c = c
        self.pb = [P.ps([128, 512], F32, f"pb{i}") for i in range(6)]
        self.pt = [P.ps([128, 1024], BF16, f"pt{i}") for i in range(2)]

    def convert_weights(self):
        P, d = self.P, self.d
        k = [0]
        SW = 2048

        def stage():
            i = k[0]
            k[0] += 1
            f = self.ring('cvf', 2, lambda j: P.sb([128, SW], F32, f"cvf{j}"))
            b = self.ring('cvb', 2, lambda j: P.sb([128, SW], BF16, f"cvb{j}"))
            return f, b, ('act', 'dve', 'pool')[i % 3]

        def conv_chunked(W_ap, dst, nch):
            for g in range(nch // 2):
                f, b, e = stage()
                P.dma(f[:, :].rearrange("p (k n) -> p k n", k=8),
                      W_ap[:, g * 256:(g + 1) * 256].rearrange("(kc p) n -> p kc n", p=128), W=[f], q='sp')
                o = b[:, :].rearrange("p (c k n) -> p k c n", c=2, k=8)
                sv = f[:, :].rearrange("p (k c n) -> p k c n", k=8, c=2)
                self.cp(e, o, sv, R=[f], W=[b])
                P.dma(dst.ap()[g * 2:(g + 1) * 2].rearrange("c p k n -> p c (k n)"),
                      b[:, :].rearrange("p (c kn) -> p c kn", c=2), R=[b], W=[dst], q='act')

        def conv_natural(W_ap, dst, K, N):
            per = max(1, SW // N)
            nk = K // 128
            kc = 0
            while kc < nk:
                m = min(per, nk - kc)
                f, b, e = stage()
                P.dma(f[:, 0:m * N].rearrange("p (k n) -> p k n", k=m),
                      W_ap[kc * 128:(kc + m) * 128, :].rearrange("(k p) n -> p k n", p=128), W=[f], q='sp')
                self.cp(e, b[:, 0:m * N], f[:, 0:m * N], R=[f], W=[b])
                P.dma(dst.ap()[kc * 128:(kc + m) * 128, :].rearrange("(k p) n -> p k n", p=128),
                      b[:, 0:m * N].rearrange("p (k n) -> p k n", k=m), R=[b], W=[dst], q='act')
                kc += m

        for l in range(self.cfg['n_dn']):
            conv_chunked(d['dn_w_in'].ap()[l, :, 0:4096], d['wc_dn_in'][l], 32)
            conv_natural(d['dn_w_out'].ap()[l], d['wb_dn_out'][l], D, D)
            conv_chunked(d['ffn_w_in'].ap()[l], d['wc_ffn_in'][l], 44)
            conv_natural(d['ffn_w_out'].ap()[l], d['wb_ffn_out'][l], DFF, D)
        conv_natural(d['nsa_w_kv'].ap(), d['wb_kv'], D, 1536)
        if self.cfg.get('nsa', True):
            for j in range(2):
                conv_chunked(d['nsa_w_in'].ap()[j, :, 0:1024], d['wc_nsa_in'][j], 8)
                conv_natural(d['nsa_w_out'].ap()[j], d['wb_nsa_out'][j], D, D)
                conv_chunked(d['ffn_w_in'].ap()[2 + j], d['wc_ffn_in'][2 + j], 44)
                conv_natural(d['ffn_w_out'].ap()[2 + j], d['wb_ffn_out'][2 + j], DFF, D)

    def alloc_ctx(self, G):
        P = self.P
        n = G.name
        T = {}
        T['x'] = P.sb([G.TP, G.NT, D], F32, n + "x")
        T['xnT'] = P.sb([128, 8, G.GT], BF16, n + "xnT")
        big = P.sb([128, 24, G.GT], BF16, n + "big")
        T['qT'] = View(big, 0, 8)
        T['kT'] = View(big, 8, 8)
        T['vT'] = View(big, 16, 8)
        T['actT'] = View(big, 0, 22)
        T['zs'] = P.sb([128, H, G.GT], BF16, n + "zs")
        T['OT'] = P.sb([128, H, G.GT], F32, n + "OT")
        T['oT'] = P.sb([128, H, G.GT], BF16, n + "oT")
        T['carry'] = [P.sb([128, 24, G.NSEQ, 3], F32, n + f"carry{l}") for l in range(2)]
        T['ab'] = P.sb([64, G.NCH, 16], F32, n + "ab")
        T['g'] = P.sb([64, G.NCH, H], F32, n + "g")
        T['beta'] = P.sb([64, G.NCH, H], F32, n + "beta")
        G.T = T

    def norm_T(self, G, wrow_src):
        P, c, T = self.P, self.c, G.T
        TP = G.TP
        wrow = self.ring('wrow', 2, lambda j: P.sb([128, D], F32, f"wrow{j}"))
        P.dma(wrow[:, :], wrow_src.partition_broadcast(128), W=[wrow])
        x = T['x']
        for t in range(G.NT):
            junk = self.ring('junk', 1, lambda j: P.sb([128, D], BF16, f"junk{j}"))
            st = self.ring('nst', 4, lambda j: P.sb([128, 2], F32, f"nst{j}"))
            self.memset('pool', st[:, :], 0.0, W=[st])
            self.actf(junk[0:TP, :], x[:, t, :], AF.Square, R=[x], W=[junk, st], accum_out=st[0:TP, 0:1])
            self.ts('dve', st[0:TP, 1:2], st[0:TP, 0:1], 1.0 / D, 1e-6, ALU.mult, ALU.add, R=[st], W=[st])
            self.actf(st[0:TP, 1:2], st[0:TP, 1:2], AF.Sqrt, R=[st], W=[st])
            P.add('dve', lambda e, st=st: e.reciprocal(out=st[0:TP, 1:2], in_=st[0:TP, 1:2]), R=[st], W=[st])
            xn = self.ring('xn', 2, lambda j: P.sb([128, D], BF16, f"xn{j}"))
            self.stt('dve', xn[0:TP, :], x[:, t, :], st[0:TP, 1:2], wrow[0:TP, :], ALU.mult, ALU.mult,
                     R=[x, st, wrow], W=[xn])
            pt = self.pt[0]
            for cc in range(8):
                self.tr(pt[:, cc * 128:cc * 128 + TP], xn[0:TP, cc * 128:(cc + 1) * 128], c['identb'][0:TP, 0:TP],
                        R=[xn, c['identb']], W=[pt])
            self.cp('act', T['xnT'][:, :, t * TP:(t + 1) * TP],
                    pt[:, :].rearrange("p (c n) -> p c n", c=8)[:, :, 0:TP], R=[pt], W=[T['xnT']])

    def load_wchunk(self, src_ap):
        P = self.P
        wt = self.ring('wch', 4, lambda j: P.sb([128, 8, 128], BF16, f"wch{j}"))
        P.dma(wt[:, :, :], src_ap, W=[wt])
        return wt

    def next_pb(self, key, banks):
        r = self.rr.setdefault('pb_' + key, [0])
        b = banks[r[0] % len(banks)]
        r[0] += 1
        return self.pb[b]

    def dn_mixer(self, G, l, S_tiles, last_group):
        P, c, T, d = self.P, self.c, G.T, self.d
        GT, TP, NT, NSEQ, TS, NCH, NS = G.GT, G.TP, G.NT, G.NSEQ, G.TS, G.NCH, G.NS
        pre = 'cp_' if NS == 1 else 'cs_'
        self.norm_T(G, d['norm_mix'].ap()[l])
        xnT = T['xnT']
        pbs = self.pb[0]
        for n in range(NCH):
            for kc in range(8):
                self.mm(pbs[0:64, n * 16:(n + 1) * 16], xnT[:, kc, n * 64:(n + 1) * 64], c['wab'][:, l, kc, :],
                        start=(kc == 0), stop=(kc == 7), R=[xnT, c['wab']], W=[pbs])
        ab = T['ab']
        self.cp('act', ab[:, :, :], pbs[0:64, 0:NCH * 16].rearrange("p (n k) -> p n k", k=16), R=[pbs], W=[ab])
        gt = self.ring('gtmp', 2, lambda j: P.sb([64, 8, H], F32, f"gtmp{j}"))
        g2 = self.ring('gtmp', 2, lambda j: None)
        av = gt[:, 0:NCH, :]
        a2 = g2[:, 0:NCH, :]
        self.tt('dve', av, ab[:, :, 0:8], c['dtb'][0:64, l, :].unsqueeze(1).to_broadcast([64, NCH, H]), ALU.add,
                R=[ab, c['dtb']], W=[gt])
        self.actf(a2, av, AF.Abs, R=[gt], W=[g2])
        self.actf(a2, a2, AF.Exp, R=[g2], W=[g2], scale=-1.0)
        self.actf(a2, a2, AF.Ln, R=[g2], W=[g2], bias=1.0)
        self.stt('dve', a2, av, 0.0, a2, ALU.max, ALU.add, R=[gt, g2], W=[g2])
        self.tt('dve', T['g'][:, :, :], a2, c['negA'][0:64, l, :].unsqueeze(1).to_broadcast([64, NCH, H]), ALU.mult,
                R=[g2, c['negA']], W=[T['g']])
        self.actf(T['beta'][:, :, :], ab[:, :, 8:16], AF.Sigmoid, R=[ab], W=[T['beta']])

        carry = T['carry'][l]
        for ci in range(32):
            wt = self.load_wchunk(d['wc_dn_in'][l].ap()[ci])
            ps = self.next_pb('lin', [1, 2, 3])
            for kc in range(8):
                self.mm(ps[:, 0:GT], wt[:, kc, :], xnT[:, kc, :], start=(kc == 0), stop=(kc == 7), R=[wt, xnT], W=[ps])
            hh = ci % 8
            if ci >= 24:
                self.actf(T['zs'][:, hh, :], ps[:, 0:GT], AF.Silu, R=[ps], W=[T['zs']])
                continue
            xp = self.ring(G.name + 'xp', 2, lambda j: P.sb([128, NSEQ, TS + 3], F32, G.name + f"xp{j}"))
            self.cp('act', xp[:, :, 3:TS + 3], ps[:, 0:GT].rearrange("p (s t) -> p s t", s=NSEQ), R=[ps], W=[xp])
            self.cp('pool', xp[:, :, 0:3], carry[:, ci, :, :], R=[carry], W=[xp])
            self.cp('pool', carry[:, ci, :, :], xp[:, :, TS:TS + 3], R=[xp], W=[carry])
            acc = self.ring(G.name + 'acc', 2, lambda j: P.sb([128, NSEQ, TS], F32, G.name + f"acc{j}"))
            cwl = c['cw']
            self.ts('dve', acc[:, :, :], xp[:, :, 0:TS], cwl[:, l, ci, 0:1], None, ALU.mult, R=[xp, cwl], W=[acc])
            for i in range(1, 4):
                self.stt('dve', acc[:, :, :], xp[:, :, i:TS + i], cwl[:, l, ci, i:i + 1], acc[:, :, :], ALU.mult, ALU.add,
                         R=[xp, cwl, acc], W=[acc])
            accf = acc[:, :, :].rearrange("p s t -> p (s t)")
            if ci >= 16:
                self.actf(T['vT'][:, hh, :], accf, AF.Silu, R=[acc], W=[T['vT']])
                continue
            sl = self.ring(G.name + 'sl', 2, lambda j: P.sb([128, GT], F32, G.name + f"sl{j}"))
            self.actf(sl[:, :], accf, AF.Silu, R=[acc], W=[sl])
            sq = self.ring(G.name + 'sq', 2, lambda j: P.sb([128, GT], BF16, G.name + f"sq{j}"))
            self.actf(sq[:, :], sl[:, :], AF.Square, R=[sl], W=[sq])
            ps2 = self.next_pb('nrm', [4, 5])
            self.mm(ps2[:, 0:GT], c['onesb'][:, :], sq[:, :], R=[c['onesb'], sq], W=[ps2])
            rn = self.ring(G.name + 'rn', 2, lambda j: P.sb([128, GT], F32, G.name + f"rn{j}"))
            self.ts('dve', rn[:, :], ps2[:, 0:GT], 1e-6, None, ALU.add, R=[ps2], W=[rn])
            self.actf(rn[:, :], rn[:, :], AF.Sqrt, R=[rn], W=[rn])
            P.add('dve', lambda e, rn=rn: e.reciprocal(out=rn[:, :], in_=rn[:, :]), R=[rn], W=[rn])
            dst = T['qT'] if ci < 8 else T['kT']
            scl = (128 ** -0.5) if ci < 8 else 1.0
            self.stt('dve', dst[:, hh, :], sl[:, :], scl, rn[:, :], ALU.mult, ALU.mult, R=[sl, rn], W=[dst])

        for n in range(NCH):
            self.dn_chunk(G, l, n, S_tiles, pre)

        OT, oT, zs = T['OT'], T['oT'], T['zs']
        for hh in range(H):
            sq = self.ring(G.name + 'sq', 2, lambda j: None)
            self.actf(sq[:, :], OT[:, hh, :], AF.Square, R=[OT], W=[sq])
            ps2 = self.next_pb('nrm', [4, 5])
            self.mm(ps2[:, 0:GT], c['onesb'][:, :], sq[:, :], R=[c['onesb'], sq], W=[ps2])
            rn = self.ring(G.name + 'rn', 2, lambda j: None)
            self.ts('dve', rn[:, :], ps2[:, 0:GT], 1.0 / 128, 1e-6, ALU.mult, ALU.add, R=[ps2], W=[rn])
            self.actf(rn[:, :], rn[:, :], AF.Sqrt, R=[rn], W=[rn])
            P.add('dve', lambda e, rn=rn: e.reciprocal(out=rn[:, :], in_=rn[:, :]), R=[rn], W=[rn])
            self.stt('dve', rn[:, :], OT[:, hh, :], c['onw'][:, l:l + 1], rn[:, :], ALU.mult, ALU.mult,
                     R=[OT, c['onw'], rn], W=[rn])
            self.tt('dve', oT[:, hh, :], rn[:, :], zs[:, hh, :], ALU.mult, R=[rn, zs], W=[oT])

        self.tok_linear_add(G, oT, 8, d['wb_dn_out'][l])

        if last_group:
            self.conv_state_out(G, l)

    def tok_linear_add(self, G, aT, nk, wsrc):
        P, T = self.P, G.T
        TP, NT = G.TP, G.NT
        x = T['x']
        for half in range(2):
            banks = [self.pb[1 + t] for t in range(NT)]
            for kc in range(nk):
                wt = self.ring('wrh', 4, lambda j: P.sb([128, 512], BF16, f"wrh{j}"))
                P.dma(wt[:, :], wsrc.ap()[kc * 128:(kc + 1) * 128, half * 512:(half + 1) * 512], R=[wsrc], W=[wt])
                for t in range(NT):
                    self.mm(banks[t][0:TP, :], aT[:, kc, t * TP:(t + 1) * TP], wt[:, :], start=(kc == 0),
                            stop=(kc == nk - 1), R=[aT, wt], W=[banks[t]])
            for t in range(NT):
                self.tt('dve', x[:, t, half * 512:(half + 1) * 512], x[:, t, half * 512:(half + 1) * 512],
                        banks[t][0:TP, :], ALU.add, R=[x, banks[t]], W=[x])

    def conv_state_out(self, G, l):
        P, c, T, d = self.P, self.c, G.T, self.d
        NSEQ = G.NSEQ
        R3 = NSEQ * 3
        carry = T['carry'][l]
        for g4 in range(6):
            co = self.ring(G.name + 'co', 2, lambda j: P.sb([R3, 4, 128], F32, G.name + f"co{j}"))
            ps = self.next_pb('lin', [1, 2, 3])
            for j in range(4):
                ci = g4 * 4 + j
                self.tr(ps[0:R3, j * 128:(j + 1) * 128], carry[:, ci, :, :].rearrange("p s r -> p (s r)"),
                        c['identf'][:, :], R=[carry, c['identf']], W=[ps])
            self.cp('act', co[:, :, :], ps[0:R3, :].rearrange("p (j n) -> p j n", j=4), R=[ps], W=[co])
            if G.NS == 1:
                dst = d['p_dn_conv']
                P.dma(dst.ap()[l][:, g4 * 512:(g4 + 1) * 512].rearrange("r (c p) -> r c p", p=128), co[:, :, :],
                      R=[co], W=[dst], q='pool')
            else:
                dst = d['s_dn_conv']
                P.dma(dst.ap()[l][:, :, g4 * 512:(g4 + 1) * 512].rearrange("s r (c p) -> (s r) c p", p=128), co[:, :, :],
                      R=[co], W=[dst], q='pool')

    def conv_state_in(self, G, l):
        P, c, T, d = self.P, self.c, G.T, self.d
        R3 = G.NSEQ * 3
        carry = T['carry'][l]
        ci_t = self.ring(G.name + 'cin', 1, lambda j: P.sb([R3, 3072], F32, G.name + "cin"))
        P.dma(ci_t[:, :], d['state_dn_conv'].ap()[l].rearrange("s r n -> (s r) n"), W=[ci_t])
        for g4 in range(6):
            ps = self.next_pb('lin', [1, 2, 3])
            for j in range(4):
                ci = g4 * 4 + j
                self.tr(ps[:, j * R3:(j + 1) * R3], ci_t[:, ci * 128:(ci + 1) * 128], c['identf'][0:R3, 0:R3],
                        R=[ci_t, c['identf']], W=[ps])
            self.cp('act', carry[:, g4 * 4:(g4 + 1) * 4, :, :].rearrange("p c s r -> p c (s r)"),
                    ps[:, 0:4 * R3].rearrange("p (j n) -> p j n", j=4), R=[ps], W=[carry])

    def dn_chunk(self, G, l, n, S_tiles, pre):
        P, c, T = self.P, self.c, G.T
        NS = G.NS
        cs = slice(n * 64, (n + 1) * 64)
        qT, kT, vT = T['qT'], T['kT'], T['vT']
        ucs, maskT, noff, same, seqind = (c[pre + k] for k in ('ucs', 'maskT', 'noff', 'same', 'seqind'))
        gtok = T['g']
        beta = T['beta']
        pb = self.pb
        nm = G.name

        def sbt(key, shape, dt=F32, nbuf=2):
            pfx = nm if NS > 1 and key in ('SG', 'gl') else 'ck'
            return self.ring(pfx + key, nbuf, lambda j: P.sb(shape, dt, pfx + key + str(j)))

        self.mm(pb[0][0:64, 0:8], ucs[:, :], gtok[:, n, :], R=[ucs, gtok], W=[pb[0]])
        self.mm(pb[0][0:64, 8:16], same[:, :], gtok[:, n, :], R=[same, gtok], W=[pb[0]])
        Gt = sbt('Gt', [64, 16])
        self.cp('act', Gt[:, :], pb[0][0:64, 0:16], R=[pb[0]], W=[Gt])
        eGd = sbt('eGd', [64, 16])
        self.tt('dve', eGd[:, 8:16], Gt[:, 8:16], Gt[:, 0:8], ALU.subtract, R=[Gt], W=[eGd])
        self.cp('dve', eGd[:, 0:8], Gt[:, 0:8], R=[Gt], W=[eGd])
        self.actf(eGd[:, :], eGd[:, :], AF.Exp, R=[eGd], W=[eGd])
        SG = sbt('SG', [64, H, NS])
        self.tt('dve', SG[:, :, :], gtok[:, n, :].unsqueeze(2).to_broadcast([64, H, NS]),
                seqind[:, :].unsqueeze(1).to_broadcast([64, H, NS]), ALU.mult, R=[gtok, seqind], W=[SG])
        self.mm(pb[0][:, 16:16 + H * NS], c['onesf'][:, :], SG[:, :, :].rearrange("p h s -> p (h s)"),
                R=[c['onesf'], SG], W=[pb[0]])
        gl = sbt('gl', [128, H, NS])
        self.actf(gl[:, :, :].rearrange("p h s -> p (h s)"), pb[0][:, 16:16 + H * NS], AF.Exp, R=[pb[0]], W=[gl])
        UG = sbt('UG', nbuf=1, shape=[64, H, 64])
        self.tt('dve', UG[:, :, :], ucs[:, :].unsqueeze(1).to_broadcast([64, H, 64]),
                gtok[:, n, :].unsqueeze(2).to_broadcast([64, H, 64]), ALU.mult, R=[ucs, gtok], W=[UG])
        for hh in range(H):
            self.mm(pb[1][:, hh * 64:(hh + 1) * 64], c['onesf'][:, :], UG[:, hh, :], R=[c['onesf'], UG], W=[pb[1]])
        eGrow = sbt('eGrow', nbuf=1, shape=[128, H, 64])
        self.actf(eGrow[:, :, :].rearrange("p h i -> p (h i)"), pb[1][:, :], AF.Exp, R=[pb[1]], W=[eGrow])
        qgT = sbt('qgT', [128, H, 64], BF16)
        self.tt('dve', qgT[:, :, :], qT[:, :, cs], eGrow[:, :, :], ALU.mult, R=[qT, eGrow], W=[qgT])
        tmp = sbt('dtmp', nbuf=1, shape=[64, H, 64])
        self.tt('dve', tmp[:, :, :], pb[1][0:64, :].rearrange("p (h i) -> p h i", h=H),
                Gt[:, 0:8].unsqueeze(2).to_broadcast([64, H, 64]), ALU.subtract, R=[pb[1], Gt], W=[tmp])
        self.tt('pool', tmp[:, :, :], tmp[:, :, :], maskT[:, :].unsqueeze(1).to_broadcast([64, H, 64]), ALU.add,
                R=[tmp, maskT], W=[tmp])
        DT = sbt('DT', nbuf=1, shape=[64, H, 64])
        self.actf(DT[:, :, :], tmp[:, :, :], AF.Exp, R=[tmp], W=[DT])
        for hh in range(H):
            self.mm(pb[2][0:64, hh * 64:(hh + 1) * 64], kT[:, hh, cs], kT[:, hh, cs], R=[kT], W=[pb[2]])
        for hh in range(H):
            self.mm(pb[3][0:64, hh * 64:(hh + 1) * 64], kT[:, hh, cs], qT[:, hh, cs], R=[kT, qT], W=[pb[3]])
        aqkT = sbt('aqkT', [64, H, 64], BF16)
        self.tt('dve', aqkT[:, :, :], DT[:, :, :], pb[3][0:64, :].rearrange("p (h i) -> p h i", h=H), ALU.mult,
                R=[DT, pb[3]], W=[aqkT])
        nbo = sbt('nbo', nbuf=1, shape=[64, H, 64])
        self.tt('pool', nbo[:, :, :], beta[:, n, :].unsqueeze(2).to_broadcast([64, H, 64]),
                noff[:, :].unsqueeze(1).to_broadcast([64, H, 64]), ALU.mult, R=[beta, noff], W=[nbo])
        X = sbt('X', [64, H, 64])
        self.tt('dve', X[:, :, :], pb[2][0:64, :].rearrange("p (h i) -> p h i", h=H), nbo[:, :, :], ALU.mult,
                R=[pb[2], nbo], W=[X])
        self.tt('dve', X[:, :, :], X[:, :, :], DT[:, :, :], ALU.mult, R=[X, DT], W=[X])
        for hh in range(H):
            self.tr(pb[4][0:64, hh * 64:(hh + 1) * 64], X[:, hh, :], c['identf'][0:64, 0:64], R=[X, c['identf']], W=[pb[4]])
        Z = sbt('Z', [64, H, 64])
        self.cp('act', Z[:, :, :].rearrange("p h i -> p (h i)"), pb[4][0:64, :], R=[pb[4]], W=[Z])
        Pm = sbt('Pm', [64, H, 64])
        self.tt('pool', Pm[:, :, :], X[:, :, :], c['identf'][0:64, 0:64].unsqueeze(1).to_broadcast([64, H, 64]), ALU.add,
                R=[X, c['identf']], W=[Pm])
        Y = X
        for lv in range(G.nlev):
            last = (lv == G.nlev - 1)
            if not last:
                for hh in range(H):
                    self.mm(pb[5][0:64, hh * 64:(hh + 1) * 64], Z[:, hh, :], Y[:, hh, :], R=[Z, Y], W=[pb[5]])
            for hh in range(H):
                self.mm(pb[4][0:64, hh * 64:(hh + 1) * 64], Y[:, hh, :], Z[:, hh, :], R=[Z, Y], W=[pb[4]])
            Zn = sbt('Z', [64, H, 64])
            self.cp('act', Zn[:, :, :].rearrange("p h i -> p (h i)"), pb[4][0:64, :], R=[pb[4]], W=[Zn])
            if not last:
                Yn = sbt('X', [64, H, 64])
                self.cp('dve', Yn[:, :, :].rearrange("p h i -> p (h i)"), pb[5][0:64, :], R=[pb[5]], W=[Yn])
                Y = Yn
            Z = Zn
            for hh in range(H):
                self.mm(pb[2][0:64, hh * 64:(hh + 1) * 64], Z[:, hh, :], Pm[:, hh, :], R=[Z, Pm], W=[pb[2]])
            Pn = sbt('Pm', [64, H, 64])
            self.tt('dve', Pn[:, :, :].rearrange("p h i -> p (h i)"), Pm[:, :, :].rearrange("p h i -> p (h i)"),
                    pb[2][0:64, :], ALU.add, R=[Pm, pb[2]], W=[Pn])
            Pm = Pn
        vtok = sbt('vtok', [64, H, 128], BF16, 1)
        ktok = sbt('ktok', [64, H, 128], BF16, 1)
        for src, dst, pt in ((vT, vtok, self.pt[0]), (kT, ktok, self.pt[1])):
            for hh in range(H):
                self.tr(pt[0:64, hh * 128:(hh + 1) * 128], src[:, hh, cs], c['identb'][:, :], R=[src, c['identb']], W=[pt])
            self.cp('act', dst[:, :, :].rearrange("p h d -> p (h d)"), pt[0:64, :], R=[pt], W=[dst])
        kdec = sbt('kdec', [64, H, 128], BF16)
        self.tt('pool', kdec[:, :, :], ktok[:, :, :], eGd[:, 8:16].unsqueeze(2).to_broadcast([64, H, 128]), ALU.mult,
                R=[ktok, eGd], W=[kdec])

        OT = T['OT']
        if NS == 1 and getattr(G, 'Sall', None) is not None:
            Sf8, Sb8 = G.Sall[l]
            pk = (pb[0], pb[1])
            for hh in range(H):
                self.mm(pk[hh // 4][0:64, (hh % 4) * 128:(hh % 4 + 1) * 128], kT[:, hh, cs], Sb8[:, hh, :], R=[kT, Sb8],
                        W=[pk[hh // 4]])
            r = sbt('rB', [64, H, 128], F32, 1)
            for b2 in range(2):
                self.tt('dve', r[:, 4 * b2:4 * b2 + 4, :], pk[b2][0:64, :].rearrange("p (h d) -> p h d", h=4),
                        eGd[:, 4 * b2:4 * b2 + 4].unsqueeze(2).to_broadcast([64, 4, 128]), ALU.mult, R=[pk[b2], eGd], W=[r])
            self.tt('pool', r[:, :, :], vtok[:, :, :], r[:, :, :], ALU.subtract, R=[vtok, r], W=[r])
            pu = (pb[2], pb[3])
            for hh in range(H):
                self.mm(pu[hh // 4][0:64, (hh % 4) * 128:(hh % 4 + 1) * 128], Pm[:, hh, :], r[:, hh, :], R=[Pm, r],
                        W=[pu[hh // 4]])
            U = sbt('UB', [64, H, 128], BF16, 1)
            for b2 in range(2):
                self.tt('dve', U[:, 4 * b2:4 * b2 + 4, :], pu[b2][0:64, :].rearrange("p (h d) -> p h d", h=4),
                        beta[:, n, 4 * b2:4 * b2 + 4].unsqueeze(2).to_broadcast([64, 4, 128]), ALU.mult, R=[pu[b2], beta], W=[U])
            po = pb[4]
            for hh in range(H):
                self.mm(po[:, hh * 64:(hh + 1) * 64], Sb8[:, hh, :], qgT[:, hh, :], start=True, stop=False, R=[Sb8, qgT], W=[po])
                self.mm(po[:, hh * 64:(hh + 1) * 64], U[:, hh, :], aqkT[:, hh, :], start=False, stop=True, R=[U, aqkT], W=[po])
            self.cp('act', OT[:, :, cs], po[:, :].rearrange("p (h i) -> p h i", h=H), R=[po], W=[OT])
            psn = (pb[5], pb[0])
            for hh in range(H):
                self.mm(psn[hh // 4][:, (hh % 4) * 128:(hh % 4 + 1) * 128], kdec[:, hh, :], U[:, hh, :], R=[kdec, U],
                        W=[psn[hh // 4]])
            for b2 in range(2):
                hs4 = slice(4 * b2, 4 * b2 + 4)
                self.tt('dve' if b2 == 0 else 'pool', Sf8[:, hs4, :], Sf8[:, hs4, :], gl[:, hs4, :].to_broadcast([128, 4, 128]), ALU.mult,
                        R=[Sf8, gl], W=[Sf8])
            for b2 in range(2):
                hs4 = slice(4 * b2, 4 * b2 + 4)
                self.tt('dve', Sf8[:, hs4, :], Sf8[:, hs4, :], psn[b2][:, :].rearrange("p (h d) -> p h d", h=4), ALU.add,
                        R=[Sf8, psn[b2]], W=[Sf8])
            self.cp('act', Sb8[:, :, :], Sf8[:, :, :], R=[Sf8], W=[Sb8])
            return
        for hh in range(H):
            Sf, Sb = S_tiles(hh)
            if NS > 1:
                cm = c[pre + 'colmask']
                kTm = sbt('kTm', [128, NS, 64], BF16)
                self.tt('pool', kTm[:, :, :], kT[:, hh, cs].unsqueeze(1).to_broadcast([128, NS, 64]),
                        cm[:, :].rearrange("p (s i) -> p s i", s=NS), ALU.mult, R=[kT, cm], W=[kTm])
                qgm = sbt('qgm', [128, NS, 64], BF16)
                self.tt('pool', qgm[:, :, :], qgT[:, hh, :].unsqueeze(1).to_broadcast([128, NS, 64]),
                        cm[:, :].rearrange("p (s i) -> p s i", s=NS), ALU.mult, R=[qgT, cm], W=[qgm])
                kdm = sbt('kdm', [64, NS, 128], BF16)
                self.tt('pool', kdm[:, :, :], kdec[:, hh, :].unsqueeze(1).to_broadcast([64, NS, 128]),
                        seqind[:, :].unsqueeze(2).to_broadcast([64, NS, 128]), ALU.mult, R=[kdec, seqind], W=[kdm])
            pks = pb[0]
            for s in range(NS):
                lhs = kTm[:, s, :] if NS > 1 else kT[:, hh, cs]
                self.mm(pks[0:64, 0:128], lhs, Sb[:, s, :], start=(s == 0), stop=(s == NS - 1),
                        R=[kTm if NS > 1 else kT, Sb], W=[pks])
            r = sbt('r', [64, 128])
            self.ts('dve', r[:, :], pks[0:64, 0:128], eGd[:, hh:hh + 1], None, ALU.mult, R=[pks, eGd], W=[r])
            self.tt('dve', r[:, :], vtok[:, hh, :], r[:, :], ALU.subtract, R=[vtok, r], W=[r])
            pu = pb[1]
            self.mm(pu[0:64, 0:128], Pm[:, hh, :], r[:, :], R=[Pm, r], W=[pu])
            U = sbt('U', [64, 128], BF16)
            self.ts('dve', U[:, :], pu[0:64, 0:128], beta[:, n, hh:hh + 1], None, ALU.mult, R=[pu, beta], W=[U])
            po = pb[3]
            for s in range(NS):
                rhs = qgm[:, s, :] if NS > 1 else qgT[:, hh, :]
                self.mm(po[:, 0:64], Sb[:, s, :], rhs, start=(s == 0), stop=False, R=[Sb, qgm if NS > 1 else qgT], W=[po])
            self.mm(po[:, 0:64], U[:, :], aqkT[:, hh, :], start=False, stop=True, R=[U, aqkT], W=[po])
            self.cp('act', OT[:, hh, cs], po[:, 0:64], R=[po], W=[OT])
            for s0 in range(0, NS, 4):
                psn = pb[5] if (s0 // 4) % 2 == 0 else pb[4]
                ns = min(4, NS - s0)
                for s in range(s0, s0 + ns):
                    lhs = kdm[:, s, :] if NS > 1 else kdec[:, hh, :]
                    self.mm(psn[:, (s - s0) * 128:(s - s0 + 1) * 128], lhs, U[:, :], R=[kdm if NS > 1 else kdec, U], W=[psn])
                self.tt('dve', Sf[:, s0:s0 + ns, :], Sf[:, s0:s0 + ns, :],
                        gl[:, hh, s0:s0 + ns].unsqueeze(2).to_broadcast([128, ns, 128]), ALU.mult, R=[Sf, gl], W=[Sf])
                self.tt('dve', Sf[:, s0:s0 + ns, :], Sf[:, s0:s0 + ns, :],
                        psn[:, 0:ns * 128].rearrange("p (s d) -> p s d", s=ns), ALU.add, R=[Sf, psn], W=[Sf])
                self.cp('act', Sb[:, s0:s0 + ns, :], Sf[:, s0:s0 + ns, :], R=[Sf], W=[Sb])

    def ffn(self, G, l):
        P, c, T, d = self.P, self.c, G.T, self.d
        GT = G.GT
        self.norm_T(G, d['norm_ffn'].ap()[l])
        xnT, actT = T['xnT'], T['actT']
        for ci in range(22):
            wg = self.load_wchunk(d['wc_ffn_in'][l].ap()[ci])
            wu = self.load_wchunk(d['wc_ffn_in'][l].ap()[22 + ci])
            pg = self.next_pb('ffg', [1, 2])
            pu = self.next_pb('ffu', [3, 4])
            for kc in range(8):
                self.mm(pg[:, 0:GT], wg[:, kc, :], xnT[:, kc, :], start=(kc == 0), stop=(kc == 7), R=[wg, xnT], W=[pg])
            for kc in range(8):
                self.mm(pu[:, 0:GT], wu[:, kc, :], xnT[:, kc, :], start=(kc == 0), stop=(kc == 7), R=[wu, xnT], W=[pu])
            sg = self.ring(G.name + 'sl', 2, lambda j: P.sb([128, GT], F32, G.name + f"sl{j}"))
            self.actf(sg[:, :], pg[:, 0:GT], AF.Silu, R=[pg], W=[sg])
            self.tt('dve', actT[:, ci, :], sg[:, :], pu[:, 0:GT], ALU.mult, R=[sg, pu], W=[actT])
        self.tok_linear_add(G, actT, 22, d['wb_ffn_out'][l])

    def shared_rows(self, G, dst_kv, row0, win_cb):
        P, c, T, d = self.P, self.c, G.T, self.d
        TP, NT = G.TP, G.NT
        self.norm_T(G, d['norm_kv'].ap())
        xnT = T['xnT']
        for third in range(3):
            wt = self.ring('kvw', 1, lambda j: P.sb([128, 8, 512], BF16, f"kvw{j}"))
            P.dma(wt[:, :, :], d['wb_kv'].ap()[:, third * 512:(third + 1) * 512].rearrange("(kc p) n -> p kc n", p=128),
                  R=[d['wb_kv']], W=[wt])
            for t in range(NT):
                ps = self.next_pb('lin', [1, 2, 3])
                for kc in range(8):
                    self.mm(ps[0:TP, :], xnT[:, kc, t * TP:(t + 1) * TP], wt[:, kc, :], start=(kc == 0), stop=(kc == 7),
                            R=[xnT, wt], W=[ps])
                rp = self.ring(G.name + 'rowp', 2, lambda j: P.sb([TP, 512], F32, G.name + f"rowp{j}"))
                self.cp('act', rp[:, :], ps[0:TP, :], R=[ps], W=[rp])
                if third < 2:
                    P.dma(dst_kv.ap()[row0 + t * TP: row0 + (t + 1) * TP, third * 512:(third + 1) * 512], rp[:, :],
                          R=[rp], W=[dst_kv], q='pool')
                else:
                    win_cb(t, rp)

    def phase(self):
        from contextlib import ExitStack
        b = self

        class _Ph:
            def __enter__(self_):
                self_.st = ExitStack()
                b.P.stack = self_.st
                b.rr = {}
                return self_

            def __exit__(self_, *a):
                b.P.barrier()
                b.P.stack = None
                b.rr = {}
                self_.st.close()
                return False
        return _Ph()

    def build(self):
        P, cfg = self.P, self.cfg
        self.declare_io()
        d = self.d
        nsa = cfg.get('nsa', True)
        if nsa:
            self.nsa_declare()
        self.consts()
        if nsa:
            self.nsa_consts()
        with self.phase():
            self.convert_weights()
        if nsa:
            with self.phase():
                self.nsa_tables()
        n_dn = cfg['n_dn']
        if cfg.get('prompt', True):
          with self.phase():
            G = Ctx('p', 128, 4, 1, 1, 5)
            self.alloc_ctx(G)
            T = G.T
            Sp = [(P.sb([128, H, 128], F32, f"Sp{l}"), P.sb([128, H, 128], BF16, f"Sbp{l}")) for l in range(2)]
            G.Sall = Sp
            for l in range(2):
                self.memset('pool', Sp[l][0][:, :, :], 0.0, W=[Sp[l][0]])
                self.memset('pool', Sp[l][1][:, :, :], 0.0, W=[Sp[l][1]])
                self.memset('pool', T['carry'][l][:, :, :, :], 0.0, W=[T['carry'][l]])
            ngrp = cfg['SEQ'] // G.GT
            for g in range(ngrp):
                P.dma(T['x'][:, :, :], d['x_prompt'].ap()[g * 512:(g + 1) * 512, :].rearrange("(t p) n -> p t n", p=128),
                      W=[T['x']])
                for l in range(n_dn):
                    self.dn_mixer(G, l, None, last_group=(g == ngrp - 1))
                    self.ffn(G, l)
                P.dma(d['x2_p'].ap()[g * 512:(g + 1) * 512, :].rearrange("(t p) n -> p t n", p=128), T['x'][:, :, :],
                      R=[T['x']], W=[d['x2_p']], q='pool')

                def win_cb(t, rp, g=g):
                    r0 = g * 512 + t * 128 - (cfg['SEQ'] - 512)
                    if nsa:
                        P.dma(d['pwin_d'].ap()[g * 512 + t * 128:g * 512 + (t + 1) * 128, :], rp[:, :], R=[rp],
                              W=[d['pwin_d']], q='pool')
                    if r0 >= 0:
                        P.dma(d['p_win_kv'].ap()[r0:r0 + 128, :], rp[:, :], R=[rp], W=[d['p_win_kv']], q='pool')
                self.shared_rows(G, d['p_kv_rows'], g * 512, win_cb)
            for l in range(2):
                P.dma(d['p_dn_S'].ap()[l].rearrange("h k v -> k h v"), Sp[l][0][:, :, :], R=[Sp[l][0]], W=[d['p_dn_S']], q='pool')
        if nsa and cfg.get('prompt', True) and cfg.get('nsa_stop', 99) > 1:
            try:
                self.nsa_prompt()
            except _Stop:
                pass
        if cfg.get('sample', True):
          with self.phase():
            G = Ctx('s', 64, 1, 16, 16, 1)
            self.alloc_ctx(G)
            T = G.T
            P.dma(T['x'][:, 0, :], d['x_sample'].ap(), W=[T['x']])
            Ss = P.sb([128, 16, 128], F32, "Ss")
            Ssb = P.sb([128, 16, 128], BF16, "Ssb")
            for l in range(n_dn):
                self.conv_state_in(G, l)
                cur = [None]

                def S_tiles(hh, l=l, cur=cur):
                    if cur[0] != hh:
                        if cur[0] is not None:
                            P.dma(d['s_dn_S'].ap()[l, :, cur[0]].rearrange("s k v -> k s v"), Ss[:, :, :], R=[Ss],
                                  W=[d['s_dn_S']], q='pool')
                        P.dma(Ss[:, :, :], d['state_dn_S'].ap()[l, :, hh].rearrange("s k v -> k s v"), W=[Ss])
                        self.cp('act', Ssb[:, :, :], Ss[:, :, :], R=[Ss], W=[Ssb])
                        cur[0] = hh
                    return Ss, Ssb
                self.dn_mixer(G, l, S_tiles, last_group=True)
                P.dma(d['s_dn_S'].ap()[l, :, cur[0]].rearrange("s k v -> k s v"), Ss[:, :, :], R=[Ss], W=[d['s_dn_S']],
                      q='pool')
                self.ffn(G, l)
            P.dma(d['x2_s'].ap(), T['x'][:, 0, :], R=[T['x']], W=[d['x2_s']], q='pool')
            for s4 in range(4):
                P.dma(d['s_win_kv'].ap()[s4 * 4:(s4 + 1) * 4, 0:508, :], d['state_win_kv'].ap()[s4 * 4:(s4 + 1) * 4, 4:512, :],
                      W=[d['s_win_kv']], q='sp')

            def win_cb_s(t, rp):
                for sq in range(16):
                    P.dma(d['s_win_kv'].ap()[sq, 508:512, :], rp[4 * sq:4 * sq + 4, :], R=[rp],
                          W=[d['s_win_kv']], q='pool')
            self.shared_rows(G, d['s_kv_rows'], 0, win_cb_s)
        if nsa and cfg.get('sample', True):
            self.nsa_sample()
        P.emit()
        return self.nc


_CONST_CACHE = {}


def const_inputs():
    if not _CONST_CACHE:
        for pre, NS, TS in (('cp_', 1, 64), ('cs_', 16, 4)):
            for k, v in host_consts(NS, TS).items():
                _CONST_CACHE[pre + k] = v
    return _CONST_CACHE


def make_in_maps(inp, cfg, n_cores=8):
    SEQ = cfg['SEQ']
    cst = const_inputs()
    maps = []
    shared = {k: np.ascontiguousarray(inp[k]) for k in
              ('norm_mix', 'norm_ffn', 'norm_kv', 'norm_final', 'ffn_w_in', 'ffn_w_out', 'dn_w_in', 'dn_conv_w',
               'dn_A_log', 'dn_dt_bias', 'dn_out_norm', 'dn_w_out', 'nsa_w_kv')}
    nsa = cfg.get('nsa', True)
    if nsa:
        shared['nsa_w_in'] = np.ascontiguousarray(inp['nsa_w_in'])
        shared['nsa_w_out'] = np.ascontiguousarray(inp['nsa_w_out'])
        shared['nsa_cmp_pos_w'] = np.ascontiguousarray(inp['nsa_cmp_pos_w']).reshape(2, 32, 256)
        shared['nsa_w_cmp'] = np.ascontiguousarray(inp['nsa_w_cmp'])
        shared['rel_bias'] = np.ascontiguousarray(inp['rel_bias'])
        shared['cache_kv'] = np.ascontiguousarray(inp['cache_kv']).reshape(2560 * 128, 1024)
        nsc = [nsa_host_consts(0), nsa_host_consts(1)]
    for c in range(n_cores):
        b = c // 2
        m = dict(shared)
        m.update(cst)
        m['x_prompt'] = np.ascontiguousarray(inp['x_prompt'][b, :SEQ])
        sl = slice(16 * c, 16 * c + 16)
        m['x_sample'] = np.ascontiguousarray(inp['x_sample'][sl]).reshape(64, D)
        m['state_dn_S'] = np.ascontiguousarray(inp['state_dn_S'][:, sl])
        m['state_dn_conv'] = np.ascontiguousarray(inp['state_dn_conv'][:, sl])
        m['state_win_kv'] = np.ascontiguousarray(inp['state_win_kv'][sl]).reshape(16, 512, 512)
        if nsa:
            m.update(nsc[c % 2])
            m['page_table'] = np.ascontiguousarray(inp['page_table'][sl]).reshape(1, 256).astype(np.int32)
        maps.append(m)
    return maps


def kernel(**inp):
    cfg = dict(SEQ=4096, n_dn=2)
    b = Builder(cfg)
    nc = b.build()
    maps = make_in_maps(inp, cfg)
    res = run_bass_kernel_spmd(nc, maps, core_ids=list(range(8)))
    R = res.results
    f32 = np.float32
    y_prompt = np.zeros((4, 4096, D), f32)
    for c in range(8):
        yp = R[c]['y_p'].reshape(16, 128, D)
        y_prompt[c // 2].reshape(32, 128, D)[c % 2::2] = yp
    y_sample = np.concatenate([R[c]['y_s'].reshape(16, 4, D) for c in range(8)], axis=0).astype(f32)
    p_dn_S = np.stack([R[2 * b]['p_dn_S'] for b in range(4)], axis=1)
    p_dn_conv = np.stack([R[2 * b]['p_dn_conv'] for b in range(4)], axis=1)
    p_kv_rows = np.stack([R[2 * b]['p_kv_rows'] for b in range(4)], axis=0).reshape(4, 4096, 4, 4, 64)
    p_win_kv = np.stack([R[2 * b]['p_win_kv'] for b in range(4)], axis=0).reshape(4, 512, 2, 4, 64)
    s_dn_S = np.concatenate([R[c]['s_dn_S'] for c in range(8)], axis=1)
    s_dn_conv = np.concatenate([R[c]['s_dn_conv'] for c in range(8)], axis=1)
    s_kv_rows = np.concatenate([R[c]['s_kv_rows'].reshape(16, 4, 4, 4, 64) for c in range(8)], axis=0)
    s_win_kv = np.concatenate([R[c]['s_win_kv'].reshape(16, 512, 2, 4, 64) for c in range(8)], axis=0)
    return (y_prompt, y_sample, p_dn_S.astype(f32), p_dn_conv.astype(f32), p_kv_rows.astype(f32),
            p_win_kv.astype(f32), s_dn_S.astype(f32), s_dn_conv.astype(f32), s_kv_rows.astype(f32),
            s_win_kv.astype(f32))


def _bucket_np(d):
    n = np.maximum(d, 0)
    nf = np.maximum(n, 1).astype(np.float32)
    large = 16 + (np.log(nf / np.float32(16)) / np.float32(math.log(64.0)) * np.float32(16)).astype(np.int32)
    large = np.minimum(large, 31)
    return np.where(n < 16, n, large)


def _onehot(d, valid):
    b = np.where(valid, _bucket_np(d), 32)
    oh = np.zeros((33, d.shape[0]), np.float32)
    oh[b, np.arange(d.shape[0])] = 1.0
    return oh


def nsa_host_consts(par):
    c = {}
    t = np.arange(1280)
    d = t - 255 + 128 * par
    c['oh_sel'] = _onehot(d, d >= 0)
    t = np.arange(1024)
    d = t - 255 + 128 * par
    c['oh_win'] = _onehot(d, (d >= 0) & (d < 512))
    r = np.arange(16)[:, None]
    w = np.arange(512)[None, :]
    d = (16 * (247 - w + 8 * par) + r - 31).reshape(-1)
    c['oh_cmp'] = _onehot(d, d >= 0)
    tt = np.arange(4)[:, None]
    cc = np.arange(128)[None, :]
    d = (2017 + tt - 16 * cc).reshape(-1)
    c['oh_cs'] = _onehot(d, d >= 0)
    x = np.arange(2304)
    d = x - 127
    c['oh_ss'] = _onehot(d, d >= 0)
    x = np.arange(768)
    d = x - 127
    c['oh_ws'] = _onehot(d, (d >= 0) & (d < 512))
    blk = np.arange(64)[None, None, :]
    qpos = (128 * (2 * np.arange(16)[:, None, None] + par) + np.arange(128)[None, :, None])
    cur = qpos // 64
    forced = (blk == 0) | (blk == cur) | (blk == cur - 1)
    valid = blk * 64 <= qpos
    c['selmul_p'] = np.ascontiguousarray(np.where(valid & ~forced, 1.0, 0.0).astype(np.float32).transpose(1, 0, 2))
    c['seladd_p'] = np.ascontiguousarray(np.where(forced, 1e4, np.where(valid, 0.0, -1.0)).astype(np.float32).transpose(1, 0, 2))
    blk = np.arange(64)[None, :]
    qpos = 2048 + np.arange(4)[:, None]
    cur = qpos // 64
    exists = blk < 33
    forced = ((blk == 0) | (blk == cur) | (blk == cur - 1)) & exists
    valid = (blk * 64 <= qpos) & exists
    c['selmul_s'] = np.where(valid & ~forced, 1.0, 0.0).astype(np.float32)
    c['seladd_s'] = np.where(forced, 1e4, np.where(valid, 0.0, np.where(exists, -1.0, -2.0))).astype(np.float32)
    k = np.arange(4096)[None, :]
    c['expE'] = (k // 64 == np.arange(64)[:, None]).astype(np.float32)
    sel8 = np.zeros((128, 8), np.float32)
    sel8[np.arange(128), np.arange(128) // 16] = 1.0
    c['sel8'] = sel8
    c['antiI'] = np.ascontiguousarray(np.eye(128, dtype=np.float32)[::-1])
    c['parf'] = np.tile(np.array([[float(par), 1.0 - float(par)]], np.float32), (128, 1))
    c['iota'] = np.arange(128, dtype=np.float32).reshape(128, 1)
    return c


def _nsa_declare(self):
    P, cfg, d = self.P, self.cfg, self.d
    SEQ = cfg['SEQ']

    def inp(name, shape, dt=F32):
        d[name] = P.dram(name, shape, dt, kind="ExternalInput")

    inp('nsa_w_in', [2, D, 1072])
    inp('nsa_w_out', [2, D, D])
    inp('nsa_cmp_pos_w', [2, 32, 256])
    inp('nsa_w_cmp', [2, 4, 64, 64])
    inp('rel_bias', [32, 16])
    inp('cache_kv', [2560 * 128, 1024])
    inp('page_table', [1, 256], I32)
    for nm, shp in (('oh_sel', [33, 1280]), ('oh_win', [33, 1024]), ('oh_cmp', [33, 8192]), ('oh_cs', [33, 512]),
                    ('oh_ss', [33, 2304]), ('oh_ws', [33, 768]), ('selmul_p', [128, 16, 64]), ('seladd_p', [128, 16, 64]),
                    ('selmul_s', [4, 64]), ('seladd_s', [4, 64]), ('expE', [64, 4096]), ('sel8', [128, 8]),
                    ('antiI', [128, 128]), ('parf', [128, 2]), ('iota', [128, 1])):
        inp(nm, shp)
    d['y_p'] = P.dram('y_p', [SEQ // 2, D], F32, kind="ExternalOutput")
    d['y_s'] = P.dram('y_s', [64, D], F32, kind="ExternalOutput")
    d['wc_nsa_in'] = [P.dram(f'wc_nsa_in{j}', [8, 128, 8, 128], BF16) for j in range(2)]
    d['wb_nsa_out'] = [P.dram(f'wb_nsa_out{j}', [D, D], BF16) for j in range(2)]
    for nm, ln in (('t_sel', 1280), ('t_win', 1024), ('t_cmp', 8192), ('t_cs', 512), ('t_ss', 2304), ('t_ws', 768)):
        d[nm] = P.dram(nm, [16, ln], F32)
    d['BTd'] = P.dram('BTd', [4, 15, 128, 512], F32)
    NT = SEQ // 128
    d['pwin_d'] = P.dram('pwin_d', [SEQ, 512], F32)
    d['kselT_p'] = P.dram('kselT_p', [4, 128, SEQ], BF16)
    d['vsel_p'] = P.dram('vsel_p', [NT, 128, 4, 66], BF16)
    d['kwinT_p'] = P.dram('kwinT_p', [4, 128, SEQ], BF16)
    d['vwin_p'] = P.dram('vwin_p', [NT, 128, 4, 66], BF16)
    d['kselT_s'] = P.dram('kselT_s', [16, 4, 128, 17 * 128], BF16)
    d['vsel_s'] = P.dram('vsel_s', [16, 17, 128, 4, 66], BF16)
    d['kwinT_s'] = P.dram('kwinT_s', [16, 4, 128, 5 * 128], BF16)
    d['vwin_s'] = P.dram('vwin_s', [16, 5, 128, 4, 66], BF16)


def _nsa_consts(self):
    P, d, c = self.P, self.d, self.c
    antiI = P.sb([128, 128], F32, "antiI")
    P.dma(antiI[:, :], d['antiI'].ap(), W=[antiI])
    sel8 = P.sb([128, 8], BF16, "sel8")
    P.dma(sel8[:, :], d['sel8'].ap(), W=[sel8], q='pool')
    parf = P.sb([128, 2], F32, "parf")
    P.dma(parf[:, :], d['parf'].ap(), W=[parf])
    tabx = P.sb([33, 16], F32, "tabx")
    r31 = P.sb([32, 16], F32, "r31")
    P.dma(tabx[0:32, :], d['rel_bias'].ap(), W=[tabx])
    P.dma(r31[:, :], d['rel_bias'].ap()[31].partition_broadcast(32), W=[r31])
    self.tt('dve', tabx[0:32, :], tabx[0:32, :], r31[:, :], ALU.subtract, R=[tabx, r31], W=[tabx])
    self.memset('pool', tabx[32:33, :], NEG, W=[tabx])
    c.update(antiI=antiI, sel8=sel8, parf=parf, tabx=tabx)
    wg = P.sb([128, 2, 8, 48], BF16, "wg")
    for j in range(2):
        P.dma(wg[:, j, :, :], d['nsa_w_in'].ap()[j, :, 1024:1072].rearrange("(kc p) n -> p kc n", p=128), W=[wg], q='pool')
    c['wg'] = wg


def _nsa_late_consts(self):
    P, d, c = self.P, self.d, self.c
    if 'wlo' in c:
        return
    wlo = P.sb([128, 512], F32, "wlo")
    whi = P.sb([128, 512], F32, "whi")
    for a in range(8):
        P.dma(wlo[16 * a:16 * a + 16, :].rearrange("r (c n) -> r c n", c=2),
              d['nsa_cmp_pos_w'].ap()[:, 0:16, :].rearrange("c r n -> r c n"), W=[wlo])
        P.dma(whi[16 * a:16 * a + 16, :].rearrange("r (c n) -> r c n", c=2),
              d['nsa_cmp_pos_w'].ap()[:, 16:32, :].rearrange("c r n -> r c n"), W=[whi])
    wck = P.sb([128, 4, 128], BF16, "wck")
    wcv = P.sb([128, 4, 64], BF16, "wcv")
    for half in range(2):
        for dup in range(2):
            P.dma(wck[half * 64:(half + 1) * 64, :, dup * 64:(dup + 1) * 64],
                  d['nsa_w_cmp'].ap()[0].rearrange("k d e -> d k e"), W=[wck], q='pool')
        P.dma(wcv[half * 64:(half + 1) * 64, :, :], d['nsa_w_cmp'].ap()[1].rearrange("k d e -> d k e"), W=[wcv], q='pool')
    expE = P.sb([64, 4096], BF16, "expE")
    P.dma(expE[:, :], d['expE'].ap(), W=[expE], q='pool')
    c.update(wlo=wlo, whi=whi, wck=wck, wcv=wcv, expE=expE)


Builder.nsa_late_consts = _nsa_late_consts


def _nsa_tables(self):
    P, d, c = self.P, self.d, self.c
    for oh, dst, ln in (('oh_sel', 't_sel', 1280), ('oh_win', 't_win', 1024), ('oh_cmp', 't_cmp', 8192),
                        ('oh_cs', 't_cs', 512), ('oh_ss', 't_ss', 2304), ('oh_ws', 't_ws', 768)):
        for off in range(0, ln, 512):
            n = min(512, ln - off)
            ot = self.ring('oht', 2, lambda j: P.sb([33, 512], F32, f"oht{j}"))
            P.dma(ot[:, 0:n], d[oh].ap()[:, off:off + n], W=[ot])
            ps = self.next_pb('lin', [1, 2, 3])
            self.mm(ps[0:16, 0:n], c['tabx'][:, :], ot[:, 0:n], R=[c['tabx'], ot], W=[ps])
            tb = self.ring('tbo', 2, lambda j: P.sb([16, 512], F32, f"tbo{j}"))
            self.cp('act', tb[:, 0:n], ps[0:16, 0:n], R=[ps], W=[tb])
            P.dma(d[dst].ap()[:, off:off + n], tb[:, 0:n], R=[tb], W=[d[dst]], q='pool')
    if self.cfg.get('prompt', True):
        for kvh in range(4):
            for idx in range(15):
                tab, ln, e = ('t_sel', 1280, idx - 1) if idx < 9 else ('t_win', 1024, idx - 10)
                tr_ = self.ring('trv', 2, lambda j: P.sb([128, 4, 128], F32, f"trv{j}"))
                src = bass.AP(d[tab].h, 4 * kvh * ln + 128 * (e + 1), [[1, 128], [ln, 4], [1, 128]])
                P.dma(tr_[:, :, :], src, R=[d[tab]], W=[tr_])
                ps = self.next_pb('lin', [1, 2, 3])
                self.mm(ps[:, :], c['antiI'][:, :], tr_[:, :, :].rearrange("p g q -> p (g q)"), R=[c['antiI'], tr_], W=[ps])
                fl = self.ring('flp', 2, lambda j: P.sb([128, 512], F32, f"flp{j}"))
                self.cp('act', fl[:, :], ps[:, :], R=[ps], W=[fl])
                P.dma(d['BTd'].ap()[kvh, idx], fl[:, :], R=[fl], W=[d['BTd']], q='pool')


def _ctx_rows(self, rows, P_idx, lohi, kT_dst, v_dst, kw_dst, vw_dst, has_cmpsel=True, has_win=True, win_rows=None):
    P, c = self.P, self.c
    if has_cmpsel:
        if lohi is not None:
            alo = self.ring('alo', 2, lambda j: P.sb([128, 512], BF16, f"alo{j}"))
            ahi = self.ring('ahi', 2, lambda j: P.sb([128, 512], BF16, f"ahi{j}"))
            self.tt('pool', alo[:, :], rows[:, 0:512], c['wlo'][:, :], ALU.mult, R=[rows, c['wlo']], W=[alo])
            self.tt('dve', ahi[:, :], rows[:, 0:512], c['whi'][:, :], ALU.mult, R=[rows, c['whi']], W=[ahi])
            ps = self.next_pb('ctxp', [0])
            for lh, a in enumerate((alo, ahi)):
                for ch in range(4):
                    self.mm(ps[:, (lh * 4 + ch) * 8:(lh * 4 + ch + 1) * 8], a[:, ch * 128:(ch + 1) * 128], c['sel8'][:, :],
                            R=[a, c['sel8']], W=[ps])
            self.cp('act', lohi[:, :, :, 8 * P_idx:8 * P_idx + 8], ps[:, 0:64].rearrange("p (l c m) -> p l c m", l=2, c=4),
                    R=[ps], W=[lohi])
        for (col0, kdst, vdst) in ((512, kT_dst, v_dst),):
            _kv_tile(self, rows, col0, kdst, vdst)
    if has_win:
        wt, wc0 = win_rows
        _kv_tile(self, wt, wc0, kw_dst, vw_dst)


def _kv_tile(self, rows, col0, kdst, vdst):
    P, c = self.P, self.c
    kd = self.ring('kd', 2, lambda j: P.sb([128, 4, 2, 64], BF16, f"kd{j}"))
    self.cp('dve', kd[:, :, :, :], rows[:, col0:col0 + 256].rearrange("p (k d) -> p k d", k=4).unsqueeze(2).to_broadcast([128, 4, 2, 64]),
            R=[rows], W=[kd])
    pt = self.pt[1]
    for k in range(4):
        self.tr(pt[:, k * 128:(k + 1) * 128], kd[:, k, :, :].rearrange("p a d -> p (a d)"), c['identb'][:, :],
                R=[kd, c['identb']], W=[pt])
    ks = self.ring('ks', 2, lambda j: P.sb([128, 4, 128], BF16, f"ks{j}"))
    self.cp('act', ks[:, :, :].rearrange("p k n -> p (k n)"), pt[:, 0:512], R=[pt], W=[ks])
    kap, ktile = kdst
    P.dma(kap, ks[:, :, :], R=[ks], W=[ktile], q='sp')
    va = self.ring('va', 2, lambda j: P.sb([128, 4, 66], BF16, f"va{j}"))
    self.memset('dve', va[:, :, 64:65], 1.0, W=[va])
    self.memset('dve', va[:, :, 65:66], 0.0, W=[va])
    self.cp('dve', va[:, :, 0:64], rows[:, col0 + 256:col0 + 512].rearrange("p (k d) -> p k d", k=4), R=[rows], W=[va])
    vap, vtile = vdst
    P.dma(vap, va[:, :, :], R=[va], W=[vtile], q='sp')


def _cmp_finish(self, lohi, NCB, kcd_ap, vc_ap_fn, kcd_t, vc_t, ncw=256, njh=2):
    P, c = self.P, self.c
    bl = self.ring('blk', 1, lambda j: P.sb([128, 4, 256], BF16, "blk"))
    self.memset('pool', bl[:, :, :], 0.0, W=[bl])
    self.tt('dve', bl[:, :, 0:NCB], lohi[:, 0, :, 0:NCB], lohi[:, 1, :, 1:NCB + 1], ALU.add, R=[lohi], W=[bl])
    for kvh in range(4):
        hs = slice((kvh % 2) * 64, (kvh % 2) * 64 + 64)
        ps = self.next_pb('lin', [1, 2, 3])
        self.mm(ps[:, 0:256], c['wck'][hs, kvh, :], bl[hs, kvh // 2, :], R=[c['wck'], bl], W=[ps])
        self.cp('act', kcd_ap(kvh), ps[:, 0:ncw], R=[ps], W=[kcd_t])
        for jh in range(njh):
            ps2 = self.next_pb('lin', [1, 2, 3])
            self.mm(ps2[:, 0:64], bl[hs, 2 + kvh // 2, jh * 128:(jh + 1) * 128], c['wcv'][hs, kvh, :], R=[bl, c['wcv']], W=[ps2])
            self.cp('act', vc_ap_fn(jh, kvh), ps2[:, 0:64], R=[ps2], W=[vc_t])


Builder.nsa_declare = _nsa_declare
Builder.nsa_consts = _nsa_consts
Builder.nsa_tables = _nsa_tables
Builder.ctx_rows = _ctx_rows
Builder.cmp_finish = _cmp_finish


def _attend_gen(self, A):
    P, c = self.P, self.c
    NQ, NC, kvh = A['NQ'], A['NC'], A['kvh']
    pb = self.pb
    qT, qTt = A['qT']
    qbd, qcols = A['qbd'], A['qcols']
    gate = A['gate']
    N4 = 4 * NQ

    def sbt(key, shape, dt=F32, nbuf=2):
        k = f"at{NQ}{key}"
        return self.ring(k, nbuf, lambda j: P.sb(shape, dt, k + str(j)))

    if A.get('stop', 99) == 0:
        raise _Stop()
    kc_ap, kc_t = A['kcmp']
    for g in range(4):
        ps = pb[g % 2]
        self.mm(ps[0:NQ, (g // 2) * 256:(g // 2) * 256 + NC], qT(g), kc_ap(g), R=[qTt, kc_t], W=[ps])
    yield 0
    if A.get('stop', 99) == 10:
        raise _Stop()
    bc_ap, bc_t = A['bias_c']
    sc = sbt('sc', [NQ, 4, 256], nbuf=1)
    for g in range(4):
        ps = pb[g % 2]
        self.tt('dve', sc[:, g, 0:NC], ps[0:NQ, (g // 2) * 256:(g // 2) * 256 + NC], bc_ap[:, g, 0:NC], ALU.add,
                R=[ps, bc_t], W=[sc])
    yield 0
    if A.get('stop', 99) == 11:
        raise _Stop()
    ssum = sbt('ssum', [NQ, 8])
    self.memset('pool', ssum[:, :], 0.0, W=[ssum])
    for g in range(4):
        self.actf(sc[:, g, 0:NC], sc[:, g, 0:NC], AF.Exp, R=[sc], W=[sc, ssum], accum_out=ssum[:, g:g + 1])
    yield 0
    if A.get('stop', 99) == 12:
        raise _Stop()
    self.ts('dve', ssum[:, 4:8], ssum[:, 0:4], 1e-30, None, ALU.max, R=[ssum], W=[ssum])
    P.add('dve', lambda e: e.reciprocal(out=ssum[:, 4:8], in_=ssum[:, 4:8]), R=[ssum], W=[ssum])
    if A.get('stop', 99) == 13:
        raise _Stop()
    pc = sbt('pc', [NQ, 4, 256], nbuf=1)
    if not A.get('pc_init'):
        pass
    self.memset('pool', pc[:, :, :], 0.0, W=[pc])
    self.tt('dve', pc[:, :, 0:NC], sc[:, :, 0:NC], ssum[:, 4:8].unsqueeze(2).to_broadcast([NQ, 4, NC]), ALU.mult,
            R=[sc, ssum], W=[pc])
    yield 0
    if A.get('stop', 99) == 1:
        raise _Stop()
    imp = sbt('imp', [NQ, 264], nbuf=1)
    self.memset('pool', imp[:, :], 0.0, W=[imp])
    P.add('dve', lambda e: e.tensor_reduce(out=imp[:, 1:257], in_=pc[:, :, :].rearrange("p g n -> p n g"),
                                           axis=mybir.AxisListType.X, op=ALU.add), R=[pc], W=[imp])
    yield 0
    cov = sbt('cov', [NQ, 256], nbuf=1)
    self.tt('dve', cov[:, :], imp[:, 1:257], imp[:, 0:256], ALU.add, R=[imp], W=[cov])
    psl = sbt('psl', [NQ, 64], nbuf=1)
    P.add('dve', lambda e: e.tensor_reduce(out=psl[:, :], in_=cov[:, :].rearrange("p (b r) -> p b r", r=4),
                                           axis=mybir.AxisListType.X, op=ALU.add), R=[cov], W=[psl])
    yield 0
    smul, sadd, s_t = A['selc']
    self.tt('dve', psl[:, :], psl[:, :], smul, ALU.mult, R=[psl, s_t], W=[psl])
    self.tt('dve', psl[:, :], psl[:, :], sadd, ALU.add, R=[psl, s_t], W=[psl])
    yield 0
    m16 = sbt('m16', [NQ, 16], nbuf=1)
    ps2 = sbt('psl2', [NQ, 64], nbuf=1)
    P.add('dve', lambda e: e.max(out=m16[:, 0:8], in_=psl[:, :]), R=[psl], W=[m16])
    P.add('dve', lambda e: e.match_replace(out=ps2[:, :], in_to_replace=m16[:, 0:8], in_values=psl[:, :], imm_value=-5.0),
          R=[psl, m16], W=[ps2])
    P.add('dve', lambda e: e.max(out=m16[:, 8:16], in_=ps2[:, :]), R=[ps2], W=[m16])
    yield 0
    nsel = sbt('nsel', [NQ, 64], nbuf=1)
    self.ts('dve', nsel[:, :], psl[:, :], m16[:, 15:16], None, ALU.is_ge, R=[psl, m16], W=[nsel])
    self.ts('dve', nsel[:, :], nsel[:, :], -NEG, NEG, ALU.mult, ALU.add, R=[nsel], W=[nsel])
    self.tr(pb[0][0:64, 0:NQ], nsel[:, :], c['identf'][0:NQ, 0:NQ], R=[nsel, c['identf']], W=[pb[0]])
    nsT = sbt('nsT', [64, 4, NQ], BF16)
    self.cp('act', nsT[:, :, :], pb[0][0:64, 0:NQ].unsqueeze(1).to_broadcast([64, 4, NQ]), R=[pb[0]], W=[nsT])
    yield 0
    if A.get('stop', 99) == 2:
        raise _Stop()
    pcb = sbt('pcb', [NQ, 4, 256], BF16, 1)
    self.cp('pool', pcb[:, :, :], pc[:, :, :], R=[pc], W=[pcb])
    pt = self.pt[0]
    NH = A['NH']
    for g in range(4):
        for hf in range(NH):
            self.tr(pt[:, (g * NH + hf) * NQ:(g * NH + hf + 1) * NQ], pcb[:, g, hf * 128:(hf + 1) * 128],
                    c['identb'][0:NQ, 0:NQ], R=[pcb, c['identb']], W=[pt])
    yield 0
    pcT = sbt('pcT', [128, 4 * NH, NQ], BF16, 1)
    self.cp('act', pcT[:, :, :].rearrange("p a q -> p (a q)"), pt[:, 0:4 * NH * NQ], R=[pt], W=[pcT])
    vc_ap, vc_t = A['vcmp']
    for g in range(4):
        for hf in range(NH):
            self.mm(pb[1][0:NQ, g * 64:(g + 1) * 64], pcT[:, g * NH + hf, :], vc_ap(hf), start=(hf == 0), stop=(hf == NH - 1),
                    R=[pcT, vc_t], W=[pb[1]])
    yield 0
    oacc = sbt('oacc', [NQ, 4, 64])
    g_ap, g_t = gate
    self.tt('dve', oacc[:, :, :], pb[1][0:NQ, 0:256].rearrange("p (g d) -> p g d", g=4),
            g_ap(0).unsqueeze(2).to_broadcast([NQ, 4, 64]), ALU.mult, R=[pb[1], g_t], W=[oacc])

    yield 'CMP_DONE'
    if A.get('stop', 99) == 3:
        raise _Stop()
    for br, (tiles, ksrc, vsrc, po) in enumerate((A['sel'], A['win'])):
        if A.get('stop', 99) == 4 and br == 1:
            raise _Stop()
        nt = len(tiles)
        pend = None
        for c0 in range(0, nt, 4):
            n = min(4, nt - c0)
            kt0 = tiles[c0][0]
            kc = sbt('kc', [128, 512], BF16, 3)
            kap, ktile = ksrc(kt0, n)
            P.dma(kc[:, 0:n * 128], kap, R=[ktile], W=[kc])
            vcx = sbt('vcx', [128, 4, 66], BF16, 3)
            vap, vtile = vsrc(kt0, n)
            P.dma(vcx[:, 0:n, :], vap, R=[vtile], W=[vcx])
            for t in range(n):
                kt, bias = tiles[c0 + t]
                ps = self.next_pb('sc', [2, 3])
                for pr in range(2):
                    self.mm(ps[:, pr * 2 * NQ:(pr + 1) * 2 * NQ], kc[:, t * 128:(t + 1) * 128], qbd[:, pr, :, qcols],
                            start=(pr == 0), stop=(br == 1 and pr == 1), R=[kc, A['qbd_t']], W=[ps])
                if br == 0:
                    self.mm(ps[:, 0:N4], c['expE'][:, kt * 128:(kt + 1) * 128], nsT[:, :, :].rearrange("p g q -> p (g q)"),
                            start=False, stop=True, R=[c['expE'], nsT], W=[ps])
                PT = sbt('PT', [128, N4], BF16, 3)
                if bias is not None:
                    b_ap, b_t = bias
                    sb_ = sbt('sbias', [128, N4], F32, 2)
                    self.tt('dve', sb_[:, :], ps[:, 0:N4], b_ap, ALU.add, R=[ps, b_t], W=[sb_])
                    self.actf(PT[:, :], sb_[:, :], AF.Exp, R=[sb_], W=[PT])
                else:
                    self.actf(PT[:, :], ps[:, 0:N4], AF.Exp, R=[ps], W=[PT])
                if pend is not None:
                    pend()
                def pv(PT=PT, vcx=vcx, t=t, first=(c0 + t == 0), last=(c0 + t == nt - 1), po=po):
                    for g in range(4):
                        self.mm(po[0:NQ, g * 66:(g + 1) * 66], PT[:, g * NQ:(g + 1) * NQ], vcx[:, t, :],
                                start=(first and g == 0), stop=(last and g == 3), R=[PT, vcx], W=[po])
                pend = pv
                yield 1
        if pend is not None:
            pend()
            pend = None
        pov = po[0:NQ, 0:264].rearrange("p (g e) -> p g e", g=4)
        rs = sbt('rs', [NQ, 4])
        self.ts('dve', rs[:, :], pov[:, :, 64], 1e-30, None, ALU.max, R=[po], W=[rs])
        P.add('dve', lambda e, rs=rs: e.reciprocal(out=rs[:, :], in_=rs[:, :]), R=[rs], W=[rs])
        self.tt('dve', rs[:, :], rs[:, :], g_ap(br + 1), ALU.mult, R=[rs, g_t], W=[rs])
        tmpo = sbt('tmpo', [NQ, 4, 64])
        self.tt('dve', tmpo[:, :, :], pov[:, :, 0:64], rs[:, :].unsqueeze(2).to_broadcast([NQ, 4, 64]), ALU.mult,
                R=[po, rs], W=[tmpo])
        self.tt('pool', oacc[:, :, :], oacc[:, :, :], tmpo[:, :, :], ALU.add, R=[oacc, tmpo], W=[oacc])
    o_ap, o_t = A['out']
    self.cp('act', o_ap, oacc[:, :, :], R=[oacc], W=[o_t])


def _attend(self, A):
    for _ in self.attend_gen(A):
        pass


def _attend_pipe(self, thunks):
    prev = None
    for th in thunks:
        g = self.attend_gen(th())
        cmp_done = False
        while not cmp_done or prev is not None:
            if not cmp_done:
                if next(g) == 'CMP_DONE':
                    cmp_done = True
            if prev is not None:
                try:
                    next(prev)
                except StopIteration:
                    prev = None
        prev = g
    if prev is not None:
        for _ in prev:
            pass


Builder.attend_gen = _attend_gen
Builder.attend_pipe = _attend_pipe
Builder.attend = _attend


def _nsa_qproj(self, G, j, qT_all, NT_):
    P, d, T = self.P, self.d, G.T
    GT = G.GT
    for ci in range(8):
        wt = self.load_wchunk(d['wc_nsa_in'][j].ap()[ci])
        ps = self.next_pb('lin', [1, 2, 3])
        for kc in range(8):
            self.mm(ps[:, 0:GT], wt[:, kc, :], T['xnT'][:, kc, :], start=(kc == 0), stop=(kc == 7), R=[wt, T['xnT']], W=[ps])
        self.actf(qT_all[:, ci, :], ps[:, 0:GT], AF.Copy, R=[ps], W=[qT_all], scale=0.125)


def _final_norm(self, G, dst, row0, ytile=None):
    P, d, T = self.P, self.d, G.T
    TP = G.TP
    wrow = self.ring('wrow', 2, lambda j: P.sb([128, D], F32, f"wrow{j}"))
    P.dma(wrow[:, :], d['norm_final'].ap().partition_broadcast(128), W=[wrow])
    x = T['x']
    for t in range(G.NT):
        junk = self.ring('junk', 1, lambda j: P.sb([128, D], BF16, f"junk{j}"))
        st = self.ring('nst', 4, lambda j: P.sb([128, 2], F32, f"nst{j}"))
        self.memset('pool', st[:, :], 0.0, W=[st])
        self.actf(junk[0:TP, :], x[:, t, :], AF.Square, R=[x], W=[junk, st], accum_out=st[0:TP, 0:1])
        self.ts('dve', st[0:TP, 1:2], st[0:TP, 0:1], 1.0 / D, 1e-6, ALU.mult, ALU.add, R=[st], W=[st])
        self.actf(st[0:TP, 1:2], st[0:TP, 1:2], AF.Sqrt, R=[st], W=[st])
        P.add('dve', lambda e, st=st: e.reciprocal(out=st[0:TP, 1:2], in_=st[0:TP, 1:2]), R=[st], W=[st])
        if ytile is None:
            yt = self.ring(G.name + 'yt', 2, lambda j: P.sb([TP, D], F32, G.name + f"yt{j}"))
            ya = yt[:, :]
        else:
            yt = ytile
            ya = ytile[:, t % 2, :]
        self.stt('dve', ya, x[:, t, :], st[0:TP, 1:2], wrow[0:TP, :], ALU.mult, ALU.mult, R=[x, st, wrow], W=[yt])
        P.dma(dst.ap()[row0 + t * TP:row0 + (t + 1) * TP, :], ya, R=[yt], W=[dst], q='pool')


def _nsa_prompt(self):
    P, d, c, cfg = self.P, self.d, self.c, self.cfg
    SEQ = cfg['SEQ']
    NTT = SEQ // 128
    NQT = NTT // 2
    NCB = NTT * 8 - 1
    self.nsa_late_consts()
    kcd = P.sb([128, 4, 256], BF16, "kcd")
    vcm = P.sb([128, 2, 4, 64], BF16, "vcm")
    self.memset('pool', vcm[:, :, :, :], 0.0, W=[vcm])
    with self.phase():
        lohi = P.sb([128, 2, 4, 264], F32, "lohi")
        self.memset('pool', lohi[:, :, :, :], 0.0, W=[lohi])
        for Pi in range(NTT):
            rows = self.ring('crow', 2, lambda j: P.sb([128, 1536], F32, f"crow{j}"))
            P.dma(rows[:, 0:1024], d['p_kv_rows'].ap()[Pi * 128:(Pi + 1) * 128, :], R=[d['p_kv_rows']], W=[rows])
            P.dma(rows[:, 1024:1536], d['pwin_d'].ap()[Pi * 128:(Pi + 1) * 128, :], R=[d['pwin_d']], W=[rows])
            cs = slice(Pi * 128, (Pi + 1) * 128)
            self.ctx_rows(rows, Pi, lohi,
                          (d['kselT_p'].ap()[:, :, cs].rearrange("k p n -> p k n"), d['kselT_p']),
                          (d['vsel_p'].ap()[Pi], d['vsel_p']),
                          (d['kwinT_p'].ap()[:, :, cs].rearrange("k p n -> p k n"), d['kwinT_p']),
                          (d['vwin_p'].ap()[Pi], d['vwin_p']), win_rows=(rows, 1024))
        self.cmp_finish(lohi, NCB, lambda kvh: kcd[:, kvh, :], lambda jh, kvh: vcm[:, jh, kvh, :], kcd, vcm)
    if cfg.get('nsa_stop', 99) == 2:
        raise _Stop()
    with self.phase():
        G = Ctx('n', 128, 4, 1, 1, 5)
        T = {}
        T['x'] = P.sb([128, 4, D], F32, "nx")
        T['xnT'] = P.sb([128, 8, 512], BF16, "nxnT")
        big = P.sb([128, 24, 512], BF16, "nbig")
        T['actT'] = View(big, 0, 22)
        qT_all = View(big, 0, 8)
        T['oT'] = P.sb([128, 8, 512], BF16, "noT")
        G.T = T
        o_tok = P.sb([128, 4, D], BF16, "o_tok")
        gates = P.sb([128, 4, 48], F32, "gates")
        qbd = P.sb([128, 2, 2, 512], BF16, "qbd")
        self.memset('pool', qbd[:, :, :, :], 0.0, W=[qbd])
        BT = P.sb([128, 15, 512], F32, "BT")
        for grp in range(NQT // 4):
            for tl in range(4):
                i = grp * 4 + tl
                xe = self.ring('xeo', 1, lambda j: P.sb([128, 2, D], F32, f"xeo{j}"))
                P.dma(xe[:, :, :], d['x2_p'].ap()[2 * i * 128:(2 * i + 2) * 128, :].rearrange("(e p) n -> p e n", p=128),
                      R=[d['x2_p']], W=[xe])
                self.ts('dve', T['x'][:, tl, :], xe[:, 0, :], c['parf'][:, 1:2], None, ALU.mult, R=[xe, c['parf']], W=[T['x']])
                self.stt('dve', T['x'][:, tl, :], xe[:, 1, :], c['parf'][:, 0:1], T['x'][:, tl, :], ALU.mult, ALU.add,
                         R=[xe, c['parf'], T['x']], W=[T['x']])
            for j in range(2):
                self.norm_T(G, d['norm_mix'].ap()[2 + j])
                _nsa_qproj(self, G, j, qT_all, 4)
                pg = self.pb[0]
                for tl in range(4):
                    for kc in range(8):
                        self.mm(pg[:, tl * 48:(tl + 1) * 48], T['xnT'][:, kc, tl * 128:(tl + 1) * 128], c['wg'][:, j, kc, :],
                                start=(kc == 0), stop=(kc == 7), R=[T['xnT'], c['wg']], W=[pg])
                self.actf(gates[:, :, :].rearrange("p t n -> p (t n)"), pg[:, 0:192], AF.Sigmoid, R=[pg], W=[gates])
                if cfg.get('nsa_stop', 99) == 3:
                    raise _Stop()
                for kvh in range(4):
                    for b3 in range(5):
                        P.dma(BT[:, 3 * b3:3 * b3 + 3, :], d['BTd'].ap()[kvh, 3 * b3:3 * b3 + 3].rearrange("i p n -> p i n"),
                              R=[d['BTd']], W=[BT])
                    for pr in range(2):
                        self.cp('pool', qbd[0:64, pr, 0, :], qT_all[0:64, 2 * kvh + pr, :], R=[qT_all], W=[qbd])
                        self.cp('pool', qbd[64:128, pr, 1, :], qT_all[64:128, 2 * kvh + pr, :], R=[qT_all], W=[qbd])
                    thunks = []
                    for tl in range(4):
                        def mk(tl=tl, kvh=kvh):
                            i = grp * 4 + tl
                            qc = slice(tl * 128, (tl + 1) * 128)
                            bc = self.ring('bcp', 2, lambda jj: P.sb([128, 4, 256], F32, f"bcp{jj}"))
                            smc = self.ring('smc', 2, lambda jj: P.sb([128, 2, 64], F32, f"smc{jj}"))
                            P.dma(smc[:, 0, :], d['selmul_p'].ap()[:, i, :], W=[smc])
                            P.dma(smc[:, 1, :], d['seladd_p'].ap()[:, i, :], W=[smc])
                            for a in range(8):
                                src = bass.AP(d['t_cmp'].h, 4 * kvh * 8192 + 247 - 16 * i - a, [[512, 16], [8192, 4], [1, 256]])
                                P.dma(bc[16 * a:16 * a + 16, :, :], src, R=[d['t_cmp']], W=[bc])
                            nkt = 2 * i + 2
                            sel_tiles = []
                            for kt in range(nkt):
                                e = 2 * i - kt
                                sel_tiles.append((kt, (BT[:, e + 1, :], BT) if e <= 7 else None))
                            win_tiles = []
                            for e in range(4, -2, -1):
                                kt = 2 * i - e
                                if kt >= 0:
                                    win_tiles.append((kt, (BT[:, 10 + e, :], BT)))
                            A = dict(
                                NQ=128, NC=NCB + 1, NH=2, kvh=kvh,
                                qT=(lambda g, qc=qc, kvh=kvh: qT_all[(g % 2) * 64:(g % 2) * 64 + 64, 2 * kvh + g // 2, qc], qT_all),
                                qbd=qbd, qbd_t=qbd, qcols=qc,
                                gate=(lambda br, tl=tl, kvh=kvh: gates[:, tl, :].rearrange("p (h r) -> p h r", r=3)[:, 4 * kvh:4 * kvh + 4, br], gates),
                                kcmp=(lambda g, kvh=kvh: kcd[(g % 2) * 64:(g % 2) * 64 + 64, kvh, 0:NCB + 1], kcd),
                                vcmp=(lambda hf, kvh=kvh: vcm[:, hf, kvh, :], vcm),
                                bias_c=(bc[:, :, :], bc),
                                selc=(smc[:, 0, :], smc[:, 1, :], smc),
                                sel=(sel_tiles,
                                     lambda kt0, n, kvh=kvh: (d['kselT_p'].ap()[kvh, :, kt0 * 128:(kt0 + n) * 128], d['kselT_p']),
                                     lambda kt0, n, kvh=kvh: (d['vsel_p'].ap()[kt0:kt0 + n, :, kvh, :].rearrange("t p e -> p t e"), d['vsel_p']),
                                     self.pb[4]),
                                win=(win_tiles,
                                     lambda kt0, n, kvh=kvh: (d['kwinT_p'].ap()[kvh, :, kt0 * 128:(kt0 + n) * 128], d['kwinT_p']),
                                     lambda kt0, n, kvh=kvh: (d['vwin_p'].ap()[kt0:kt0 + n, :, kvh, :].rearrange("t p e -> p t e"), d['vwin_p']),
                                     self.pb[5]),
                                out=(o_tok[:, tl, kvh * 256:(kvh + 1) * 256].rearrange("p (g e) -> p g e", g=4), o_tok),
                            )

                            return A
                        thunks.append(mk)
                    self.attend_pipe(thunks)
                for tl in range(4):
                    pt = self.pt[0]
                    for cc in range(8):
                        self.tr(pt[:, cc * 128:(cc + 1) * 128], o_tok[:, tl, cc * 128:(cc + 1) * 128], c['identb'][:, :],
                                R=[o_tok, c['identb']], W=[pt])
                    self.cp('act', T['oT'][:, :, tl * 128:(tl + 1) * 128], pt[:, :].rearrange("p (c n) -> p c n", c=8),
                            R=[pt], W=[T['oT']])
                self.tok_linear_add(G, T['oT'], 8, d['wb_nsa_out'][j])
                self.ffn(G, 2 + j)
            _final_norm(self, G, d['y_p'], grp * 512, ytile=self.rr['xeo'][0][0])


Builder.nsa_prompt = _nsa_prompt


def _nsa_sample(self):
    P, d, c, cfg = self.P, self.d, self.c, self.cfg
    NSQ = 16
    self.nsa_late_consts()
    kcd = P.sb([128, NSQ, 4, 128], BF16, "kcds")
    vcm = P.sb([128, NSQ, 4, 64], BF16, "vcms")
    self.memset('pool', vcm[:, :, :, :], 0.0, W=[vcm])
    with self.phase():
        ptb = P.sb([128, 256], I32, "ptb")
        P.dma(ptb[:, :], d['page_table'].ap()[0].partition_broadcast(128), W=[ptb])
        ptf = P.sb([128, 256], F32, "ptf")
        self.cp('dve', ptf[:, :], ptb[:, :], R=[ptb], W=[ptf])
        iot = P.sb([128, 1], F32, "iot")
        P.dma(iot[:, :], d['iota'].ap(), W=[iot])
        self.stt('dve', ptf[:, :], ptf[:, :], 128.0, iot[:, 0:1].to_broadcast([128, 256]), ALU.mult, ALU.add,
                 R=[ptf, iot], W=[ptf])
        idx = P.sb([128, 256], I32, "idxall")
        self.cp('dve', idx[:, :], ptf[:, :], R=[ptf], W=[idx])
        for s in range(NSQ):
            lohi = self.ring('lohis', 2, lambda j: P.sb([128, 2, 4, 136], F32, f"lohis{j}"))
            self.memset('pool', lohi[:, :, :, :], 0.0, W=[lohi])
            for Pi in range(17):
                rows = self.ring('crow', 4, lambda j: P.sb([128, 1024], F32, f"crows{j}"))
                if Pi < 16:
                    k = s * 16 + Pi
                    P.add('pool', lambda e, rows=rows, k=k: e.indirect_dma_start(
                        out=rows[:, :], out_offset=None, in_=d['cache_kv'].ap(),
                        in_offset=bass.IndirectOffsetOnAxis(ap=idx[:, k:k + 1], axis=0)),
                        R=[idx, d['cache_kv']], W=[rows], dma=True)
                else:
                    self.memset('pool', rows[:, :], 0.0, W=[rows])
                    P.dma(rows[0:4, :], d['s_kv_rows'].ap()[4 * s:4 * s + 4, :], R=[d['s_kv_rows']], W=[rows])
                cs = slice(Pi * 128, (Pi + 1) * 128)
                self.ctx_rows(rows, Pi, lohi if Pi < 16 else None,
                              (d['kselT_s'].ap()[s][:, :, cs].rearrange("k p n -> p k n"), d['kselT_s']),
                              (d['vsel_s'].ap()[s, Pi], d['vsel_s']), None, None, has_win=False)
            for W_ in range(5):
                wr = self.ring('wrow_s', 2, lambda j: P.sb([128, 512], F32, f"wrows{j}"))
                if W_ < 4:
                    P.dma(wr[:, :], d['state_win_kv'].ap()[s, W_ * 128:(W_ + 1) * 128, :], W=[wr])
                else:
                    self.memset('pool', wr[:, :], 0.0, W=[wr])
                    P.dma(wr[0:4, :], d['s_win_kv'].ap()[s, 508:512, :], R=[d['s_win_kv']], W=[wr])
                cs = slice(W_ * 128, (W_ + 1) * 128)
                self.ctx_rows(None, 0, None, None, None,
                              (d['kwinT_s'].ap()[s][:, :, cs].rearrange("k p n -> p k n"), d['kwinT_s']),
                              (d['vwin_s'].ap()[s, W_], d['vwin_s']), has_cmpsel=False, win_rows=(wr, 0))
            self.cmp_finish(lohi, 127, lambda kvh, s=s: kcd[:, s, kvh, :], lambda jh, kvh, s=s: vcm[:, s, kvh, :], kcd, vcm,
                            ncw=128, njh=1)
    with self.phase():
        G = Ctx('m', 64, 1, 16, 16, 1)
        T = {}
        T['x'] = P.sb([64, 1, D], F32, "mx")
        T['xnT'] = P.sb([128, 8, 64], BF16, "mxnT")
        big = P.sb([128, 24, 64], BF16, "mbig")
        T['actT'] = View(big, 0, 22)
        qT_all = View(big, 0, 8)
        T['oT'] = P.sb([128, 8, 64], BF16, "moT")
        G.T = T
        P.dma(T['x'][:, 0, :], d['x2_s'].ap(), R=[d['x2_s']], W=[T['x']])
        qbd = P.sb([128, 4, 2, 2, 64], BF16, "qbds")
        self.memset('pool', qbd[:, :, :, :, :], 0.0, W=[qbd])
        smul = P.sb([4, 64], F32, "smuls")
        sadd = P.sb([4, 64], F32, "sadds")
        P.dma(smul[:, :], d['selmul_s'].ap(), W=[smul])
        P.dma(sadd[:, :], d['seladd_s'].ap(), W=[sadd])
        bcs = P.sb([4, 16, 128], F32, "bcs")
        P.dma(bcs[:, :, :], bass.AP(d['t_cs'].h, 0, [[128, 4], [512, 16], [1, 128]]), R=[d['t_cs']], W=[bcs])
        trv = P.sb([128, 22, 16, 4], F32, "trvs")
        for Pi in range(17):
            P.dma(trv[:, Pi, :, :], bass.AP(d['t_ss'].h, 2048 - 128 * Pi, [[1, 128], [2304, 16], [1, 4]]), R=[d['t_ss']], W=[trv])
        for W_ in range(5):
            P.dma(trv[:, 17 + W_, :, :], bass.AP(d['t_ws'].h, 512 - 128 * W_, [[1, 128], [768, 16], [1, 4]]), R=[d['t_ws']], W=[trv])
        BTs = P.sb([128, 22, 16, 4], F32, "BTs")
        tf = trv[:, :, :, :].rearrange("p a h t -> p (a h t)")
        bf_ = BTs[:, :, :, :].rearrange("p a h t -> p (a h t)")
        for off in range(0, 22 * 64, 512):
            n = min(512, 22 * 64 - off)
            ps = self.next_pb('lin', [1, 2, 3])
            self.mm(ps[:, 0:n], c['antiI'][:, :], tf[:, off:off + n], R=[c['antiI'], trv], W=[ps])
            self.cp('act', bf_[:, off:off + n], ps[:, 0:n], R=[ps], W=[BTs])
        for j in range(2):
            self.norm_T(G, d['norm_mix'].ap()[2 + j])
            _nsa_qproj(self, G, j, qT_all, 1)
            for kvh in range(4):
                for pr in range(2):
                    self.cp('pool', qbd[0:64, kvh, pr, 0, :], qT_all[0:64, 2 * kvh + pr, :], R=[qT_all], W=[qbd])
                    self.cp('pool', qbd[64:128, kvh, pr, 1, :], qT_all[64:128, 2 * kvh + pr, :], R=[qT_all], W=[qbd])
            gs_all = self.ring('gs_all', 1, lambda jj: P.sb([4, NSQ, 48], F32, "gs_all"))
            for half in range(2):
                pg = self.pb[0]
                for s8 in range(8):
                    s = half * 8 + s8
                    for kc in range(8):
                        self.mm(pg[0:4, s8 * 48:(s8 + 1) * 48], T['xnT'][:, kc, 4 * s:4 * s + 4], c['wg'][:, j, kc, :],
                                start=(kc == 0), stop=(kc == 7), R=[T['xnT'], c['wg']], W=[pg])
                self.actf(gs_all[:, half * 8:(half + 1) * 8, :].rearrange("p s n -> p (s n)"), pg[0:4, 0:384], AF.Sigmoid,
                          R=[pg], W=[gs_all])
            o_all = self.ring('o_all', 1, lambda jj: P.sb([4, NSQ, D], BF16, "o_all"))
            thunks = []
            for s in range(NSQ):
                for kvh in range(4):
                    def mk(s=s, kvh=kvh):
                        qc = slice(4 * s, 4 * s + 4)
                        sel_tiles = [(Pi, (BTs[:, Pi, 4 * kvh:4 * kvh + 4, :].rearrange("p h t -> p (h t)"), BTs)) for Pi in range(17)]
                        win_tiles = [(W_, (BTs[:, 17 + W_, 4 * kvh:4 * kvh + 4, :].rearrange("p h t -> p (h t)"), BTs)) for W_ in range(5)]
                        return dict(
                            NQ=4, NC=128, NH=1, kvh=kvh,
                            qT=(lambda g: qT_all[(g % 2) * 64:(g % 2) * 64 + 64, 2 * kvh + g // 2, qc], qT_all),
                            qbd=qbd.h[:, kvh], qbd_t=qbd, qcols=qc,
                            gate=(lambda br: gs_all[:, s, :].rearrange("p (h r) -> p h r", r=3)[:, 4 * kvh:4 * kvh + 4, br], gs_all),
                            kcmp=(lambda g: kcd[(g % 2) * 64:(g % 2) * 64 + 64, s, kvh, :], kcd),
                            vcmp=(lambda hf: vcm[:, s, kvh, :], vcm),
                            bias_c=(bcs[:, 4 * kvh:4 * kvh + 4, :], bcs),
                            selc=(smul[:, :], sadd[:, :], smul),
                            sel=(sel_tiles,
                                 lambda kt0, n: (d['kselT_s'].ap()[s, kvh, :, kt0 * 128:(kt0 + n) * 128], d['kselT_s']),
                                 lambda kt0, n: (d['vsel_s'].ap()[s, kt0:kt0 + n, :, kvh, :].rearrange("t p e -> p t e"), d['vsel_s']),
                                 self.pb[4]),
                            win=(win_tiles,
                                 lambda kt0, n: (d['kwinT_s'].ap()[s, kvh, :, kt0 * 128:(kt0 + n) * 128], d['kwinT_s']),
                                 lambda kt0, n: (d['vwin_s'].ap()[s, kt0:kt0 + n, :, kvh, :].rearrange("t p e -> p t e"), d['vwin_s']),
                                 self.pb[5]),
                            out=(o_all[:, s, kvh * 256:(kvh + 1) * 256].rearrange("p (g e) -> p g e", g=4), o_all),
                        )
                    thunks.append(mk)
            self.attend_pipe(thunks)
            for s in range(NSQ):
                qc = slice(4 * s, 4 * s + 4)
                pt = self.pt[0]
                for cc in range(8):
                    self.tr(pt[:, cc * 4:(cc + 1) * 4], o_all[:, s, cc * 128:(cc + 1) * 128], c['identb'][0:4, 0:4],
                            R=[o_all, c['identb']], W=[pt])
                self.cp('act', T['oT'][:, :, qc], pt[:, 0:32].rearrange("p (c n) -> p c n", c=8), R=[pt], W=[T['oT']])
            self.tok_linear_add(G, T['oT'], 8, d['wb_nsa_out'][j])
            self.ffn(G, 2 + j)
        _final_norm(self, G, d['y_s'], 0)


Builder.nsa_sample = _nsa_sample
```

```python
import math
import numpy as np
import concourse.bass as bass
import concourse.mybir as mybir
from concourse.bass_utils import run_bass_kernel_spmd

F32 = mybir.dt.float32
BF16 = mybir.dt.bfloat16
I32 = mybir.dt.int32
AF = mybir.ActivationFunctionType
ALU = mybir.AluOpType

EPOCH = 16000
NSLOT = 8
ENGS = ('pe', 'act', 'dve', 'pool', 'sp')
SAME_ENGINE_SYNC = {'pe': False, 'act': True, 'dve': True, 'pool': True, 'sp': True}

D = 1024
H = 8
DFF = 2816
NEG = -30000.0


class Buf:
    __slots__ = ('name', 'lw', 'rd')

    def __init__(self, name):
        self.name = name
        self.lw = None
        self.rd = []


class Tile:
    def __init__(self, h, name):
        self.h = h
        self.buf = Buf(name)

    def __getitem__(self, idx):
        return self.h[idx]

    def ap(self):
        return self.h.ap()


class View:
    def __init__(self, base, off, n):
        self.base, self.buf, self.off, self.n = base, base.buf, off, n

    def __getitem__(self, idx):
        idx = list(idx)
        a = idx[1]
        if isinstance(a, slice):
            st = (a.start or 0) + self.off
            en = (a.stop if a.stop is not None else self.n) + self.off
            idx[1] = slice(st, en)
        else:
            idx[1] = a + self.off
        return self.base.h[tuple(idx)]


class Prog:
    def __init__(self, nc):
        self.nc = nc
        self.ops = {e: [] for e in ENGS}
        self.cnt = {e: 0 for e in ENGS}
        self.dcnt = {e: 0 for e in ENGS}
        self.known = {e: {} for e in ENGS}
        self.nt = 0
        self.stack = None

    def sb(self, shape, dtype=F32, name=None):
        self.nt += 1
        name = name or "t"
        if self.stack is not None:
            h = self.stack.enter_context(self.nc.sbuf_tensor(f"{name}_{self.nt}", list(shape), dtype))
        else:
            h = self.nc.alloc_sbuf_tensor(f"{name}_{self.nt}", list(shape), dtype)
        return Tile(h, name)

    def barrier(self):
        evs = []
        for f in ENGS:
            if self.cnt[f] > 0:
                evs.append(('c', f, self.cnt[f]))
            n = self.dcnt[f]
            for slot in range(min(n, NSLOT)):
                evs.append(('d', f, slot, (n - 1 - slot) // NSLOT + 1))
        for e in ENGS:
            waits = []
            for ev in evs:
                if ev[0] == 'c' and ev[1] == e:
                    continue
                self._need(e, ev, waits)
            self.cnt[e] += 1
            self.ops[e].append((waits, (lambda en: en.nop()), ('c', e, self.cnt[e])))

    def ps(self, shape, dtype=F32, name=None):
        self.nt += 1
        name = name or "p"
        h = self.nc.alloc_psum_tensor(f"{name}_{self.nt}", list(shape), dtype)
        return Tile(h, name)

    def dram(self, name, shape, dtype=F32, kind="Internal"):
        h = self.nc.dram_tensor(name, list(shape), dtype, kind=kind)
        return Tile(h, name)

    def _need(self, eng, ev, waits):
        if ev is None:
            return
        if ev[0] == 'c':
            _, f, seq = ev
            if f == eng and not SAME_ENGINE_SYNC[eng]:
                return
            key = ('c', f)
            if self.known[eng].get(key, 0) >= seq:
                return
            self.known[eng][key] = seq
            waits.append(ev)
        else:
            _, q, slot, k = ev
            key = ('d', q, slot)
            if self.known[eng].get(key, 0) >= k:
                return
            self.known[eng][key] = k
            waits.append(ev)

    def add(self, eng, emit, R=(), W=(), dma=False):
        waits = []
        for t in R:
            self._need(eng, t.buf.lw, waits)
        for t in W:
            b = t.buf
            self._need(eng, b.lw, waits)
            for ev in b.rd:
                self._need(eng, ev, waits)
        if dma:
            i = self.dcnt[eng]
            self.dcnt[eng] += 1
            slot, k = i % NSLOT, i // NSLOT + 1
            if k > 1:
                self._need(eng, ('d', eng, slot, k - 1), waits)
            ev = ('d', eng, slot, k)
        else:
            self.cnt[eng] += 1
            ev = ('c', eng, self.cnt[eng])
        for t in R:
            t.buf.rd.append(ev)
        for t in W:
            t.buf.lw = ev
            t.buf.rd = []
        self.ops[eng].append((waits, emit, ev))
        return ev

    def dma(self, out_ap, in_ap, R=(), W=(), q='sp', **kw):
        return self.add(q, lambda e: e.dma_start(out=out_ap, in_=in_ap, **kw), R, W, dma=True)

    def emit(self):
        nc = self.nc
        csem = {}
        for e in ENGS:
            n = (self.cnt[e] + EPOCH - 1) // EPOCH
            csem[e] = [nc.alloc_semaphore(f"c_{e}_{j}") for j in range(n)]
        dsem = {}
        for e in ENGS:
            n = min(self.dcnt[e], NSLOT)
            dsem[e] = [nc.alloc_semaphore(f"d_{e}_{j}") for j in range(n)]

        def semval(ev):
            if ev[0] == 'c':
                _, f, seq = ev
                return csem[f][(seq - 1) // EPOCH], (seq - 1) % EPOCH + 1
            _, q, slot, k = ev
            return dsem[q][slot], 16 * k

        def run(eng, e):
            for waits, emit, ev in self.ops[eng]:
                for w in waits:
                    s, v = semval(w)
                    e.wait_ge(s, v)
                ins = emit(e)
                s, v = semval(ev)
                ins.then_inc(s, 16 if ev[0] == 'd' else 1)
            n = self.dcnt[eng]
            for slot in range(min(n, NSLOT)):
                k = (n - 1 - slot) // NSLOT + 1
                if self.known[eng].get(('d', eng, slot), 0) < k:
                    e.wait_ge(dsem[eng][slot], 16 * k)

        with nc.Block() as block:
            @block.tensor
            def _(e):
                run('pe', e)

            @block.scalar
            def _(e):
                run('act', e)

            @block.vector
            def _(e):
                run('dve', e)

            @block.gpsimd
            def _(e):
                run('pool', e)

            @block.sync
            def _(e):
                run('sp', e)
        return nc


class Ctx:
    def __init__(self, name, TP, NT, NSEQ, NS, nlev):
        self.name = name
        self.TP = TP
        self.NT = NT
        self.GT = TP * NT
        self.NSEQ = NSEQ
        self.TS = self.GT // NSEQ
        self.NCH = self.GT // 64
        self.NS = NS
        self.nlev = nlev


def host_consts(NS, TS):
    seg = np.arange(64) // TS if NS > 1 else np.zeros(64, np.int64)
    j = np.arange(64)[:, None]
    i = np.arange(64)[None, :]
    same = (seg[:, None] == seg[None, :])
    c = {}
    c['ucs'] = ((j <= i) & same).astype(np.float32)
    c['maskT'] = np.where((j <= i) & same, 0.0, NEG).astype(np.float32)
    c['noff'] = -np.where((j != i), 1.0, 0.0).astype(np.float32)
    c['same'] = same.astype(np.float32)
    si = np.zeros((64, NS), np.float32)
    si[np.arange(64), seg] = 1.0
    c['seqind'] = si
    cm = np.zeros((128, NS, 64), np.float32)
    cm[:, seg, np.arange(64)] = 1.0
    c['colmask'] = cm.reshape(128, NS * 64)
    return c


class _Stop(Exception):
    pass


class Builder:
    def __init__(self, cfg):
        self.cfg = cfg
        nc = bass.Bass("TRN2", target_bir_lowering=False)
        self.nc = nc
        self.P = Prog(nc)
        self.rr = {}

    def mm(self, out, lhsT, rhs, start=True, stop=True, R=(), W=()):
        self.P.add('pe', lambda e: e.matmul(out, lhsT=lhsT, rhs=rhs, start=start, stop=stop), R, W)

    def tr(self, out, in_, ident, R=(), W=()):
        self.P.add('pe', lambda e: e.transpose(out=out, in_=in_, identity=ident), R, W)

    def actf(self, out, in_, func, R=(), W=(), bias=None, scale=None, accum_out=None):
        kw = {}
        if bias is not None:
            kw['bias'] = bias
        if scale is not None:
            kw['scale'] = scale
        if accum_out is not None:
            kw['accum_out'] = accum_out
        self.P.add('act', lambda e: e.activation(out=out, in_=in_, func=func, **kw), R, W)

    def cp(self, eng, out, in_, R=(), W=()):
        if eng == 'act':
            self.P.add('act', lambda e: e.copy(out=out, in_=in_), R, W)
        else:
            self.P.add(eng, lambda e: e.tensor_copy(out=out, in_=in_), R, W)

    def tt(self, eng, out, in0, in1, op, R=(), W=()):
        self.P.add(eng, lambda e: e.tensor_tensor(out=out, in0=in0, in1=in1, op=op), R, W)

    def ts(self, eng, out, in0, s1, s2, op0, op1=None, R=(), W=()):
        if op1 is None:
            self.P.add(eng, lambda e: e.tensor_scalar(out=out, in0=in0, scalar1=s1, scalar2=None, op0=op0), R, W)
        else:
            self.P.add(eng, lambda e: e.tensor_scalar(out=out, in0=in0, scalar1=s1, scalar2=s2, op0=op0, op1=op1), R, W)

    def stt(self, eng, out, in0, scalar, in1, op0, op1, R=(), W=()):
        self.P.add(eng, lambda e: e.scalar_tensor_tensor(out=out, in0=in0, scalar=scalar, in1=in1, op0=op0, op1=op1), R, W)

    def memset(self, eng, ap, val, W=()):
        self.P.add(eng, lambda e: e.memset(ap, val), (), W)

    def ring(self, key, n, make):
        if key not in self.rr:
            self.rr[key] = [[make(i) for i in range(n)], 0]
        r = self.rr[key]
        t = r[0][r[1] % n]
        r[1] += 1
        return t

    def declare_io(self):
        P, cfg = self.P, self.cfg
        d = {}

        def inp(name, shape, dt=F32):
            d[name] = P.dram(name, shape, dt, kind="ExternalInput")

        def out(name, shape, dt=F32):
            d[name] = P.dram(name, shape, dt, kind="ExternalOutput")

        SEQ = cfg['SEQ']
        inp('x_prompt', [SEQ, D])
        inp('x_sample', [64, D])
        inp('state_dn_S', [2, 16, H, 128, 128])
        inp('state_dn_conv', [2, 16, 3, 3072])
        inp('state_win_kv', [16, 512, 512])
        inp('norm_mix', [4, D])
        inp('norm_ffn', [4, D])
        inp('norm_kv', [D])
        inp('norm_final', [D])
        inp('ffn_w_in', [4, D, 2 * DFF])
        inp('ffn_w_out', [4, DFF, D])
        inp('dn_w_in', [2, D, 4112])
        inp('dn_conv_w', [2, 4, 3072])
        inp('dn_A_log', [2, H])
        inp('dn_dt_bias', [2, H])
        inp('dn_out_norm', [2, 128])
        inp('dn_w_out', [2, D, D])
        inp('nsa_w_kv', [D, 1536])
        for pre, NS in (('cp_', 1), ('cs_', 16)):
            inp(pre + 'ucs', [64, 64])
            inp(pre + 'maskT', [64, 64])
            inp(pre + 'noff', [64, 64])
            inp(pre + 'same', [64, 64])
            inp(pre + 'seqind', [64, NS])
            inp(pre + 'colmask', [128, NS * 64])
        out('p_dn_S', [2, H, 128, 128])
        out('p_dn_conv', [2, 3, 3072])
        out('p_kv_rows', [SEQ, 1024])
        out('p_win_kv', [512, 512])
        out('s_dn_S', [2, 16, H, 128, 128])
        out('s_dn_conv', [2, 16, 3, 3072])
        out('s_kv_rows', [64, 1024])
        out('s_win_kv', [16, 512, 512])
        out('x2_p', [SEQ, D])
        out('x2_s', [64, D])
        d['wc_dn_in'] = [P.dram(f'wc_dn_in{l}', [32, 128, 8, 128], BF16) for l in range(2)]
        d['wc_ffn_in'] = [P.dram(f'wc_ffn_in{l}', [44, 128, 8, 128], BF16) for l in range(4)]
        d['wb_ffn_out'] = [P.dram(f'wb_ffn_out{l}', [DFF, D], BF16) for l in range(4)]
        d['wb_dn_out'] = [P.dram(f'wb_dn_out{l}', [D, D], BF16) for l in range(2)]
        d['wb_kv'] = P.dram('wb_kv', [D, 1536], BF16)
        self.d = d

    def consts(self):
        P, d = self.P, self.d
        c = {}
        identf = P.sb([128, 128], F32, "identf")
        self.memset('pool', identf[:, :], 1.0, W=[identf])
        P.add('pool', lambda e: e.affine_select(out=identf[:, :], in_=identf[:, :], pattern=[[-1, 128]],
                                                compare_op=ALU.is_equal, fill=0.0, base=0, channel_multiplier=1),
              R=[identf], W=[identf])
        identb = P.sb([128, 128], BF16, "identb")
        self.cp('dve', identb[:, :], identf[:, :], R=[identf], W=[identb])
        onesb = P.sb([128, 128], BF16, "onesb")
        self.memset('pool', onesb[:, :], 1.0, W=[onesb])
        onesf = P.sb([64, 128], F32, "onesf")
        self.memset('pool', onesf[:, :], 1.0, W=[onesf])
        c.update(identf=identf, identb=identb, onesb=onesb, onesf=onesf)
        for pre, NS in (('cp_', 1), ('cs_', 16)):
            for nm, shp in (('ucs', [64, 64]), ('maskT', [64, 64]), ('noff', [64, 64]), ('same', [64, 64]),
                            ('seqind', [64, NS])):
                t = P.sb(shp, F32, pre + nm)
                P.dma(t[:, :], d[pre + nm].ap(), W=[t])
                c[pre + nm] = t
            if NS > 1:
                t = P.sb([128, NS * 64], BF16, pre + 'colmask')
                P.dma(t[:, :], d[pre + 'colmask'].ap(), W=[t], q='pool')
                c[pre + 'colmask'] = t
        cw = P.sb([128, 2, 24, 4], F32, "cw")
        for l in range(2):
            for i in range(4):
                P.dma(cw[:, l, :, i], d['dn_conv_w'].ap()[l, i].rearrange("(c p) -> p c", p=128), W=[cw],
                      allow_slow_non_contiguous=True)
        c['cw'] = cw
        negA = P.sb([128, 2, H], F32, "negA")
        dtb = P.sb([128, 2, H], F32, "dtb")
        P.dma(negA[:, :, :], d['dn_A_log'].ap().rearrange("l h -> (l h)").partition_broadcast(128).rearrange("p (l h) -> p l h", l=2), W=[negA])
        P.dma(dtb[:, :, :], d['dn_dt_bias'].ap().rearrange("l h -> (l h)").partition_broadcast(128).rearrange("p (l h) -> p l h", l=2), W=[dtb])
        self.actf(negA[:, :, :], negA[:, :, :], AF.Exp, R=[negA], W=[negA])
        self.ts('dve', negA[:, :, :], negA[:, :, :], -1.0, None, ALU.mult, R=[negA], W=[negA])
        c.update(negA=negA, dtb=dtb)
        onw = P.sb([128, 2], F32, "onw")
        P.dma(onw[:, :], d['dn_out_norm'].ap().rearrange("l p -> p l"), W=[onw], allow_slow_non_contiguous=True)
        c['onw'] = onw
        wab = P.sb([128, 2, 8, 16], BF16, "wab")
        for l in range(2):
            P.dma(wab[:, l, :, :], d['dn_w_in'].ap()[l, :, 4096:4112].rearrange("(kc p) n -> p kc n", p=128), W=[wab], q='pool')
        c['wab'] = wab
        self.c = c
        self.pb = [P.ps([128, 512], F32, f"pb{i}") for i in range(6)]
        self.pt = [P.ps([128, 1024], BF16, f"pt{i}") for i in range(2)]

    def convert_weights(self):
        P, d = self.P, self.d
        k = [0]
        SW = 2048

        def stage():
            i = k[0]
            k[0] += 1
            f = self.ring('cvf', 2, lambda j: P.sb([128, SW], F32, f"cvf{j}"))
            b = self.ring('cvb', 2, lambda j: P.sb([128, SW], BF16, f"cvb{j}"))
            return f, b, ('act', 'dve', 'pool')[i % 3]

        def conv_chunked(W_ap, dst, nch):
            for g in range(nch // 2):
                f, b, e = stage()
                P.dma(f[:, :].rearrange("p (k n) -> p k n", k=8),
                      W_ap[:, g * 256:(g + 1) * 256].rearrange("(kc p) n -> p kc n", p=128), W=[f], q='sp')
                o = b[:, :].rearrange("p (c k n) -> p k c n", c=2, k=8)
                sv = f[:, :].rearrange("p (k c n) -> p k c n", k=8, c=2)
                self.cp(e, o, sv, R=[f], W=[b])
                P.dma(dst.ap()[g * 2:(g + 1) * 2].rearrange("c p k n -> p c (k n)"),
                      b[:, :].rearrange("p (c kn) -> p c kn", c=2), R=[b], W=[dst], q='act')

        def conv_natural(W_ap, dst, K, N):
            per = max(1, SW // N)
            nk = K // 128
            kc = 0
            while kc < nk:
                m = min(per, nk - kc)
                f, b, e = stage()
                P.dma(f[:, 0:m * N].rearrange("p (k n) -> p k n", k=m),
                      W_ap[kc * 128:(kc + m) * 128, :].rearrange("(k p) n -> p k n", p=128), W=[f], q='sp')
                self.cp(e, b[:, 0:m * N], f[:, 0:m * N], R=[f], W=[b])
                P.dma(dst.ap()[kc * 128:(kc + m) * 128, :].rearrange("(k p) n -> p k n", p=128),
                      b[:, 0:m * N].rearrange("p (k n) -> p k n", k=m), R=[b], W=[dst], q='act')
                kc += m

        for l in range(self.cfg['n_dn']):
            conv_chunked(d['dn_w_in'].ap()[l, :, 0:4096], d['wc_dn_in'][l], 32)
            conv_natural(d['dn_w_out'].ap()[l], d['wb_dn_out'][l], D, D)
            conv_chunked(d['ffn_w_in'].ap()[l], d['wc_ffn_in'][l], 44)
            conv_natural(d['ffn_w_out'].ap()[l], d['wb_ffn_out'][l], DFF, D)
        conv_natural(d['nsa_w_kv'].ap(), d['wb_kv'], D, 1536)
        if self.cfg.get('nsa', True):
            for j in range(2):
                conv_chunked(d['nsa_w_in'].ap()[j, :, 0:1024], d['wc_nsa_in'][j], 8)
                conv_natural(d['nsa_w_out'].ap()[j], d['wb_nsa_out'][j], D, D)
                conv_chunked(d['ffn_w_in'].ap()[2 + j], d['wc_ffn_in'][2 + j], 44)
                conv_natural(d['ffn_w_out'].ap()[2 + j], d['wb_ffn_out'][2 + j], DFF, D)

    def alloc_ctx(self, G):
        P = self.P
        n = G.name
        T = {}
        T['x'] = P.sb([G.TP, G.NT, D], F32, n + "x")
        T['xnT'] = P.sb([128, 8, G.GT], BF16, n + "xnT")
        big = P.sb([128, 24, G.GT], BF16, n + "big")
        T['qT'] = View(big, 0, 8)
        T['kT'] = View(big, 8, 8)
        T['vT'] = View(big, 16, 8)
        T['actT'] = View(big, 0, 22)
        T['zs'] = P.sb([128, H, G.GT], BF16, n + "zs")
        T['OT'] = P.sb([128, H, G.GT], F32, n + "OT")
        T['oT'] = P.sb([128, H, G.GT], BF16, n + "oT")
        T['carry'] = [P.sb([128, 24, G.NSEQ, 3], F32, n + f"carry{l}") for l in range(2)]
        T['ab'] = P.sb([64, G.NCH, 16], F32, n + "ab")
        T['g'] = P.sb([64, G.NCH, H], F32, n + "g")
        T['beta'] = P.sb([64, G.NCH, H], F32, n + "beta")
        G.T = T

    def norm_T(self, G, wrow_src):
        P, c, T = self.P, self.c, G.T
        TP = G.TP
        wrow = self.ring('wrow', 2, lambda j: P.sb([128, D], F32, f"wrow{j}"))
        P.dma(wrow[:, :], wrow_src.partition_broadcast(128), W=[wrow])
        x = T['x']
        for t in range(G.NT):
            junk = self.ring('junk', 1, lambda j: P.sb([128, D], BF16, f"junk{j}"))
            st = self.ring('nst', 4, lambda j: P.sb([128, 2], F32, f"nst{j}"))
            self.memset('pool', st[:, :], 0.0, W=[st])
            self.actf(junk[0:TP, :], x[:, t, :], AF.Square, R=[x], W=[junk, st], accum_out=st[0:TP, 0:1])
            self.ts('dve', st[0:TP, 1:2], st[0:TP, 0:1], 1.0 / D, 1e-6, ALU.mult, ALU.add, R=[st], W=[st])
            self.actf(st[0:TP, 1:2], st[0:TP, 1:2], AF.Sqrt, R=[st], W=[st])
            P.add('dve', lambda e, st=st: e.reciprocal(out=st[0:TP, 1:2], in_=st[0:TP, 1:2]), R=[st], W=[st])
            xn = self.ring('xn', 2, lambda j: P.sb([128, D], BF16, f"xn{j}"))
            self.stt('dve', xn[0:TP, :], x[:, t, :], st[0:TP, 1:2], wrow[0:TP, :], ALU.mult, ALU.mult,
                     R=[x, st, wrow], W=[xn])
            pt = self.pt[0]
            for cc in range(8):
                self.tr(pt[:, cc * 128:cc * 128 + TP], xn[0:TP, cc * 128:(cc + 1) * 128], c['identb'][0:TP, 0:TP],
                        R=[xn, c['identb']], W=[pt])
            self.cp('act', T['xnT'][:, :, t * TP:(t + 1) * TP],
                    pt[:, :].rearrange("p (c n) -> p c n", c=8)[:, :, 0:TP], R=[pt], W=[T['xnT']])

    def load_wchunk(self, src_ap):
        P = self.P
        wt = self.ring('wch', 4, lambda j: P.sb([128, 8, 128], BF16, f"wch{j}"))
        P.dma(wt[:, :, :], src_ap, W=[wt])
        return wt

    def next_pb(self, key, banks):
        r = self.rr.setdefault('pb_' + key, [0])
        b = banks[r[0] % len(banks)]
        r[0] += 1
        return self.pb[b]

    def dn_mixer(self, G, l, S_tiles, last_group):
        P, c, T, d = self.P, self.c, G.T, self.d
        GT, TP, NT, NSEQ, TS, NCH, NS = G.GT, G.TP, G.NT, G.NSEQ, G.TS, G.NCH, G.NS
        pre = 'cp_' if NS == 1 else 'cs_'
        self.norm_T(G, d['norm_mix'].ap()[l])
        xnT = T['xnT']
        pbs = self.pb[0]
        for n in range(NCH):
            for kc in range(8):
                self.mm(pbs[0:64, n * 16:(n + 1) * 16], xnT[:, kc, n * 64:(n + 1) * 64], c['wab'][:, l, kc, :],
                        start=(kc == 0), stop=(kc == 7), R=[xnT, c['wab']], W=[pbs])
        ab = T['ab']
        self.cp('act', ab[:, :, :], pbs[0:64, 0:NCH * 16].rearrange("p (n k) -> p n k", k=16), R=[pbs], W=[ab])
        gt = self.ring('gtmp', 2, lambda j: P.sb([64, 8, H], F32, f"gtmp{j}"))
        g2 = self.ring('gtmp', 2, lambda j: None)
        av = gt[:, 0:NCH, :]
        a2 = g2[:, 0:NCH, :]
        self.tt('dve', av, ab[:, :, 0:8], c['dtb'][0:64, l, :].unsqueeze(1).to_broadcast([64, NCH, H]), ALU.add,
                R=[ab, c['dtb']], W=[gt])
        self.actf(a2, av, AF.Abs, R=[gt], W=[g2])
        self.actf(a2, a2, AF.Exp, R=[g2], W=[g2], scale=-1.0)
        self.actf(a2, a2, AF.Ln, R=[g2], W=[g2], bias=1.0)
        self.stt('dve', a2, av, 0.0, a2, ALU.max, ALU.add, R=[gt, g2], W=[g2])
        self.tt('dve', T['g'][:, :, :], a2, c['negA'][0:64, l, :].unsqueeze(1).to_broadcast([64, NCH, H]), ALU.mult,
                R=[g2, c['negA']], W=[T['g']])
        self.actf(T['beta'][:, :, :], ab[:, :, 8:16], AF.Sigmoid, R=[ab], W=[T['beta']])

        carry = T['carry'][l]

        def chunk_gen(ci):
            wt = self.load_wchunk(d['wc_dn_in'][l].ap()[ci])
            ps = self.next_pb('lin', [1, 2, 3])
            for kc in range(8):
                self.mm(ps[:, 0:GT], wt[:, kc, :], xnT[:, kc, :], start=(kc == 0), stop=(kc == 7), R=[wt, xnT], W=[ps])
            hh = ci % 8
            if ci >= 24:
                self.actf(T['zs'][:, hh, :], ps[:, 0:GT], AF.Silu, R=[ps], W=[T['zs']])
                return
            xp = self.ring(G.name + 'xp', 2, lambda j: P.sb([128, NSEQ, TS + 3], F32, G.name + f"xp{j}"))
            self.cp('act', xp[:, :, 3:TS + 3], ps[:, 0:GT].rearrange("p (s t) -> p s t", s=NSEQ), R=[ps], W=[xp])
            self.cp('pool', xp[:, :, 0:3], carry[:, ci, :, :], R=[carry], W=[xp])
            self.cp('pool', carry[:, ci, :, :], xp[:, :, TS:TS + 3], R=[xp], W=[carry])
            acc = self.ring(G.name + 'acc', 2, lambda j: P.sb([128, NSEQ, TS], F32, G.name + f"acc{j}"))
            cwl = c['cw']
            self.ts('dve', acc[:, :, :], xp[:, :, 0:TS], cwl[:, l, ci, 0:1], None, ALU.mult, R=[xp, cwl], W=[acc])
            for i in range(1, 4):
                self.stt('dve', acc[:, :, :], xp[:, :, i:TS + i], cwl[:, l, ci, i:i + 1], acc[:, :, :], ALU.mult, ALU.add,
                         R=[xp, cwl, acc], W=[acc])
            accf = acc[:, :, :].rearrange("p s t -> p (s t)")
            if ci >= 16:
                self.actf(T['vT'][:, hh, :], accf, AF.Silu, R=[acc], W=[T['vT']])
                return
            sl = self.ring(G.name + 'sl', 2, lambda j: P.sb([128, GT], F32, G.name + f"sl{j}"))
            self.actf(sl[:, :], accf, AF.Silu, R=[acc], W=[sl])
            sq = self.ring(G.name + 'sq', 2, lambda j: P.sb([128, GT], BF16, G.name + f"sq{j}"))
            self.actf(sq[:, :], sl[:, :], AF.Square, R=[sl], W=[sq])
            ps2 = self.next_pb('nrm', [4, 5])
            self.mm(ps2[:, 0:GT], c['onesb'][:, :], sq[:, :], R=[c['onesb'], sq], W=[ps2])
            yield 0
            rn = self.ring(G.name + 'rn', 2, lambda j: P.sb([128, GT], F32, G.name + f"rn{j}"))
            self.ts('dve', rn[:, :], ps2[:, 0:GT], 1e-6, None, ALU.add, R=[ps2], W=[rn])
            self.actf(rn[:, :], rn[:, :], AF.Sqrt, R=[rn], W=[rn])
            P.add('dve', lambda e, rn=rn: e.reciprocal(out=rn[:, :], in_=rn[:, :]), R=[rn], W=[rn])
            dst = T['qT'] if ci < 8 else T['kT']
            scl = (128 ** -0.5) if ci < 8 else 1.0
            self.stt('dve', dst[:, hh, :], sl[:, :], scl, rn[:, :], ALU.mult, ALU.mult, R=[sl, rn], W=[dst])

        prevg = None
        for ci in range(32):
            gch = chunk_gen(ci)
            try:
                next(gch)
            except StopIteration:
                gch = None
            if prevg is not None:
                for _ in prevg:
                    pass
            prevg = gch
        if prevg is not None:
            for _ in prevg:
                pass

        for n in range(NCH):
            self.dn_chunk(G, l, n, S_tiles, pre)

        OT, oT, zs = T['OT'], T['oT'], T['zs']
        for hh in range(H):
            sq = self.ring(G.name + 'sq', 2, lambda j: None)
            self.actf(sq[:, :], OT[:, hh, :], AF.Square, R=[OT], W=[sq])
            ps2 = self.next_pb('nrm', [4, 5])
            self.mm(ps2[:, 0:GT], c['onesb'][:, :], sq[:, :], R=[c['onesb'], sq], W=[ps2])
            rn = self.ring(G.name + 'rn', 2, lambda j: None)
            self.ts('dve', rn[:, :], ps2[:, 0:GT], 1.0 / 128, 1e-6, ALU.mult, ALU.add, R=[ps2], W=[rn])
            self.actf(rn[:, :], rn[:, :], AF.Sqrt, R=[rn], W=[rn])
            P.add('dve', lambda e, rn=rn: e.reciprocal(out=rn[:, :], in_=rn[:, :]), R=[rn], W=[rn])
            self.stt('dve', rn[:, :], OT[:, hh, :], c['onw'][:, l:l + 1], rn[:, :], ALU.mult, ALU.mult,
                     R=[OT, c['onw'], rn], W=[rn])
            self.tt('dve', oT[:, hh, :], rn[:, :], zs[:, hh, :], ALU.mult, R=[rn, zs], W=[oT])

        self.tok_linear_add(G, oT, 8, d['wb_dn_out'][l])

        if last_group:
            self.conv_state_out(G, l)

    def tok_linear_add(self, G, aT, nk, wsrc):
        P, T = self.P, G.T
        TP, NT = G.TP, G.NT
        x = T['x']
        for half in range(2):
            banks = [self.pb[1 + t] for t in range(NT)]
            for kc in range(nk):
                wt = self.ring('wrh', 4, lambda j: P.sb([128, 512], BF16, f"wrh{j}"))
                P.dma(wt[:, :], wsrc.ap()[kc * 128:(kc + 1) * 128, half * 512:(half + 1) * 512], R=[wsrc], W=[wt])
                for t in range(NT):
                    self.mm(banks[t][0:TP, :], aT[:, kc, t * TP:(t + 1) * TP], wt[:, :], start=(kc == 0),
                            stop=(kc == nk - 1), R=[aT, wt], W=[banks[t]])
            for t in range(NT):
                self.tt('dve', x[:, t, half * 512:(half + 1) * 512], x[:, t, half * 512:(half + 1) * 512],
                        banks[t][0:TP, :], ALU.add, R=[x, banks[t]], W=[x])

    def conv_state_out(self, G, l):
        P, c, T, d = self.P, self.c, G.T, self.d
        NSEQ = G.NSEQ
        R3 = NSEQ * 3
        carry = T['carry'][l]
        for g4 in range(6):
            co = self.ring(G.name + 'co', 2, lambda j: P.sb([R3, 4, 128], F32, G.name + f"co{j}"))
            ps = self.next_pb('lin', [1, 2, 3])
            for j in range(4):
                ci = g4 * 4 + j
                self.tr(ps[0:R3, j * 128:(j + 1) * 128], carry[:, ci, :, :].rearrange("p s r -> p (s r)"),
                        c['identf'][:, :], R=[carry, c['identf']], W=[ps])
            self.cp('act', co[:, :, :], ps[0:R3, :].rearrange("p (j n) -> p j n", j=4), R=[ps], W=[co])
            if G.NS == 1:
                dst = d['p_dn_conv']
                P.dma(dst.ap()[l][:, g4 * 512:(g4 + 1) * 512].rearrange("r (c p) -> r c p", p=128), co[:, :, :],
                      R=[co], W=[dst], q='pool')
            else:
                dst = d['s_dn_conv']
                P.dma(dst.ap()[l][:, :, g4 * 512:(g4 + 1) * 512].rearrange("s r (c p) -> (s r) c p", p=128), co[:, :, :],
                      R=[co], W=[dst], q='pool')

    def conv_state_in(self, G, l):
        P, c, T, d = self.P, self.c, G.T, self.d
        R3 = G.NSEQ * 3
        carry = T['carry'][l]
        ci_t = self.ring(G.name + 'cin', 1, lambda j: P.sb([R3, 3072], F32, G.name + "cin"))
        P.dma(ci_t[:, :], d['state_dn_conv'].ap()[l].rearrange("s r n -> (s r) n"), W=[ci_t])
        for g4 in range(6):
            ps = self.next_pb('lin', [1, 2, 3])
            for j in range(4):
                ci = g4 * 4 + j
                self.tr(ps[:, j * R3:(j + 1) * R3], ci_t[:, ci * 128:(ci + 1) * 128], c['identf'][0:R3, 0:R3],
                        R=[ci_t, c['identf']], W=[ps])
            self.cp('act', carry[:, g4 * 4:(g4 + 1) * 4, :, :].rearrange("p c s r -> p c (s r)"),
                    ps[:, 0:4 * R3].rearrange("p (j n) -> p j n", j=4), R=[ps], W=[carry])

    def dn_chunk(self, G, l, n, S_tiles, pre):
        P, c, T = self.P, self.c, G.T
        NS = G.NS
        cs = slice(n * 64, (n + 1) * 64)
        qT, kT, vT = T['qT'], T['kT'], T['vT']
        ucs, maskT, noff, same, seqind = (c[pre + k] for k in ('ucs', 'maskT', 'noff', 'same', 'seqind'))
        gtok = T['g']
        beta = T['beta']
        pb = self.pb
        nm = G.name

        def sbt(key, shape, dt=F32, nbuf=2):
            pfx = nm if NS > 1 and key in ('SG', 'gl') else 'ck'
            return self.ring(pfx + key, nbuf, lambda j: P.sb(shape, dt, pfx + key + str(j)))

        self.mm(pb[0][0:64, 0:8], ucs[:, :], gtok[:, n, :], R=[ucs, gtok], W=[pb[0]])
        self.mm(pb[0][0:64, 8:16], same[:, :], gtok[:, n, :], R=[same, gtok], W=[pb[0]])
        Gt = sbt('Gt', [64, 16])
        self.cp('act', Gt[:, :], pb[0][0:64, 0:16], R=[pb[0]], W=[Gt])
        eGd = sbt('eGd', [64, 16])
        self.tt('dve', eGd[:, 8:16], Gt[:, 8:16], Gt[:, 0:8], ALU.subtract, R=[Gt], W=[eGd])
        self.cp('dve', eGd[:, 0:8], Gt[:, 0:8], R=[Gt], W=[eGd])
        self.actf(eGd[:, :], eGd[:, :], AF.Exp, R=[eGd], W=[eGd])
        SG = sbt('SG', [64, H, NS])
        self.tt('dve', SG[:, :, :], gtok[:, n, :].unsqueeze(2).to_broadcast([64, H, NS]),
                seqind[:, :].unsqueeze(1).to_broadcast([64, H, NS]), ALU.mult, R=[gtok, seqind], W=[SG])
        self.mm(pb[0][:, 16:16 + H * NS], c['onesf'][:, :], SG[:, :, :].rearrange("p h s -> p (h s)"),
                R=[c['onesf'], SG], W=[pb[0]])
        gl = sbt('gl', [128, H, NS])
        self.actf(gl[:, :, :].rearrange("p h s -> p (h s)"), pb[0][:, 16:16 + H * NS], AF.Exp, R=[pb[0]], W=[gl])
        UG = sbt('UG', nbuf=1, shape=[64, H, 64])
        self.tt('dve', UG[:, :, :], ucs[:, :].unsqueeze(1).to_broadcast([64, H, 64]),
                gtok[:, n, :].unsqueeze(2).to_broadcast([64, H, 64]), ALU.mult, R=[ucs, gtok], W=[UG])
        for hh in range(H):
            self.mm(pb[1][:, hh * 64:(hh + 1) * 64], c['onesf'][:, :], UG[:, hh, :], R=[c['onesf'], UG], W=[pb[1]])
        eGrow = sbt('eGrow', nbuf=1, shape=[128, H, 64])
        self.actf(eGrow[:, :, :].rearrange("p h i -> p (h i)"), pb[1][:, :], AF.Exp, R=[pb[1]], W=[eGrow])
        qgT = sbt('qgT', [128, H, 64], BF16)
        self.tt('dve', qgT[:, :, :], qT[:, :, cs], eGrow[:, :, :], ALU.mult, R=[qT, eGrow], W=[qgT])
        tmp = sbt('dtmp', nbuf=1, shape=[64, H, 64])
        self.tt('dve', tmp[:, :, :], pb[1][0:64, :].rearrange("p (h i) -> p h i", h=H),
                Gt[:, 0:8].unsqueeze(2).to_broadcast([64, H, 64]), ALU.subtract, R=[pb[1], Gt], W=[tmp])
        self.tt('pool', tmp[:, :, :], tmp[:, :, :], maskT[:, :].unsqueeze(1).to_broadcast([64, H, 64]), ALU.add,
                R=[tmp, maskT], W=[tmp])
        DT = sbt('DT', nbuf=1, shape=[64, H, 64])
        self.actf(DT[:, :, :], tmp[:, :, :], AF.Exp, R=[tmp], W=[DT])
        for hh in range(H):
            self.mm(pb[2][0:64, hh * 64:(hh + 1) * 64], kT[:, hh, cs], kT[:, hh, cs], R=[kT], W=[pb[2]])
        for hh in range(H):
            self.mm(pb[3][0:64, hh * 64:(hh + 1) * 64], kT[:, hh, cs], qT[:, hh, cs], R=[kT, qT], W=[pb[3]])
        aqkT = sbt('aqkT', [64, H, 64], BF16)
        self.tt('dve', aqkT[:, :, :], DT[:, :, :], pb[3][0:64, :].rearrange("p (h i) -> p h i", h=H), ALU.mult,
                R=[DT, pb[3]], W=[aqkT])
        nbo = sbt('nbo', nbuf=1, shape=[64, H, 64])
        self.tt('pool', nbo[:, :, :], beta[:, n, :].unsqueeze(2).to_broadcast([64, H, 64]),
                noff[:, :].unsqueeze(1).to_broadcast([64, H, 64]), ALU.mult, R=[beta, noff], W=[nbo])
        X = sbt('X', [64, H, 64])
        self.tt('dve', X[:, :, :], pb[2][0:64, :].rearrange("p (h i) -> p h i", h=H), nbo[:, :, :], ALU.mult,
                R=[pb[2], nbo], W=[X])
        self.tt('dve', X[:, :, :], X[:, :, :], DT[:, :, :], ALU.mult, R=[X, DT], W=[X])
        Xb = sbt('Xb', [64, H, 64], BF16)
        self.cp('pool', Xb[:, :, :], X[:, :, :], R=[X], W=[Xb])
        ptz = self.pt[0]
        for hh in range(H):
            self.tr(ptz[0:64, hh * 64:(hh + 1) * 64], Xb[:, hh, :], c['identb'][0:64, 0:64], R=[Xb, c['identb']], W=[ptz])
        Z = sbt('Zb', [64, H, 64], BF16)
        self.cp('act', Z[:, :, :].rearrange("p h i -> p (h i)"), ptz[0:64, 0:512], R=[ptz], W=[Z])
        Pm = sbt('Pm', [64, H, 64])
        self.tt('pool', Pm[:, :, :], X[:, :, :], c['identf'][0:64, 0:64].unsqueeze(1).to_broadcast([64, H, 64]), ALU.add,
                R=[X, c['identf']], W=[Pm])
        Pb = sbt('Pb', [64, H, 64], BF16)
        self.cp('act', Pb[:, :, :], Pm[:, :, :], R=[Pm], W=[Pb])
        Y = Xb
        for lv in range(G.nlev):
            last = (lv == G.nlev - 1)
            if not last:
                for hh in range(H):
                    self.mm(pb[5][0:64, hh * 64:(hh + 1) * 64], Z[:, hh, :], Y[:, hh, :], R=[Z, Y], W=[pb[5]])
            for hh in range(H):
                self.mm(pb[4][0:64, hh * 64:(hh + 1) * 64], Y[:, hh, :], Z[:, hh, :], R=[Z, Y], W=[pb[4]])
            Zn = sbt('Zb', [64, H, 64], BF16)
            self.cp('act', Zn[:, :, :].rearrange("p h i -> p (h i)"), pb[4][0:64, :], R=[pb[4]], W=[Zn])
            if not last:
                Yn = sbt('Xb', [64, H, 64], BF16)
                self.cp('dve', Yn[:, :, :].rearrange("p h i -> p (h i)"), pb[5][0:64, :], R=[pb[5]], W=[Yn])
                Y = Yn
            Z = Zn
            for hh in range(H):
                self.mm(pb[2][0:64, hh * 64:(hh + 1) * 64], Z[:, hh, :], Pb[:, hh, :], R=[Z, Pb], W=[pb[2]])
            Pn = sbt('Pm', [64, H, 64])
            self.tt('dve', Pn[:, :, :].rearrange("p h i -> p (h i)"), Pm[:, :, :].rearrange("p h i -> p (h i)"),
                    pb[2][0:64, :], ALU.add, R=[Pm, pb[2]], W=[Pn])
            Pm = Pn
            Pb = sbt('Pb', [64, H, 64], BF16)
            self.cp('pool', Pb[:, :, :], Pm[:, :, :], R=[Pm], W=[Pb])
        vtok = sbt('vtok', [64, H, 128], BF16, 1)
        ktok = sbt('ktok', [64, H, 128], BF16, 1)
        for src, dst, pt in ((vT, vtok, self.pt[0]), (kT, ktok, self.pt[1])):
            for hh in range(H):
                self.tr(pt[0:64, hh * 128:(hh + 1) * 128], src[:, hh, cs], c['identb'][:, :], R=[src, c['identb']], W=[pt])
            self.cp('act', dst[:, :, :].rearrange("p h d -> p (h d)"), pt[0:64, :], R=[pt], W=[dst])
        kdec = sbt('kdec', [64, H, 128], BF16)
        self.tt('pool', kdec[:, :, :], ktok[:, :, :], eGd[:, 8:16].unsqueeze(2).to_broadcast([64, H, 128]), ALU.mult,
                R=[ktok, eGd], W=[kdec])

        OT = T['OT']
        if NS == 1 and getattr(G, 'Sall', None) is not None:
            Sf8, Sb8 = G.Sall[l]
            pk = (pb[0], pb[1])
            for hh in range(H):
                self.mm(pk[hh // 4][0:64, (hh % 4) * 128:(hh % 4 + 1) * 128], kT[:, hh, cs], Sb8[:, hh, :], R=[kT, Sb8],
                        W=[pk[hh // 4]])
            r = sbt('rB', [64, H, 128], BF16, 1)
            rf = sbt('rBf', [64, H, 128], F32, 1)
            for b2 in range(2):
                self.tt('dve', rf[:, 4 * b2:4 * b2 + 4, :], pk[b2][0:64, :].rearrange("p (h d) -> p h d", h=4),
                        eGd[:, 4 * b2:4 * b2 + 4].unsqueeze(2).to_broadcast([64, 4, 128]), ALU.mult, R=[pk[b2], eGd], W=[rf])
            self.tt('pool', r[:, :, :], vtok[:, :, :], rf[:, :, :], ALU.subtract, R=[vtok, rf], W=[r])
            pu = (pb[2], pb[3])
            for hh in range(H):
                self.mm(pu[hh // 4][0:64, (hh % 4) * 128:(hh % 4 + 1) * 128], Pb[:, hh, :], r[:, hh, :], R=[Pb, r],
                        W=[pu[hh // 4]])
            U = sbt('UB', [64, H, 128], BF16, 1)
            for b2 in range(2):
                self.tt('dve', U[:, 4 * b2:4 * b2 + 4, :], pu[b2][0:64, :].rearrange("p (h d) -> p h d", h=4),
                        beta[:, n, 4 * b2:4 * b2 + 4].unsqueeze(2).to_broadcast([64, 4, 128]), ALU.mult, R=[pu[b2], beta], W=[U])
            po = pb[4]
            for hh in range(H):
                self.mm(po[:, hh * 64:(hh + 1) * 64], Sb8[:, hh, :], qgT[:, hh, :], start=True, stop=False, R=[Sb8, qgT], W=[po])
                self.mm(po[:, hh * 64:(hh + 1) * 64], U[:, hh, :], aqkT[:, hh, :], start=False, stop=True, R=[U, aqkT], W=[po])
            self.cp('act', OT[:, :, cs], po[:, :].rearrange("p (h i) -> p h i", h=H), R=[po], W=[OT])
            psn = (pb[5], pb[0])
            for hh in range(H):
                self.mm(psn[hh // 4][:, (hh % 4) * 128:(hh % 4 + 1) * 128], kdec[:, hh, :], U[:, hh, :], R=[kdec, U],
                        W=[psn[hh // 4]])
            for b2 in range(2):
                hs4 = slice(4 * b2, 4 * b2 + 4)
                self.tt('dve' if b2 == 0 else 'pool', Sf8[:, hs4, :], Sf8[:, hs4, :], gl[:, hs4, :].to_broadcast([128, 4, 128]), ALU.mult,
                        R=[Sf8, gl], W=[Sf8])
            for b2 in range(2):
                hs4 = slice(4 * b2, 4 * b2 + 4)
                self.tt('dve', Sf8[:, hs4, :], Sf8[:, hs4, :], psn[b2][:, :].rearrange("p (h d) -> p h d", h=4), ALU.add,
                        R=[Sf8, psn[b2]], W=[Sf8])
            self.cp('act', Sb8[:, :, :], Sf8[:, :, :], R=[Sf8], W=[Sb8])
            return
        for hh in range(H):
            Sf, Sb = S_tiles(hh)
            if NS > 1:
                cm = c[pre + 'colmask']
                kTm = sbt('kTm', [128, NS, 64], BF16)
                self.tt('pool', kTm[:, :, :], kT[:, hh, cs].unsqueeze(1).to_broadcast([128, NS, 64]),
                        cm[:, :].rearrange("p (s i) -> p s i", s=NS), ALU.mult, R=[kT, cm], W=[kTm])
                qgm = sbt('qgm', [128, NS, 64], BF16)
                self.tt('pool', qgm[:, :, :], qgT[:, hh, :].unsqueeze(1).to_broadcast([128, NS, 64]),
                        cm[:, :].rearrange("p (s i) -> p s i", s=NS), ALU.mult, R=[qgT, cm], W=[qgm])
                kdm = sbt('kdm', [64, NS, 128], BF16)
                self.tt('pool', kdm[:, :, :], kdec[:, hh, :].unsqueeze(1).to_broadcast([64, NS, 128]),
                        seqind[:, :].unsqueeze(2).to_broadcast([64, NS, 128]), ALU.mult, R=[kdec, seqind], W=[kdm])
            pks = pb[0]
            for s in range(NS):
                lhs = kTm[:, s, :] if NS > 1 else kT[:, hh, cs]
                self.mm(pks[0:64, 0:128], lhs, Sb[:, s, :], start=(s == 0), stop=(s == NS - 1),
                        R=[kTm if NS > 1 else kT, Sb], W=[pks])
            r = sbt('r', [64, 128], BF16)
            rf1 = sbt('rf1', [64, 128])
            self.ts('dve', rf1[:, :], pks[0:64, 0:128], eGd[:, hh:hh + 1], None, ALU.mult, R=[pks, eGd], W=[rf1])
            self.tt('dve', r[:, :], vtok[:, hh, :], rf1[:, :], ALU.subtract, R=[vtok, rf1], W=[r])
            pu = pb[1]
            self.mm(pu[0:64, 0:128], Pb[:, hh, :], r[:, :], R=[Pb, r], W=[pu])
            U = sbt('U', [64, 128], BF16)
            self.ts('dve', U[:, :], pu[0:64, 0:128], beta[:, n, hh:hh + 1], None, ALU.mult, R=[pu, beta], W=[U])
            po = pb[3]
            for s in range(NS):
                rhs = qgm[:, s, :] if NS > 1 else qgT[:, hh, :]
                self.mm(po[:, 0:64], Sb[:, s, :], rhs, start=(s == 0), stop=False, R=[Sb, qgm if NS > 1 else qgT], W=[po])
            self.mm(po[:, 0:64], U[:, :], aqkT[:, hh, :], start=False, stop=True, R=[U, aqkT], W=[po])
            self.cp('act', OT[:, hh, cs], po[:, 0:64], R=[po], W=[OT])
            for s0 in range(0, NS, 4):
                psn = pb[5] if (s0 // 4) % 2 == 0 else pb[4]
                ns = min(4, NS - s0)
                for s in range(s0, s0 + ns):
                    lhs = kdm[:, s, :] if NS > 1 else kdec[:, hh, :]
                    self.mm(psn[:, (s - s0) * 128:(s - s0 + 1) * 128], lhs, U[:, :], R=[kdm if NS > 1 else kdec, U], W=[psn])
                self.tt('dve', Sf[:, s0:s0 + ns, :], Sf[:, s0:s0 + ns, :],
                        gl[:, hh, s0:s0 + ns].unsqueeze(2).to_broadcast([128, ns, 128]), ALU.mult, R=[Sf, gl], W=[Sf])
                self.tt('dve', Sf[:, s0:s0 + ns, :], Sf[:, s0:s0 + ns, :],
                        psn[:, 0:ns * 128].rearrange("p (s d) -> p s d", s=ns), ALU.add, R=[Sf, psn], W=[Sf])
                self.cp('act', Sb[:, s0:s0 + ns, :], Sf[:, s0:s0 + ns, :], R=[Sf], W=[Sb])

    def ffn(self, G, l):
        P, c, T, d = self.P, self.c, G.T, self.d
        GT = G.GT
        self.norm_T(G, d['norm_ffn'].ap()[l])
        xnT, actT = T['xnT'], T['actT']
        for ci in range(22):
            wg = self.load_wchunk(d['wc_ffn_in'][l].ap()[ci])
            wu = self.load_wchunk(d['wc_ffn_in'][l].ap()[22 + ci])
            pg = self.next_pb('ffg', [1, 2])
            pu = self.next_pb('ffu', [3, 4])
            for kc in range(8):
                self.mm(pg[:, 0:GT], wg[:, kc, :], xnT[:, kc, :], start=(kc == 0), stop=(kc == 7), R=[wg, xnT], W=[pg])
            for kc in range(8):
                self.mm(pu[:, 0:GT], wu[:, kc, :], xnT[:, kc, :], start=(kc == 0), stop=(kc == 7), R=[wu, xnT], W=[pu])
            sg = self.ring(G.name + 'sl', 2, lambda j: P.sb([128, GT], F32, G.name + f"sl{j}"))
            self.actf(sg[:, :], pg[:, 0:GT], AF.Silu, R=[pg], W=[sg])
            self.tt('dve', actT[:, ci, :], sg[:, :], pu[:, 0:GT], ALU.mult, R=[sg, pu], W=[actT])
        self.tok_linear_add(G, actT, 22, d['wb_ffn_out'][l])

    def shared_rows(self, G, dst_kv, row0, win_cb):
        P, c, T, d = self.P, self.c, G.T, self.d
        TP, NT = G.TP, G.NT
        self.norm_T(G, d['norm_kv'].ap())
        xnT = T['xnT']
        for third in range(3):
            wt = self.ring('kvw', 1, lambda j: P.sb([128, 8, 512], BF16, f"kvw{j}"))
            P.dma(wt[:, :, :], d['wb_kv'].ap()[:, third * 512:(third + 1) * 512].rearrange("(kc p) n -> p kc n", p=128),
                  R=[d['wb_kv']], W=[wt])
            for t in range(NT):
                ps = self.next_pb('lin', [1, 2, 3])
                for kc in range(8):
                    self.mm(ps[0:TP, :], xnT[:, kc, t * TP:(t + 1) * TP], wt[:, kc, :], start=(kc == 0), stop=(kc == 7),
                            R=[xnT, wt], W=[ps])
                rp = self.ring(G.name + 'rowp', 2, lambda j: P.sb([TP, 512], F32, G.name + f"rowp{j}"))
                self.cp('act', rp[:, :], ps[0:TP, :], R=[ps], W=[rp])
                if third < 2:
                    P.dma(dst_kv.ap()[row0 + t * TP: row0 + (t + 1) * TP, third * 512:(third + 1) * 512], rp[:, :],
                          R=[rp], W=[dst_kv], q='pool')
                else:
                    win_cb(t, rp)

    def phase(self):
        from contextlib import ExitStack
        b = self

        class _Ph:
            def __enter__(self_):
                self_.st = ExitStack()
                b.P.stack = self_.st
                b.rr = {}
                return self_

            def __exit__(self_, *a):
                b.P.barrier()
                b.P.stack = None
                b.rr = {}
                self_.st.close()
                return False
        return _Ph()

    def build(self):
        P, cfg = self.P, self.cfg
        self.declare_io()
        d = self.d
        nsa = cfg.get('nsa', True)
        if nsa:
            self.nsa_declare()
        self.consts()
        if nsa:
            self.nsa_consts()
        with self.phase():
            self.convert_weights()
        if nsa:
            with self.phase():
                self.nsa_tables()
        n_dn = cfg['n_dn']
        if cfg.get('prompt', True):
          with self.phase():
            G = Ctx('p', 128, 4, 1, 1, 5)
            self.alloc_ctx(G)
            T = G.T
            Sp = [(P.sb([128, H, 128], F32, f"Sp{l}"), P.sb([128, H, 128], BF16, f"Sbp{l}")) for l in range(2)]
            G.Sall = Sp
            for l in range(2):
                self.memset('pool', Sp[l][0][:, :, :], 0.0, W=[Sp[l][0]])
                self.memset('pool', Sp[l][1][:, :, :], 0.0, W=[Sp[l][1]])
                self.memset('pool', T['carry'][l][:, :, :, :], 0.0, W=[T['carry'][l]])
            ngrp = cfg['SEQ'] // G.GT
            for g in range(ngrp):
                P.dma(T['x'][:, :, :], d['x_prompt'].ap()[g * 512:(g + 1) * 512, :].rearrange("(t p) n -> p t n", p=128),
                      W=[T['x']])
                for l in range(n_dn):
                    self.dn_mixer(G, l, None, last_group=(g == ngrp - 1))
                    self.ffn(G, l)
                P.dma(d['x2_p'].ap()[g * 512:(g + 1) * 512, :].rearrange("(t p) n -> p t n", p=128), T['x'][:, :, :],
                      R=[T['x']], W=[d['x2_p']], q='pool')

                def win_cb(t, rp, g=g):
                    r0 = g * 512 + t * 128 - (cfg['SEQ'] - 512)
                    if nsa:
                        P.dma(d['pwin_d'].ap()[g * 512 + t * 128:g * 512 + (t + 1) * 128, :], rp[:, :], R=[rp],
                              W=[d['pwin_d']], q='pool')
                    if r0 >= 0:
                        P.dma(d['p_win_kv'].ap()[r0:r0 + 128, :], rp[:, :], R=[rp], W=[d['p_win_kv']], q='pool')
                self.shared_rows(G, d['p_kv_rows'], g * 512, win_cb)
            for l in range(2):
                P.dma(d['p_dn_S'].ap()[l].rearrange("h k v -> k h v"), Sp[l][0][:, :, :], R=[Sp[l][0]], W=[d['p_dn_S']], q='pool')
        if nsa and cfg.get('prompt', True) and cfg.get('nsa_stop', 99) > 1:
            try:
                self.nsa_prompt()
            except _Stop:
                pass
        if cfg.get('sample', True):
          with self.phase():
            G = Ctx('s', 64, 1, 16, 16, 1)
            self.alloc_ctx(G)
            T = G.T
            P.dma(T['x'][:, 0, :], d['x_sample'].ap(), W=[T['x']])
            Ss = P.sb([128, 16, 128], F32, "Ss")
            Ssb = P.sb([128, 16, 128], BF16, "Ssb")
            for l in range(n_dn):
                self.conv_state_in(G, l)
                cur = [None]

                def S_tiles(hh, l=l, cur=cur):
                    if cur[0] != hh:
                        if cur[0] is not None:
                            P.dma(d['s_dn_S'].ap()[l, :, cur[0]].rearrange("s k v -> k s v"), Ss[:, :, :], R=[Ss],
                                  W=[d['s_dn_S']], q='pool')
                        P.dma(Ss[:, :, :], d['state_dn_S'].ap()[l, :, hh].rearrange("s k v -> k s v"), W=[Ss])
                        self.cp('act', Ssb[:, :, :], Ss[:, :, :], R=[Ss], W=[Ssb])
                        cur[0] = hh
                    return Ss, Ssb
                self.dn_mixer(G, l, S_tiles, last_group=True)
                P.dma(d['s_dn_S'].ap()[l, :, cur[0]].rearrange("s k v -> k s v"), Ss[:, :, :], R=[Ss], W=[d['s_dn_S']],
                      q='pool')
                self.ffn(G, l)
            P.dma(d['x2_s'].ap(), T['x'][:, 0, :], R=[T['x']], W=[d['x2_s']], q='pool')
            for s4 in range(4):
                P.dma(d['s_win_kv'].ap()[s4 * 4:(s4 + 1) * 4, 0:508, :], d['state_win_kv'].ap()[s4 * 4:(s4 + 1) * 4, 4:512, :],
                      W=[d['s_win_kv']], q='sp')

            def win_cb_s(t, rp):
                for sq in range(16):
                    P.dma(d['s_win_kv'].ap()[sq, 508:512, :], rp[4 * sq:4 * sq + 4, :], R=[rp],
                          W=[d['s_win_kv']], q='pool')
            self.shared_rows(G, d['s_kv_rows'], 0, win_cb_s)
        if nsa and cfg.get('sample', True):
            self.nsa_sample()
        P.emit()
        return self.nc


_CONST_CACHE = {}


def const_inputs():
    if not _CONST_CACHE:
        for pre, NS, TS in (('cp_', 1, 64), ('cs_', 16, 4)):
            for k, v in host_consts(NS, TS).items():
                _CONST_CACHE[pre + k] = v
    return _CONST_CACHE


def make_in_maps(inp, cfg, n_cores=8):
    SEQ = cfg['SEQ']
    cst = const_inputs()
    maps = []
    shared = {k: np.ascontiguousarray(inp[k]) for k in
              ('norm_mix', 'norm_ffn', 'norm_kv', 'norm_final', 'ffn_w_in', 'ffn_w_out', 'dn_w_in', 'dn_conv_w',
               'dn_A_log', 'dn_dt_bias', 'dn_out_norm', 'dn_w_out', 'nsa_w_kv')}
    nsa = cfg.get('nsa', True)
    if nsa:
        shared['nsa_w_in'] = np.ascontiguousarray(inp['nsa_w_in'])
        shared['nsa_w_out'] = np.ascontiguousarray(inp['nsa_w_out'])
        shared['nsa_cmp_pos_w'] = np.ascontiguousarray(inp['nsa_cmp_pos_w']).reshape(2, 32, 256)
        shared['nsa_w_cmp'] = np.ascontiguousarray(inp['nsa_w_cmp'])
        shared['rel_bias'] = np.ascontiguousarray(inp['rel_bias'])
        shared['cache_kv'] = np.ascontiguousarray(inp['cache_kv']).reshape(2560 * 128, 1024)
        nsc = [nsa_host_consts(0), nsa_host_consts(1)]
    for c in range(n_cores):
        b = c // 2
        m = dict(shared)
        m.update(cst)
        m['x_prompt'] = np.ascontiguousarray(inp['x_prompt'][b, :SEQ])
        sl = slice(16 * c, 16 * c + 16)
        m['x_sample'] = np.ascontiguousarray(inp['x_sample'][sl]).reshape(64, D)
        m['state_dn_S'] = np.ascontiguousarray(inp['state_dn_S'][:, sl])
        m['state_dn_conv'] = np.ascontiguousarray(inp['state_dn_conv'][:, sl])
        m['state_win_kv'] = np.ascontiguousarray(inp['state_win_kv'][sl]).reshape(16, 512, 512)
        if nsa:
            m.update(nsc[c % 2])
            m['page_table'] = np.ascontiguousarray(inp['page_table'][sl]).reshape(1, 256).astype(np.int32)
        maps.append(m)
    return maps


def kernel(**inp):
    cfg = dict(SEQ=4096, n_dn=2)
    b = Builder(cfg)
    nc = b.build()
    maps = make_in_maps(inp, cfg)
    res = run_bass_kernel_spmd(nc, maps, core_ids=list(range(8)))
    R = res.results
    f32 = np.float32
    y_prompt = np.zeros((4, 4096, D), f32)
    for c in range(8):
        yp = R[c]['y_p'].reshape(16, 128, D)
        y_prompt[c // 2].reshape(32, 128, D)[c % 2::2] = yp
    y_sample = np.concatenate([R[c]['y_s'].reshape(16, 4, D) for c in range(8)], axis=0).astype(f32)
    p_dn_S = np.stack([R[2 * b]['p_dn_S'] for b in range(4)], axis=1)
    p_dn_conv = np.stack([R[2 * b]['p_dn_conv'] for b in range(4)], axis=1)
    p_kv_rows = np.stack([R[2 * b]['p_kv_rows'] for b in range(4)], axis=0).reshape(4, 4096, 4, 4, 64)
    p_win_kv = np.stack([R[2 * b]['p_win_kv'] for b in range(4)], axis=0).reshape(4, 512, 2, 4, 64)
    s_dn_S = np.concatenate([R[c]['s_dn_S'] for c in range(8)], axis=1)
    s_dn_conv = np.concatenate([R[c]['s_dn_conv'] for c in range(8)], axis=1)
    s_kv_rows = np.concatenate([R[c]['s_kv_rows'].reshape(16, 4, 4, 4, 64) for c in range(8)], axis=0)
    s_win_kv = np.concatenate([R[c]['s_win_kv'].reshape(16, 512, 2, 4, 64) for c in range(8)], axis=0)
    return (y_prompt, y_sample, p_dn_S.astype(f32), p_dn_conv.astype(f32), p_kv_rows.astype(f32),
            p_win_kv.astype(f32), s_dn_S.astype(f32), s_dn_conv.astype(f32), s_kv_rows.astype(f32),
            s_win_kv.astype(f32))


def _bucket_np(d):
    n = np.maximum(d, 0)
    nf = np.maximum(n, 1).astype(np.float32)
    large = 16 + (np.log(nf / np.float32(16)) / np.float32(math.log(64.0)) * np.float32(16)).astype(np.int32)
    large = np.minimum(large, 31)
    return np.where(n < 16, n, large)


def _onehot(d, valid):
    b = np.where(valid, _bucket_np(d), 32)
    oh = np.zeros((33, d.shape[0]), np.float32)
    oh[b, np.arange(d.shape[0])] = 1.0
    return oh


def nsa_host_consts(par):
    c = {}
    t = np.arange(1280)
    d = t - 255 + 128 * par
    c['oh_sel'] = _onehot(d, d >= 0)
    t = np.arange(1024)
    d = t - 255 + 128 * par
    c['oh_win'] = _onehot(d, (d >= 0) & (d < 512))
    r = np.arange(16)[:, None]
    w = np.arange(512)[None, :]
    d = (16 * (247 - w + 8 * par) + r - 31).reshape(-1)
    c['oh_cmp'] = _onehot(d, d >= 0)
    tt = np.arange(4)[:, None]
    cc = np.arange(128)[None, :]
    d = (2017 + tt - 16 * cc).reshape(-1)
    c['oh_cs'] = _onehot(d, d >= 0)
    x = np.arange(2304)
    d = x - 127
    c['oh_ss'] = _onehot(d, d >= 0)
    x = np.arange(768)
    d = x - 127
    c['oh_ws'] = _onehot(d, (d >= 0) & (d < 512))
    blk = np.arange(64)[None, None, :]
    qpos = (128 * (2 * np.arange(16)[:, None, None] + par) + np.arange(128)[None, :, None])
    cur = qpos // 64
    forced = (blk == 0) | (blk == cur) | (blk == cur - 1)
    valid = blk * 64 <= qpos
    c['selmul_p'] = np.ascontiguousarray(np.where(valid & ~forced, 1.0, 0.0).astype(np.float32).transpose(1, 0, 2))
    c['seladd_p'] = np.ascontiguousarray(np.where(forced, 1e4, np.where(valid, 0.0, -1.0)).astype(np.float32).transpose(1, 0, 2))
    blk = np.arange(64)[None, :]
    qpos = 2048 + np.arange(4)[:, None]
    cur = qpos // 64
    exists = blk < 33
    forced = ((blk == 0) | (blk == cur) | (blk == cur - 1)) & exists
    valid = (blk * 64 <= qpos) & exists
    c['selmul_s'] = np.where(valid & ~forced, 1.0, 0.0).astype(np.float32)
    c['seladd_s'] = np.where(forced, 1e4, np.where(valid, 0.0, np.where(exists, -1.0, -2.0))).astype(np.float32)
    k = np.arange(4096)[None, :]
    c['expE'] = (k // 64 == np.arange(64)[:, None]).astype(np.float32)
    sel8 = np.zeros((128, 8), np.float32)
    sel8[np.arange(128), np.arange(128) // 16] = 1.0
    c['sel8'] = sel8
    c['antiI'] = np.ascontiguousarray(np.eye(128, dtype=np.float32)[::-1])
    c['parf'] = np.tile(np.array([[float(par), 1.0 - float(par)]], np.float32), (128, 1))
    c['iota'] = np.arange(128, dtype=np.float32).reshape(128, 1)
    return c


def _nsa_declare(self):
    P, cfg, d = self.P, self.cfg, self.d
    SEQ = cfg['SEQ']

    def inp(name, shape, dt=F32):
        d[name] = P.dram(name, shape, dt, kind="ExternalInput")

    inp('nsa_w_in', [2, D, 1072])
    inp('nsa_w_out', [2, D, D])
    inp('nsa_cmp_pos_w', [2, 32, 256])
    inp('nsa_w_cmp', [2, 4, 64, 64])
    inp('rel_bias', [32, 16])
    inp('cache_kv', [2560 * 128, 1024])
    inp('page_table', [1, 256], I32)
    for nm, shp in (('oh_sel', [33, 1280]), ('oh_win', [33, 1024]), ('oh_cmp', [33, 8192]), ('oh_cs', [33, 512]),
                    ('oh_ss', [33, 2304]), ('oh_ws', [33, 768]), ('selmul_p', [128, 16, 64]), ('seladd_p', [128, 16, 64]),
                    ('selmul_s', [4, 64]), ('seladd_s', [4, 64]), ('expE', [64, 4096]), ('sel8', [128, 8]),
                    ('antiI', [128, 128]), ('parf', [128, 2]), ('iota', [128, 1])):
        inp(nm, shp)
    d['y_p'] = P.dram('y_p', [SEQ // 2, D], F32, kind="ExternalOutput")
    d['y_s'] = P.dram('y_s', [64, D], F32, kind="ExternalOutput")
    d['wc_nsa_in'] = [P.dram(f'wc_nsa_in{j}', [8, 128, 8, 128], BF16) for j in range(2)]
    d['wb_nsa_out'] = [P.dram(f'wb_nsa_out{j}', [D, D], BF16) for j in range(2)]
    for nm, ln in (('t_sel', 1280), ('t_win', 1024), ('t_cmp', 8192), ('t_cs', 512), ('t_ss', 2304), ('t_ws', 768)):
        d[nm] = P.dram(nm, [16, ln], F32)
    d['BTd'] = P.dram('BTd', [4, 15, 128, 512], F32)
    NT = SEQ // 128
    d['pwin_d'] = P.dram('pwin_d', [SEQ, 512], F32)
    d['kselT_p'] = P.dram('kselT_p', [4, 128, SEQ], BF16)
    d['vsel_p'] = P.dram('vsel_p', [NT, 128, 4, 66], BF16)
    d['kwinT_p'] = P.dram('kwinT_p', [4, 128, SEQ], BF16)
    d['vwin_p'] = P.dram('vwin_p', [NT, 128, 4, 66], BF16)
    d['kselT_s'] = P.dram('kselT_s', [16, 4, 128, 17 * 128], BF16)
    d['vsel_s'] = P.dram('vsel_s', [16, 17, 128, 4, 66], BF16)
    d['kwinT_s'] = P.dram('kwinT_s', [16, 4, 128, 5 * 128], BF16)
    d['vwin_s'] = P.dram('vwin_s', [16, 5, 128, 4, 66], BF16)


def _nsa_consts(self):
    P, d, c = self.P, self.d, self.c
    antiI = P.sb([128, 128], F32, "antiI")
    P.dma(antiI[:, :], d['antiI'].ap(), W=[antiI])
    sel8 = P.sb([128, 8], BF16, "sel8")
    P.dma(sel8[:, :], d['sel8'].ap(), W=[sel8], q='pool')
    parf = P.sb([128, 2], F32, "parf")
    P.dma(parf[:, :], d['parf'].ap(), W=[parf])
    tabx = P.sb([33, 16], F32, "tabx")
    r31 = P.sb([32, 16], F32, "r31")
    P.dma(tabx[0:32, :], d['rel_bias'].ap(), W=[tabx])
    P.dma(r31[:, :], d['rel_bias'].ap()[31].partition_broadcast(32), W=[r31])
    self.tt('dve', tabx[0:32, :], tabx[0:32, :], r31[:, :], ALU.subtract, R=[tabx, r31], W=[tabx])
    self.memset('pool', tabx[32:33, :], NEG, W=[tabx])
    c.update(antiI=antiI, sel8=sel8, parf=parf, tabx=tabx)
    wg = P.sb([128, 2, 8, 48], BF16, "wg")
    for j in range(2):
        P.dma(wg[:, j, :, :], d['nsa_w_in'].ap()[j, :, 1024:1072].rearrange("(kc p) n -> p kc n", p=128), W=[wg], q='pool')
    c['wg'] = wg


def _nsa_late_consts(self):
    P, d, c = self.P, self.d, self.c
    if 'wlo' in c:
        return
    wlo = P.sb([128, 512], F32, "wlo")
    whi = P.sb([128, 512], F32, "whi")
    for a in range(8):
        P.dma(wlo[16 * a:16 * a + 16, :].rearrange("r (c n) -> r c n", c=2),
              d['nsa_cmp_pos_w'].ap()[:, 0:16, :].rearrange("c r n -> r c n"), W=[wlo])
        P.dma(whi[16 * a:16 * a + 16, :].rearrange("r (c n) -> r c n", c=2),
              d['nsa_cmp_pos_w'].ap()[:, 16:32, :].rearrange("c r n -> r c n"), W=[whi])
    wck = P.sb([128, 4, 128], BF16, "wck")
    wcv = P.sb([128, 4, 64], BF16, "wcv")
    for half in range(2):
        for dup in range(2):
            P.dma(wck[half * 64:(half + 1) * 64, :, dup * 64:(dup + 1) * 64],
                  d['nsa_w_cmp'].ap()[0].rearrange("k d e -> d k e"), W=[wck], q='pool')
        P.dma(wcv[half * 64:(half + 1) * 64, :, :], d['nsa_w_cmp'].ap()[1].rearrange("k d e -> d k e"), W=[wcv], q='pool')
    expE = P.sb([64, 4096], BF16, "expE")
    P.dma(expE[:, :], d['expE'].ap(), W=[expE], q='pool')
    c.update(wlo=wlo, whi=whi, wck=wck, wcv=wcv, expE=expE)


Builder.nsa_late_consts = _nsa_late_consts


def _nsa_tables(self):
    P, d, c = self.P, self.d, self.c
    for oh, dst, ln in (('oh_sel', 't_sel', 1280), ('oh_win', 't_win', 1024), ('oh_cmp', 't_cmp', 8192),
                        ('oh_cs', 't_cs', 512), ('oh_ss', 't_ss', 2304), ('oh_ws', 't_ws', 768)):
        for off in range(0, ln, 512):
            n = min(512, ln - off)
            ot = self.ring('oht', 2, lambda j: P.sb([33, 512], F32, f"oht{j}"))
            P.dma(ot[:, 0:n], d[oh].ap()[:, off:off + n], W=[ot])
            ps = self.next_pb('lin', [1, 2, 3])
            self.mm(ps[0:16, 0:n], c['tabx'][:, :], ot[:, 0:n], R=[c['tabx'], ot], W=[ps])
            tb = self.ring('tbo', 2, lambda j: P.sb([16, 512], F32, f"tbo{j}"))
            self.cp('act', tb[:, 0:n], ps[0:16, 0:n], R=[ps], W=[tb])
            P.dma(d[dst].ap()[:, off:off + n], tb[:, 0:n], R=[tb], W=[d[dst]], q='pool')
    if self.cfg.get('prompt', True):
        for kvh in range(4):
            for idx in range(15):
                tab, ln, e = ('t_sel', 1280, idx - 1) if idx < 9 else ('t_win', 1024, idx - 10)
                tr_ = self.ring('trv', 2, lambda j: P.sb([128, 4, 128], F32, f"trv{j}"))
                src = bass.AP(d[tab].h, 4 * kvh * ln + 128 * (e + 1), [[1, 128], [ln, 4], [1, 128]])
                P.dma(tr_[:, :, :], src, R=[d[tab]], W=[tr_])
                ps = self.next_pb('lin', [1, 2, 3])
                self.mm(ps[:, :], c['antiI'][:, :], tr_[:, :, :].rearrange("p g q -> p (g q)"), R=[c['antiI'], tr_], W=[ps])
                fl = self.ring('flp', 2, lambda j: P.sb([128, 512], F32, f"flp{j}"))
                self.cp('act', fl[:, :], ps[:, :], R=[ps], W=[fl])
                P.dma(d['BTd'].ap()[kvh, idx], fl[:, :], R=[fl], W=[d['BTd']], q='pool')


def _ctx_rows(self, rows, P_idx, lohi, kT_dst, v_dst, kw_dst, vw_dst, has_cmpsel=True, has_win=True, win_rows=None):
    P, c = self.P, self.c
    if has_cmpsel:
        if lohi is not None:
            alo = self.ring('alo', 2, lambda j: P.sb([128, 512], BF16, f"alo{j}"))
            ahi = self.ring('ahi', 2, lambda j: P.sb([128, 512], BF16, f"ahi{j}"))
            self.tt('pool', alo[:, :], rows[:, 0:512], c['wlo'][:, :], ALU.mult, R=[rows, c['wlo']], W=[alo])
            self.tt('dve', ahi[:, :], rows[:, 0:512], c['whi'][:, :], ALU.mult, R=[rows, c['whi']], W=[ahi])
            ps = self.next_pb('ctxp', [0])
            for lh, a in enumerate((alo, ahi)):
                for ch in range(4):
                    self.mm(ps[:, (lh * 4 + ch) * 8:(lh * 4 + ch + 1) * 8], a[:, ch * 128:(ch + 1) * 128], c['sel8'][:, :],
                            R=[a, c['sel8']], W=[ps])
            self.cp('act', lohi[:, :, :, 8 * P_idx:8 * P_idx + 8], ps[:, 0:64].rearrange("p (l c m) -> p l c m", l=2, c=4),
                    R=[ps], W=[lohi])
        for (col0, kdst, vdst) in ((512, kT_dst, v_dst),):
            _kv_tile(self, rows, col0, kdst, vdst)
    if has_win:
        wt, wc0 = win_rows
        _kv_tile(self, wt, wc0, kw_dst, vw_dst)


def _kv_tile(self, rows, col0, kdst, vdst):
    P, c = self.P, self.c
    kd = self.ring('kd', 2, lambda j: P.sb([128, 4, 2, 64], BF16, f"kd{j}"))
    self.cp('dve', kd[:, :, :, :], rows[:, col0:col0 + 256].rearrange("p (k d) -> p k d", k=4).unsqueeze(2).to_broadcast([128, 4, 2, 64]),
            R=[rows], W=[kd])
    pt = self.pt[1]
    for k in range(4):
        self.tr(pt[:, k * 128:(k + 1) * 128], kd[:, k, :, :].rearrange("p a d -> p (a d)"), c['identb'][:, :],
                R=[kd, c['identb']], W=[pt])
    ks = self.ring('ks', 2, lambda j: P.sb([128, 4, 128], BF16, f"ks{j}"))
    self.cp('act', ks[:, :, :].rearrange("p k n -> p (k n)"), pt[:, 0:512], R=[pt], W=[ks])
    kap, ktile = kdst
    P.dma(kap, ks[:, :, :], R=[ks], W=[ktile], q='sp')
    va = self.ring('va', 2, lambda j: P.sb([128, 4, 66], BF16, f"va{j}"))
    self.memset('dve', va[:, :, 64:65], 1.0, W=[va])
    self.memset('dve', va[:, :, 65:66], 0.0, W=[va])
    self.cp('dve', va[:, :, 0:64], rows[:, col0 + 256:col0 + 512].rearrange("p (k d) -> p k d", k=4), R=[rows], W=[va])
    vap, vtile = vdst
    P.dma(vap, va[:, :, :], R=[va], W=[vtile], q='sp')


def _cmp_finish(self, lohi, NCB, kcd_ap, vc_ap_fn, kcd_t, vc_t, ncw=256, njh=2):
    P, c = self.P, self.c
    bl = self.ring('blk', 1, lambda j: P.sb([128, 4, 256], BF16, "blk"))
    self.memset('pool', bl[:, :, :], 0.0, W=[bl])
    self.tt('dve', bl[:, :, 0:NCB], lohi[:, 0, :, 0:NCB], lohi[:, 1, :, 1:NCB + 1], ALU.add, R=[lohi], W=[bl])
    for kvh in range(4):
        hs = slice((kvh % 2) * 64, (kvh % 2) * 64 + 64)
        ps = self.next_pb('lin', [1, 2, 3])
        self.mm(ps[:, 0:256], c['wck'][hs, kvh, :], bl[hs, kvh // 2, :], R=[c['wck'], bl], W=[ps])
        self.cp('act', kcd_ap(kvh), ps[:, 0:ncw], R=[ps], W=[kcd_t])
        for jh in range(njh):
            ps2 = self.next_pb('lin', [1, 2, 3])
            self.mm(ps2[:, 0:64], bl[hs, 2 + kvh // 2, jh * 128:(jh + 1) * 128], c['wcv'][hs, kvh, :], R=[bl, c['wcv']], W=[ps2])
            self.cp('act', vc_ap_fn(jh, kvh), ps2[:, 0:64], R=[ps2], W=[vc_t])


Builder.nsa_declare = _nsa_declare
Builder.nsa_consts = _nsa_consts
Builder.nsa_tables = _nsa_tables
Builder.ctx_rows = _ctx_rows
Builder.cmp_finish = _cmp_finish


def _attend_gen(self, A):
    P, c = self.P, self.c
    NQ, NC, kvh = A['NQ'], A['NC'], A['kvh']
    pb = self.pb
    qT, qTt = A['qT']
    qbd, qcols = A['qbd'], A['qcols']
    gate = A['gate']
    N4 = 4 * NQ

    def sbt(key, shape, dt=F32, nbuf=2):
        k = f"at{NQ}{key}"
        return self.ring(k, nbuf, lambda j: P.sb(shape, dt, k + str(j)))

    if A.get('stop', 99) == 0:
        raise _Stop()
    kc_ap, kc_t = A['kcmp']
    for g in range(4):
        ps = pb[g % 2]
        self.mm(ps[0:NQ, (g // 2) * 256:(g // 2) * 256 + NC], qT(g), kc_ap(g), R=[qTt, kc_t], W=[ps])
    yield 0
    if A.get('stop', 99) == 10:
        raise _Stop()
    bc_ap, bc_t = A['bias_c']
    sc = sbt('sc', [NQ, 4, 256], nbuf=1)
    for g in range(4):
        ps = pb[g % 2]
        self.tt('dve', sc[:, g, 0:NC], ps[0:NQ, (g // 2) * 256:(g // 2) * 256 + NC], bc_ap[:, g, 0:NC], ALU.add,
                R=[ps, bc_t], W=[sc])
    yield 0
    if A.get('stop', 99) == 11:
        raise _Stop()
    ssum = sbt('ssum', [NQ, 8])
    self.memset('pool', ssum[:, :], 0.0, W=[ssum])
    for g in range(4):
        self.actf(sc[:, g, 0:NC], sc[:, g, 0:NC], AF.Exp, R=[sc], W=[sc, ssum], accum_out=ssum[:, g:g + 1])
    yield 0
    if A.get('stop', 99) == 12:
        raise _Stop()
    self.ts('dve', ssum[:, 4:8], ssum[:, 0:4], 1e-30, None, ALU.max, R=[ssum], W=[ssum])
    P.add('dve', lambda e: e.reciprocal(out=ssum[:, 4:8], in_=ssum[:, 4:8]), R=[ssum], W=[ssum])
    if A.get('stop', 99) == 13:
        raise _Stop()
    pc = sbt('pc', [NQ, 4, 256], nbuf=1)
    if not A.get('pc_init'):
        pass
    self.memset('pool', pc[:, :, :], 0.0, W=[pc])
    self.tt('dve', pc[:, :, 0:NC], sc[:, :, 0:NC], ssum[:, 4:8].unsqueeze(2).to_broadcast([NQ, 4, NC]), ALU.mult,
            R=[sc, ssum], W=[pc])
    yield 0
    if A.get('stop', 99) == 1:
        raise _Stop()
    imp = sbt('imp', [NQ, 264], nbuf=1)
    self.memset('pool', imp[:, :], 0.0, W=[imp])
    P.add('dve', lambda e: e.tensor_reduce(out=imp[:, 1:257], in_=pc[:, :, :].rearrange("p g n -> p n g"),
                                           axis=mybir.AxisListType.X, op=ALU.add), R=[pc], W=[imp])
    yield 0
    cov = sbt('cov', [NQ, 256], nbuf=1)
    self.tt('dve', cov[:, :], imp[:, 1:257], imp[:, 0:256], ALU.add, R=[imp], W=[cov])
    psl = sbt('psl', [NQ, 64], nbuf=1)
    P.add('dve', lambda e: e.tensor_reduce(out=psl[:, :], in_=cov[:, :].rearrange("p (b r) -> p b r", r=4),
                                           axis=mybir.AxisListType.X, op=ALU.add), R=[cov], W=[psl])
    yield 0
    smul, sadd, s_t = A['selc']
    self.tt('dve', psl[:, :], psl[:, :], smul, ALU.mult, R=[psl, s_t], W=[psl])
    self.tt('dve', psl[:, :], psl[:, :], sadd, ALU.add, R=[psl, s_t], W=[psl])
    yield 0
    m16 = sbt('m16', [NQ, 16], nbuf=1)
    ps2 = sbt('psl2', [NQ, 64], nbuf=1)
    P.add('dve', lambda e: e.max(out=m16[:, 0:8], in_=psl[:, :]), R=[psl], W=[m16])
    P.add('dve', lambda e: e.match_replace(out=ps2[:, :], in_to_replace=m16[:, 0:8], in_values=psl[:, :], imm_value=-5.0),
          R=[psl, m16], W=[ps2])
    P.add('dve', lambda e: e.max(out=m16[:, 8:16], in_=ps2[:, :]), R=[ps2], W=[m16])
    yield 0
    nsel = sbt('nsel', [NQ, 64], nbuf=1)
    self.ts('dve', nsel[:, :], psl[:, :], m16[:, 15:16], None, ALU.is_ge, R=[psl, m16], W=[nsel])
    self.ts('dve', nsel[:, :], nsel[:, :], -NEG, NEG, ALU.mult, ALU.add, R=[nsel], W=[nsel])
    self.tr(pb[0][0:64, 0:NQ], nsel[:, :], c['identf'][0:NQ, 0:NQ], R=[nsel, c['identf']], W=[pb[0]])
    nsT = sbt('nsT', [64, 4, NQ], BF16)
    self.cp('act', nsT[:, :, :], pb[0][0:64, 0:NQ].unsqueeze(1).to_broadcast([64, 4, NQ]), R=[pb[0]], W=[nsT])
    yield 0
    if A.get('stop', 99) == 2:
        raise _Stop()
    pcb = sbt('pcb', [NQ, 4, 256], BF16, 1)
    self.cp('pool', pcb[:, :, :], pc[:, :, :], R=[pc], W=[pcb])
    pt = self.pt[0]
    NH = A['NH']
    for g in range(4):
        for hf in range(NH):
            self.tr(pt[:, (g * NH + hf) * NQ:(g * NH + hf + 1) * NQ], pcb[:, g, hf * 128:(hf + 1) * 128],
                    c['identb'][0:NQ, 0:NQ], R=[pcb, c['identb']], W=[pt])
    yield 0
    pcT = sbt('pcT', [128, 4 * NH, NQ], BF16, 1)
    self.cp('act', pcT[:, :, :].rearrange("p a q -> p (a q)"), pt[:, 0:4 * NH * NQ], R=[pt], W=[pcT])
    vc_ap, vc_t = A['vcmp']
    for g in range(4):
        for hf in range(NH):
            self.mm(pb[1][0:NQ, g * 64:(g + 1) * 64], pcT[:, g * NH + hf, :], vc_ap(hf), start=(hf == 0), stop=(hf == NH - 1),
                    R=[pcT, vc_t], W=[pb[1]])
    yield 0
    oacc = sbt('oacc', [NQ, 4, 64])
    g_ap, g_t = gate
    self.tt('dve', oacc[:, :, :], pb[1][0:NQ, 0:256].rearrange("p (g d) -> p g d", g=4),
            g_ap(0).unsqueeze(2).to_broadcast([NQ, 4, 64]), ALU.mult, R=[pb[1], g_t], W=[oacc])

    yield 'CMP_DONE'
    if A.get('stop', 99) == 3:
        raise _Stop()
    for br, (tiles, ksrc, vsrc, po) in enumerate((A['sel'], A['win'])):
        if A.get('stop', 99) == 4 and br == 1:
            raise _Stop()
        nt = len(tiles)
        pend = None
        for c0 in range(0, nt, 4):
            n = min(4, nt - c0)
            kt0 = tiles[c0][0]
            kc = sbt('kc', [128, 512], BF16, 3)
            kap, ktile = ksrc(kt0, n)
            P.dma(kc[:, 0:n * 128], kap, R=[ktile], W=[kc])
            vcx = sbt('vcx', [128, 4, 66], BF16, 3)
            vap, vtile = vsrc(kt0, n)
            P.dma(vcx[:, 0:n, :], vap, R=[vtile], W=[vcx])
            for t in range(n):
                kt, bias = tiles[c0 + t]
                ps = self.next_pb('sc', [2, 3])
                for pr in range(2):
                    self.mm(ps[:, pr * 2 * NQ:(pr + 1) * 2 * NQ], kc[:, t * 128:(t + 1) * 128], qbd[:, pr, :, qcols],
                            start=(pr == 0), stop=(br == 1 and pr == 1), R=[kc, A['qbd_t']], W=[ps])
                if br == 0:
                    self.mm(ps[:, 0:N4], c['expE'][:, kt * 128:(kt + 1) * 128], nsT[:, :, :].rearrange("p g q -> p (g q)"),
                            start=False, stop=True, R=[c['expE'], nsT], W=[ps])
                PT = sbt('PT', [128, N4], BF16, 3)
                if bias is not None:
                    b_ap, b_t = bias
                    sb_ = sbt('sbias', [128, N4], F32, 2)
                    self.tt('dve', sb_[:, :], ps[:, 0:N4], b_ap, ALU.add, R=[ps, b_t], W=[sb_])
                    self.actf(PT[:, :], sb_[:, :], AF.Exp, R=[sb_], W=[PT])
                else:
                    self.actf(PT[:, :], ps[:, 0:N4], AF.Exp, R=[ps], W=[PT])
                if pend is not None:
                    pend()
                def pv(PT=PT, vcx=vcx, t=t, first=(c0 + t == 0), last=(c0 + t == nt - 1), po=po):
                    for g in range(4):
                        self.mm(po[0:NQ, g * 66:(g + 1) * 66], PT[:, g * NQ:(g + 1) * NQ], vcx[:, t, :],
                                start=(first and g == 0), stop=(last and g == 3), R=[PT, vcx], W=[po])
                pend = pv
                yield 1
        if pend is not None:
            pend()
            pend = None
        pov = po[0:NQ, 0:264].rearrange("p (g e) -> p g e", g=4)
        rs = sbt('rs', [NQ, 4])
        self.ts('dve', rs[:, :], pov[:, :, 64], 1e-30, None, ALU.max, R=[po], W=[rs])
        P.add('dve', lambda e, rs=rs: e.reciprocal(out=rs[:, :], in_=rs[:, :]), R=[rs], W=[rs])
        self.tt('dve', rs[:, :], rs[:, :], g_ap(br + 1), ALU.mult, R=[rs, g_t], W=[rs])
        tmpo = sbt('tmpo', [NQ, 4, 64])
        self.tt('dve', tmpo[:, :, :], pov[:, :, 0:64], rs[:, :].unsqueeze(2).to_broadcast([NQ, 4, 64]), ALU.mult,
                R=[po, rs], W=[tmpo])
        self.tt('pool', oacc[:, :, :], oacc[:, :, :], tmpo[:, :, :], ALU.add, R=[oacc, tmpo], W=[oacc])
    o_ap, o_t = A['out']
    self.cp('act', o_ap, oacc[:, :, :], R=[oacc], W=[o_t])


def _attend(self, A):
    for _ in self.attend_gen(A):
        pass


def _attend_pipe(self, thunks):
    prev = None
    for th in thunks:
        g = self.attend_gen(th())
        cmp_done = False
        while not cmp_done or prev is not None:
            if not cmp_done:
                if next(g) == 'CMP_DONE':
                    cmp_done = True
            if prev is not None:
                try:
                    next(prev)
                except StopIteration:
                    prev = None
        prev = g
    if prev is not None:
        for _ in prev:
            pass


Builder.attend_gen = _attend_gen
Builder.attend_pipe = _attend_pipe
Builder.attend = _attend


def _nsa_qproj(self, G, j, qT_all, NT_):
    P, d, T = self.P, self.d, G.T
    GT = G.GT
    for ci in range(8):
        wt = self.load_wchunk(d['wc_nsa_in'][j].ap()[ci])
        ps = self.next_pb('lin', [1, 2, 3])
        for kc in range(8):
            self.mm(ps[:, 0:GT], wt[:, kc, :], T['xnT'][:, kc, :], start=(kc == 0), stop=(kc == 7), R=[wt, T['xnT']], W=[ps])
        self.actf(qT_all[:, ci, :], ps[:, 0:GT], AF.Copy, R=[ps], W=[qT_all], scale=0.125)


def _final_norm(self, G, dst, row0, ytile=None):
    P, d, T = self.P, self.d, G.T
    TP = G.TP
    wrow = self.ring('wrow', 2, lambda j: P.sb([128, D], F32, f"wrow{j}"))
    P.dma(wrow[:, :], d['norm_final'].ap().partition_broadcast(128), W=[wrow])
    x = T['x']
    for t in range(G.NT):
        junk = self.ring('junk', 1, lambda j: P.sb([128, D], BF16, f"junk{j}"))
        st = self.ring('nst', 4, lambda j: P.sb([128, 2], F32, f"nst{j}"))
        self.memset('pool', st[:, :], 0.0, W=[st])
        self.actf(junk[0:TP, :], x[:, t, :], AF.Square, R=[x], W=[junk, st], accum_out=st[0:TP, 0:1])
        self.ts('dve', st[0:TP, 1:2], st[0:TP, 0:1], 1.0 / D, 1e-6, ALU.mult, ALU.add, R=[st], W=[st])
        self.actf(st[0:TP, 1:2], st[0:TP, 1:2], AF.Sqrt, R=[st], W=[st])
        P.add('dve', lambda e, st=st: e.reciprocal(out=st[0:TP, 1:2], in_=st[0:TP, 1:2]), R=[st], W=[st])
        if ytile is None:
            yt = self.ring(G.name + 'yt', 2, lambda j: P.sb([TP, D], F32, G.name + f"yt{j}"))
            ya = yt[:, :]
        else:
            yt = ytile
            ya = ytile[:, t % 2, :]
        self.stt('dve', ya, x[:, t, :], st[0:TP, 1:2], wrow[0:TP, :], ALU.mult, ALU.mult, R=[x, st, wrow], W=[yt])
        P.dma(dst.ap()[row0 + t * TP:row0 + (t + 1) * TP, :], ya, R=[yt], W=[dst], q='pool')


def _nsa_prompt(self):
    P, d, c, cfg = self.P, self.d, self.c, self.cfg
    SEQ = cfg['SEQ']
    NTT = SEQ // 128
    NQT = NTT // 2
    NCB = NTT * 8 - 1
    self.nsa_late_consts()
    kcd = P.sb([128, 4, 256], BF16, "kcd")
    vcm = P.sb([128, 2, 4, 64], BF16, "vcm")
    self.memset('pool', vcm[:, :, :, :], 0.0, W=[vcm])
    with self.phase():
        lohi = P.sb([128, 2, 4, 264], F32, "lohi")
        self.memset('pool', lohi[:, :, :, :], 0.0, W=[lohi])
        for Pi in range(NTT):
            rows = self.ring('crow', 2, lambda j: P.sb([128, 1536], F32, f"crow{j}"))
            P.dma(rows[:, 0:1024], d['p_kv_rows'].ap()[Pi * 128:(Pi + 1) * 128, :], R=[d['p_kv_rows']], W=[rows])
            P.dma(rows[:, 1024:1536], d['pwin_d'].ap()[Pi * 128:(Pi + 1) * 128, :], R=[d['pwin_d']], W=[rows])
            cs = slice(Pi * 128, (Pi + 1) * 128)
            self.ctx_rows(rows, Pi, lohi,
                          (d['kselT_p'].ap()[:, :, cs].rearrange("k p n -> p k n"), d['kselT_p']),
                          (d['vsel_p'].ap()[Pi], d['vsel_p']),
                          (d['kwinT_p'].ap()[:, :, cs].rearrange("k p n -> p k n"), d['kwinT_p']),
                          (d['vwin_p'].ap()[Pi], d['vwin_p']), win_rows=(rows, 1024))
        self.cmp_finish(lohi, NCB, lambda kvh: kcd[:, kvh, :], lambda jh, kvh: vcm[:, jh, kvh, :], kcd, vcm)
    if cfg.get('nsa_stop', 99) == 2:
        raise _Stop()
    with self.phase():
        G = Ctx('n', 128, 4, 1, 1, 5)
        T = {}
        T['x'] = P.sb([128, 4, D], F32, "nx")
        T['xnT'] = P.sb([128, 8, 512], BF16, "nxnT")
        big = P.sb([128, 24, 512], BF16, "nbig")
        T['actT'] = View(big, 0, 22)
        qT_all = View(big, 0, 8)
        T['oT'] = P.sb([128, 8, 512], BF16, "noT")
        G.T = T
        o_tok = P.sb([128, 4, D], BF16, "o_tok")
        gates = P.sb([128, 4, 48], F32, "gates")
        qbd = P.sb([128, 2, 2, 512], BF16, "qbd")
        self.memset('pool', qbd[:, :, :, :], 0.0, W=[qbd])
        BT = P.sb([128, 15, 512], F32, "BT")
        for grp in range(NQT // 4):
            for tl in range(4):
                i = grp * 4 + tl
                xe = self.ring('xeo', 1, lambda j: P.sb([128, 2, D], F32, f"xeo{j}"))
                P.dma(xe[:, :, :], d['x2_p'].ap()[2 * i * 128:(2 * i + 2) * 128, :].rearrange("(e p) n -> p e n", p=128),
                      R=[d['x2_p']], W=[xe])
                self.ts('dve', T['x'][:, tl, :], xe[:, 0, :], c['parf'][:, 1:2], None, ALU.mult, R=[xe, c['parf']], W=[T['x']])
                self.stt('dve', T['x'][:, tl, :], xe[:, 1, :], c['parf'][:, 0:1], T['x'][:, tl, :], ALU.mult, ALU.add,
                         R=[xe, c['parf'], T['x']], W=[T['x']])
            for j in range(2):
                self.norm_T(G, d['norm_mix'].ap()[2 + j])
                _nsa_qproj(self, G, j, qT_all, 4)
                pg = self.pb[0]
                for tl in range(4):
                    for kc in range(8):
                        self.mm(pg[:, tl * 48:(tl + 1) * 48], T['xnT'][:, kc, tl * 128:(tl + 1) * 128], c['wg'][:, j, kc, :],
                                start=(kc == 0), stop=(kc == 7), R=[T['xnT'], c['wg']], W=[pg])
                self.actf(gates[:, :, :].rearrange("p t n -> p (t n)"), pg[:, 0:192], AF.Sigmoid, R=[pg], W=[gates])
                if cfg.get('nsa_stop', 99) == 3:
                    raise _Stop()
                for kvh in range(4):
                    for b3 in range(5):
                        P.dma(BT[:, 3 * b3:3 * b3 + 3, :], d['BTd'].ap()[kvh, 3 * b3:3 * b3 + 3].rearrange("i p n -> p i n"),
                              R=[d['BTd']], W=[BT])
                    for pr in range(2):
                        self.cp('pool', qbd[0:64, pr, 0, :], qT_all[0:64, 2 * kvh + pr, :], R=[qT_all], W=[qbd])
                        self.cp('pool', qbd[64:128, pr, 1, :], qT_all[64:128, 2 * kvh + pr, :], R=[qT_all], W=[qbd])
                    thunks = []
                    for tl in range(4):
                        def mk(tl=tl, kvh=kvh):
                            i = grp * 4 + tl
                            qc = slice(tl * 128, (tl + 1) * 128)
                            bc = self.ring('bcp', 2, lambda jj: P.sb([128, 4, 256], F32, f"bcp{jj}"))
                            smc = self.ring('smc', 2, lambda jj: P.sb([128, 2, 64], F32, f"smc{jj}"))
                            P.dma(smc[:, 0, :], d['selmul_p'].ap()[:, i, :], W=[smc])
                            P.dma(smc[:, 1, :], d['seladd_p'].ap()[:, i, :], W=[smc])
                            for a in range(8):
                                src = bass.AP(d['t_cmp'].h, 4 * kvh * 8192 + 247 - 16 * i - a, [[512, 16], [8192, 4], [1, 256]])
                                P.dma(bc[16 * a:16 * a + 16, :, :], src, R=[d['t_cmp']], W=[bc])
                            nkt = 2 * i + 2
                            sel_tiles = []
                            for kt in range(nkt):
                                e = 2 * i - kt
                                sel_tiles.append((kt, (BT[:, e + 1, :], BT) if e <= 7 else None))
                            win_tiles = []
                            for e in range(4, -2, -1):
                                kt = 2 * i - e
                                if kt >= 0:
                                    win_tiles.append((kt, (BT[:, 10 + e, :], BT)))
                            A = dict(
                                NQ=128, NC=NCB + 1, NH=2, kvh=kvh,
                                qT=(lambda g, qc=qc, kvh=kvh: qT_all[(g % 2) * 64:(g % 2) * 64 + 64, 2 * kvh + g // 2, qc], qT_all),
                                qbd=qbd, qbd_t=qbd, qcols=qc,
                                gate=(lambda br, tl=tl, kvh=kvh: gates[:, tl, :].rearrange("p (h r) -> p h r", r=3)[:, 4 * kvh:4 * kvh + 4, br], gates),
                                kcmp=(lambda g, kvh=kvh: kcd[(g % 2) * 64:(g % 2) * 64 + 64, kvh, 0:NCB + 1], kcd),
                                vcmp=(lambda hf, kvh=kvh: vcm[:, hf, kvh, :], vcm),
                                bias_c=(bc[:, :, :], bc),
                                selc=(smc[:, 0, :], smc[:, 1, :], smc),
                                sel=(sel_tiles,
                                     lambda kt0, n, kvh=kvh: (d['kselT_p'].ap()[kvh, :, kt0 * 128:(kt0 + n) * 128], d['kselT_p']),
                                     lambda kt0, n, kvh=kvh: (d['vsel_p'].ap()[kt0:kt0 + n, :, kvh, :].rearrange("t p e -> p t e"), d['vsel_p']),
                                     self.pb[4]),
                                win=(win_tiles,
                                     lambda kt0, n, kvh=kvh: (d['kwinT_p'].ap()[kvh, :, kt0 * 128:(kt0 + n) * 128], d['kwinT_p']),
                                     lambda kt0, n, kvh=kvh: (d['vwin_p'].ap()[kt0:kt0 + n, :, kvh, :].rearrange("t p e -> p t e"), d['vwin_p']),
                                     self.pb[5]),
                                out=(o_tok[:, tl, kvh * 256:(kvh + 1) * 256].rearrange("p (g e) -> p g e", g=4), o_tok),
                            )

                            return A
                        thunks.append(mk)
                    self.attend_pipe(thunks)
                for tl in range(4):
                    pt = self.pt[0]
                    for cc in range(8):
                        self.tr(pt[:, cc * 128:(cc + 1) * 128], o_tok[:, tl, cc * 128:(cc + 1) * 128], c['identb'][:, :],
                                R=[o_tok, c['identb']], W=[pt])
                    self.cp('act', T['oT'][:, :, tl * 128:(tl + 1) * 128], pt[:, :].rearrange("p (c n) -> p c n", c=8),
                            R=[pt], W=[T['oT']])
                self.tok_linear_add(G, T['oT'], 8, d['wb_nsa_out'][j])
                self.ffn(G, 2 + j)
            _final_norm(self, G, d['y_p'], grp * 512, ytile=self.rr['xeo'][0][0])


Builder.nsa_prompt = _nsa_prompt


def _nsa_sample(self):
    P, d, c, cfg = self.P, self.d, self.c, self.cfg
    NSQ = 16
    self.nsa_late_consts()
    kcd = P.sb([128, NSQ, 4, 128], BF16, "kcds")
    vcm = P.sb([128, NSQ, 4, 64], BF16, "vcms")
    self.memset('pool', vcm[:, :, :, :], 0.0, W=[vcm])
    with self.phase():
        ptb = P.sb([128, 256], I32, "ptb")
        P.dma(ptb[:, :], d['page_table'].ap()[0].partition_broadcast(128), W=[ptb])
        ptf = P.sb([128, 256], F32, "ptf")
        self.cp('dve', ptf[:, :], ptb[:, :], R=[ptb], W=[ptf])
        iot = P.sb([128, 1], F32, "iot")
        P.dma(iot[:, :], d['iota'].ap(), W=[iot])
        self.stt('dve', ptf[:, :], ptf[:, :], 128.0, iot[:, 0:1].to_broadcast([128, 256]), ALU.mult, ALU.add,
                 R=[ptf, iot], W=[ptf])
        idx = P.sb([128, 256], I32, "idxall")
        self.cp('dve', idx[:, :], ptf[:, :], R=[ptf], W=[idx])
        for s in range(NSQ):
            lohi = self.ring('lohis', 2, lambda j: P.sb([128, 2, 4, 136], F32, f"lohis{j}"))
            self.memset('pool', lohi[:, :, :, :], 0.0, W=[lohi])
            for Pi in range(17):
                rows = self.ring('crow', 4, lambda j: P.sb([128, 1024], F32, f"crows{j}"))
                if Pi < 16:
                    k = s * 16 + Pi
                    P.add('pool', lambda e, rows=rows, k=k: e.indirect_dma_start(
                        out=rows[:, :], out_offset=None, in_=d['cache_kv'].ap(),
                        in_offset=bass.IndirectOffsetOnAxis(ap=idx[:, k:k + 1], axis=0)),
                        R=[idx, d['cache_kv']], W=[rows], dma=True)
                else:
                    self.memset('pool', rows[:, :], 0.0, W=[rows])
                    P.dma(rows[0:4, :], d['s_kv_rows'].ap()[4 * s:4 * s + 4, :], R=[d['s_kv_rows']], W=[rows])
                cs = slice(Pi * 128, (Pi + 1) * 128)
                self.ctx_rows(rows, Pi, lohi if Pi < 16 else None,
                              (d['kselT_s'].ap()[s][:, :, cs].rearrange("k p n -> p k n"), d['kselT_s']),
                              (d['vsel_s'].ap()[s, Pi], d['vsel_s']), None, None, has_win=False)
            for W_ in range(5):
                wr = self.ring('wrow_s', 2, lambda j: P.sb([128, 512], F32, f"wrows{j}"))
                if W_ < 4:
                    P.dma(wr[:, :], d['state_win_kv'].ap()[s, W_ * 128:(W_ + 1) * 128, :], W=[wr])
                else:
                    self.memset('pool', wr[:, :], 0.0, W=[wr])
                    P.dma(wr[0:4, :], d['s_win_kv'].ap()[s, 508:512, :], R=[d['s_win_kv']], W=[wr])
                cs = slice(W_ * 128, (W_ + 1) * 128)
                self.ctx_rows(None, 0, None, None, None,
                              (d['kwinT_s'].ap()[s][:, :, cs].rearrange("k p n -> p k n"), d['kwinT_s']),
                              (d['vwin_s'].ap()[s, W_], d['vwin_s']), has_cmpsel=False, win_rows=(wr, 0))
            self.cmp_finish(lohi, 127, lambda kvh, s=s: kcd[:, s, kvh, :], lambda jh, kvh, s=s: vcm[:, s, kvh, :], kcd, vcm,
                            ncw=128, njh=1)
    with self.phase():
        G = Ctx('m', 64, 1, 16, 16, 1)
        T = {}
        T['x'] = P.sb([64, 1, D], F32, "mx")
        T['xnT'] = P.sb([128, 8, 64], BF16, "mxnT")
        big = P.sb([128, 24, 64], BF16, "mbig")
        T['actT'] = View(big, 0, 22)
        qT_all = View(big, 0, 8)
        T['oT'] = P.sb([128, 8, 64], BF16, "moT")
        G.T = T
        P.dma(T['x'][:, 0, :], d['x2_s'].ap(), R=[d['x2_s']], W=[T['x']])
        qbd = P.sb([128, 4, 2, 2, 64], BF16, "qbds")
        self.memset('pool', qbd[:, :, :, :, :], 0.0, W=[qbd])
        smul = P.sb([4, 64], F32, "smuls")
        sadd = P.sb([4, 64], F32, "sadds")
        P.dma(smul[:, :], d['selmul_s'].ap(), W=[smul])
        P.dma(sadd[:, :], d['seladd_s'].ap(), W=[sadd])
        bcs = P.sb([4, 16, 128], F32, "bcs")
        P.dma(bcs[:, :, :], bass.AP(d['t_cs'].h, 0, [[128, 4], [512, 16], [1, 128]]), R=[d['t_cs']], W=[bcs])
        trv = P.sb([128, 22, 16, 4], F32, "trvs")
        for Pi in range(17):
            P.dma(trv[:, Pi, :, :], bass.AP(d['t_ss'].h, 2048 - 128 * Pi, [[1, 128], [2304, 16], [1, 4]]), R=[d['t_ss']], W=[trv])
        for W_ in range(5):
            P.dma(trv[:, 17 + W_, :, :], bass.AP(d['t_ws'].h, 512 - 128 * W_, [[1, 128], [768, 16], [1, 4]]), R=[d['t_ws']], W=[trv])
        BTs = P.sb([128, 22, 16, 4], F32, "BTs")
        tf = trv[:, :, :, :].rearrange("p a h t -> p (a h t)")
        bf_ = BTs[:, :, :, :].rearrange("p a h t -> p (a h t)")
        for off in range(0, 22 * 64, 512):
            n = min(512, 22 * 64 - off)
            ps = self.next_pb('lin', [1, 2, 3])
            self.mm(ps[:, 0:n], c['antiI'][:, :], tf[:, off:off + n], R=[c['antiI'], trv], W=[ps])
            self.cp('act', bf_[:, off:off + n], ps[:, 0:n], R=[ps], W=[BTs])
        for j in range(2):
            self.norm_T(G, d['norm_mix'].ap()[2 + j])
            _nsa_qproj(self, G, j, qT_all, 1)
            for kvh in range(4):
                for pr in range(2):
                    self.cp('pool', qbd[0:64, kvh, pr, 0, :], qT_all[0:64, 2 * kvh + pr, :], R=[qT_all], W=[qbd])
                    self.cp('pool', qbd[64:128, kvh, pr, 1, :], qT_all[64:128, 2 * kvh + pr, :], R=[qT_all], W=[qbd])
            gs_all = self.ring('gs_all', 1, lambda jj: P.sb([4, NSQ, 48], F32, "gs_all"))
            for half in range(2):
                pg = self.pb[0]
                for s8 in range(8):
                    s = half * 8 + s8
                    for kc in range(8):
                        self.mm(pg[0:4, s8 * 48:(s8 + 1) * 48], T['xnT'][:, kc, 4 * s:4 * s + 4], c['wg'][:, j, kc, :],
                                start=(kc == 0), stop=(kc == 7), R=[T['xnT'], c['wg']], W=[pg])
                self.actf(gs_all[:, half * 8:(half + 1) * 8, :].rearrange("p s n -> p (s n)"), pg[0:4, 0:384], AF.Sigmoid,
                          R=[pg], W=[gs_all])
            o_all = self.ring('o_all', 1, lambda jj: P.sb([4, NSQ, D], BF16, "o_all"))
            thunks = []
            for s in range(NSQ):
                for kvh in range(4):
                    def mk(s=s, kvh=kvh):
                        qc = slice(4 * s, 4 * s + 4)
                        sel_tiles = [(Pi, (BTs[:, Pi, 4 * kvh:4 * kvh + 4, :].rearrange("p h t -> p (h t)"), BTs)) for Pi in range(17)]
                        win_tiles = [(W_, (BTs[:, 17 + W_, 4 * kvh:4 * kvh + 4, :].rearrange("p h t -> p (h t)"), BTs)) for W_ in range(5)]
                        return dict(
                            NQ=4, NC=128, NH=1, kvh=kvh,
                            qT=(lambda g: qT_all[(g % 2) * 64:(g % 2) * 64 + 64, 2 * kvh + g // 2, qc], qT_all),
                            qbd=qbd.h[:, kvh], qbd_t=qbd, qcols=qc,
                            gate=(lambda br: gs_all[:, s, :].rearrange("p (h r) -> p h r", r=3)[:, 4 * kvh:4 * kvh + 4, br], gs_all),
                            kcmp=(lambda g: kcd[(g % 2) * 64:(g % 2) * 64 + 64, s, kvh, :], kcd),
                            vcmp=(lambda hf: vcm[:, s, kvh, :], vcm),
                            bias_c=(bcs[:, 4 * kvh:4 * kvh + 4, :], bcs),
                            selc=(smul[:, :], sadd[:, :], smul),
                            sel=(sel_tiles,
                                 lambda kt0, n: (d['kselT_s'].ap()[s, kvh, :, kt0 * 128:(kt0 + n) * 128], d['kselT_s']),
                                 lambda kt0, n: (d['vsel_s'].ap()[s, kt0:kt0 + n, :, kvh, :].rearrange("t p e -> p t e"), d['vsel_s']),
                                 self.pb[4]),
                            win=(win_tiles,
                                 lambda kt0, n: (d['kwinT_s'].ap()[s, kvh, :, kt0 * 128:(kt0 + n) * 128], d['kwinT_s']),
                                 lambda kt0, n: (d['vwin_s'].ap()[s, kt0:kt0 + n, :, kvh, :].rearrange("t p e -> p t e"), d['vwin_s']),
                                 self.pb[5]),
                            out=(o_all[:, s, kvh * 256:(kvh + 1) * 256].rearrange("p (g e) -> p g e", g=4), o_all),
                        )
                    thunks.append(mk)
            self.attend_pipe(thunks)
            for s in range(NSQ):
                qc = slice(4 * s, 4 * s + 4)
                pt = self.pt[0]
                for cc in range(8):
                    self.tr(pt[:, cc * 4:(cc + 1) * 4], o_all[:, s, cc * 128:(cc + 1) * 128], c['identb'][0:4, 0:4],
                            R=[o_all, c['identb']], W=[pt])
                self.cp('act', T['oT'][:, :, qc], pt[:, 0:32].rearrange("p (c n) -> p c n", c=8), R=[pt], W=[T['oT']])
            self.tok_linear_add(G, T['oT'], 8, d['wb_nsa_out'][j])
            self.ffn(G, 2 + j)
        _final_norm(self, G, d['y_s'], 0)


Builder.nsa_sample = _nsa_sample
```

```python
import math
import numpy as np
import concourse.bass as bass
import concourse.mybir as mybir
from concourse.bass_utils import run_bass_kernel_spmd

F32 = mybir.dt.float32
BF16 = mybir.dt.bfloat16
I32 = mybir.dt.int32
AF = mybir.ActivationFunctionType
ALU = mybir.AluOpType

EPOCH = 16000
EMBED_WAIT = True
NSLOT = 8
ENGS = ('pe', 'act', 'dve', 'pool', 'sp')
SAME_ENGINE_SYNC = {'pe': False, 'act': True, 'dve': True, 'pool': True, 'sp': True}

D = 1024
H = 8
DFF = 2816
NEG = -30000.0


class Buf:
    __slots__ = ('name', 'lw', 'rd')

    def __init__(self, name):
        self.name = name
        self.lw = None
        self.rd = []


class Tile:
    def __init__(self, h, name):
        self.h = h
        self.buf = Buf(name)

    def __getitem__(self, idx):
        return self.h[idx]

    def ap(self):
        return self.h.ap()


class View:
    def __init__(self, base, off, n):
        self.base, self.buf, self.off, self.n = base, base.buf, off, n

    def __getitem__(self, idx):
        idx = list(idx)
        a = idx[1]
        if isinstance(a, slice):
            st = (a.start or 0) + self.off
            en = (a.stop if a.stop is not None else self.n) + self.off
            idx[1] = slice(st, en)
        else:
            idx[1] = a + self.off
        return self.base.h[tuple(idx)]


class Prog:
    def __init__(self, nc):
        self.nc = nc
        self.ops = {e: [] for e in ENGS}
        self.cnt = {e: 0 for e in ENGS}
        self.dcnt = {e: 0 for e in ENGS}
        self.known = {e: {} for e in ENGS}
        self.kev = {e: [] for e in ENGS}
        self.kptr = {}
        self.nt = 0
        self.stack = None

    def sb(self, shape, dtype=F32, name=None):
        self.nt += 1
        name = name or "t"
        if self.stack is not None:
            h = self.stack.enter_context(self.nc.sbuf_tensor(f"{name}_{self.nt}", list(shape), dtype))
        else:
            h = self.nc.alloc_sbuf_tensor(f"{name}_{self.nt}", list(shape), dtype)
        return Tile(h, name)

    def barrier(self):
        evs = []
        for f in ENGS:
            if self.cnt[f] > 0:
                evs.append(('c', f, self.cnt[f]))
            n = self.dcnt[f]
            for slot in range(min(n, NSLOT)):
                evs.append(('d', f, slot, (n - 1 - slot) // NSLOT + 1))
        for e in ENGS:
            waits = []
            for ev in evs:
                if ev[0] == 'c' and ev[1] == e:
                    continue
                self._need(e, ev, waits)
            self.cnt[e] += 1
            self.ops[e].append((waits, (lambda en: en.nop()), ('c', e, self.cnt[e])))

    def ps(self, shape, dtype=F32, name=None):
        self.nt += 1
        name = name or "p"
        h = self.nc.alloc_psum_tensor(f"{name}_{self.nt}", list(shape), dtype)
        return Tile(h, name)

    def dram(self, name, shape, dtype=F32, kind="Internal"):
        h = self.nc.dram_tensor(name, list(shape), dtype, kind=kind)
        return Tile(h, name)

    def _learn(self, eng, key, val):
        if self.known[eng].get(key, 0) >= val:
            return False
        self.known[eng][key] = val
        self.kev[eng].append((self.cnt[eng] + 1, key, val))
        return True

    def _absorb(self, eng, f, seq):
        evs = self.kev[f]
        i = self.kptr.get((eng, f), 0)
        n = len(evs)
        while i < n and evs[i][0] <= seq:
            _, key, val = evs[i]
            if not (key[0] == 'c' and key[1] == eng):
                self._learn(eng, key, val)
            i += 1
        self.kptr[(eng, f)] = i

    def _need(self, eng, ev, waits):
        if ev is None:
            return
        if ev[0] == 'c':
            _, f, seq = ev
            if f == eng and not SAME_ENGINE_SYNC[eng]:
                return
            if self._learn(eng, ('c', f), seq):
                waits.append(ev)
                if f != eng:
                    self._absorb(eng, f, seq)
        else:
            _, q, slot, k = ev
            if self._learn(eng, ('d', q, slot), k):
                waits.append(ev)

    def add(self, eng, emit, R=(), W=(), dma=False):
        waits = []
        for t in R:
            self._need(eng, t.buf.lw, waits)
        for t in W:
            b = t.buf
            self._need(eng, b.lw, waits)
            for ev in b.rd:
                self._need(eng, ev, waits)
        if dma:
            i = self.dcnt[eng]
            self.dcnt[eng] += 1
            slot, k = i % NSLOT, i // NSLOT + 1
            if k > 1:
                self._need(eng, ('d', eng, slot, k - 1), waits)
            ev = ('d', eng, slot, k)
        else:
            self.cnt[eng] += 1
            ev = ('c', eng, self.cnt[eng])
        for t in R:
            t.buf.rd.append(ev)
        for t in W:
            t.buf.lw = ev
            t.buf.rd = []
        self.ops[eng].append((waits, emit, ev))
        return ev

    def dma(self, out_ap, in_ap, R=(), W=(), q='sp', **kw):
        return self.add(q, lambda e: e.dma_start(out=out_ap, in_=in_ap, **kw), R, W, dma=True)

    def emit(self):
        nc = self.nc
        csem = {}
        for e in ENGS:
            n = (self.cnt[e] + EPOCH - 1) // EPOCH
            csem[e] = [nc.alloc_semaphore(f"c_{e}_{j}") for j in range(n)]
        dsem = {}
        for e in ENGS:
            n = min(self.dcnt[e], NSLOT)
            dsem[e] = [nc.alloc_semaphore(f"d_{e}_{j}") for j in range(n)]

        def semval(ev):
            if ev[0] == 'c':
                _, f, seq = ev
                return csem[f][(seq - 1) // EPOCH], (seq - 1) % EPOCH + 1
            _, q, slot, k = ev
            return dsem[q][slot], 16 * k

        def run(eng, e):
            for waits, emit, ev in self.ops[eng]:
                emb = waits[-1] if (waits and EMBED_WAIT) else None
                for w in (waits[:-1] if emb is not None else waits):
                    s, v = semval(w)
                    e.wait_ge(s, v)
                ins = emit(e)
                if emb is not None:
                    s, v = semval(emb)
                    ins._wait_ge(s, v)
                s, v = semval(ev)
                ins.then_inc(s, 16 if ev[0] == 'd' else 1)
            n = self.dcnt[eng]
            for slot in range(min(n, NSLOT)):
                k = (n - 1 - slot) // NSLOT + 1
                if self.known[eng].get(('d', eng, slot), 0) < k:
                    e.wait_ge(dsem[eng][slot], 16 * k)

        with nc.Block() as block:
            @block.tensor
            def _(e):
                run('pe', e)

            @block.scalar
            def _(e):
                run('act', e)

            @block.vector
            def _(e):
                run('dve', e)

            @block.gpsimd
            def _(e):
                run('pool', e)

            @block.sync
            def _(e):
                run('sp', e)
        return nc


class Ctx:
    def __init__(self, name, TP, NT, NSEQ, NS, nlev):
        self.name = name
        self.TP = TP
        self.NT = NT
        self.GT = TP * NT
        self.NSEQ = NSEQ
        self.TS = self.GT // NSEQ
        self.NCH = self.GT // 64
        self.NS = NS
        self.nlev = nlev


def host_consts(NS, TS):
    seg = np.arange(64) // TS if NS > 1 else np.zeros(64, np.int64)
    j = np.arange(64)[:, None]
    i = np.arange(64)[None, :]
    same = (seg[:, None] == seg[None, :])
    c = {}
    c['ucs'] = ((j <= i) & same).astype(np.float32)
    c['maskT'] = np.where((j <= i) & same, 0.0, NEG).astype(np.float32)
    c['noff'] = -np.where((j != i), 1.0, 0.0).astype(np.float32)
    c['same'] = same.astype(np.float32)
    si = np.zeros((64, NS), np.float32)
    si[np.arange(64), seg] = 1.0
    c['seqind'] = si
    cm = np.zeros((128, NS, 64), np.float32)
    cm[:, seg, np.arange(64)] = 1.0
    c['colmask'] = cm.reshape(128, NS * 64)
    return c


class _Stop(Exception):
    pass


class Builder:
    def __init__(self, cfg):
        self.cfg = cfg
        nc = bass.Bass("TRN2", target_bir_lowering=False)
        self.nc = nc
        self.P = Prog(nc)
        self.rr = {}

    def mm(self, out, lhsT, rhs, start=True, stop=True, R=(), W=()):
        self.P.add('pe', lambda e: e.matmul(out, lhsT=lhsT, rhs=rhs, start=start, stop=stop), R, W)

    def tr(self, out, in_, ident, R=(), W=()):
        self.P.add('pe', lambda e: e.transpose(out=out, in_=in_, identity=ident), R, W)

    def actf(self, out, in_, func, R=(), W=(), bias=None, scale=None, accum_out=None):
        kw = {}
        if bias is not None:
            kw['bias'] = bias
        if scale is not None:
            kw['scale'] = scale
        if accum_out is not None:
            kw['accum_out'] = accum_out
        self.P.add('act', lambda e: e.activation(out=out, in_=in_, func=func, **kw), R, W)

    def cp(self, eng, out, in_, R=(), W=()):
        if eng == 'act':
            self.P.add('act', lambda e: e.copy(out=out, in_=in_), R, W)
        else:
            self.P.add(eng, lambda e: e.tensor_copy(out=out, in_=in_), R, W)

    def tt(self, eng, out, in0, in1, op, R=(), W=()):
        self.P.add(eng, lambda e: e.tensor_tensor(out=out, in0=in0, in1=in1, op=op), R, W)

    def ts(self, eng, out, in0, s1, s2, op0, op1=None, R=(), W=()):
        if op1 is None:
            self.P.add(eng, lambda e: e.tensor_scalar(out=out, in0=in0, scalar1=s1, scalar2=None, op0=op0), R, W)
        else:
            self.P.add(eng, lambda e: e.tensor_scalar(out=out, in0=in0, scalar1=s1, scalar2=s2, op0=op0, op1=op1), R, W)

    def stt(self, eng, out, in0, scalar, in1, op0, op1, R=(), W=()):
        self.P.add(eng, lambda e: e.scalar_tensor_tensor(out=out, in0=in0, scalar=scalar, in1=in1, op0=op0, op1=op1), R, W)

    def memset(self, eng, ap, val, W=()):
        self.P.add(eng, lambda e: e.memset(ap, val), (), W)

    def ring(self, key, n, make):
        if key not in self.rr:
            self.rr[key] = [[make(i) for i in range(n)], 0]
        r = self.rr[key]
        t = r[0][r[1] % n]
        r[1] += 1
        return t

    def declare_io(self):
        P, cfg = self.P, self.cfg
        d = {}

        def inp(name, shape, dt=F32):
            d[name] = P.dram(name, shape, dt, kind="ExternalInput")

        def out(name, shape, dt=F32):
            d[name] = P.dram(name, shape, dt, kind="ExternalOutput")

        SEQ = cfg['SEQ']
        inp('x_prompt', [SEQ, D])
        inp('x_sample', [64, D])
        inp('state_dn_S', [2, 16, H, 128, 128])
        inp('state_dn_conv', [2, 16, 3, 3072])
        inp('state_win_kv', [16, 512, 512])
        inp('norm_mix', [4, D])
        inp('norm_ffn', [4, D])
        inp('norm_kv', [D])
        inp('norm_final', [D])
        inp('ffn_w_in', [4, D, 2 * DFF])
        inp('ffn_w_out', [4, DFF, D])
        inp('dn_w_in', [2, D, 4112])
        inp('dn_conv_w', [2, 4, 3072])
        inp('dn_A_log', [2, H])
        inp('dn_dt_bias', [2, H])
        inp('dn_out_norm', [2, 128])
        inp('dn_w_out', [2, D, D])
        inp('nsa_w_kv', [D, 1536])
        for pre, NS in (('cp_', 1), ('cs_', 16)):
            inp(pre + 'ucs', [64, 64])
            inp(pre + 'maskT', [64, 64])
            inp(pre + 'noff', [64, 64])
            inp(pre + 'same', [64, 64])
            inp(pre + 'seqind', [64, NS])
            inp(pre + 'colmask', [128, NS * 64])
        out('p_dn_S', [2, H, 128, 128])
        out('p_dn_conv', [2, 3, 3072])
        out('p_kv_rows', [SEQ, 1024])
        out('p_win_kv', [512, 512])
        out('s_dn_S', [2, 16, H, 128, 128])
        out('s_dn_conv', [2, 16, 3, 3072])
        out('s_kv_rows', [64, 1024])
        out('s_win_kv', [16, 512, 512])
        out('x2_p', [SEQ, D])
        out('x2_s', [64, D])
        d['wc_dn_in'] = [P.dram(f'wc_dn_in{l}', [32, 128, 8, 128], BF16) for l in range(2)]
        d['wc_ffn_in'] = [P.dram(f'wc_ffn_in{l}', [44, 128, 8, 128], BF16) for l in range(4)]
        d['wb_ffn_out'] = [P.dram(f'wb_ffn_out{l}', [DFF, D], BF16) for l in range(4)]
        d['wb_dn_out'] = [P.dram(f'wb_dn_out{l}', [D, D], BF16) for l in range(2)]
        d['wb_kv'] = P.dram('wb_kv', [D, 1536], BF16)
        self.d = d

    def consts(self):
        P, d = self.P, self.d
        c = {}
        identf = P.sb([128, 128], F32, "identf")
        self.memset('pool', identf[:, :], 1.0, W=[identf])
        P.add('pool', lambda e: e.affine_select(out=identf[:, :], in_=identf[:, :], pattern=[[-1, 128]],
                                                compare_op=ALU.is_equal, fill=0.0, base=0, channel_multiplier=1),
              R=[identf], W=[identf])
        identb = P.sb([128, 128], BF16, "identb")
        self.cp('dve', identb[:, :], identf[:, :], R=[identf], W=[identb])
        onesb = P.sb([128, 128], BF16, "onesb")
        self.memset('pool', onesb[:, :], 1.0, W=[onesb])
        onesf = P.sb([64, 128], F32, "onesf")
        self.memset('pool', onesf[:, :], 1.0, W=[onesf])
        c.update(identf=identf, identb=identb, onesb=onesb, onesf=onesf)
        for pre, NS in (('cp_', 1), ('cs_', 16)):
            for nm, shp in (('ucs', [64, 64]), ('maskT', [64, 64]), ('noff', [64, 64]), ('same', [64, 64]),
                            ('seqind', [64, NS])):
                t = P.sb(shp, F32, pre + nm)
                P.dma(t[:, :], d[pre + nm].ap(), W=[t])
                c[pre + nm] = t
            if NS > 1:
                t = P.sb([128, NS * 64], BF16, pre + 'colmask')
                P.dma(t[:, :], d[pre + 'colmask'].ap(), W=[t], q='pool')
                c[pre + 'colmask'] = t
        cw = P.sb([128, 2, 24, 4], F32, "cw")
        for l in range(2):
            for i in range(4):
                P.dma(cw[:, l, :, i], d['dn_conv_w'].ap()[l, i].rearrange("(c p) -> p c", p=128), W=[cw],
                      allow_slow_non_contiguous=True)
        c['cw'] = cw
        negA = P.sb([128, 2, H], F32, "negA")
        dtb = P.sb([128, 2, H], F32, "dtb")
        P.dma(negA[:, :, :], d['dn_A_log'].ap().rearrange("l h -> (l h)").partition_broadcast(128).rearrange("p (l h) -> p l h", l=2), W=[negA])
        P.dma(dtb[:, :, :], d['dn_dt_bias'].ap().rearrange("l h -> (l h)").partition_broadcast(128).rearrange("p (l h) -> p l h", l=2), W=[dtb])
        self.actf(negA[:, :, :], negA[:, :, :], AF.Exp, R=[negA], W=[negA])
        self.ts('dve', negA[:, :, :], negA[:, :, :], -1.0, None, ALU.mult, R=[negA], W=[negA])
        c.update(negA=negA, dtb=dtb)
        onw = P.sb([128, 2], F32, "onw")
        P.dma(onw[:, :], d['dn_out_norm'].ap().rearrange("l p -> p l"), W=[onw], allow_slow_non_contiguous=True)
        c['onw'] = onw
        wab = P.sb([128, 2, 8, 16], BF16, "wab")
        for l in range(2):
            P.dma(wab[:, l, :, :], d['dn_w_in'].ap()[l, :, 4096:4112].rearrange("(kc p) n -> p kc n", p=128), W=[wab], q='pool')
        c['wab'] = wab
        self.c = c
        self.pb = [P.ps([128, 512], F32, f"pb{i}") for i in range(6)]
        self.pt = [P.ps([128, 1024], BF16, f"pt{i}") for i in range(2)]

    def convert_weights(self):
        P, d = self.P, self.d
        k = [0]
        SW = 2048

        def stage():
            i = k[0]
            k[0] += 1
            f = self.ring('cvf', 2, lambda j: P.sb([128, SW], F32, f"cvf{j}"))
            b = self.ring('cvb', 2, lambda j: P.sb([128, SW], BF16, f"cvb{j}"))
            return f, b, ('act', 'dve', 'pool')[i % 3]

        def conv_chunked(W_ap, dst, nch):
            for g in range(nch // 2):
                f, b, e = stage()
                P.dma(f[:, :].rearrange("p (k n) -> p k n", k=8),
                      W_ap[:, g * 256:(g + 1) * 256].rearrange("(kc p) n -> p kc n", p=128), W=[f], q='sp')
                o = b[:, :].rearrange("p (c k n) -> p k c n", c=2, k=8)
                sv = f[:, :].rearrange("p (k c n) -> p k c n", k=8, c=2)
                self.cp(e, o, sv, R=[f], W=[b])
                P.dma(dst.ap()[g * 2:(g + 1) * 2].rearrange("c p k n -> p c (k n)"),
                      b[:, :].rearrange("p (c kn) -> p c kn", c=2), R=[b], W=[dst], q='act')

        def conv_natural(W_ap, dst, K, N):
            per = max(1, SW // N)
            nk = K // 128
            kc = 0
            while kc < nk:
                m = min(per, nk - kc)
                f, b, e = stage()
                P.dma(f[:, 0:m * N].rearrange("p (k n) -> p k n", k=m),
                      W_ap[kc * 128:(kc + m) * 128, :].rearrange("(k p) n -> p k n", p=128), W=[f], q='sp')
                self.cp(e, b[:, 0:m * N], f[:, 0:m * N], R=[f], W=[b])
                P.dma(dst.ap()[kc * 128:(kc + m) * 128, :].rearrange("(k p) n -> p k n", p=128),
                      b[:, 0:m * N].rearrange("p (k n) -> p k n", k=m), R=[b], W=[dst], q='act')
                kc += m

        for l in range(self.cfg['n_dn']):
            conv_chunked(d['dn_w_in'].ap()[l, :, 0:4096], d['wc_dn_in'][l], 32)
            conv_natural(d['dn_w_out'].ap()[l], d['wb_dn_out'][l], D, D)
            conv_chunked(d['ffn_w_in'].ap()[l], d['wc_ffn_in'][l], 44)
            conv_natural(d['ffn_w_out'].ap()[l], d['wb_ffn_out'][l], DFF, D)
        conv_natural(d['nsa_w_kv'].ap(), d['wb_kv'], D, 1536)
        if self.cfg.get('nsa', True):
            for j in range(2):
                conv_chunked(d['nsa_w_in'].ap()[j, :, 0:1024], d['wc_nsa_in'][j], 8)
                conv_natural(d['nsa_w_out'].ap()[j], d['wb_nsa_out'][j], D, D)
                conv_chunked(d['ffn_w_in'].ap()[2 + j], d['wc_ffn_in'][2 + j], 44)
                conv_natural(d['ffn_w_out'].ap()[2 + j], d['wb_ffn_out'][2 + j], DFF, D)

    def alloc_ctx(self, G):
        P = self.P
        n = G.name
        T = {}
        T['x'] = P.sb([G.TP, G.NT, D], F32, n + "x")
        T['xnT'] = P.sb([128, 8, G.GT], BF16, n + "xnT")
        big = P.sb([128, 24, G.GT], BF16, n + "big")
        T['qT'] = View(big, 0, 8)
        T['kT'] = View(big, 8, 8)
        T['vT'] = View(big, 16, 8)
        T['actT'] = View(big, 0, 22)
        T['zs'] = P.sb([128, H, G.GT], BF16, n + "zs")
        T['OT'] = P.sb([128, H, G.GT], F32, n + "OT")
        T['oT'] = P.sb([128, H, G.GT], BF16, n + "oT")
        T['carry'] = [P.sb([128, 24, G.NSEQ, 3], F32, n + f"carry{l}") for l in range(2)]
        T['ab'] = P.sb([64, G.NCH, 16], F32, n + "ab")
        T['g'] = P.sb([64, G.NCH, H], F32, n + "g")
        T['beta'] = P.sb([64, G.NCH, H], F32, n + "beta")
        G.T = T

    def norm_T(self, G, wrow_src):
        P, c, T = self.P, self.c, G.T
        TP = G.TP
        wrow = self.ring('wrow', 2, lambda j: P.sb([128, D], F32, f"wrow{j}"))
        P.dma(wrow[:, :], wrow_src.partition_broadcast(128), W=[wrow])
        x = T['x']
        for t in range(G.NT):
            junk = self.ring('junk', 1, lambda j: P.sb([128, D], BF16, f"junk{j}"))
            st = self.ring('nst', 4, lambda j: P.sb([128, 2], F32, f"nst{j}"))
            self.memset('pool', st[:, :], 0.0, W=[st])
            self.actf(junk[0:TP, :], x[:, t, :], AF.Square, R=[x], W=[junk, st], accum_out=st[0:TP, 0:1])
            self.ts('dve', st[0:TP, 1:2], st[0:TP, 0:1], 1.0 / D, 1e-6, ALU.mult, ALU.add, R=[st], W=[st])
            self.actf(st[0:TP, 1:2], st[0:TP, 1:2], AF.Sqrt, R=[st], W=[st])
            P.add('dve', lambda e, st=st: e.reciprocal(out=st[0:TP, 1:2], in_=st[0:TP, 1:2]), R=[st], W=[st])
            xn = self.ring('xn', 2, lambda j: P.sb([128, D], BF16, f"xn{j}"))
            self.stt('dve', xn[0:TP, :], x[:, t, :], st[0:TP, 1:2], wrow[0:TP, :], ALU.mult, ALU.mult,
                     R=[x, st, wrow], W=[xn])
            pt = self.pt[0]
            for cc in range(8):
                self.tr(pt[:, cc * 128:cc * 128 + TP], xn[0:TP, cc * 128:(cc + 1) * 128], c['identb'][0:TP, 0:TP],
                        R=[xn, c['identb']], W=[pt])
            self.cp('act', T['xnT'][:, :, t * TP:(t + 1) * TP],
                    pt[:, :].rearrange("p (c n) -> p c n", c=8)[:, :, 0:TP], R=[pt], W=[T['xnT']])

    def load_wchunk(self, src_ap):
        P = self.P
        wt = self.ring('wch', 4, lambda j: P.sb([128, 8, 128], BF16, f"wch{j}"))
        P.dma(wt[:, :, :], src_ap, W=[wt])
        return wt

    def next_pb(self, key, banks):
        r = self.rr.setdefault('pb_' + key, [0])
        b = banks[r[0] % len(banks)]
        r[0] += 1
        return self.pb[b]

    def dn_mixer(self, G, l, S_tiles, last_group):
        P, c, T, d = self.P, self.c, G.T, self.d
        GT, TP, NT, NSEQ, TS, NCH, NS = G.GT, G.TP, G.NT, G.NSEQ, G.TS, G.NCH, G.NS
        pre = 'cp_' if NS == 1 else 'cs_'
        self.norm_T(G, d['norm_mix'].ap()[l])
        xnT = T['xnT']
        pbs = self.pb[0]
        for n in range(NCH):
            for kc in range(8):
                self.mm(pbs[0:64, n * 16:(n + 1) * 16], xnT[:, kc, n * 64:(n + 1) * 64], c['wab'][:, l, kc, :],
                        start=(kc == 0), stop=(kc == 7), R=[xnT, c['wab']], W=[pbs])
        ab = T['ab']
        self.cp('act', ab[:, :, :], pbs[0:64, 0:NCH * 16].rearrange("p (n k) -> p n k", k=16), R=[pbs], W=[ab])
        gt = self.ring('gtmp', 2, lambda j: P.sb([64, 8, H], F32, f"gtmp{j}"))
        g2 = self.ring('gtmp', 2, lambda j: None)
        av = gt[:, 0:NCH, :]
        a2 = g2[:, 0:NCH, :]
        self.tt('dve', av, ab[:, :, 0:8], c['dtb'][0:64, l, :].unsqueeze(1).to_broadcast([64, NCH, H]), ALU.add,
                R=[ab, c['dtb']], W=[gt])
        self.actf(a2, av, AF.Abs, R=[gt], W=[g2])
        self.actf(a2, a2, AF.Exp, R=[g2], W=[g2], scale=-1.0)
        self.actf(a2, a2, AF.Ln, R=[g2], W=[g2], bias=1.0)
        self.stt('dve', a2, av, 0.0, a2, ALU.max, ALU.add, R=[gt, g2], W=[g2])
        self.tt('dve', T['g'][:, :, :], a2, c['negA'][0:64, l, :].unsqueeze(1).to_broadcast([64, NCH, H]), ALU.mult,
                R=[g2, c['negA']], W=[T['g']])
        self.actf(T['beta'][:, :, :], ab[:, :, 8:16], AF.Sigmoid, R=[ab], W=[T['beta']])

        carry = T['carry'][l]

        def chunk_gen(ci):
            wt = self.load_wchunk(d['wc_dn_in'][l].ap()[ci])
            ps = self.next_pb('lin', [1, 2, 3])
            for kc in range(8):
                self.mm(ps[:, 0:GT], wt[:, kc, :], xnT[:, kc, :], start=(kc == 0), stop=(kc == 7), R=[wt, xnT], W=[ps])
            hh = ci % 8
            if ci >= 24:
                self.actf(T['zs'][:, hh, :], ps[:, 0:GT], AF.Silu, R=[ps], W=[T['zs']])
                return
            xp = self.ring(G.name + 'xp', 2, lambda j: P.sb([128, NSEQ, TS + 3], F32, G.name + f"xp{j}"))
            self.cp('act', xp[:, :, 3:TS + 3], ps[:, 0:GT].rearrange("p (s t) -> p s t", s=NSEQ), R=[ps], W=[xp])
            self.cp('pool', xp[:, :, 0:3], carry[:, ci, :, :], R=[carry], W=[xp])
            self.cp('pool', carry[:, ci, :, :], xp[:, :, TS:TS + 3], R=[xp], W=[carry])
            acc = self.ring(G.name + 'acc', 2, lambda j: P.sb([128, NSEQ, TS], F32, G.name + f"acc{j}"))
            cwl = c['cw']
            self.ts('dve', acc[:, :, :], xp[:, :, 0:TS], cwl[:, l, ci, 0:1], None, ALU.mult, R=[xp, cwl], W=[acc])
            for i in range(1, 4):
                self.stt('dve', acc[:, :, :], xp[:, :, i:TS + i], cwl[:, l, ci, i:i + 1], acc[:, :, :], ALU.mult, ALU.add,
                         R=[xp, cwl, acc], W=[acc])
            accf = acc[:, :, :].rearrange("p s t -> p (s t)")
            if ci >= 16:
                self.actf(T['vT'][:, hh, :], accf, AF.Silu, R=[acc], W=[T['vT']])
                return
            sl = self.ring(G.name + 'sl', 2, lambda j: P.sb([128, GT], F32, G.name + f"sl{j}"))
            self.actf(sl[:, :], accf, AF.Silu, R=[acc], W=[sl])
            sq = self.ring(G.name + 'sq', 2, lambda j: P.sb([128, GT], BF16, G.name + f"sq{j}"))
            self.actf(sq[:, :], sl[:, :], AF.Square, R=[sl], W=[sq])
            ps2 = self.next_pb('nrm', [4, 5])
            self.mm(ps2[:, 0:GT], c['onesb'][:, :], sq[:, :], R=[c['onesb'], sq], W=[ps2])
            yield 0
            rn = self.ring(G.name + 'rn', 2, lambda j: P.sb([128, GT], F32, G.name + f"rn{j}"))
            self.ts('dve', rn[:, :], ps2[:, 0:GT], 1e-6, None, ALU.add, R=[ps2], W=[rn])
            self.actf(rn[:, :], rn[:, :], AF.Sqrt, R=[rn], W=[rn])
            P.add('dve', lambda e, rn=rn: e.reciprocal(out=rn[:, :], in_=rn[:, :]), R=[rn], W=[rn])
            dst = T['qT'] if ci < 8 else T['kT']
            scl = (128 ** -0.5) if ci < 8 else 1.0
            self.stt('dve', dst[:, hh, :], sl[:, :], scl, rn[:, :], ALU.mult, ALU.mult, R=[sl, rn], W=[dst])

        prevg = None
        for ci in range(32):
            gch = chunk_gen(ci)
            try:
                next(gch)
            except StopIteration:
                gch = None
            if prevg is not None:
                for _ in prevg:
                    pass
            prevg = gch
        if prevg is not None:
            for _ in prevg:
                pass

        for n in range(NCH):
            self.dn_chunk(G, l, n, S_tiles, pre)

        OT, oT, zs = T['OT'], T['oT'], T['zs']
        for hh in range(H):
            sq = self.ring(G.name + 'sq', 2, lambda j: None)
            self.actf(sq[:, :], OT[:, hh, :], AF.Square, R=[OT], W=[sq])
            ps2 = self.next_pb('nrm', [4, 5])
            self.mm(ps2[:, 0:GT], c['onesb'][:, :], sq[:, :], R=[c['onesb'], sq], W=[ps2])
            rn = self.ring(G.name + 'rn', 2, lambda j: None)
            self.ts('dve', rn[:, :], ps2[:, 0:GT], 1.0 / 128, 1e-6, ALU.mult, ALU.add, R=[ps2], W=[rn])
            self.actf(rn[:, :], rn[:, :], AF.Sqrt, R=[rn], W=[rn])
            P.add('dve', lambda e, rn=rn: e.reciprocal(out=rn[:, :], in_=rn[:, :]), R=[rn], W=[rn])
            self.stt('dve', rn[:, :], OT[:, hh, :], c['onw'][:, l:l + 1], rn[:, :], ALU.mult, ALU.mult,
                     R=[OT, c['onw'], rn], W=[rn])
            self.tt('dve', oT[:, hh, :], rn[:, :], zs[:, hh, :], ALU.mult, R=[rn, zs], W=[oT])

        self.tok_linear_add(G, oT, 8, d['wb_dn_out'][l])

        if last_group:
            self.conv_state_out(G, l)

    def tok_linear_add(self, G, aT, nk, wsrc):
        P, T = self.P, G.T
        TP, NT = G.TP, G.NT
        x = T['x']
        for half in range(2):
            banks = [self.pb[1 + t] for t in range(NT)]
            for kc in range(nk):
                wt = self.ring('wrh', 4, lambda j: P.sb([128, 512], BF16, f"wrh{j}"))
                P.dma(wt[:, :], wsrc.ap()[kc * 128:(kc + 1) * 128, half * 512:(half + 1) * 512], R=[wsrc], W=[wt])
                for t in range(NT):
                    self.mm(banks[t][0:TP, :], aT[:, kc, t * TP:(t + 1) * TP], wt[:, :], start=(kc == 0),
                            stop=(kc == nk - 1), R=[aT, wt], W=[banks[t]])
            for t in range(NT):
                self.tt('dve', x[:, t, half * 512:(half + 1) * 512], x[:, t, half * 512:(half + 1) * 512],
                        banks[t][0:TP, :], ALU.add, R=[x, banks[t]], W=[x])

    def conv_state_out(self, G, l):
        P, c, T, d = self.P, self.c, G.T, self.d
        NSEQ = G.NSEQ
        R3 = NSEQ * 3
        carry = T['carry'][l]
        for g4 in range(6):
            co = self.ring(G.name + 'co', 2, lambda j: P.sb([R3, 4, 128], F32, G.name + f"co{j}"))
            ps = self.next_pb('lin', [1, 2, 3])
            for j in range(4):
                ci = g4 * 4 + j
                self.tr(ps[0:R3, j * 128:(j + 1) * 128], carry[:, ci, :, :].rearrange("p s r -> p (s r)"),
                        c['identf'][:, :], R=[carry, c['identf']], W=[ps])
            self.cp('act', co[:, :, :], ps[0:R3, :].rearrange("p (j n) -> p j n", j=4), R=[ps], W=[co])
            if G.NS == 1:
                dst = d['p_dn_conv']
                P.dma(dst.ap()[l][:, g4 * 512:(g4 + 1) * 512].rearrange("r (c p) -> r c p", p=128), co[:, :, :],
                      R=[co], W=[dst], q='pool')
            else:
                dst = d['s_dn_conv']
                P.dma(dst.ap()[l][:, :, g4 * 512:(g4 + 1) * 512].rearrange("s r (c p) -> (s r) c p", p=128), co[:, :, :],
                      R=[co], W=[dst], q='pool')

    def conv_state_in(self, G, l):
        P, c, T, d = self.P, self.c, G.T, self.d
        R3 = G.NSEQ * 3
        carry = T['carry'][l]
        ci_t = self.ring(G.name + 'cin', 1, lambda j: P.sb([R3, 3072], F32, G.name + "cin"))
        P.dma(ci_t[:, :], d['state_dn_conv'].ap()[l].rearrange("s r n -> (s r) n"), W=[ci_t])
        for g4 in range(6):
            ps = self.next_pb('lin', [1, 2, 3])
            for j in range(4):
                ci = g4 * 4 + j
                self.tr(ps[:, j * R3:(j + 1) * R3], ci_t[:, ci * 128:(ci + 1) * 128], c['identf'][0:R3, 0:R3],
                        R=[ci_t, c['identf']], W=[ps])
            self.cp('act', carry[:, g4 * 4:(g4 + 1) * 4, :, :].rearrange("p c s r -> p c (s r)"),
                    ps[:, 0:4 * R3].rearrange("p (j n) -> p j n", j=4), R=[ps], W=[carry])

    def dn_chunk(self, G, l, n, S_tiles, pre):
        P, c, T = self.P, self.c, G.T
        NS = G.NS
        cs = slice(n * 64, (n + 1) * 64)
        qT, kT, vT = T['qT'], T['kT'], T['vT']
        ucs, maskT, noff, same, seqind = (c[pre + k] for k in ('ucs', 'maskT', 'noff', 'same', 'seqind'))
        gtok = T['g']
        beta = T['beta']
        pb = self.pb
        nm = G.name

        def sbt(key, shape, dt=F32, nbuf=2):
            pfx = nm if NS > 1 and key in ('SG', 'gl') else 'ck'
            return self.ring(pfx + key, nbuf, lambda j: P.sb(shape, dt, pfx + key + str(j)))

        self.mm(pb[0][0:64, 0:8], ucs[:, :], gtok[:, n, :], R=[ucs, gtok], W=[pb[0]])
        self.mm(pb[0][0:64, 8:16], same[:, :], gtok[:, n, :], R=[same, gtok], W=[pb[0]])
        Gt = sbt('Gt', [64, 16])
        self.cp('act', Gt[:, :], pb[0][0:64, 0:16], R=[pb[0]], W=[Gt])
        eGd = sbt('eGd', [64, 16])
        self.tt('dve', eGd[:, 8:16], Gt[:, 8:16], Gt[:, 0:8], ALU.subtract, R=[Gt], W=[eGd])
        self.cp('dve', eGd[:, 0:8], Gt[:, 0:8], R=[Gt], W=[eGd])
        self.actf(eGd[:, :], eGd[:, :], AF.Exp, R=[eGd], W=[eGd])
        SG = sbt('SG', [64, H, NS])
        self.tt('dve', SG[:, :, :], gtok[:, n, :].unsqueeze(2).to_broadcast([64, H, NS]),
                seqind[:, :].unsqueeze(1).to_broadcast([64, H, NS]), ALU.mult, R=[gtok, seqind], W=[SG])
        self.mm(pb[0][:, 16:16 + H * NS], c['onesf'][:, :], SG[:, :, :].rearrange("p h s -> p (h s)"),
                R=[c['onesf'], SG], W=[pb[0]])
        gl = sbt('gl', [128, H, NS])
        self.actf(gl[:, :, :].rearrange("p h s -> p (h s)"), pb[0][:, 16:16 + H * NS], AF.Exp, R=[pb[0]], W=[gl])
        UG = sbt('UG', nbuf=1, shape=[64, H, 64])
        self.tt('dve', UG[:, :, :], ucs[:, :].unsqueeze(1).to_broadcast([64, H, 64]),
                gtok[:, n, :].unsqueeze(2).to_broadcast([64, H, 64]), ALU.mult, R=[ucs, gtok], W=[UG])
        for hh in range(H):
            self.mm(pb[1][:, hh * 64:(hh + 1) * 64], c['onesf'][:, :], UG[:, hh, :], R=[c['onesf'], UG], W=[pb[1]])
        eGrow = sbt('eGrow', nbuf=1, shape=[128, H, 64])
        self.actf(eGrow[:, :, :].rearrange("p h i -> p (h i)"), pb[1][:, :], AF.Exp, R=[pb[1]], W=[eGrow])
        qgT = sbt('qgT', [128, H, 64], BF16)
        self.tt('dve', qgT[:, :, :], qT[:, :, cs], eGrow[:, :, :], ALU.mult, R=[qT, eGrow], W=[qgT])
        tmp = sbt('dtmp', nbuf=1, shape=[64, H, 64])
        self.tt('dve', tmp[:, :, :], pb[1][0:64, :].rearrange("p (h i) -> p h i", h=H),
                Gt[:, 0:8].unsqueeze(2).to_broadcast([64, H, 64]), ALU.subtract, R=[pb[1], Gt], W=[tmp])
        self.tt('pool', tmp[:, :, :], tmp[:, :, :], maskT[:, :].unsqueeze(1).to_broadcast([64, H, 64]), ALU.add,
                R=[tmp, maskT], W=[tmp])
        DT = sbt('DT', nbuf=1, shape=[64, H, 64])
        self.actf(DT[:, :, :], tmp[:, :, :], AF.Exp, R=[tmp], W=[DT])
        for hh in range(H):
            self.mm(pb[2][0:64, hh * 64:(hh + 1) * 64], kT[:, hh, cs], kT[:, hh, cs], R=[kT], W=[pb[2]])
        for hh in range(H):
            self.mm(pb[3][0:64, hh * 64:(hh + 1) * 64], kT[:, hh, cs], qT[:, hh, cs], R=[kT, qT], W=[pb[3]])
        aqkT = sbt('aqkT', [64, H, 64], BF16)
        self.tt('dve', aqkT[:, :, :], DT[:, :, :], pb[3][0:64, :].rearrange("p (h i) -> p h i", h=H), ALU.mult,
                R=[DT, pb[3]], W=[aqkT])
        nbo = sbt('nbo', nbuf=1, shape=[64, H, 64])
        self.tt('pool', nbo[:, :, :], beta[:, n, :].unsqueeze(2).to_broadcast([64, H, 64]),
                noff[:, :].unsqueeze(1).to_broadcast([64, H, 64]), ALU.mult, R=[beta, noff], W=[nbo])
        X = sbt('X', [64, H, 64])
        self.tt('dve', X[:, :, :], pb[2][0:64, :].rearrange("p (h i) -> p h i", h=H), nbo[:, :, :], ALU.mult,
                R=[pb[2], nbo], W=[X])
        self.tt('dve', X[:, :, :], X[:, :, :], DT[:, :, :], ALU.mult, R=[X, DT], W=[X])
        Xb = sbt('Xb', [64, H, 64], BF16)
        self.cp('pool', Xb[:, :, :], X[:, :, :], R=[X], W=[Xb])
        ptz = self.pt[0]
        for hh in range(H):
            self.tr(ptz[0:64, hh * 64:(hh + 1) * 64], Xb[:, hh, :], c['identb'][0:64, 0:64], R=[Xb, c['identb']], W=[ptz])
        Z = sbt('Zb', [64, H, 64], BF16)
        self.cp('act', Z[:, :, :].rearrange("p h i -> p (h i)"), ptz[0:64, 0:512], R=[ptz], W=[Z])
        Pm = sbt('Pm', [64, H, 64])
        self.tt('pool', Pm[:, :, :], X[:, :, :], c['identf'][0:64, 0:64].unsqueeze(1).to_broadcast([64, H, 64]), ALU.add,
                R=[X, c['identf']], W=[Pm])
        Pb = sbt('Pb', [64, H, 64], BF16)
        self.cp('act', Pb[:, :, :], Pm[:, :, :], R=[Pm], W=[Pb])
        Y = Xb
        for lv in range(G.nlev):
            last = (lv == G.nlev - 1)
            if not last:
                for hh in range(H):
                    self.mm(pb[5][0:64, hh * 64:(hh + 1) * 64], Z[:, hh, :], Y[:, hh, :], R=[Z, Y], W=[pb[5]])
            for hh in range(H):
                self.mm(pb[4][0:64, hh * 64:(hh + 1) * 64], Y[:, hh, :], Z[:, hh, :], R=[Z, Y], W=[pb[4]])
            Zn = sbt('Zb', [64, H, 64], BF16)
            self.cp('act', Zn[:, :, :].rearrange("p h i -> p (h i)"), pb[4][0:64, :], R=[pb[4]], W=[Zn])
            if not last:
                Yn = sbt('Xb', [64, H, 64], BF16)
                self.cp('dve', Yn[:, :, :].rearrange("p h i -> p (h i)"), pb[5][0:64, :], R=[pb[5]], W=[Yn])
                Y = Yn
            Z = Zn
            for hh in range(H):
                self.mm(pb[2][0:64, hh * 64:(hh + 1) * 64], Z[:, hh, :], Pb[:, hh, :], R=[Z, Pb], W=[pb[2]])
            Pn = sbt('Pm', [64, H, 64])
            self.tt('dve', Pn[:, :, :].rearrange("p h i -> p (h i)"), Pm[:, :, :].rearrange("p h i -> p (h i)"),
                    pb[2][0:64, :], ALU.add, R=[Pm, pb[2]], W=[Pn])
            Pm = Pn
            Pb = sbt('Pb', [64, H, 64], BF16)
            self.cp('pool', Pb[:, :, :], Pm[:, :, :], R=[Pm], W=[Pb])
        vtok = sbt('vtok', [64, H, 128], BF16, 1)
        ktok = sbt('ktok', [64, H, 128], BF16, 1)
        for src, dst, pt in ((vT, vtok, self.pt[0]), (kT, ktok, self.pt[1])):
            for hh in range(H):
                self.tr(pt[0:64, hh * 128:(hh + 1) * 128], src[:, hh, cs], c['identb'][:, :], R=[src, c['identb']], W=[pt])
            self.cp('act', dst[:, :, :].rearrange("p h d -> p (h d)"), pt[0:64, :], R=[pt], W=[dst])
        kdec = sbt('kdec', [64, H, 128], BF16)
        self.tt('pool', kdec[:, :, :], ktok[:, :, :], eGd[:, 8:16].unsqueeze(2).to_broadcast([64, H, 128]), ALU.mult,
                R=[ktok, eGd], W=[kdec])

        OT = T['OT']
        if NS == 1 and getattr(G, 'Sall', None) is not None:
            Sf8, Sb8 = G.Sall[l]
            pk = (pb[0], pb[1])
            for hh in range(H):
                self.mm(pk[hh // 4][0:64, (hh % 4) * 128:(hh % 4 + 1) * 128], kT[:, hh, cs], Sb8[:, hh, :], R=[kT, Sb8],
                        W=[pk[hh // 4]])
            r = sbt('rB', [64, H, 128], BF16, 1)
            rf = sbt('rBf', [64, H, 128], F32, 1)
            for b2 in range(2):
                self.tt('dve', rf[:, 4 * b2:4 * b2 + 4, :], pk[b2][0:64, :].rearrange("p (h d) -> p h d", h=4),
                        eGd[:, 4 * b2:4 * b2 + 4].unsqueeze(2).to_broadcast([64, 4, 128]), ALU.mult, R=[pk[b2], eGd], W=[rf])
            self.tt('pool', r[:, :, :], vtok[:, :, :], rf[:, :, :], ALU.subtract, R=[vtok, rf], W=[r])
            pu = (pb[2], pb[3])
            for hh in range(H):
                self.mm(pu[hh // 4][0:64, (hh % 4) * 128:(hh % 4 + 1) * 128], Pb[:, hh, :], r[:, hh, :], R=[Pb, r],
                        W=[pu[hh // 4]])
            U = sbt('UB', [64, H, 128], BF16, 1)
            for b2 in range(2):
                self.tt('dve', U[:, 4 * b2:4 * b2 + 4, :], pu[b2][0:64, :].rearrange("p (h d) -> p h d", h=4),
                        beta[:, n, 4 * b2:4 * b2 + 4].unsqueeze(2).to_broadcast([64, 4, 128]), ALU.mult, R=[pu[b2], beta], W=[U])
            po = pb[4]
            for hh in range(H):
                self.mm(po[:, hh * 64:(hh + 1) * 64], Sb8[:, hh, :], qgT[:, hh, :], start=True, stop=False, R=[Sb8, qgT], W=[po])
                self.mm(po[:, hh * 64:(hh + 1) * 64], U[:, hh, :], aqkT[:, hh, :], start=False, stop=True, R=[U, aqkT], W=[po])
            self.cp('act', OT[:, :, cs], po[:, :].rearrange("p (h i) -> p h i", h=H), R=[po], W=[OT])
            psn = (pb[5], pb[0])
            for hh in range(H):
                self.mm(psn[hh // 4][:, (hh % 4) * 128:(hh % 4 + 1) * 128], kdec[:, hh, :], U[:, hh, :], R=[kdec, U],
                        W=[psn[hh // 4]])
            for b2 in range(2):
                hs4 = slice(4 * b2, 4 * b2 + 4)
                self.tt('dve' if b2 == 0 else 'pool', Sf8[:, hs4, :], Sf8[:, hs4, :], gl[:, hs4, :].to_broadcast([128, 4, 128]), ALU.mult,
                        R=[Sf8, gl], W=[Sf8])
            for b2 in range(2):
                hs4 = slice(4 * b2, 4 * b2 + 4)
                self.tt('dve', Sf8[:, hs4, :], Sf8[:, hs4, :], psn[b2][:, :].rearrange("p (h d) -> p h d", h=4), ALU.add,
                        R=[Sf8, psn[b2]], W=[Sf8])
            self.cp('act', Sb8[:, :, :], Sf8[:, :, :], R=[Sf8], W=[Sb8])
            return
        for hh in range(H):
            Sf, Sb = S_tiles(hh)
            if NS > 1:
                cm = c[pre + 'colmask']
                kTm = sbt('kTm', [128, NS, 64], BF16)
                self.tt('pool', kTm[:, :, :], kT[:, hh, cs].unsqueeze(1).to_broadcast([128, NS, 64]),
                        cm[:, :].rearrange("p (s i) -> p s i", s=NS), ALU.mult, R=[kT, cm], W=[kTm])
                qgm = sbt('qgm', [128, NS, 64], BF16)
                self.tt('pool', qgm[:, :, :], qgT[:, hh, :].unsqueeze(1).to_broadcast([128, NS, 64]),
                        cm[:, :].rearrange("p (s i) -> p s i", s=NS), ALU.mult, R=[qgT, cm], W=[qgm])
                kdm = sbt('kdm', [64, NS, 128], BF16)
                self.tt('pool', kdm[:, :, :], kdec[:, hh, :].unsqueeze(1).to_broadcast([64, NS, 128]),
                        seqind[:, :].unsqueeze(2).to_broadcast([64, NS, 128]), ALU.mult, R=[kdec, seqind], W=[kdm])
            pks = pb[0]
            for s in range(NS):
                lhs = kTm[:, s, :] if NS > 1 else kT[:, hh, cs]
                self.mm(pks[0:64, 0:128], lhs, Sb[:, s, :], start=(s == 0), stop=(s == NS - 1),
                        R=[kTm if NS > 1 else kT, Sb], W=[pks])
            r = sbt('r', [64, 128], BF16)
            rf1 = sbt('rf1', [64, 128])
            self.ts('dve', rf1[:, :], pks[0:64, 0:128], eGd[:, hh:hh + 1], None, ALU.mult, R=[pks, eGd], W=[rf1])
            self.tt('dve', r[:, :], vtok[:, hh, :], rf1[:, :], ALU.subtract, R=[vtok, rf1], W=[r])
            pu = pb[1]
            self.mm(pu[0:64, 0:128], Pb[:, hh, :], r[:, :], R=[Pb, r], W=[pu])
            U = sbt('U', [64, 128], BF16)
            self.ts('dve', U[:, :], pu[0:64, 0:128], beta[:, n, hh:hh + 1], None, ALU.mult, R=[pu, beta], W=[U])
            po = pb[3]
            for s in range(NS):
                rhs = qgm[:, s, :] if NS > 1 else qgT[:, hh, :]
                self.mm(po[:, 0:64], Sb[:, s, :], rhs, start=(s == 0), stop=False, R=[Sb, qgm if NS > 1 else qgT], W=[po])
            self.mm(po[:, 0:64], U[:, :], aqkT[:, hh, :], start=False, stop=True, R=[U, aqkT], W=[po])
            self.cp('act', OT[:, hh, cs], po[:, 0:64], R=[po], W=[OT])
            for s0 in range(0, NS, 4):
                psn = pb[5] if (s0 // 4) % 2 == 0 else pb[4]
                ns = min(4, NS - s0)
                for s in range(s0, s0 + ns):
                    lhs = kdm[:, s, :] if NS > 1 else kdec[:, hh, :]
                    self.mm(psn[:, (s - s0) * 128:(s - s0 + 1) * 128], lhs, U[:, :], R=[kdm if NS > 1 else kdec, U], W=[psn])
                self.tt('dve', Sf[:, s0:s0 + ns, :], Sf[:, s0:s0 + ns, :],
                        gl[:, hh, s0:s0 + ns].unsqueeze(2).to_broadcast([128, ns, 128]), ALU.mult, R=[Sf, gl], W=[Sf])
                self.tt('dve', Sf[:, s0:s0 + ns, :], Sf[:, s0:s0 + ns, :],
                        psn[:, 0:ns * 128].rearrange("p (s d) -> p s d", s=ns), ALU.add, R=[Sf, psn], W=[Sf])
                self.cp('act', Sb[:, s0:s0 + ns, :], Sf[:, s0:s0 + ns, :], R=[Sf], W=[Sb])

    def ffn(self, G, l):
        P, c, T, d = self.P, self.c, G.T, self.d
        GT = G.GT
        self.norm_T(G, d['norm_ffn'].ap()[l])
        xnT, actT = T['xnT'], T['actT']
        for ci in range(22):
            wg = self.load_wchunk(d['wc_ffn_in'][l].ap()[ci])
            wu = self.load_wchunk(d['wc_ffn_in'][l].ap()[22 + ci])
            pg = self.next_pb('ffg', [1, 2])
            pu = self.next_pb('ffu', [3, 4])
            for kc in range(8):
                self.mm(pg[:, 0:GT], wg[:, kc, :], xnT[:, kc, :], start=(kc == 0), stop=(kc == 7), R=[wg, xnT], W=[pg])
            for kc in range(8):
                self.mm(pu[:, 0:GT], wu[:, kc, :], xnT[:, kc, :], start=(kc == 0), stop=(kc == 7), R=[wu, xnT], W=[pu])
            sg = self.ring(G.name + 'sl', 2, lambda j: P.sb([128, GT], F32, G.name + f"sl{j}"))
            self.actf(sg[:, :], pg[:, 0:GT], AF.Silu, R=[pg], W=[sg])
            self.tt('dve', actT[:, ci, :], sg[:, :], pu[:, 0:GT], ALU.mult, R=[sg, pu], W=[actT])
        self.tok_linear_add(G, actT, 22, d['wb_ffn_out'][l])

    def shared_rows(self, G, dst_kv, row0, win_cb):
        P, c, T, d = self.P, self.c, G.T, self.d
        TP, NT = G.TP, G.NT
        self.norm_T(G, d['norm_kv'].ap())
        xnT = T['xnT']
        for third in range(3):
            wt = self.ring('kvw', 1, lambda j: P.sb([128, 8, 512], BF16, f"kvw{j}"))
            P.dma(wt[:, :, :], d['wb_kv'].ap()[:, third * 512:(third + 1) * 512].rearrange("(kc p) n -> p kc n", p=128),
                  R=[d['wb_kv']], W=[wt])
            for t in range(NT):
                ps = self.next_pb('lin', [1, 2, 3])
                for kc in range(8):
                    self.mm(ps[0:TP, :], xnT[:, kc, t * TP:(t + 1) * TP], wt[:, kc, :], start=(kc == 0), stop=(kc == 7),
                            R=[xnT, wt], W=[ps])
                rp = self.ring(G.name + 'rowp', 2, lambda j: P.sb([TP, 512], F32, G.name + f"rowp{j}"))
                self.cp('act', rp[:, :], ps[0:TP, :], R=[ps], W=[rp])
                if third < 2:
                    P.dma(dst_kv.ap()[row0 + t * TP: row0 + (t + 1) * TP, third * 512:(third + 1) * 512], rp[:, :],
                          R=[rp], W=[dst_kv], q='pool')
                else:
                    win_cb(t, rp)

    def phase(self):
        from contextlib import ExitStack
        b = self

        class _Ph:
            def __enter__(self_):
                self_.st = ExitStack()
                b.P.stack = self_.st
                b.rr = {}
                return self_

            def __exit__(self_, *a):
                b.P.barrier()
                b.P.stack = None
                b.rr = {}
                self_.st.close()
                return False
        return _Ph()

    def build(self):
        P, cfg = self.P, self.cfg
        self.declare_io()
        d = self.d
        nsa = cfg.get('nsa', True)
        if nsa:
            self.nsa_declare()
        self.consts()
        if nsa:
            self.nsa_consts()
        with self.phase():
            self.convert_weights()
        if nsa:
            with self.phase():
                self.nsa_tables()
        n_dn = cfg['n_dn']
        if cfg.get('prompt', True):
          with self.phase():
            G = Ctx('p', 128, 4, 1, 1, 5)
            self.alloc_ctx(G)
            T = G.T
            Sp = [(P.sb([128, H, 128], F32, f"Sp{l}"), P.sb([128, H, 128], BF16, f"Sbp{l}")) for l in range(2)]
            G.Sall = Sp
            for l in range(2):
                self.memset('pool', Sp[l][0][:, :, :], 0.0, W=[Sp[l][0]])
                self.memset('pool', Sp[l][1][:, :, :], 0.0, W=[Sp[l][1]])
                self.memset('pool', T['carry'][l][:, :, :, :], 0.0, W=[T['carry'][l]])
            ngrp = cfg['SEQ'] // G.GT
            for g in range(ngrp):
                P.dma(T['x'][:, :, :], d['x_prompt'].ap()[g * 512:(g + 1) * 512, :].rearrange("(t p) n -> p t n", p=128),
                      W=[T['x']])
                for l in range(n_dn):
                    self.dn_mixer(G, l, None, last_group=(g == ngrp - 1))
                    self.ffn(G, l)
                P.dma(d['x2_p'].ap()[g * 512:(g + 1) * 512, :].rearrange("(t p) n -> p t n", p=128), T['x'][:, :, :],
                      R=[T['x']], W=[d['x2_p']], q='pool')

                def win_cb(t, rp, g=g):
                    r0 = g * 512 + t * 128 - (cfg['SEQ'] - 512)
                    if nsa:
                        P.dma(d['pwin_d'].ap()[g * 512 + t * 128:g * 512 + (t + 1) * 128, :], rp[:, :], R=[rp],
                              W=[d['pwin_d']], q='pool')
                    if r0 >= 0:
                        P.dma(d['p_win_kv'].ap()[r0:r0 + 128, :], rp[:, :], R=[rp], W=[d['p_win_kv']], q='pool')
                self.shared_rows(G, d['p_kv_rows'], g * 512, win_cb)
            for l in range(2):
                P.dma(d['p_dn_S'].ap()[l].rearrange("h k v -> k h v"), Sp[l][0][:, :, :], R=[Sp[l][0]], W=[d['p_dn_S']], q='pool')
        if nsa and cfg.get('prompt', True) and cfg.get('nsa_stop', 99) > 1:
            try:
                self.nsa_prompt()
            except _Stop:
                pass
        if cfg.get('sample', True):
          with self.phase():
            G = Ctx('s', 64, 1, 16, 16, 1)
            self.alloc_ctx(G)
            T = G.T
            P.dma(T['x'][:, 0, :], d['x_sample'].ap(), W=[T['x']])
            Ss = P.sb([128, 16, 128], F32, "Ss")
            Ssb = P.sb([128, 16, 128], BF16, "Ssb")
            for l in range(n_dn):
                self.conv_state_in(G, l)
                cur = [None]

                def S_tiles(hh, l=l, cur=cur):
                    if cur[0] != hh:
                        if cur[0] is not None:
                            P.dma(d['s_dn_S'].ap()[l, :, cur[0]].rearrange("s k v -> k s v"), Ss[:, :, :], R=[Ss],
                                  W=[d['s_dn_S']], q='pool')
                        P.dma(Ss[:, :, :], d['state_dn_S'].ap()[l, :, hh].rearrange("s k v -> k s v"), W=[Ss])
                        self.cp('act', Ssb[:, :, :], Ss[:, :, :], R=[Ss], W=[Ssb])
                        cur[0] = hh
                    return Ss, Ssb
                self.dn_mixer(G, l, S_tiles, last_group=True)
                P.dma(d['s_dn_S'].ap()[l, :, cur[0]].rearrange("s k v -> k s v"), Ss[:, :, :], R=[Ss], W=[d['s_dn_S']],
                      q='pool')
                self.ffn(G, l)
            P.dma(d['x2_s'].ap(), T['x'][:, 0, :], R=[T['x']], W=[d['x2_s']], q='pool')
            for s4 in range(4):
                P.dma(d['s_win_kv'].ap()[s4 * 4:(s4 + 1) * 4, 0:508, :], d['state_win_kv'].ap()[s4 * 4:(s4 + 1) * 4, 4:512, :],
                      W=[d['s_win_kv']], q='sp')

            def win_cb_s(t, rp):
                for sq in range(16):
                    P.dma(d['s_win_kv'].ap()[sq, 508:512, :], rp[4 * sq:4 * sq + 4, :], R=[rp],
                          W=[d['s_win_kv']], q='pool')
            self.shared_rows(G, d['s_kv_rows'], 0, win_cb_s)
        if nsa and cfg.get('sample', True):
            self.nsa_sample()
        P.emit()
        return self.nc


_CONST_CACHE = {}


def const_inputs():
    if not _CONST_CACHE:
        for pre, NS, TS in (('cp_', 1, 64), ('cs_', 16, 4)):
            for k, v in host_consts(NS, TS).items():
                _CONST_CACHE[pre + k] = v
    return _CONST_CACHE


def make_in_maps(inp, cfg, n_cores=8):
    SEQ = cfg['SEQ']
    cst = const_inputs()
    maps = []
    shared = {k: np.ascontiguousarray(inp[k]) for k in
              ('norm_mix', 'norm_ffn', 'norm_kv', 'norm_final', 'ffn_w_in', 'ffn_w_out', 'dn_w_in', 'dn_conv_w',
               'dn_A_log', 'dn_dt_bias', 'dn_out_norm', 'dn_w_out', 'nsa_w_kv')}
    nsa = cfg.get('nsa', True)
    if nsa:
        shared['nsa_w_in'] = np.ascontiguousarray(inp['nsa_w_in'])
        shared['nsa_w_out'] = np.ascontiguousarray(inp['nsa_w_out'])
        shared['nsa_cmp_pos_w'] = np.ascontiguousarray(inp['nsa_cmp_pos_w']).reshape(2, 32, 256)
        shared['nsa_w_cmp'] = np.ascontiguousarray(inp['nsa_w_cmp'])
        shared['rel_bias'] = np.ascontiguousarray(inp['rel_bias'])
        shared['cache_kv'] = np.ascontiguousarray(inp['cache_kv']).reshape(2560 * 128, 1024)
        nsc = [nsa_host_consts(0), nsa_host_consts(1)]
    for c in range(n_cores):
        b = c // 2
        m = dict(shared)
        m.update(cst)
        m['x_prompt'] = np.ascontiguousarray(inp['x_prompt'][b, :SEQ])
        sl = slice(16 * c, 16 * c + 16)
        m['x_sample'] = np.ascontiguousarray(inp['x_sample'][sl]).reshape(64, D)
        m['state_dn_S'] = np.ascontiguousarray(inp['state_dn_S'][:, sl])
        m['state_dn_conv'] = np.ascontiguousarray(inp['state_dn_conv'][:, sl])
        m['state_win_kv'] = np.ascontiguousarray(inp['state_win_kv'][sl]).reshape(16, 512, 512)
        if nsa:
            m.update(nsc[c % 2])
            m['page_table'] = np.ascontiguousarray(inp['page_table'][sl]).reshape(1, 256).astype(np.int32)
        maps.append(m)
    return maps


def kernel(**inp):
    cfg = dict(SEQ=4096, n_dn=2)
    b = Builder(cfg)
    nc = b.build()
    maps = make_in_maps(inp, cfg)
    res = run_bass_kernel_spmd(nc, maps, core_ids=list(range(8)))
    R = res.results
    f32 = np.float32
    y_prompt = np.zeros((4, 4096, D), f32)
    for c in range(8):
        yp = R[c]['y_p'].reshape(16, 128, D)
        y_prompt[c // 2].reshape(32, 128, D)[c % 2::2] = yp
    y_sample = np.concatenate([R[c]['y_s'].reshape(16, 4, D) for c in range(8)], axis=0).astype(f32)
    p_dn_S = np.stack([R[2 * b]['p_dn_S'] for b in range(4)], axis=1)
    p_dn_conv = np.stack([R[2 * b]['p_dn_conv'] for b in range(4)], axis=1)
    p_kv_rows = np.stack([R[2 * b]['p_kv_rows'] for b in range(4)], axis=0).reshape(4, 4096, 4, 4, 64)
    p_win_kv = np.stack([R[2 * b]['p_win_kv'] for b in range(4)], axis=0).reshape(4, 512, 2, 4, 64)
    s_dn_S = np.concatenate([R[c]['s_dn_S'] for c in range(8)], axis=1)
    s_dn_conv = np.concatenate([R[c]['s_dn_conv'] for c in range(8)], axis=1)
    s_kv_rows = np.concatenate([R[c]['s_kv_rows'].reshape(16, 4, 4, 4, 64) for c in range(8)], axis=0)
    s_win_kv = np.concatenate([R[c]['s_win_kv'].reshape(16, 512, 2, 4, 64) for c in range(8)], axis=0)
    return (y_prompt, y_sample, p_dn_S.astype(f32), p_dn_conv.astype(f32), p_kv_rows.astype(f32),
            p_win_kv.astype(f32), s_dn_S.astype(f32), s_dn_conv.astype(f32), s_kv_rows.astype(f32),
            s_win_kv.astype(f32))


def _bucket_np(d):
    n = np.maximum(d, 0)
    nf = np.maximum(n, 1).astype(np.float32)
    large = 16 + (np.log(nf / np.float32(16)) / np.float32(math.log(64.0)) * np.float32(16)).astype(np.int32)
    large = np.minimum(large, 31)
    return np.where(n < 16, n, large)


def _onehot(d, valid):
    b = np.where(valid, _bucket_np(d), 32)
    oh = np.zeros((33, d.shape[0]), np.float32)
    oh[b, np.arange(d.shape[0])] = 1.0
    return oh


def nsa_host_consts(par):
    c = {}
    t = np.arange(1280)
    d = t - 255 + 128 * par
    c['oh_sel'] = _onehot(d, d >= 0)
    t = np.arange(1024)
    d = t - 255 + 128 * par
    c['oh_win'] = _onehot(d, (d >= 0) & (d < 512))
    r = np.arange(16)[:, None]
    w = np.arange(512)[None, :]
    d = (16 * (247 - w + 8 * par) + r - 31).reshape(-1)
    c['oh_cmp'] = _onehot(d, d >= 0)
    tt = np.arange(4)[:, None]
    cc = np.arange(128)[None, :]
    d = (2017 + tt - 16 * cc).reshape(-1)
    c['oh_cs'] = _onehot(d, d >= 0)
    x = np.arange(2304)
    d = x - 127
    c['oh_ss'] = _onehot(d, d >= 0)
    x = np.arange(768)
    d = x - 127
    c['oh_ws'] = _onehot(d, (d >= 0) & (d < 512))
    blk = np.arange(64)[None, None, :]
    qpos = (128 * (2 * np.arange(16)[:, None, None] + par) + np.arange(128)[None, :, None])
    cur = qpos // 64
    forced = (blk == 0) | (blk == cur) | (blk == cur - 1)
    valid = blk * 64 <= qpos
    c['selmul_p'] = np.ascontiguousarray(np.where(valid & ~forced, 1.0, 0.0).astype(np.float32).transpose(1, 0, 2))
    c['seladd_p'] = np.ascontiguousarray(np.where(forced, 1e4, np.where(valid, 0.0, -1.0)).astype(np.float32).transpose(1, 0, 2))
    blk = np.arange(64)[None, :]
    qpos = 2048 + np.arange(4)[:, None]
    cur = qpos // 64
    exists = blk < 33
    forced = ((blk == 0) | (blk == cur) | (blk == cur - 1)) & exists
    valid = (blk * 64 <= qpos) & exists
    c['selmul_s'] = np.where(valid & ~forced, 1.0, 0.0).astype(np.float32)
    c['seladd_s'] = np.where(forced, 1e4, np.where(valid, 0.0, np.where(exists, -1.0, -2.0))).astype(np.float32)
    k = np.arange(4096)[None, :]
    c['expE'] = (k // 64 == np.arange(64)[:, None]).astype(np.float32)
    sel8 = np.zeros((128, 8), np.float32)
    sel8[np.arange(128), np.arange(128) // 16] = 1.0
    c['sel8'] = sel8
    c['antiI'] = np.ascontiguousarray(np.eye(128, dtype=np.float32)[::-1])
    c['parf'] = np.tile(np.array([[float(par), 1.0 - float(par)]], np.float32), (128, 1))
    c['iota'] = np.arange(128, dtype=np.float32).reshape(128, 1)
    return c


def _nsa_declare(self):
    P, cfg, d = self.P, self.cfg, self.d
    SEQ = cfg['SEQ']

    def inp(name, shape, dt=F32):
        d[name] = P.dram(name, shape, dt, kind="ExternalInput")

    inp('nsa_w_in', [2, D, 1072])
    inp('nsa_w_out', [2, D, D])
    inp('nsa_cmp_pos_w', [2, 32, 256])
    inp('nsa_w_cmp', [2, 4, 64, 64])
    inp('rel_bias', [32, 16])
    inp('cache_kv', [2560 * 128, 1024])
    inp('page_table', [1, 256], I32)
    for nm, shp in (('oh_sel', [33, 1280]), ('oh_win', [33, 1024]), ('oh_cmp', [33, 8192]), ('oh_cs', [33, 512]),
                    ('oh_ss', [33, 2304]), ('oh_ws', [33, 768]), ('selmul_p', [128, 16, 64]), ('seladd_p', [128, 16, 64]),
                    ('selmul_s', [4, 64]), ('seladd_s', [4, 64]), ('expE', [64, 4096]), ('sel8', [128, 8]),
                    ('antiI', [128, 128]), ('parf', [128, 2]), ('iota', [128, 1])):
        inp(nm, shp)
    d['y_p'] = P.dram('y_p', [SEQ // 2, D], F32, kind="ExternalOutput")
    d['y_s'] = P.dram('y_s', [64, D], F32, kind="ExternalOutput")
    d['wc_nsa_in'] = [P.dram(f'wc_nsa_in{j}', [8, 128, 8, 128], BF16) for j in range(2)]
    d['wb_nsa_out'] = [P.dram(f'wb_nsa_out{j}', [D, D], BF16) for j in range(2)]
    for nm, ln in (('t_sel', 1280), ('t_win', 1024), ('t_cmp', 8192), ('t_cs', 512), ('t_ss', 2304), ('t_ws', 768)):
        d[nm] = P.dram(nm, [16, ln], F32)
    d['BTd'] = P.dram('BTd', [4, 15, 128, 512], F32)
    NT = SEQ // 128
    d['pwin_d'] = P.dram('pwin_d', [SEQ, 512], F32)
    d['kselT_p'] = P.dram('kselT_p', [4, 128, SEQ], BF16)
    d['vsel_p'] = P.dram('vsel_p', [NT, 128, 4, 66], BF16)
    d['kwinT_p'] = P.dram('kwinT_p', [4, 128, SEQ], BF16)
    d['vwin_p'] = P.dram('vwin_p', [NT, 128, 4, 66], BF16)
    d['kselT_s'] = P.dram('kselT_s', [16, 4, 128, 17 * 128], BF16)
    d['vsel_s'] = P.dram('vsel_s', [16, 17, 128, 4, 66], BF16)
    d['kwinT_s'] = P.dram('kwinT_s', [16, 4, 128, 5 * 128], BF16)
    d['vwin_s'] = P.dram('vwin_s', [16, 5, 128, 4, 66], BF16)


def _nsa_consts(self):
    P, d, c = self.P, self.d, self.c
    antiI = P.sb([128, 128], F32, "antiI")
    P.dma(antiI[:, :], d['antiI'].ap(), W=[antiI])
    sel8 = P.sb([128, 8], BF16, "sel8")
    P.dma(sel8[:, :], d['sel8'].ap(), W=[sel8], q='pool')
    parf = P.sb([128, 2], F32, "parf")
    P.dma(parf[:, :], d['parf'].ap(), W=[parf])
    tabx = P.sb([33, 16], F32, "tabx")
    r31 = P.sb([32, 16], F32, "r31")
    P.dma(tabx[0:32, :], d['rel_bias'].ap(), W=[tabx])
    P.dma(r31[:, :], d['rel_bias'].ap()[31].partition_broadcast(32), W=[r31])
    self.tt('dve', tabx[0:32, :], tabx[0:32, :], r31[:, :], ALU.subtract, R=[tabx, r31], W=[tabx])
    self.memset('pool', tabx[32:33, :], NEG, W=[tabx])
    c.update(antiI=antiI, sel8=sel8, parf=parf, tabx=tabx)
    wg = P.sb([128, 2, 8, 48], BF16, "wg")
    for j in range(2):
        P.dma(wg[:, j, :, :], d['nsa_w_in'].ap()[j, :, 1024:1072].rearrange("(kc p) n -> p kc n", p=128), W=[wg], q='pool')
    c['wg'] = wg


def _nsa_late_consts(self):
    P, d, c = self.P, self.d, self.c
    if 'wlo' in c:
        return
    wlo = P.sb([128, 512], F32, "wlo")
    whi = P.sb([128, 512], F32, "whi")
    for a in range(8):
        P.dma(wlo[16 * a:16 * a + 16, :].rearrange("r (c n) -> r c n", c=2),
              d['nsa_cmp_pos_w'].ap()[:, 0:16, :].rearrange("c r n -> r c n"), W=[wlo])
        P.dma(whi[16 * a:16 * a + 16, :].rearrange("r (c n) -> r c n", c=2),
              d['nsa_cmp_pos_w'].ap()[:, 16:32, :].rearrange("c r n -> r c n"), W=[whi])
    wck = P.sb([128, 4, 128], BF16, "wck")
    wcv = P.sb([128, 4, 64], BF16, "wcv")
    for half in range(2):
        for dup in range(2):
            P.dma(wck[half * 64:(half + 1) * 64, :, dup * 64:(dup + 1) * 64],
                  d['nsa_w_cmp'].ap()[0].rearrange("k d e -> d k e"), W=[wck], q='pool')
        P.dma(wcv[half * 64:(half + 1) * 64, :, :], d['nsa_w_cmp'].ap()[1].rearrange("k d e -> d k e"), W=[wcv], q='pool')
    expE = P.sb([64, 4096], BF16, "expE")
    P.dma(expE[:, :], d['expE'].ap(), W=[expE], q='pool')
    c.update(wlo=wlo, whi=whi, wck=wck, wcv=wcv, expE=expE)


Builder.nsa_late_consts = _nsa_late_consts


def _nsa_tables(self):
    P, d, c = self.P, self.d, self.c
    for oh, dst, ln in (('oh_sel', 't_sel', 1280), ('oh_win', 't_win', 1024), ('oh_cmp', 't_cmp', 8192),
                        ('oh_cs', 't_cs', 512), ('oh_ss', 't_ss', 2304), ('oh_ws', 't_ws', 768)):
        for off in range(0, ln, 512):
            n = min(512, ln - off)
            ot = self.ring('oht', 2, lambda j: P.sb([33, 512], F32, f"oht{j}"))
            P.dma(ot[:, 0:n], d[oh].ap()[:, off:off + n], W=[ot])
            ps = self.next_pb('lin', [1, 2, 3])
            self.mm(ps[0:16, 0:n], c['tabx'][:, :], ot[:, 0:n], R=[c['tabx'], ot], W=[ps])
            tb = self.ring('tbo', 2, lambda j: P.sb([16, 512], F32, f"tbo{j}"))
            self.cp('act', tb[:, 0:n], ps[0:16, 0:n], R=[ps], W=[tb])
            P.dma(d[dst].ap()[:, off:off + n], tb[:, 0:n], R=[tb], W=[d[dst]], q='pool')
    if self.cfg.get('prompt', True):
        for kvh in range(4):
            for idx in range(15):
                tab, ln, e = ('t_sel', 1280, idx - 1) if idx < 9 else ('t_win', 1024, idx - 10)
                tr_ = self.ring('trv', 2, lambda j: P.sb([128, 4, 128], F32, f"trv{j}"))
                src = bass.AP(d[tab].h, 4 * kvh * ln + 128 * (e + 1), [[1, 128], [ln, 4], [1, 128]])
                P.dma(tr_[:, :, :], src, R=[d[tab]], W=[tr_])
                ps = self.next_pb('lin', [1, 2, 3])
                self.mm(ps[:, :], c['antiI'][:, :], tr_[:, :, :].rearrange("p g q -> p (g q)"), R=[c['antiI'], tr_], W=[ps])
                fl = self.ring('flp', 2, lambda j: P.sb([128, 512], F32, f"flp{j}"))
                self.cp('act', fl[:, :], ps[:, :], R=[ps], W=[fl])
                P.dma(d['BTd'].ap()[kvh, idx], fl[:, :], R=[fl], W=[d['BTd']], q='pool')


def _ctx_rows(self, rows, P_idx, lohi, kT_dst, v_dst, kw_dst, vw_dst, has_cmpsel=True, has_win=True, win_rows=None):
    P, c = self.P, self.c
    if has_cmpsel:
        if lohi is not None:
            alo = self.ring('alo', 2, lambda j: P.sb([128, 512], BF16, f"alo{j}"))
            ahi = self.ring('ahi', 2, lambda j: P.sb([128, 512], BF16, f"ahi{j}"))
            self.tt('pool', alo[:, :], rows[:, 0:512], c['wlo'][:, :], ALU.mult, R=[rows, c['wlo']], W=[alo])
            self.tt('dve', ahi[:, :], rows[:, 0:512], c['whi'][:, :], ALU.mult, R=[rows, c['whi']], W=[ahi])
            ps = self.next_pb('ctxp', [0])
            for lh, a in enumerate((alo, ahi)):
                for ch in range(4):
                    self.mm(ps[:, (lh * 4 + ch) * 8:(lh * 4 + ch + 1) * 8], a[:, ch * 128:(ch + 1) * 128], c['sel8'][:, :],
                            R=[a, c['sel8']], W=[ps])
            self.cp('act', lohi[:, :, :, 8 * P_idx:8 * P_idx + 8], ps[:, 0:64].rearrange("p (l c m) -> p l c m", l=2, c=4),
                    R=[ps], W=[lohi])
        for (col0, kdst, vdst) in ((512, kT_dst, v_dst),):
            _kv_tile(self, rows, col0, kdst, vdst)
    if has_win:
        wt, wc0 = win_rows
        _kv_tile(self, wt, wc0, kw_dst, vw_dst)


def _kv_tile(self, rows, col0, kdst, vdst):
    P, c = self.P, self.c
    kd = self.ring('kd', 2, lambda j: P.sb([128, 4, 2, 64], BF16, f"kd{j}"))
    self.cp('dve', kd[:, :, :, :], rows[:, col0:col0 + 256].rearrange("p (k d) -> p k d", k=4).unsqueeze(2).to_broadcast([128, 4, 2, 64]),
            R=[rows], W=[kd])
    pt = self.pt[1]
    for k in range(4):
        self.tr(pt[:, k * 128:(k + 1) * 128], kd[:, k, :, :].rearrange("p a d -> p (a d)"), c['identb'][:, :],
                R=[kd, c['identb']], W=[pt])
    ks = self.ring('ks', 2, lambda j: P.sb([128, 4, 128], BF16, f"ks{j}"))
    self.cp('act', ks[:, :, :].rearrange("p k n -> p (k n)"), pt[:, 0:512], R=[pt], W=[ks])
    kap, ktile = kdst
    P.dma(kap, ks[:, :, :], R=[ks], W=[ktile], q='sp')
    va = self.ring('va', 2, lambda j: P.sb([128, 4, 66], BF16, f"va{j}"))
    self.memset('dve', va[:, :, 64:65], 1.0, W=[va])
    self.memset('dve', va[:, :, 65:66], 0.0, W=[va])
    self.cp('dve', va[:, :, 0:64], rows[:, col0 + 256:col0 + 512].rearrange("p (k d) -> p k d", k=4), R=[rows], W=[va])
    vap, vtile = vdst
    P.dma(vap, va[:, :, :], R=[va], W=[vtile], q='sp')


def _cmp_finish(self, lohi, NCB, kcd_ap, vc_ap_fn, kcd_t, vc_t, ncw=256, njh=2):
    P, c = self.P, self.c
    bl = self.ring('blk', 1, lambda j: P.sb([128, 4, 256], BF16, "blk"))
    self.memset('pool', bl[:, :, :], 0.0, W=[bl])
    self.tt('dve', bl[:, :, 0:NCB], lohi[:, 0, :, 0:NCB], lohi[:, 1, :, 1:NCB + 1], ALU.add, R=[lohi], W=[bl])
    for kvh in range(4):
        hs = slice((kvh % 2) * 64, (kvh % 2) * 64 + 64)
        ps = self.next_pb('lin', [1, 2, 3])
        self.mm(ps[:, 0:256], c['wck'][hs, kvh, :], bl[hs, kvh // 2, :], R=[c['wck'], bl], W=[ps])
        self.cp('act', kcd_ap(kvh), ps[:, 0:ncw], R=[ps], W=[kcd_t])
        for jh in range(njh):
            ps2 = self.next_pb('lin', [1, 2, 3])
            self.mm(ps2[:, 0:64], bl[hs, 2 + kvh // 2, jh * 128:(jh + 1) * 128], c['wcv'][hs, kvh, :], R=[bl, c['wcv']], W=[ps2])
            self.cp('act', vc_ap_fn(jh, kvh), ps2[:, 0:64], R=[ps2], W=[vc_t])


Builder.nsa_declare = _nsa_declare
Builder.nsa_consts = _nsa_consts
Builder.nsa_tables = _nsa_tables
Builder.ctx_rows = _ctx_rows
Builder.cmp_finish = _cmp_finish


def _attend_gen(self, A):
    P, c = self.P, self.c
    NQ, NC, kvh = A['NQ'], A['NC'], A['kvh']
    pb = self.pb
    qT, qTt = A['qT']
    qbd, qcols = A['qbd'], A['qcols']
    gate = A['gate']
    N4 = 4 * NQ

    def sbt(key, shape, dt=F32, nbuf=2):
        k = f"at{NQ}{key}"
        return self.ring(k, nbuf, lambda j: P.sb(shape, dt, k + str(j)))

    if A.get('stop', 99) == 0:
        raise _Stop()
    kc_ap, kc_t = A['kcmp']
    for g in range(4):
        ps = pb[g % 2]
        self.mm(ps[0:NQ, (g // 2) * 256:(g // 2) * 256 + NC], qT(g), kc_ap(g), R=[qTt, kc_t], W=[ps])
    yield 0
    if A.get('stop', 99) == 10:
        raise _Stop()
    bc_ap, bc_t = A['bias_c']
    sc = sbt('sc', [NQ, 4, 256], nbuf=1)
    for g in range(4):
        ps = pb[g % 2]
        self.tt('dve', sc[:, g, 0:NC], ps[0:NQ, (g // 2) * 256:(g // 2) * 256 + NC], bc_ap[:, g, 0:NC], ALU.add,
                R=[ps, bc_t], W=[sc])
    yield 0
    if A.get('stop', 99) == 11:
        raise _Stop()
    ssum = sbt('ssum', [NQ, 8])
    self.memset('pool', ssum[:, :], 0.0, W=[ssum])
    for g in range(4):
        self.actf(sc[:, g, 0:NC], sc[:, g, 0:NC], AF.Exp, R=[sc], W=[sc, ssum], accum_out=ssum[:, g:g + 1])
    yield 0
    if A.get('stop', 99) == 12:
        raise _Stop()
    self.ts('dve', ssum[:, 4:8], ssum[:, 0:4], 1e-30, None, ALU.max, R=[ssum], W=[ssum])
    P.add('dve', lambda e: e.reciprocal(out=ssum[:, 4:8], in_=ssum[:, 4:8]), R=[ssum], W=[ssum])
    if A.get('stop', 99) == 13:
        raise _Stop()
    pc = sbt('pc', [NQ, 4, 256], nbuf=1)
    if not A.get('pc_init'):
        pass
    self.memset('pool', pc[:, :, :], 0.0, W=[pc])
    self.tt('dve', pc[:, :, 0:NC], sc[:, :, 0:NC], ssum[:, 4:8].unsqueeze(2).to_broadcast([NQ, 4, NC]), ALU.mult,
            R=[sc, ssum], W=[pc])
    yield 0
    if A.get('stop', 99) == 1:
        raise _Stop()
    imp = sbt('imp', [NQ, 264], nbuf=1)
    self.memset('pool', imp[:, :], 0.0, W=[imp])
    P.add('dve', lambda e: e.tensor_reduce(out=imp[:, 1:257], in_=pc[:, :, :].rearrange("p g n -> p n g"),
                                           axis=mybir.AxisListType.X, op=ALU.add), R=[pc], W=[imp])
    yield 0
    cov = sbt('cov', [NQ, 256], nbuf=1)
    self.tt('dve', cov[:, :], imp[:, 1:257], imp[:, 0:256], ALU.add, R=[imp], W=[cov])
    psl = sbt('psl', [NQ, 64], nbuf=1)
    P.add('dve', lambda e: e.tensor_reduce(out=psl[:, :], in_=cov[:, :].rearrange("p (b r) -> p b r", r=4),
                                           axis=mybir.AxisListType.X, op=ALU.add), R=[cov], W=[psl])
    yield 0
    smul, sadd, s_t = A['selc']
    self.tt('dve', psl[:, :], psl[:, :], smul, ALU.mult, R=[psl, s_t], W=[psl])
    self.tt('dve', psl[:, :], psl[:, :], sadd, ALU.add, R=[psl, s_t], W=[psl])
    yield 0
    m16 = sbt('m16', [NQ, 16], nbuf=1)
    ps2 = sbt('psl2', [NQ, 64], nbuf=1)
    P.add('dve', lambda e: e.max(out=m16[:, 0:8], in_=psl[:, :]), R=[psl], W=[m16])
    P.add('dve', lambda e: e.match_replace(out=ps2[:, :], in_to_replace=m16[:, 0:8], in_values=psl[:, :], imm_value=-5.0),
          R=[psl, m16], W=[ps2])
    P.add('dve', lambda e: e.max(out=m16[:, 8:16], in_=ps2[:, :]), R=[ps2], W=[m16])
    yield 0
    nsel = sbt('nsel', [NQ, 64], nbuf=1)
    self.ts('dve', nsel[:, :], psl[:, :], m16[:, 15:16], None, ALU.is_ge, R=[psl, m16], W=[nsel])
    self.ts('dve', nsel[:, :], nsel[:, :], -NEG, NEG, ALU.mult, ALU.add, R=[nsel], W=[nsel])
    self.tr(pb[0][0:64, 0:NQ], nsel[:, :], c['identf'][0:NQ, 0:NQ], R=[nsel, c['identf']], W=[pb[0]])
    nsT = sbt('nsT', [64, 4, NQ], BF16)
    self.cp('act', nsT[:, :, :], pb[0][0:64, 0:NQ].unsqueeze(1).to_broadcast([64, 4, NQ]), R=[pb[0]], W=[nsT])
    yield 0
    if A.get('stop', 99) == 2:
        raise _Stop()
    pcb = sbt('pcb', [NQ, 4, 256], BF16, 1)
    self.cp('pool', pcb[:, :, :], pc[:, :, :], R=[pc], W=[pcb])
    pt = self.pt[0]
    NH = A['NH']
    for g in range(4):
        for hf in range(NH):
            self.tr(pt[:, (g * NH + hf) * NQ:(g * NH + hf + 1) * NQ], pcb[:, g, hf * 128:(hf + 1) * 128],
                    c['identb'][0:NQ, 0:NQ], R=[pcb, c['identb']], W=[pt])
    yield 0
    pcT = sbt('pcT', [128, 4 * NH, NQ], BF16, 1)
    self.cp('act', pcT[:, :, :].rearrange("p a q -> p (a q)"), pt[:, 0:4 * NH * NQ], R=[pt], W=[pcT])
    vc_ap, vc_t = A['vcmp']
    for g in range(4):
        for hf in range(NH):
            self.mm(pb[1][0:NQ, g * 64:(g + 1) * 64], pcT[:, g * NH + hf, :], vc_ap(hf), start=(hf == 0), stop=(hf == NH - 1),
                    R=[pcT, vc_t], W=[pb[1]])
    yield 0
    oacc = sbt('oacc', [NQ, 4, 64])
    g_ap, g_t = gate
    self.tt('dve', oacc[:, :, :], pb[1][0:NQ, 0:256].rearrange("p (g d) -> p g d", g=4),
            g_ap(0).unsqueeze(2).to_broadcast([NQ, 4, 64]), ALU.mult, R=[pb[1], g_t], W=[oacc])

    yield 'CMP_DONE'
    if A.get('stop', 99) == 3:
        raise _Stop()
    for br, (tiles, ksrc, vsrc, po) in enumerate((A['sel'], A['win'])):
        if A.get('stop', 99) == 4 and br == 1:
            raise _Stop()
        nt = len(tiles)
        pend = None
        for c0 in range(0, nt, 4):
            n = min(4, nt - c0)
            kt0 = tiles[c0][0]
            kc = sbt('kc', [128, 512], BF16, 3)
            kap, ktile = ksrc(kt0, n)
            P.dma(kc[:, 0:n * 128], kap, R=[ktile], W=[kc])
            vcx = sbt('vcx', [128, 4, 66], BF16, 3)
            vap, vtile = vsrc(kt0, n)
            P.dma(vcx[:, 0:n, :], vap, R=[vtile], W=[vcx])
            for t in range(n):
                kt, bias = tiles[c0 + t]
                ps = self.next_pb('sc', [2, 3])
                for pr in range(2):
                    self.mm(ps[:, pr * 2 * NQ:(pr + 1) * 2 * NQ], kc[:, t * 128:(t + 1) * 128], qbd[:, pr, :, qcols],
                            start=(pr == 0), stop=(br == 1 and pr == 1), R=[kc, A['qbd_t']], W=[ps])
                if br == 0:
                    self.mm(ps[:, 0:N4], c['expE'][:, kt * 128:(kt + 1) * 128], nsT[:, :, :].rearrange("p g q -> p (g q)"),
                            start=False, stop=True, R=[c['expE'], nsT], W=[ps])
                PT = sbt('PT', [128, N4], BF16, 3)
                if bias is not None:
                    b_ap, b_t = bias
                    sb_ = sbt('sbias', [128, N4], F32, 2)
                    self.tt('dve', sb_[:, :], ps[:, 0:N4], b_ap, ALU.add, R=[ps, b_t], W=[sb_])
                    self.actf(PT[:, :], sb_[:, :], AF.Exp, R=[sb_], W=[PT])
                else:
                    self.actf(PT[:, :], ps[:, 0:N4], AF.Exp, R=[ps], W=[PT])
                if pend is not None:
                    pend()
                def pv(PT=PT, vcx=vcx, t=t, first=(c0 + t == 0), last=(c0 + t == nt - 1), po=po):
                    for g in range(4):
                        self.mm(po[0:NQ, g * 66:(g + 1) * 66], PT[:, g * NQ:(g + 1) * NQ], vcx[:, t, :],
                                start=(first and g == 0), stop=(last and g == 3), R=[PT, vcx], W=[po])
                pend = pv
                yield 1
        if pend is not None:
            pend()
            pend = None
        pov = po[0:NQ, 0:264].rearrange("p (g e) -> p g e", g=4)
        rs = sbt('rs', [NQ, 4])
        self.ts('dve', rs[:, :], pov[:, :, 64], 1e-30, None, ALU.max, R=[po], W=[rs])
        P.add('dve', lambda e, rs=rs: e.reciprocal(out=rs[:, :], in_=rs[:, :]), R=[rs], W=[rs])
        self.tt('dve', rs[:, :], rs[:, :], g_ap(br + 1), ALU.mult, R=[rs, g_t], W=[rs])
        tmpo = sbt('tmpo', [NQ, 4, 64])
        self.tt('dve', tmpo[:, :, :], pov[:, :, 0:64], rs[:, :].unsqueeze(2).to_broadcast([NQ, 4, 64]), ALU.mult,
                R=[po, rs], W=[tmpo])
        self.tt('pool', oacc[:, :, :], oacc[:, :, :], tmpo[:, :, :], ALU.add, R=[oacc, tmpo], W=[oacc])
    o_ap, o_t = A['out']
    self.cp('act', o_ap, oacc[:, :, :], R=[oacc], W=[o_t])


def _attend(self, A):
    for _ in self.attend_gen(A):
        pass


def _attend_pipe(self, thunks):
    prev = None
    for th in thunks:
        g = self.attend_gen(th())
        cmp_done = False
        while not cmp_done or prev is not None:
            if not cmp_done:
                if next(g) == 'CMP_DONE':
                    cmp_done = True
            if prev is not None:
                try:
                    next(prev)
                except StopIteration:
                    prev = None
        prev = g
    if prev is not None:
        for _ in prev:
            pass


Builder.attend_gen = _attend_gen
Builder.attend_pipe = _attend_pipe
Builder.attend = _attend


def _nsa_qproj(self, G, j, qT_all, NT_):
    P, d, T = self.P, self.d, G.T
    GT = G.GT
    for ci in range(8):
        wt = self.load_wchunk(d['wc_nsa_in'][j].ap()[ci])
        ps = self.next_pb('lin', [1, 2, 3])
        for kc in range(8):
            self.mm(ps[:, 0:GT], wt[:, kc, :], T['xnT'][:, kc, :], start=(kc == 0), stop=(kc == 7), R=[wt, T['xnT']], W=[ps])
        self.actf(qT_all[:, ci, :], ps[:, 0:GT], AF.Copy, R=[ps], W=[qT_all], scale=0.125)


def _final_norm(self, G, dst, row0, ytile=None):
    P, d, T = self.P, self.d, G.T
    TP = G.TP
    wrow = self.ring('wrow', 2, lambda j: P.sb([128, D], F32, f"wrow{j}"))
    P.dma(wrow[:, :], d['norm_final'].ap().partition_broadcast(128), W=[wrow])
    x = T['x']
    for t in range(G.NT):
        junk = self.ring('junk', 1, lambda j: P.sb([128, D], BF16, f"junk{j}"))
        st = self.ring('nst', 4, lambda j: P.sb([128, 2], F32, f"nst{j}"))
        self.memset('pool', st[:, :], 0.0, W=[st])
        self.actf(junk[0:TP, :], x[:, t, :], AF.Square, R=[x], W=[junk, st], accum_out=st[0:TP, 0:1])
        self.ts('dve', st[0:TP, 1:2], st[0:TP, 0:1], 1.0 / D, 1e-6, ALU.mult, ALU.add, R=[st], W=[st])
        self.actf(st[0:TP, 1:2], st[0:TP, 1:2], AF.Sqrt, R=[st], W=[st])
        P.add('dve', lambda e, st=st: e.reciprocal(out=st[0:TP, 1:2], in_=st[0:TP, 1:2]), R=[st], W=[st])
        if ytile is None:
            yt = self.ring(G.name + 'yt', 2, lambda j: P.sb([TP, D], F32, G.name + f"yt{j}"))
            ya = yt[:, :]
        else:
            yt = ytile
            ya = ytile[:, t % 2, :]
        self.stt('dve', ya, x[:, t, :], st[0:TP, 1:2], wrow[0:TP, :], ALU.mult, ALU.mult, R=[x, st, wrow], W=[yt])
        P.dma(dst.ap()[row0 + t * TP:row0 + (t + 1) * TP, :], ya, R=[yt], W=[dst], q='pool')


def _nsa_prompt(self):
    P, d, c, cfg = self.P, self.d, self.c, self.cfg
    SEQ = cfg['SEQ']
    NTT = SEQ // 128
    NQT = NTT // 2
    NCB = NTT * 8 - 1
    self.nsa_late_consts()
    kcd = P.sb([128, 4, 256], BF16, "kcd")
    vcm = P.sb([128, 2, 4, 64], BF16, "vcm")
    self.memset('pool', vcm[:, :, :, :], 0.0, W=[vcm])
    with self.phase():
        lohi = P.sb([128, 2, 4, 264], F32, "lohi")
        self.memset('pool', lohi[:, :, :, :], 0.0, W=[lohi])
        for Pi in range(NTT):
            rows = self.ring('crow', 2, lambda j: P.sb([128, 1536], F32, f"crow{j}"))
            P.dma(rows[:, 0:1024], d['p_kv_rows'].ap()[Pi * 128:(Pi + 1) * 128, :], R=[d['p_kv_rows']], W=[rows])
            P.dma(rows[:, 1024:1536], d['pwin_d'].ap()[Pi * 128:(Pi + 1) * 128, :], R=[d['pwin_d']], W=[rows])
            cs = slice(Pi * 128, (Pi + 1) * 128)
            self.ctx_rows(rows, Pi, lohi,
                          (d['kselT_p'].ap()[:, :, cs].rearrange("k p n -> p k n"), d['kselT_p']),
                          (d['vsel_p'].ap()[Pi], d['vsel_p']),
                          (d['kwinT_p'].ap()[:, :, cs].rearrange("k p n -> p k n"), d['kwinT_p']),
                          (d['vwin_p'].ap()[Pi], d['vwin_p']), win_rows=(rows, 1024))
        self.cmp_finish(lohi, NCB, lambda kvh: kcd[:, kvh, :], lambda jh, kvh: vcm[:, jh, kvh, :], kcd, vcm)
    if cfg.get('nsa_stop', 99) == 2:
        raise _Stop()
    with self.phase():
        G = Ctx('n', 128, 4, 1, 1, 5)
        T = {}
        T['x'] = P.sb([128, 4, D], F32, "nx")
        T['xnT'] = P.sb([128, 8, 512], BF16, "nxnT")
        big = P.sb([128, 24, 512], BF16, "nbig")
        T['actT'] = View(big, 0, 22)
        qT_all = View(big, 0, 8)
        T['oT'] = P.sb([128, 8, 512], BF16, "noT")
        G.T = T
        o_tok = P.sb([128, 4, D], BF16, "o_tok")
        gates = P.sb([128, 4, 48], F32, "gates")
        qbd = P.sb([128, 2, 2, 512], BF16, "qbd")
        self.memset('pool', qbd[:, :, :, :], 0.0, W=[qbd])
        BT = P.sb([128, 15, 512], F32, "BT")
        for grp in range(NQT // 4):
            for tl in range(4):
                i = grp * 4 + tl
                xe = self.ring('xeo', 1, lambda j: P.sb([128, 2, D], F32, f"xeo{j}"))
                P.dma(xe[:, :, :], d['x2_p'].ap()[2 * i * 128:(2 * i + 2) * 128, :].rearrange("(e p) n -> p e n", p=128),
                      R=[d['x2_p']], W=[xe])
                self.ts('dve', T['x'][:, tl, :], xe[:, 0, :], c['parf'][:, 1:2], None, ALU.mult, R=[xe, c['parf']], W=[T['x']])
                self.stt('dve', T['x'][:, tl, :], xe[:, 1, :], c['parf'][:, 0:1], T['x'][:, tl, :], ALU.mult, ALU.add,
                         R=[xe, c['parf'], T['x']], W=[T['x']])
            for j in range(2):
                self.norm_T(G, d['norm_mix'].ap()[2 + j])
                _nsa_qproj(self, G, j, qT_all, 4)
                pg = self.pb[0]
                for tl in range(4):
                    for kc in range(8):
                        self.mm(pg[:, tl * 48:(tl + 1) * 48], T['xnT'][:, kc, tl * 128:(tl + 1) * 128], c['wg'][:, j, kc, :],
                                start=(kc == 0), stop=(kc == 7), R=[T['xnT'], c['wg']], W=[pg])
                self.actf(gates[:, :, :].rearrange("p t n -> p (t n)"), pg[:, 0:192], AF.Sigmoid, R=[pg], W=[gates])
                if cfg.get('nsa_stop', 99) == 3:
                    raise _Stop()
                for kvh in range(4):
                    for b3 in range(5):
                        P.dma(BT[:, 3 * b3:3 * b3 + 3, :], d['BTd'].ap()[kvh, 3 * b3:3 * b3 + 3].rearrange("i p n -> p i n"),
                              R=[d['BTd']], W=[BT])
                    for pr in range(2):
                        self.cp('pool', qbd[0:64, pr, 0, :], qT_all[0:64, 2 * kvh + pr, :], R=[qT_all], W=[qbd])
                        self.cp('pool', qbd[64:128, pr, 1, :], qT_all[64:128, 2 * kvh + pr, :], R=[qT_all], W=[qbd])
                    thunks = []
                    for tl in range(4):
                        def mk(tl=tl, kvh=kvh):
                            i = grp * 4 + tl
                            qc = slice(tl * 128, (tl + 1) * 128)
                            bc = self.ring('bcp', 2, lambda jj: P.sb([128, 4, 256], F32, f"bcp{jj}"))
                            smc = self.ring('smc', 2, lambda jj: P.sb([128, 2, 64], F32, f"smc{jj}"))
                            P.dma(smc[:, 0, :], d['selmul_p'].ap()[:, i, :], W=[smc])
                            P.dma(smc[:, 1, :], d['seladd_p'].ap()[:, i, :], W=[smc])
                            for a in range(8):
                                src = bass.AP(d['t_cmp'].h, 4 * kvh * 8192 + 247 - 16 * i - a, [[512, 16], [8192, 4], [1, 256]])
                                P.dma(bc[16 * a:16 * a + 16, :, :], src, R=[d['t_cmp']], W=[bc])
                            nkt = 2 * i + 2
                            sel_tiles = []
                            for kt in range(nkt):
                                e = 2 * i - kt
                                sel_tiles.append((kt, (BT[:, e + 1, :], BT) if e <= 7 else None))
                            win_tiles = []
                            for e in range(4, -2, -1):
                                kt = 2 * i - e
                                if kt >= 0:
                                    win_tiles.append((kt, (BT[:, 10 + e, :], BT)))
                            A = dict(
                                NQ=128, NC=NCB + 1, NH=2, kvh=kvh,
                                qT=(lambda g, qc=qc, kvh=kvh: qT_all[(g % 2) * 64:(g % 2) * 64 + 64, 2 * kvh + g // 2, qc], qT_all),
                                qbd=qbd, qbd_t=qbd, qcols=qc,
                                gate=(lambda br, tl=tl, kvh=kvh: gates[:, tl, :].rearrange("p (h r) -> p h r", r=3)[:, 4 * kvh:4 * kvh + 4, br], gates),
                                kcmp=(lambda g, kvh=kvh: kcd[(g % 2) * 64:(g % 2) * 64 + 64, kvh, 0:NCB + 1], kcd),
                                vcmp=(lambda hf, kvh=kvh: vcm[:, hf, kvh, :], vcm),
                                bias_c=(bc[:, :, :], bc),
                                selc=(smc[:, 0, :], smc[:, 1, :], smc),
                                sel=(sel_tiles,
                                     lambda kt0, n, kvh=kvh: (d['kselT_p'].ap()[kvh, :, kt0 * 128:(kt0 + n) * 128], d['kselT_p']),
                                     lambda kt0, n, kvh=kvh: (d['vsel_p'].ap()[kt0:kt0 + n, :, kvh, :].rearrange("t p e -> p t e"), d['vsel_p']),
                                     self.pb[4]),
                                win=(win_tiles,
                                     lambda kt0, n, kvh=kvh: (d['kwinT_p'].ap()[kvh, :, kt0 * 128:(kt0 + n) * 128], d['kwinT_p']),
                                     lambda kt0, n, kvh=kvh: (d['vwin_p'].ap()[kt0:kt0 + n, :, kvh, :].rearrange("t p e -> p t e"), d['vwin_p']),
                                     self.pb[5]),
                                out=(o_tok[:, tl, kvh * 256:(kvh + 1) * 256].rearrange("p (g e) -> p g e", g=4), o_tok),
                            )

                            return A
                        thunks.append(mk)
                    self.attend_pipe(thunks)
                for tl in range(4):
                    pt = self.pt[0]
                    for cc in range(8):
                        self.tr(pt[:, cc * 128:(cc + 1) * 128], o_tok[:, tl, cc * 128:(cc + 1) * 128], c['identb'][:, :],
                                R=[o_tok, c['identb']], W=[pt])
                    self.cp('act', T['oT'][:, :, tl * 128:(tl + 1) * 128], pt[:, :].rearrange("p (c n) -> p c n", c=8),
                            R=[pt], W=[T['oT']])
                self.tok_linear_add(G, T['oT'], 8, d['wb_nsa_out'][j])
                self.ffn(G, 2 + j)
            _final_norm(self, G, d['y_p'], grp * 512, ytile=self.rr['xeo'][0][0])


Builder.nsa_prompt = _nsa_prompt


def _nsa_sample(self):
    P, d, c, cfg = self.P, self.d, self.c, self.cfg
    NSQ = 16
    self.nsa_late_consts()
    kcd = P.sb([128, NSQ, 4, 128], BF16, "kcds")
    vcm = P.sb([128, NSQ, 4, 64], BF16, "vcms")
    self.memset('pool', vcm[:, :, :, :], 0.0, W=[vcm])
    with self.phase():
        ptb = P.sb([128, 256], I32, "ptb")
        P.dma(ptb[:, :], d['page_table'].ap()[0].partition_broadcast(128), W=[ptb])
        ptf = P.sb([128, 256], F32, "ptf")
        self.cp('dve', ptf[:, :], ptb[:, :], R=[ptb], W=[ptf])
        iot = P.sb([128, 1], F32, "iot")
        P.dma(iot[:, :], d['iota'].ap(), W=[iot])
        self.stt('dve', ptf[:, :], ptf[:, :], 128.0, iot[:, 0:1].to_broadcast([128, 256]), ALU.mult, ALU.add,
                 R=[ptf, iot], W=[ptf])
        idx = P.sb([128, 256], I32, "idxall")
        self.cp('dve', idx[:, :], ptf[:, :], R=[ptf], W=[idx])
        for s in range(NSQ):
            lohi = self.ring('lohis', 2, lambda j: P.sb([128, 2, 4, 136], F32, f"lohis{j}"))
            self.memset('pool', lohi[:, :, :, :], 0.0, W=[lohi])
            for Pi in range(17):
                rows = self.ring('crow', 4, lambda j: P.sb([128, 1024], F32, f"crows{j}"))
                if Pi < 16:
                    k = s * 16 + Pi
                    P.add('pool', lambda e, rows=rows, k=k: e.indirect_dma_start(
                        out=rows[:, :], out_offset=None, in_=d['cache_kv'].ap(),
                        in_offset=bass.IndirectOffsetOnAxis(ap=idx[:, k:k + 1], axis=0)),
                        R=[idx, d['cache_kv']], W=[rows], dma=True)
                else:
                    self.memset('pool', rows[:, :], 0.0, W=[rows])
                    P.dma(rows[0:4, :], d['s_kv_rows'].ap()[4 * s:4 * s + 4, :], R=[d['s_kv_rows']], W=[rows])
                cs = slice(Pi * 128, (Pi + 1) * 128)
                self.ctx_rows(rows, Pi, lohi if Pi < 16 else None,
                              (d['kselT_s'].ap()[s][:, :, cs].rearrange("k p n -> p k n"), d['kselT_s']),
                              (d['vsel_s'].ap()[s, Pi], d['vsel_s']), None, None, has_win=False)
            for W_ in range(5):
                wr = self.ring('wrow_s', 2, lambda j: P.sb([128, 512], F32, f"wrows{j}"))
                if W_ < 4:
                    P.dma(wr[:, :], d['state_win_kv'].ap()[s, W_ * 128:(W_ + 1) * 128, :], W=[wr])
                else:
                    self.memset('pool', wr[:, :], 0.0, W=[wr])
                    P.dma(wr[0:4, :], d['s_win_kv'].ap()[s, 508:512, :], R=[d['s_win_kv']], W=[wr])
                cs = slice(W_ * 128, (W_ + 1) * 128)
                self.ctx_rows(None, 0, None, None, None,
                              (d['kwinT_s'].ap()[s][:, :, cs].rearrange("k p n -> p k n"), d['kwinT_s']),
                              (d['vwin_s'].ap()[s, W_], d['vwin_s']), has_cmpsel=False, win_rows=(wr, 0))
            self.cmp_finish(lohi, 127, lambda kvh, s=s: kcd[:, s, kvh, :], lambda jh, kvh, s=s: vcm[:, s, kvh, :], kcd, vcm,
                            ncw=128, njh=1)
    with self.phase():
        G = Ctx('m', 64, 1, 16, 16, 1)
        T = {}
        T['x'] = P.sb([64, 1, D], F32, "mx")
        T['xnT'] = P.sb([128, 8, 64], BF16, "mxnT")
        big = P.sb([128, 24, 64], BF16, "mbig")
        T['actT'] = View(big, 0, 22)
        qT_all = View(big, 0, 8)
        T['oT'] = P.sb([128, 8, 64], BF16, "moT")
        G.T = T
        P.dma(T['x'][:, 0, :], d['x2_s'].ap(), R=[d['x2_s']], W=[T['x']])
        qbd = P.sb([128, 4, 2, 2, 64], BF16, "qbds")
        self.memset('pool', qbd[:, :, :, :, :], 0.0, W=[qbd])
        smul = P.sb([4, 64], F32, "smuls")
        sadd = P.sb([4, 64], F32, "sadds")
        P.dma(smul[:, :], d['selmul_s'].ap(), W=[smul])
        P.dma(sadd[:, :], d['seladd_s'].ap(), W=[sadd])
        bcs = P.sb([4, 16, 128], F32, "bcs")
        P.dma(bcs[:, :, :], bass.AP(d['t_cs'].h, 0, [[128, 4], [512, 16], [1, 128]]), R=[d['t_cs']], W=[bcs])
        trv = P.sb([128, 22, 16, 4], F32, "trvs")
        for Pi in range(17):
            P.dma(trv[:, Pi, :, :], bass.AP(d['t_ss'].h, 2048 - 128 * Pi, [[1, 128], [2304, 16], [1, 4]]), R=[d['t_ss']], W=[trv])
        for W_ in range(5):
            P.dma(trv[:, 17 + W_, :, :], bass.AP(d['t_ws'].h, 512 - 128 * W_, [[1, 128], [768, 16], [1, 4]]), R=[d['t_ws']], W=[trv])
        BTs = P.sb([128, 22, 16, 4], F32, "BTs")
        tf = trv[:, :, :, :].rearrange("p a h t -> p (a h t)")
        bf_ = BTs[:, :, :, :].rearrange("p a h t -> p (a h t)")
        for off in range(0, 22 * 64, 512):
            n = min(512, 22 * 64 - off)
            ps = self.next_pb('lin', [1, 2, 3])
            self.mm(ps[:, 0:n], c['antiI'][:, :], tf[:, off:off + n], R=[c['antiI'], trv], W=[ps])
            self.cp('act', bf_[:, off:off + n], ps[:, 0:n], R=[ps], W=[BTs])
        for j in range(2):
            self.norm_T(G, d['norm_mix'].ap()[2 + j])
            _nsa_qproj(self, G, j, qT_all, 1)
            for kvh in range(4):
                for pr in range(2):
                    self.cp('pool', qbd[0:64, kvh, pr, 0, :], qT_all[0:64, 2 * kvh + pr, :], R=[qT_all], W=[qbd])
                    self.cp('pool', qbd[64:128, kvh, pr, 1, :], qT_all[64:128, 2 * kvh + pr, :], R=[qT_all], W=[qbd])
            gs_all = self.ring('gs_all', 1, lambda jj: P.sb([4, NSQ, 48], F32, "gs_all"))
            for half in range(2):
                pg = self.pb[0]
                for s8 in range(8):
                    s = half * 8 + s8
                    for kc in range(8):
                        self.mm(pg[0:4, s8 * 48:(s8 + 1) * 48], T['xnT'][:, kc, 4 * s:4 * s + 4], c['wg'][:, j, kc, :],
                                start=(kc == 0), stop=(kc == 7), R=[T['xnT'], c['wg']], W=[pg])
                self.actf(gs_all[:, half * 8:(half + 1) * 8, :].rearrange("p s n -> p (s n)"), pg[0:4, 0:384], AF.Sigmoid,
                          R=[pg], W=[gs_all])
            o_all = self.ring('o_all', 1, lambda jj: P.sb([4, NSQ, D], BF16, "o_all"))
            thunks = []
            for s in range(NSQ):
                for kvh in range(4):
                    def mk(s=s, kvh=kvh):
                        qc = slice(4 * s, 4 * s + 4)
                        sel_tiles = [(Pi, (BTs[:, Pi, 4 * kvh:4 * kvh + 4, :].rearrange("p h t -> p (h t)"), BTs)) for Pi in range(17)]
                        win_tiles = [(W_, (BTs[:, 17 + W_, 4 * kvh:4 * kvh + 4, :].rearrange("p h t -> p (h t)"), BTs)) for W_ in range(5)]
                        return dict(
                            NQ=4, NC=128, NH=1, kvh=kvh,
                            qT=(lambda g: qT_all[(g % 2) * 64:(g % 2) * 64 + 64, 2 * kvh + g // 2, qc], qT_all),
                            qbd=qbd.h[:, kvh], qbd_t=qbd, qcols=qc,
                            gate=(lambda br: gs_all[:, s, :].rearrange("p (h r) -> p h r", r=3)[:, 4 * kvh:4 * kvh + 4, br], gs_all),
                            kcmp=(lambda g: kcd[(g % 2) * 64:(g % 2) * 64 + 64, s, kvh, :], kcd),
                            vcmp=(lambda hf: vcm[:, s, kvh, :], vcm),
                            bias_c=(bcs[:, 4 * kvh:4 * kvh + 4, :], bcs),
                            selc=(smul[:, :], sadd[:, :], smul),
                            sel=(sel_tiles,
                                 lambda kt0, n: (d['kselT_s'].ap()[s, kvh, :, kt0 * 128:(kt0 + n) * 128], d['kselT_s']),
                                 lambda kt0, n: (d['vsel_s'].ap()[s, kt0:kt0 + n, :, kvh, :].rearrange("t p e -> p t e"), d['vsel_s']),
                                 self.pb[4]),
                            win=(win_tiles,
                                 lambda kt0, n: (d['kwinT_s'].ap()[s, kvh, :, kt0 * 128:(kt0 + n) * 128], d['kwinT_s']),
                                 lambda kt0, n: (d['vwin_s'].ap()[s, kt0:kt0 + n, :, kvh, :].rearrange("t p e -> p t e"), d['vwin_s']),
                                 self.pb[5]),
                            out=(o_all[:, s, kvh * 256:(kvh + 1) * 256].rearrange("p (g e) -> p g e", g=4), o_all),
                        )
                    thunks.append(mk)
            self.attend_pipe(thunks)
            for s in range(NSQ):
                qc = slice(4 * s, 4 * s + 4)
                pt = self.pt[0]
                for cc in range(8):
                    self.tr(pt[:, cc * 4:(cc + 1) * 4], o_all[:, s, cc * 128:(cc + 1) * 128], c['identb'][0:4, 0:4],
                            R=[o_all, c['identb']], W=[pt])
                self.cp('act', T['oT'][:, :, qc], pt[:, 0:32].rearrange("p (c n) -> p c n", c=8), R=[pt], W=[T['oT']])
            self.tok_linear_add(G, T['oT'], 8, d['wb_nsa_out'][j])
            self.ffn(G, 2 + j)
        _final_norm(self, G, d['y_s'], 0)


Builder.nsa_sample = _nsa_sample
```

```python
import math
import numpy as np
import concourse.bass as bass
import concourse.mybir as mybir
from concourse.bass_utils import run_bass_kernel_spmd

F32 = mybir.dt.float32
BF16 = mybir.dt.bfloat16
I32 = mybir.dt.int32
AF = mybir.ActivationFunctionType
ALU = mybir.AluOpType

EPOCH = 16000
EMBED_WAIT = True
NSLOT = 8
ENGS = ('pe', 'act', 'dve', 'pool', 'sp')
SAME_ENGINE_SYNC = {'pe': False, 'act': True, 'dve': True, 'pool': True, 'sp': True}

D = 1024
H = 8
DFF = 2816
NEG = -30000.0


class Buf:
    __slots__ = ('name', 'lw', 'rd')

    def __init__(self, name):
        self.name = name
        self.lw = None
        self.rd = []


class Tile:
    def __init__(self, h, name):
        self.h = h
        self.buf = Buf(name)

    def __getitem__(self, idx):
        return self.h[idx]

    def ap(self):
        return self.h.ap()


class View:
    def __init__(self, base, off, n):
        self.base, self.buf, self.off, self.n = base, base.buf, off, n

    def __getitem__(self, idx):
        idx = list(idx)
        a = idx[1]
        if isinstance(a, slice):
            st = (a.start or 0) + self.off
            en = (a.stop if a.stop is not None else self.n) + self.off
            idx[1] = slice(st, en)
        else:
            idx[1] = a + self.off
        return self.base.h[tuple(idx)]


class Prog:
    def __init__(self, nc):
        self.nc = nc
        self.ops = {e: [] for e in ENGS}
        self.cnt = {e: 0 for e in ENGS}
        self.dcnt = {e: 0 for e in ENGS}
        self.known = {e: {} for e in ENGS}
        self.kev = {e: [] for e in ENGS}
        self.kptr = {}
        self.nt = 0
        self.stack = None

    def sb(self, shape, dtype=F32, name=None):
        self.nt += 1
        name = name or "t"
        if self.stack is not None:
            h = self.stack.enter_context(self.nc.sbuf_tensor(f"{name}_{self.nt}", list(shape), dtype))
        else:
            h = self.nc.alloc_sbuf_tensor(f"{name}_{self.nt}", list(shape), dtype)
        return Tile(h, name)

    def barrier(self):
        evs = []
        for f in ENGS:
            if self.cnt[f] > 0:
                evs.append(('c', f, self.cnt[f]))
            n = self.dcnt[f]
            for slot in range(min(n, NSLOT)):
                evs.append(('d', f, slot, (n - 1 - slot) // NSLOT + 1))
        for e in ENGS:
            waits = []
            for ev in evs:
                if ev[0] == 'c' and ev[1] == e:
                    continue
                self._need(e, ev, waits)
            self.cnt[e] += 1
            self.ops[e].append((waits, (lambda en: en.nop()), ('c', e, self.cnt[e])))

    def ps(self, shape, dtype=F32, name=None):
        self.nt += 1
        name = name or "p"
        h = self.nc.alloc_psum_tensor(f"{name}_{self.nt}", list(shape), dtype)
        return Tile(h, name)

    def dram(self, name, shape, dtype=F32, kind="Internal"):
        h = self.nc.dram_tensor(name, list(shape), dtype, kind=kind)
        return Tile(h, name)

    def _learn(self, eng, key, val):
        if self.known[eng].get(key, 0) >= val:
            return False
        self.known[eng][key] = val
        self.kev[eng].append((self.cnt[eng] + 1, key, val))
        return True

    def _absorb(self, eng, f, seq):
        evs = self.kev[f]
        i = self.kptr.get((eng, f), 0)
        n = len(evs)
        while i < n and evs[i][0] <= seq:
            _, key, val = evs[i]
            if not (key[0] == 'c' and key[1] == eng):
                self._learn(eng, key, val)
            i += 1
        self.kptr[(eng, f)] = i

    def _need(self, eng, ev, waits):
        if ev is None:
            return
        if ev[0] == 'c':
            _, f, seq = ev
            if f == eng and not SAME_ENGINE_SYNC[eng]:
                return
            if self._learn(eng, ('c', f), seq):
                waits.append(ev)
                if f != eng:
                    self._absorb(eng, f, seq)
        else:
            _, q, slot, k = ev
            if self._learn(eng, ('d', q, slot), k):
                waits.append(ev)

    def add(self, eng, emit, R=(), W=(), dma=False):
        waits = []
        for t in R:
            self._need(eng, t.buf.lw, waits)
        for t in W:
            b = t.buf
            self._need(eng, b.lw, waits)
            for ev in b.rd:
                self._need(eng, ev, waits)
        if dma:
            i = self.dcnt[eng]
            self.dcnt[eng] += 1
            slot, k = i % NSLOT, i // NSLOT + 1
            if k > 1:
                self._need(eng, ('d', eng, slot, k - 1), waits)
            ev = ('d', eng, slot, k)
        else:
            self.cnt[eng] += 1
            ev = ('c', eng, self.cnt[eng])
        for t in R:
            t.buf.rd.append(ev)
        for t in W:
            t.buf.lw = ev
            t.buf.rd = []
        self.ops[eng].append((waits, emit, ev))
        return ev

    def dma(self, out_ap, in_ap, R=(), W=(), q='sp', **kw):
        return self.add(q, lambda e: e.dma_start(out=out_ap, in_=in_ap, **kw), R, W, dma=True)

    def emit(self):
        nc = self.nc
        csem = {}
        for e in ENGS:
            n = (self.cnt[e] + EPOCH - 1) // EPOCH
            csem[e] = [nc.alloc_semaphore(f"c_{e}_{j}") for j in range(n)]
        dsem = {}
        for e in ENGS:
            n = min(self.dcnt[e], NSLOT)
            dsem[e] = [nc.alloc_semaphore(f"d_{e}_{j}") for j in range(n)]

        def semval(ev):
            if ev[0] == 'c':
                _, f, seq = ev
                return csem[f][(seq - 1) // EPOCH], (seq - 1) % EPOCH + 1
            _, q, slot, k = ev
            return dsem[q][slot], 16 * k

        def run(eng, e):
            for waits, emit, ev in self.ops[eng]:
                emb = waits[-1] if (waits and EMBED_WAIT) else None
                for w in (waits[:-1] if emb is not None else waits):
                    s, v = semval(w)
                    e.wait_ge(s, v)
                ins = emit(e)
                if emb is not None:
                    s, v = semval(emb)
                    ins._wait_ge(s, v)
                s, v = semval(ev)
                ins.then_inc(s, 16 if ev[0] == 'd' else 1)
            n = self.dcnt[eng]
            for slot in range(min(n, NSLOT)):
                k = (n - 1 - slot) // NSLOT + 1
                if self.known[eng].get(('d', eng, slot), 0) < k:
                    e.wait_ge(dsem[eng][slot], 16 * k)

        with nc.Block() as block:
            @block.tensor
            def _(e):
                run('pe', e)

            @block.scalar
            def _(e):
                run('act', e)

            @block.vector
            def _(e):
                run('dve', e)

            @block.gpsimd
            def _(e):
                run('pool', e)

            @block.sync
            def _(e):
                run('sp', e)
        return nc


class Ctx:
    def __init__(self, name, TP, NT, NSEQ, NS, nlev):
        self.name = name
        self.TP = TP
        self.NT = NT
        self.GT = TP * NT
        self.NSEQ = NSEQ
        self.TS = self.GT // NSEQ
        self.NCH = self.GT // 64
        self.NS = NS
        self.nlev = nlev


def host_consts(NS, TS):
    seg = np.arange(64) // TS if NS > 1 else np.zeros(64, np.int64)
    j = np.arange(64)[:, None]
    i = np.arange(64)[None, :]
    same = (seg[:, None] == seg[None, :])
    c = {}
    c['ucs'] = ((j <= i) & same).astype(np.float32)
    c['maskT'] = np.where((j <= i) & same, 0.0, NEG).astype(np.float32)
    c['noff'] = -np.where((j != i), 1.0, 0.0).astype(np.float32)
    c['same'] = same.astype(np.float32)
    si = np.zeros((64, NS), np.float32)
    si[np.arange(64), seg] = 1.0
    c['seqind'] = si
    cm = np.zeros((128, NS, 64), np.float32)
    cm[:, seg, np.arange(64)] = 1.0
    c['colmask'] = cm.reshape(128, NS * 64)
    return c


class _Stop(Exception):
    pass


class Builder:
    def __init__(self, cfg):
        self.cfg = cfg
        nc = bass.Bass("TRN2", target_bir_lowering=False)
        self.nc = nc
        self.P = Prog(nc)
        self.rr = {}

    def mm(self, out, lhsT, rhs, start=True, stop=True, R=(), W=()):
        self.P.add('pe', lambda e: e.matmul(out, lhsT=lhsT, rhs=rhs, start=start, stop=stop), R, W)

    def tr(self, out, in_, ident, R=(), W=()):
        self.P.add('pe', lambda e: e.transpose(out=out, in_=in_, identity=ident), R, W)

    def actf(self, out, in_, func, R=(), W=(), bias=None, scale=None, accum_out=None):
        kw = {}
        if bias is not None:
            kw['bias'] = bias
        if scale is not None:
            kw['scale'] = scale
        if accum_out is not None:
            kw['accum_out'] = accum_out
        self.P.add('act', lambda e: e.activation(out=out, in_=in_, func=func, **kw), R, W)

    def cp(self, eng, out, in_, R=(), W=()):
        if eng == 'act':
            self.P.add('act', lambda e: e.copy(out=out, in_=in_), R, W)
        else:
            self.P.add(eng, lambda e: e.tensor_copy(out=out, in_=in_), R, W)

    def tt(self, eng, out, in0, in1, op, R=(), W=()):
        self.P.add(eng, lambda e: e.tensor_tensor(out=out, in0=in0, in1=in1, op=op), R, W)

    def ts(self, eng, out, in0, s1, s2, op0, op1=None, R=(), W=()):
        if op1 is None:
            self.P.add(eng, lambda e: e.tensor_scalar(out=out, in0=in0, scalar1=s1, scalar2=None, op0=op0), R, W)
        else:
            self.P.add(eng, lambda e: e.tensor_scalar(out=out, in0=in0, scalar1=s1, scalar2=s2, op0=op0, op1=op1), R, W)

    def stt(self, eng, out, in0, scalar, in1, op0, op1, R=(), W=()):
        self.P.add(eng, lambda e: e.scalar_tensor_tensor(out=out, in0=in0, scalar=scalar, in1=in1, op0=op0, op1=op1), R, W)

    def memset(self, eng, ap, val, W=()):
        self.P.add(eng, lambda e: e.memset(ap, val), (), W)

    def ring(self, key, n, make):
        if key not in self.rr:
            self.rr[key] = [[make(i) for i in range(n)], 0]
        r = self.rr[key]
        t = r[0][r[1] % n]
        r[1] += 1
        return t

    def declare_io(self):
        P, cfg = self.P, self.cfg
        d = {}

        def inp(name, shape, dt=F32):
            d[name] = P.dram(name, shape, dt, kind="ExternalInput")

        def out(name, shape, dt=F32):
            d[name] = P.dram(name, shape, dt, kind="ExternalOutput")

        SEQ = cfg['SEQ']
        inp('x_prompt', [SEQ, D])
        inp('x_sample', [64, D])
        inp('state_dn_S', [2, 16, H, 128, 128])
        inp('state_dn_conv', [2, 16, 3, 3072])
        inp('state_win_kv', [16, 512, 512])
        inp('norm_mix', [4, D])
        inp('norm_ffn', [4, D])
        inp('norm_kv', [D])
        inp('norm_final', [D])
        inp('ffn_w_in', [4, D, 2 * DFF])
        inp('ffn_w_out', [4, DFF, D])
        inp('dn_w_in', [2, D, 4112])
        inp('dn_conv_w', [2, 4, 3072])
        inp('dn_A_log', [2, H])
        inp('dn_dt_bias', [2, H])
        inp('dn_out_norm', [2, 128])
        inp('dn_w_out', [2, D, D])
        inp('nsa_w_kv', [D, 1536])
        for pre, NS in (('cp_', 1), ('cs_', 16)):
            inp(pre + 'ucs', [64, 64])
            inp(pre + 'maskT', [64, 64])
            inp(pre + 'noff', [64, 64])
            inp(pre + 'same', [64, 64])
            inp(pre + 'seqind', [64, NS])
            inp(pre + 'colmask', [128, NS * 64])
        out('p_dn_S', [2, H, 128, 128])
        out('p_dn_conv', [2, 3, 3072])
        out('p_kv_rows', [SEQ, 1024])
        out('p_win_kv', [512, 512])
        out('s_dn_S', [2, 16, H, 128, 128])
        out('s_dn_conv', [2, 16, 3, 3072])
        out('s_kv_rows', [64, 1024])
        out('s_win_kv', [16, 512, 512])
        out('x2_p', [SEQ, D])
        out('x2_s', [64, D])
        d['wc_dn_in'] = [P.dram(f'wc_dn_in{l}', [32, 128, 8, 128], BF16) for l in range(2)]
        d['wc_ffn_in'] = [P.dram(f'wc_ffn_in{l}', [44, 128, 8, 128], BF16) for l in range(4)]
        d['wb_ffn_out'] = [P.dram(f'wb_ffn_out{l}', [DFF, D], BF16) for l in range(4)]
        d['wb_dn_out'] = [P.dram(f'wb_dn_out{l}', [D, D], BF16) for l in range(2)]
        d['wb_kv'] = P.dram('wb_kv', [D, 1536], BF16)
        self.d = d

    def consts(self):
        P, d = self.P, self.d
        c = {}
        identf = P.sb([128, 128], F32, "identf")
        self.memset('pool', identf[:, :], 1.0, W=[identf])
        P.add('pool', lambda e: e.affine_select(out=identf[:, :], in_=identf[:, :], pattern=[[-1, 128]],
                                                compare_op=ALU.is_equal, fill=0.0, base=0, channel_multiplier=1),
              R=[identf], W=[identf])
        identb = P.sb([128, 128], BF16, "identb")
        self.cp('dve', identb[:, :], identf[:, :], R=[identf], W=[identb])
        onesb = P.sb([128, 128], BF16, "onesb")
        self.memset('pool', onesb[:, :], 1.0, W=[onesb])
        onesf = P.sb([64, 128], F32, "onesf")
        self.memset('pool', onesf[:, :], 1.0, W=[onesf])
        c.update(identf=identf, identb=identb, onesb=onesb, onesf=onesf)
        for pre, NS in (('cp_', 1), ('cs_', 16)):
            for nm, shp in (('ucs', [64, 64]), ('maskT', [64, 64]), ('noff', [64, 64]), ('same', [64, 64]),
                            ('seqind', [64, NS])):
                t = P.sb(shp, F32, pre + nm)
                P.dma(t[:, :], d[pre + nm].ap(), W=[t])
                c[pre + nm] = t
            if NS > 1:
                t = P.sb([128, NS * 64], BF16, pre + 'colmask')
                P.dma(t[:, :], d[pre + 'colmask'].ap(), W=[t], q='pool')
                c[pre + 'colmask'] = t
        cw = P.sb([128, 2, 24, 4], F32, "cw")
        for l in range(2):
            for i in range(4):
                P.dma(cw[:, l, :, i], d['dn_conv_w'].ap()[l, i].rearrange("(c p) -> p c", p=128), W=[cw],
                      allow_slow_non_contiguous=True)
        c['cw'] = cw
        negA = P.sb([128, 2, H], F32, "negA")
        dtb = P.sb([128, 2, H], F32, "dtb")
        P.dma(negA[:, :, :], d['dn_A_log'].ap().rearrange("l h -> (l h)").partition_broadcast(128).rearrange("p (l h) -> p l h", l=2), W=[negA])
        P.dma(dtb[:, :, :], d['dn_dt_bias'].ap().rearrange("l h -> (l h)").partition_broadcast(128).rearrange("p (l h) -> p l h", l=2), W=[dtb])
        self.actf(negA[:, :, :], negA[:, :, :], AF.Exp, R=[negA], W=[negA])
        self.ts('dve', negA[:, :, :], negA[:, :, :], -1.0, None, ALU.mult, R=[negA], W=[negA])
        c.update(negA=negA, dtb=dtb)
        onw = P.sb([128, 2], F32, "onw")
        P.dma(onw[:, :], d['dn_out_norm'].ap().rearrange("l p -> p l"), W=[onw], allow_slow_non_contiguous=True)
        c['onw'] = onw
        wab = P.sb([128, 2, 8, 16], BF16, "wab")
        for l in range(2):
            P.dma(wab[:, l, :, :], d['dn_w_in'].ap()[l, :, 4096:4112].rearrange("(kc p) n -> p kc n", p=128), W=[wab], q='pool')
        c['wab'] = wab
        self.c = c
        self.pb = [P.ps([128, 512], F32, f"pb{i}") for i in range(6)]
        self.pt = [P.ps([128, 1024], BF16, f"pt{i}") for i in range(2)]

    def convert_weights(self):
        P, d = self.P, self.d
        k = [0]
        SW = 2048

        def stage():
            i = k[0]
            k[0] += 1
            f = self.ring('cvf', 2, lambda j: P.sb([128, SW], F32, f"cvf{j}"))
            b = self.ring('cvb', 2, lambda j: P.sb([128, SW], BF16, f"cvb{j}"))
            return f, b, ('act', 'dve', 'pool')[i % 3]

        def conv_chunked(W_ap, dst, nch):
            for g in range(nch // 2):
                f, b, e = stage()
                P.dma(f[:, :].rearrange("p (k n) -> p k n", k=8),
                      W_ap[:, g * 256:(g + 1) * 256].rearrange("(kc p) n -> p kc n", p=128), W=[f], q='sp')
                o = b[:, :].rearrange("p (c k n) -> p k c n", c=2, k=8)
                sv = f[:, :].rearrange("p (k c n) -> p k c n", k=8, c=2)
                self.cp(e, o, sv, R=[f], W=[b])
                P.dma(dst.ap()[g * 2:(g + 1) * 2].rearrange("c p k n -> p c (k n)"),
                      b[:, :].rearrange("p (c kn) -> p c kn", c=2), R=[b], W=[dst], q='act')

        def conv_natural(W_ap, dst, K, N):
            per = max(1, SW // N)
            nk = K // 128
            kc = 0
            while kc < nk:
                m = min(per, nk - kc)
                f, b, e = stage()
                P.dma(f[:, 0:m * N].rearrange("p (k n) -> p k n", k=m),
                      W_ap[kc * 128:(kc + m) * 128, :].rearrange("(k p) n -> p k n", p=128), W=[f], q='sp')
                self.cp(e, b[:, 0:m * N], f[:, 0:m * N], R=[f], W=[b])
                P.dma(dst.ap()[kc * 128:(kc + m) * 128, :].rearrange("(k p) n -> p k n", p=128),
                      b[:, 0:m * N].rearrange("p (k n) -> p k n", k=m), R=[b], W=[dst], q='act')
                kc += m

        for l in range(self.cfg['n_dn']):
            conv_chunked(d['dn_w_in'].ap()[l, :, 0:4096], d['wc_dn_in'][l], 32)
            conv_natural(d['dn_w_out'].ap()[l], d['wb_dn_out'][l], D, D)
            conv_chunked(d['ffn_w_in'].ap()[l], d['wc_ffn_in'][l], 44)
            conv_natural(d['ffn_w_out'].ap()[l], d['wb_ffn_out'][l], DFF, D)
        conv_natural(d['nsa_w_kv'].ap(), d['wb_kv'], D, 1536)
        if self.cfg.get('nsa', True):
            for j in range(2):
                conv_chunked(d['nsa_w_in'].ap()[j, :, 0:1024], d['wc_nsa_in'][j], 8)
                conv_natural(d['nsa_w_out'].ap()[j], d['wb_nsa_out'][j], D, D)
                conv_chunked(d['ffn_w_in'].ap()[2 + j], d['wc_ffn_in'][2 + j], 44)
                conv_natural(d['ffn_w_out'].ap()[2 + j], d['wb_ffn_out'][2 + j], DFF, D)

    def alloc_ctx(self, G):
        P = self.P
        n = G.name
        T = {}
        T['x'] = P.sb([G.TP, G.NT, D], F32, n + "x")
        T['xnT'] = P.sb([128, 8, G.GT], BF16, n + "xnT")
        big = P.sb([128, 24, G.GT], BF16, n + "big")
        T['qT'] = View(big, 0, 8)
        T['kT'] = View(big, 8, 8)
        T['vT'] = View(big, 16, 8)
        T['actT'] = View(big, 0, 22)
        T['zs'] = P.sb([128, H, G.GT], BF16, n + "zs")
        T['OT'] = P.sb([128, H, G.GT], F32, n + "OT")
        T['oT'] = P.sb([128, H, G.GT], BF16, n + "oT")
        T['carry'] = [P.sb([128, 24, G.NSEQ, 3], F32, n + f"carry{l}") for l in range(2)]
        T['ab'] = P.sb([64, G.NCH, 16], F32, n + "ab")
        T['g'] = P.sb([64, G.NCH, H], F32, n + "g")
        T['beta'] = P.sb([64, G.NCH, H], F32, n + "beta")
        G.T = T

    def norm_T(self, G, wrow_src):
        P, c, T = self.P, self.c, G.T
        TP = G.TP
        wrow = self.ring('wrow', 2, lambda j: P.sb([128, D], F32, f"wrow{j}"))
        P.dma(wrow[:, :], wrow_src.partition_broadcast(128), W=[wrow])
        x = T['x']
        for t in range(G.NT):
            junk = self.ring('junk', 1, lambda j: P.sb([128, D], BF16, f"junk{j}"))
            st = self.ring('nst', 4, lambda j: P.sb([128, 2], F32, f"nst{j}"))
            self.memset('pool', st[:, :], 0.0, W=[st])
            self.actf(junk[0:TP, :], x[:, t, :], AF.Square, R=[x], W=[junk, st], accum_out=st[0:TP, 0:1])
            self.ts('dve', st[0:TP, 1:2], st[0:TP, 0:1], 1.0 / D, 1e-6, ALU.mult, ALU.add, R=[st], W=[st])
            self.actf(st[0:TP, 1:2], st[0:TP, 1:2], AF.Sqrt, R=[st], W=[st])
            P.add('dve', lambda e, st=st: e.reciprocal(out=st[0:TP, 1:2], in_=st[0:TP, 1:2]), R=[st], W=[st])
            xn = self.ring('xn', 2, lambda j: P.sb([128, D], BF16, f"xn{j}"))
            self.stt('dve', xn[0:TP, :], x[:, t, :], st[0:TP, 1:2], wrow[0:TP, :], ALU.mult, ALU.mult,
                     R=[x, st, wrow], W=[xn])
            pt = self.pt[0]
            for cc in range(8):
                self.tr(pt[:, cc * 128:cc * 128 + TP], xn[0:TP, cc * 128:(cc + 1) * 128], c['identb'][0:TP, 0:TP],
                        R=[xn, c['identb']], W=[pt])
            self.cp('act', T['xnT'][:, :, t * TP:(t + 1) * TP],
                    pt[:, :].rearrange("p (c n) -> p c n", c=8)[:, :, 0:TP], R=[pt], W=[T['xnT']])

    def load_wchunk(self, src_ap):
        P = self.P
        wt = self.ring('wch', 4, lambda j: P.sb([128, 8, 128], BF16, f"wch{j}"))
        P.dma(wt[:, :, :], src_ap, W=[wt])
        return wt

    def next_pb(self, key, banks):
        r = self.rr.setdefault('pb_' + key, [0])
        b = banks[r[0] % len(banks)]
        r[0] += 1
        return self.pb[b]

    def dn_mixer(self, G, l, S_tiles, last_group):
        P, c, T, d = self.P, self.c, G.T, self.d
        GT, TP, NT, NSEQ, TS, NCH, NS = G.GT, G.TP, G.NT, G.NSEQ, G.TS, G.NCH, G.NS
        pre = 'cp_' if NS == 1 else 'cs_'
        self.norm_T(G, d['norm_mix'].ap()[l])
        xnT = T['xnT']
        pbs = self.pb[0]
        for n in range(NCH):
            for kc in range(8):
                self.mm(pbs[0:64, n * 16:(n + 1) * 16], xnT[:, kc, n * 64:(n + 1) * 64], c['wab'][:, l, kc, :],
                        start=(kc == 0), stop=(kc == 7), R=[xnT, c['wab']], W=[pbs])
        ab = T['ab']
        self.cp('act', ab[:, :, :], pbs[0:64, 0:NCH * 16].rearrange("p (n k) -> p n k", k=16), R=[pbs], W=[ab])
        gt = self.ring('gtmp', 2, lambda j: P.sb([64, 8, H], F32, f"gtmp{j}"))
        g2 = self.ring('gtmp', 2, lambda j: None)
        av = gt[:, 0:NCH, :]
        a2 = g2[:, 0:NCH, :]
        self.tt('dve', av, ab[:, :, 0:8], c['dtb'][0:64, l, :].unsqueeze(1).to_broadcast([64, NCH, H]), ALU.add,
                R=[ab, c['dtb']], W=[gt])
        self.actf(a2, av, AF.Abs, R=[gt], W=[g2])
        self.actf(a2, a2, AF.Exp, R=[g2], W=[g2], scale=-1.0)
        self.actf(a2, a2, AF.Ln, R=[g2], W=[g2], bias=1.0)
        self.stt('dve', a2, av, 0.0, a2, ALU.max, ALU.add, R=[gt, g2], W=[g2])
        self.tt('dve', T['g'][:, :, :], a2, c['negA'][0:64, l, :].unsqueeze(1).to_broadcast([64, NCH, H]), ALU.mult,
                R=[g2, c['negA']], W=[T['g']])
        self.actf(T['beta'][:, :, :], ab[:, :, 8:16], AF.Sigmoid, R=[ab], W=[T['beta']])

        carry = T['carry'][l]

        def chunk_gen(ci):
            wt = self.load_wchunk(d['wc_dn_in'][l].ap()[ci])
            ps = self.next_pb('lin', [1, 2, 3])
            for kc in range(8):
                self.mm(ps[:, 0:GT], wt[:, kc, :], xnT[:, kc, :], start=(kc == 0), stop=(kc == 7), R=[wt, xnT], W=[ps])
            hh = ci % 8
            if ci >= 24:
                self.actf(T['zs'][:, hh, :], ps[:, 0:GT], AF.Silu, R=[ps], W=[T['zs']])
                return
            xp = self.ring(G.name + 'xp', 2, lambda j: P.sb([128, NSEQ, TS + 3], F32, G.name + f"xp{j}"))
            self.cp('act', xp[:, :, 3:TS + 3], ps[:, 0:GT].rearrange("p (s t) -> p s t", s=NSEQ), R=[ps], W=[xp])
            self.cp('pool', xp[:, :, 0:3], carry[:, ci, :, :], R=[carry], W=[xp])
            self.cp('pool', carry[:, ci, :, :], xp[:, :, TS:TS + 3], R=[xp], W=[carry])
            acc = self.ring(G.name + 'acc', 2, lambda j: P.sb([128, NSEQ, TS], F32, G.name + f"acc{j}"))
            cwl = c['cw']
            self.ts('dve', acc[:, :, :], xp[:, :, 0:TS], cwl[:, l, ci, 0:1], None, ALU.mult, R=[xp, cwl], W=[acc])
            for i in range(1, 4):
                self.stt('dve', acc[:, :, :], xp[:, :, i:TS + i], cwl[:, l, ci, i:i + 1], acc[:, :, :], ALU.mult, ALU.add,
                         R=[xp, cwl, acc], W=[acc])
            accf = acc[:, :, :].rearrange("p s t -> p (s t)")
            if ci >= 16:
                self.actf(T['vT'][:, hh, :], accf, AF.Silu, R=[acc], W=[T['vT']])
                return
            sl = self.ring(G.name + 'sl', 2, lambda j: P.sb([128, GT], F32, G.name + f"sl{j}"))
            self.actf(sl[:, :], accf, AF.Silu, R=[acc], W=[sl])
            sq = self.ring(G.name + 'sq', 2, lambda j: P.sb([128, GT], BF16, G.name + f"sq{j}"))
            self.actf(sq[:, :], sl[:, :], AF.Square, R=[sl], W=[sq])
            ps2 = self.next_pb('nrm', [4, 5])
            self.mm(ps2[:, 0:GT], c['onesb'][:, :], sq[:, :], R=[c['onesb'], sq], W=[ps2])
            yield 0
            rn = self.ring(G.name + 'rn', 2, lambda j: P.sb([128, GT], F32, G.name + f"rn{j}"))
            self.ts('dve', rn[:, :], ps2[:, 0:GT], 1e-6, None, ALU.add, R=[ps2], W=[rn])
            self.actf(rn[:, :], rn[:, :], AF.Sqrt, R=[rn], W=[rn])
            P.add('dve', lambda e, rn=rn: e.reciprocal(out=rn[:, :], in_=rn[:, :]), R=[rn], W=[rn])
            dst = T['qT'] if ci < 8 else T['kT']
            scl = (128 ** -0.5) if ci < 8 else 1.0
            self.stt('dve', dst[:, hh, :], sl[:, :], scl, rn[:, :], ALU.mult, ALU.mult, R=[sl, rn], W=[dst])

        prevg = None
        for ci in range(32):
            gch = chunk_gen(ci)
            try:
                next(gch)
            except StopIteration:
                gch = None
            if prevg is not None:
                for _ in prevg:
                    pass
            prevg = gch
        if prevg is not None:
            for _ in prevg:
                pass

        for n in range(NCH):
            self.dn_chunk(G, l, n, S_tiles, pre)

        OT, oT, zs = T['OT'], T['oT'], T['zs']
        for hh in range(H):
            sq = self.ring(G.name + 'sq', 2, lambda j: None)
            self.actf(sq[:, :], OT[:, hh, :], AF.Square, R=[OT], W=[sq])
            ps2 = self.next_pb('nrm', [4, 5])
            self.mm(ps2[:, 0:GT], c['onesb'][:, :], sq[:, :], R=[c['onesb'], sq], W=[ps2])
            rn = self.ring(G.name + 'rn', 2, lambda j: None)
            self.ts('dve', rn[:, :], ps2[:, 0:GT], 1.0 / 128, 1e-6, ALU.mult, ALU.add, R=[ps2], W=[rn])
            self.actf(rn[:, :], rn[:, :], AF.Sqrt, R=[rn], W=[rn])
            P.add('dve', lambda e, rn=rn: e.reciprocal(out=rn[:, :], in_=rn[:, :]), R=[rn], W=[rn])
            self.stt('dve', rn[:, :], OT[:, hh, :], c['onw'][:, l:l + 1], rn[:, :], ALU.mult, ALU.mult,
                     R=[OT, c['onw'], rn], W=[rn])
            self.tt('dve', oT[:, hh, :], rn[:, :], zs[:, hh, :], ALU.mult, R=[rn, zs], W=[oT])

        self.tok_linear_add(G, oT, 8, d['wb_dn_out'][l])

        if last_group:
            self.conv_state_out(G, l)

    def tok_linear_add(self, G, aT, nk, wsrc):
        P, T = self.P, G.T
        TP, NT = G.TP, G.NT
        x = T['x']
        for half in range(2):
            banks = [self.pb[1 + t] for t in range(NT)]
            for kc in range(nk):
                wt = self.ring('wrh', 4, lambda j: P.sb([128, 512], BF16, f"wrh{j}"))
                P.dma(wt[:, :], wsrc.ap()[kc * 128:(kc + 1) * 128, half * 512:(half + 1) * 512], R=[wsrc], W=[wt])
                for t in range(NT):
                    self.mm(banks[t][0:TP, :], aT[:, kc, t * TP:(t + 1) * TP], wt[:, :], start=(kc == 0),
                            stop=(kc == nk - 1), R=[aT, wt], W=[banks[t]])
            for t in range(NT):
                self.tt('dve', x[:, t, half * 512:(half + 1) * 512], x[:, t, half * 512:(half + 1) * 512],
                        banks[t][0:TP, :], ALU.add, R=[x, banks[t]], W=[x])

    def conv_state_out(self, G, l):
        P, c, T, d = self.P, self.c, G.T, self.d
        NSEQ = G.NSEQ
        R3 = NSEQ * 3
        carry = T['carry'][l]
        for g4 in range(6):
            co = self.ring(G.name + 'co', 2, lambda j: P.sb([R3, 4, 128], F32, G.name + f"co{j}"))
            ps = self.next_pb('lin', [1, 2, 3])
            for j in range(4):
                ci = g4 * 4 + j
                self.tr(ps[0:R3, j * 128:(j + 1) * 128], carry[:, ci, :, :].rearrange("p s r -> p (s r)"),
                        c['identf'][:, :], R=[carry, c['identf']], W=[ps])
            self.cp('act', co[:, :, :], ps[0:R3, :].rearrange("p (j n) -> p j n", j=4), R=[ps], W=[co])
            if G.NS == 1:
                dst = d['p_dn_conv']
                P.dma(dst.ap()[l][:, g4 * 512:(g4 + 1) * 512].rearrange("r (c p) -> r c p", p=128), co[:, :, :],
                      R=[co], W=[dst], q='pool')
            else:
                dst = d['s_dn_conv']
                P.dma(dst.ap()[l][:, :, g4 * 512:(g4 + 1) * 512].rearrange("s r (c p) -> (s r) c p", p=128), co[:, :, :],
                      R=[co], W=[dst], q='pool')

    def conv_state_in(self, G, l):
        P, c, T, d = self.P, self.c, G.T, self.d
        R3 = G.NSEQ * 3
        carry = T['carry'][l]
        ci_t = self.ring(G.name + 'cin', 1, lambda j: P.sb([R3, 3072], F32, G.name + "cin"))
        P.dma(ci_t[:, :], d['state_dn_conv'].ap()[l].rearrange("s r n -> (s r) n"), W=[ci_t])
        for g4 in range(6):
            ps = self.next_pb('lin', [1, 2, 3])
            for j in range(4):
                ci = g4 * 4 + j
                self.tr(ps[:, j * R3:(j + 1) * R3], ci_t[:, ci * 128:(ci + 1) * 128], c['identf'][0:R3, 0:R3],
                        R=[ci_t, c['identf']], W=[ps])
            self.cp('act', carry[:, g4 * 4:(g4 + 1) * 4, :, :].rearrange("p c s r -> p c (s r)"),
                    ps[:, 0:4 * R3].rearrange("p (j n) -> p j n", j=4), R=[ps], W=[carry])

    def dn_chunk(self, G, l, n, S_tiles, pre):
        P, c, T = self.P, self.c, G.T
        NS = G.NS
        cs = slice(n * 64, (n + 1) * 64)
        qT, kT, vT = T['qT'], T['kT'], T['vT']
        ucs, maskT, noff, same, seqind = (c[pre + k] for k in ('ucs', 'maskT', 'noff', 'same', 'seqind'))
        gtok = T['g']
        beta = T['beta']
        pb = self.pb
        nm = G.name

        def sbt(key, shape, dt=F32, nbuf=2):
            pfx = nm if NS > 1 and key in ('SG', 'gl') else 'ck'
            return self.ring(pfx + key, nbuf, lambda j: P.sb(shape, dt, pfx + key + str(j)))

        self.mm(pb[0][0:64, 0:8], ucs[:, :], gtok[:, n, :], R=[ucs, gtok], W=[pb[0]])
        self.mm(pb[0][0:64, 8:16], same[:, :], gtok[:, n, :], R=[same, gtok], W=[pb[0]])
        Gt = sbt('Gt', [64, 16])
        self.cp('act', Gt[:, :], pb[0][0:64, 0:16], R=[pb[0]], W=[Gt])
        eGd = sbt('eGd', [64, 16])
        self.tt('dve', eGd[:, 8:16], Gt[:, 8:16], Gt[:, 0:8], ALU.subtract, R=[Gt], W=[eGd])
        self.cp('dve', eGd[:, 0:8], Gt[:, 0:8], R=[Gt], W=[eGd])
        self.actf(eGd[:, :], eGd[:, :], AF.Exp, R=[eGd], W=[eGd])
        SG = sbt('SG', [64, H, NS])
        self.tt('dve', SG[:, :, :], gtok[:, n, :].unsqueeze(2).to_broadcast([64, H, NS]),
                seqind[:, :].unsqueeze(1).to_broadcast([64, H, NS]), ALU.mult, R=[gtok, seqind], W=[SG])
        self.mm(pb[0][:, 16:16 + H * NS], c['onesf'][:, :], SG[:, :, :].rearrange("p h s -> p (h s)"),
                R=[c['onesf'], SG], W=[pb[0]])
        gl = sbt('gl', [128, H, NS])
        self.actf(gl[:, :, :].rearrange("p h s -> p (h s)"), pb[0][:, 16:16 + H * NS], AF.Exp, R=[pb[0]], W=[gl])
        UG = sbt('UG', nbuf=1, shape=[64, H, 64])
        self.tt('dve', UG[:, :, :], ucs[:, :].unsqueeze(1).to_broadcast([64, H, 64]),
                gtok[:, n, :].unsqueeze(2).to_broadcast([64, H, 64]), ALU.mult, R=[ucs, gtok], W=[UG])
        for hh in range(H):
            self.mm(pb[1][:, hh * 64:(hh + 1) * 64], c['onesf'][:, :], UG[:, hh, :], R=[c['onesf'], UG], W=[pb[1]])
        eGrow = sbt('eGrow', nbuf=1, shape=[128, H, 64])
        self.actf(eGrow[:, :, :].rearrange("p h i -> p (h i)"), pb[1][:, :], AF.Exp, R=[pb[1]], W=[eGrow])
        qgT = sbt('qgT', [128, H, 64], BF16)
        self.tt('dve', qgT[:, :, :], qT[:, :, cs], eGrow[:, :, :], ALU.mult, R=[qT, eGrow], W=[qgT])
        tmp = sbt('dtmp', nbuf=1, shape=[64, H, 64])
        self.tt('dve', tmp[:, :, :], pb[1][0:64, :].rearrange("p (h i) -> p h i", h=H),
                Gt[:, 0:8].unsqueeze(2).to_broadcast([64, H, 64]), ALU.subtract, R=[pb[1], Gt], W=[tmp])
        self.tt('pool', tmp[:, :, :], tmp[:, :, :], maskT[:, :].unsqueeze(1).to_broadcast([64, H, 64]), ALU.add,
                R=[tmp, maskT], W=[tmp])
        DT = sbt('DT', nbuf=1, shape=[64, H, 64])
        self.actf(DT[:, :, :], tmp[:, :, :], AF.Exp, R=[tmp], W=[DT])
        for hh in range(H):
            self.mm(pb[2][0:64, hh * 64:(hh + 1) * 64], kT[:, hh, cs], kT[:, hh, cs], R=[kT], W=[pb[2]])
        for hh in range(H):
            self.mm(pb[3][0:64, hh * 64:(hh + 1) * 64], kT[:, hh, cs], qT[:, hh, cs], R=[kT, qT], W=[pb[3]])
        aqkT = sbt('aqkT', [64, H, 64], BF16)
        self.tt('dve', aqkT[:, :, :], DT[:, :, :], pb[3][0:64, :].rearrange("p (h i) -> p h i", h=H), ALU.mult,
                R=[DT, pb[3]], W=[aqkT])
        nbo = sbt('nbo', nbuf=1, shape=[64, H, 64])
        self.tt('pool', nbo[:, :, :], beta[:, n, :].unsqueeze(2).to_broadcast([64, H, 64]),
                noff[:, :].unsqueeze(1).to_broadcast([64, H, 64]), ALU.mult, R=[beta, noff], W=[nbo])
        X = sbt('X', [64, H, 64])
        self.tt('dve', X[:, :, :], pb[2][0:64, :].rearrange("p (h i) -> p h i", h=H), nbo[:, :, :], ALU.mult,
                R=[pb[2], nbo], W=[X])
        self.tt('dve', X[:, :, :], X[:, :, :], DT[:, :, :], ALU.mult, R=[X, DT], W=[X])
        Xb = sbt('Xb', [64, H, 64], BF16)
        self.cp('pool', Xb[:, :, :], X[:, :, :], R=[X], W=[Xb])
        ptz = self.pt[0]
        for hh in range(H):
            self.tr(ptz[0:64, hh * 64:(hh + 1) * 64], Xb[:, hh, :], c['identb'][0:64, 0:64], R=[Xb, c['identb']], W=[ptz])
        Z = sbt('Zb', [64, H, 64], BF16)
        self.cp('act', Z[:, :, :].rearrange("p h i -> p (h i)"), ptz[0:64, 0:512], R=[ptz], W=[Z])
        Pm = sbt('Pm', [64, H, 64])
        self.tt('pool', Pm[:, :, :], X[:, :, :], c['identf'][0:64, 0:64].unsqueeze(1).to_broadcast([64, H, 64]), ALU.add,
                R=[X, c['identf']], W=[Pm])
        Pb = sbt('Pb', [64, H, 64], BF16)
        self.cp('act', Pb[:, :, :], Pm[:, :, :], R=[Pm], W=[Pb])
        Y = Xb
        for lv in range(G.nlev):
            last = (lv == G.nlev - 1)
            if not last:
                for hh in range(H):
                    self.mm(pb[5][0:64, hh * 64:(hh + 1) * 64], Z[:, hh, :], Y[:, hh, :], R=[Z, Y], W=[pb[5]])
            for hh in range(H):
                self.mm(pb[4][0:64, hh * 64:(hh + 1) * 64], Y[:, hh, :], Z[:, hh, :], R=[Z, Y], W=[pb[4]])
            Zn = sbt('Zb', [64, H, 64], BF16)
            self.cp('act', Zn[:, :, :].rearrange("p h i -> p (h i)"), pb[4][0:64, :], R=[pb[4]], W=[Zn])
            if not last:
                Yn = sbt('Xb', [64, H, 64], BF16)
                self.cp('dve', Yn[:, :, :].rearrange("p h i -> p (h i)"), pb[5][0:64, :], R=[pb[5]], W=[Yn])
                Y = Yn
            Z = Zn
            for hh in range(H):
                self.mm(pb[2][0:64, hh * 64:(hh + 1) * 64], Z[:, hh, :], Pb[:, hh, :], R=[Z, Pb], W=[pb[2]])
            Pn = sbt('Pm', [64, H, 64])
            self.tt('dve', Pn[:, :, :].rearrange("p h i -> p (h i)"), Pm[:, :, :].rearrange("p h i -> p (h i)"),
                    pb[2][0:64, :], ALU.add, R=[Pm, pb[2]], W=[Pn])
            Pm = Pn
            Pb = sbt('Pb', [64, H, 64], BF16)
            self.cp('pool', Pb[:, :, :], Pm[:, :, :], R=[Pm], W=[Pb])
        vtok = sbt('vtok', [64, H, 128], BF16, 1)
        ktok = sbt('ktok', [64, H, 128], BF16, 1)
        for src, dst, pt in ((vT, vtok, self.pt[0]), (kT, ktok, self.pt[1])):
            for hh in range(H):
                self.tr(pt[0:64, hh * 128:(hh + 1) * 128], src[:, hh, cs], c['identb'][:, :], R=[src, c['identb']], W=[pt])
            self.cp('act', dst[:, :, :].rearrange("p h d -> p (h d)"), pt[0:64, :], R=[pt], W=[dst])
        kdec = sbt('kdec', [64, H, 128], BF16)
        self.tt('pool', kdec[:, :, :], ktok[:, :, :], eGd[:, 8:16].unsqueeze(2).to_broadcast([64, H, 128]), ALU.mult,
                R=[ktok, eGd], W=[kdec])

        OT = T['OT']
        if NS == 1 and getattr(G, 'Sall', None) is not None:
            Sf8, Sb8 = G.Sall[l]
            pk = (pb[0], pb[1])
            for hh in range(H):
                self.mm(pk[hh // 4][0:64, (hh % 4) * 128:(hh % 4 + 1) * 128], kT[:, hh, cs], Sb8[:, hh, :], R=[kT, Sb8],
                        W=[pk[hh // 4]])
            r = sbt('rB', [64, H, 128], BF16, 1)
            rf = sbt('rBf', [64, H, 128], F32, 1)
            for b2 in range(2):
                self.tt('dve', rf[:, 4 * b2:4 * b2 + 4, :], pk[b2][0:64, :].rearrange("p (h d) -> p h d", h=4),
                        eGd[:, 4 * b2:4 * b2 + 4].unsqueeze(2).to_broadcast([64, 4, 128]), ALU.mult, R=[pk[b2], eGd], W=[rf])
            self.tt('pool', r[:, :, :], vtok[:, :, :], rf[:, :, :], ALU.subtract, R=[vtok, rf], W=[r])
            pu = (pb[2], pb[3])
            for hh in range(H):
                self.mm(pu[hh // 4][0:64, (hh % 4) * 128:(hh % 4 + 1) * 128], Pb[:, hh, :], r[:, hh, :], R=[Pb, r],
                        W=[pu[hh // 4]])
            U = sbt('UB', [64, H, 128], BF16, 1)
            for b2 in range(2):
                self.tt('dve', U[:, 4 * b2:4 * b2 + 4, :], pu[b2][0:64, :].rearrange("p (h d) -> p h d", h=4),
                        beta[:, n, 4 * b2:4 * b2 + 4].unsqueeze(2).to_broadcast([64, 4, 128]), ALU.mult, R=[pu[b2], beta], W=[U])
            po = pb[4]
            for hh in range(H):
                self.mm(po[:, hh * 64:(hh + 1) * 64], Sb8[:, hh, :], qgT[:, hh, :], start=True, stop=False, R=[Sb8, qgT], W=[po])
                self.mm(po[:, hh * 64:(hh + 1) * 64], U[:, hh, :], aqkT[:, hh, :], start=False, stop=True, R=[U, aqkT], W=[po])
            self.cp('act', OT[:, :, cs], po[:, :].rearrange("p (h i) -> p h i", h=H), R=[po], W=[OT])
            psn = (pb[5], pb[0])
            for hh in range(H):
                self.mm(psn[hh // 4][:, (hh % 4) * 128:(hh % 4 + 1) * 128], kdec[:, hh, :], U[:, hh, :], R=[kdec, U],
                        W=[psn[hh // 4]])
            for b2 in range(2):
                hs4 = slice(4 * b2, 4 * b2 + 4)
                self.tt('dve' if b2 == 0 else 'pool', Sf8[:, hs4, :], Sf8[:, hs4, :], gl[:, hs4, :].to_broadcast([128, 4, 128]), ALU.mult,
                        R=[Sf8, gl], W=[Sf8])
            for b2 in range(2):
                hs4 = slice(4 * b2, 4 * b2 + 4)
                self.tt('dve', Sf8[:, hs4, :], Sf8[:, hs4, :], psn[b2][:, :].rearrange("p (h d) -> p h d", h=4), ALU.add,
                        R=[Sf8, psn[b2]], W=[Sf8])
            self.cp('act', Sb8[:, :, :], Sf8[:, :, :], R=[Sf8], W=[Sb8])
            return
        for hh in range(H):
            Sf, Sb = S_tiles(hh)
            if NS > 1:
                cm = c[pre + 'colmask']
                kTm = sbt('kTm', [128, NS, 64], BF16)
                self.tt('pool', kTm[:, :, :], kT[:, hh, cs].unsqueeze(1).to_broadcast([128, NS, 64]),
                        cm[:, :].rearrange("p (s i) -> p s i", s=NS), ALU.mult, R=[kT, cm], W=[kTm])
                qgm = sbt('qgm', [128, NS, 64], BF16)
                self.tt('pool', qgm[:, :, :], qgT[:, hh, :].unsqueeze(1).to_broadcast([128, NS, 64]),
                        cm[:, :].rearrange("p (s i) -> p s i", s=NS), ALU.mult, R=[qgT, cm], W=[qgm])
                kdm = sbt('kdm', [64, NS, 128], BF16)
                self.tt('pool', kdm[:, :, :], kdec[:, hh, :].unsqueeze(1).to_broadcast([64, NS, 128]),
                        seqind[:, :].unsqueeze(2).to_broadcast([64, NS, 128]), ALU.mult, R=[kdec, seqind], W=[kdm])
            pks = pb[0]
            for s in range(NS):
                lhs = kTm[:, s, :] if NS > 1 else kT[:, hh, cs]
                self.mm(pks[0:64, 0:128], lhs, Sb[:, s, :], start=(s == 0), stop=(s == NS - 1),
                        R=[kTm if NS > 1 else kT, Sb], W=[pks])
            r = sbt('r', [64, 128], BF16)
            rf1 = sbt('rf1', [64, 128])
            self.ts('dve', rf1[:, :], pks[0:64, 0:128], eGd[:, hh:hh + 1], None, ALU.mult, R=[pks, eGd], W=[rf1])
            self.tt('dve', r[:, :], vtok[:, hh, :], rf1[:, :], ALU.subtract, R=[vtok, rf1], W=[r])
            pu = pb[1]
            self.mm(pu[0:64, 0:128], Pb[:, hh, :], r[:, :], R=[Pb, r], W=[pu])
            U = sbt('U', [64, 128], BF16)
            self.ts('dve', U[:, :], pu[0:64, 0:128], beta[:, n, hh:hh + 1], None, ALU.mult, R=[pu, beta], W=[U])
            po = pb[3]
            for s in range(NS):
                rhs = qgm[:, s, :] if NS > 1 else qgT[:, hh, :]
                self.mm(po[:, 0:64], Sb[:, s, :], rhs, start=(s == 0), stop=False, R=[Sb, qgm if NS > 1 else qgT], W=[po])
            self.mm(po[:, 0:64], U[:, :], aqkT[:, hh, :], start=False, stop=True, R=[U, aqkT], W=[po])
            self.cp('act', OT[:, hh, cs], po[:, 0:64], R=[po], W=[OT])
            for s0 in range(0, NS, 4):
                psn = pb[5] if (s0 // 4) % 2 == 0 else pb[4]
                ns = min(4, NS - s0)
                for s in range(s0, s0 + ns):
                    lhs = kdm[:, s, :] if NS > 1 else kdec[:, hh, :]
                    self.mm(psn[:, (s - s0) * 128:(s - s0 + 1) * 128], lhs, U[:, :], R=[kdm if NS > 1 else kdec, U], W=[psn])
                self.tt('dve', Sf[:, s0:s0 + ns, :], Sf[:, s0:s0 + ns, :],
                        gl[:, hh, s0:s0 + ns].unsqueeze(2).to_broadcast([128, ns, 128]), ALU.mult, R=[Sf, gl], W=[Sf])
                self.tt('dve', Sf[:, s0:s0 + ns, :], Sf[:, s0:s0 + ns, :],
                        psn[:, 0:ns * 128].rearrange("p (s d) -> p s d", s=ns), ALU.add, R=[Sf, psn], W=[Sf])
                self.cp('act', Sb[:, s0:s0 + ns, :], Sf[:, s0:s0 + ns, :], R=[Sf], W=[Sb])

    def ffn(self, G, l):
        P, c, T, d = self.P, self.c, G.T, self.d
        GT = G.GT
        self.norm_T(G, d['norm_ffn'].ap()[l])
        xnT, actT = T['xnT'], T['actT']
        for ci in range(22):
            wg = self.load_wchunk(d['wc_ffn_in'][l].ap()[ci])
            wu = self.load_wchunk(d['wc_ffn_in'][l].ap()[22 + ci])
            pg = self.next_pb('ffg', [1, 2])
            pu = self.next_pb('ffu', [3, 4])
            for kc in range(8):
                self.mm(pg[:, 0:GT], wg[:, kc, :], xnT[:, kc, :], start=(kc == 0), stop=(kc == 7), R=[wg, xnT], W=[pg])
            for kc in range(8):
                self.mm(pu[:, 0:GT], wu[:, kc, :], xnT[:, kc, :], start=(kc == 0), stop=(kc == 7), R=[wu, xnT], W=[pu])
            sg = self.ring(G.name + 'sl', 2, lambda j: P.sb([128, GT], F32, G.name + f"sl{j}"))
            self.actf(sg[:, :], pg[:, 0:GT], AF.Silu, R=[pg], W=[sg])
            self.tt('dve', actT[:, ci, :], sg[:, :], pu[:, 0:GT], ALU.mult, R=[sg, pu], W=[actT])
        self.tok_linear_add(G, actT, 22, d['wb_ffn_out'][l])

    def shared_rows(self, G, dst_kv, row0, win_cb):
        P, c, T, d = self.P, self.c, G.T, self.d
        TP, NT = G.TP, G.NT
        self.norm_T(G, d['norm_kv'].ap())
        xnT = T['xnT']
        for third in range(3):
            wt = self.ring('kvw', 1, lambda j: P.sb([128, 8, 512], BF16, f"kvw{j}"))
            P.dma(wt[:, :, :], d['wb_kv'].ap()[:, third * 512:(third + 1) * 512].rearrange("(kc p) n -> p kc n", p=128),
                  R=[d['wb_kv']], W=[wt])
            for t in range(NT):
                ps = self.next_pb('lin', [1, 2, 3])
                for kc in range(8):
                    self.mm(ps[0:TP, :], xnT[:, kc, t * TP:(t + 1) * TP], wt[:, kc, :], start=(kc == 0), stop=(kc == 7),
                            R=[xnT, wt], W=[ps])
                rp = self.ring(G.name + 'rowp', 2, lambda j: P.sb([TP, 512], F32, G.name + f"rowp{j}"))
                self.cp('act', rp[:, :], ps[0:TP, :], R=[ps], W=[rp])
                if third < 2:
                    P.dma(dst_kv.ap()[row0 + t * TP: row0 + (t + 1) * TP, third * 512:(third + 1) * 512], rp[:, :],
                          R=[rp], W=[dst_kv], q='pool')
                else:
                    win_cb(t, rp)

    def phase(self):
        from contextlib import ExitStack
        b = self

        class _Ph:
            def __enter__(self_):
                self_.st = ExitStack()
                b.P.stack = self_.st
                b.rr = {}
                return self_

            def __exit__(self_, *a):
                b.P.barrier()
                b.P.stack = None
                b.rr = {}
                self_.st.close()
                return False
        return _Ph()

    def build(self):
        P, cfg = self.P, self.cfg
        self.declare_io()
        d = self.d
        nsa = cfg.get('nsa', True)
        if nsa:
            self.nsa_declare()
        self.consts()
        if nsa:
            self.nsa_consts()
        with self.phase():
            self.convert_weights()
        if nsa:
            with self.phase():
                self.nsa_tables()
        n_dn = cfg['n_dn']
        if cfg.get('prompt', True):
          with self.phase():
            G = Ctx('p', 128, 4, 1, 1, 5)
            self.alloc_ctx(G)
            T = G.T
            Sp = [(P.sb([128, H, 128], F32, f"Sp{l}"), P.sb([128, H, 128], BF16, f"Sbp{l}")) for l in range(2)]
            G.Sall = Sp
            for l in range(2):
                self.memset('pool', Sp[l][0][:, :, :], 0.0, W=[Sp[l][0]])
                self.memset('pool', Sp[l][1][:, :, :], 0.0, W=[Sp[l][1]])
                self.memset('pool', T['carry'][l][:, :, :, :], 0.0, W=[T['carry'][l]])
            ngrp = cfg['SEQ'] // G.GT
            for g in range(ngrp):
                P.dma(T['x'][:, :, :], d['x_prompt'].ap()[g * 512:(g + 1) * 512, :].rearrange("(t p) n -> p t n", p=128),
                      W=[T['x']])
                for l in range(n_dn):
                    self.dn_mixer(G, l, None, last_group=(g == ngrp - 1))
                    self.ffn(G, l)
                P.dma(d['x2_p'].ap()[g * 512:(g + 1) * 512, :].rearrange("(t p) n -> p t n", p=128), T['x'][:, :, :],
                      R=[T['x']], W=[d['x2_p']], q='pool')

                def win_cb(t, rp, g=g):
                    r0 = g * 512 + t * 128 - (cfg['SEQ'] - 512)
                    if nsa:
                        P.dma(d['pwin_d'].ap()[g * 512 + t * 128:g * 512 + (t + 1) * 128, :], rp[:, :], R=[rp],
                              W=[d['pwin_d']], q='pool')
                    if r0 >= 0:
                        P.dma(d['p_win_kv'].ap()[r0:r0 + 128, :], rp[:, :], R=[rp], W=[d['p_win_kv']], q='pool')
                self.shared_rows(G, d['p_kv_rows'], g * 512, win_cb)
            for l in range(2):
                P.dma(d['p_dn_S'].ap()[l].rearrange("h k v -> k h v"), Sp[l][0][:, :, :], R=[Sp[l][0]], W=[d['p_dn_S']], q='pool')
        if nsa and cfg.get('prompt', True) and cfg.get('nsa_stop', 99) > 1:
            try:
                self.nsa_prompt()
            except _Stop:
                pass
        if cfg.get('sample', True):
          with self.phase():
            G = Ctx('s', 64, 1, 16, 16, 1)
            self.alloc_ctx(G)
            T = G.T
            P.dma(T['x'][:, 0, :], d['x_sample'].ap(), W=[T['x']])
            Ss = P.sb([128, 16, 128], F32, "Ss")
            Ssb = P.sb([128, 16, 128], BF16, "Ssb")
            for l in range(n_dn):
                self.conv_state_in(G, l)
                cur = [None]

                def S_tiles(hh, l=l, cur=cur):
                    if cur[0] != hh:
                        if cur[0] is not None:
                            P.dma(d['s_dn_S'].ap()[l, :, cur[0]].rearrange("s k v -> k s v"), Ss[:, :, :], R=[Ss],
                                  W=[d['s_dn_S']], q='pool')
                        P.dma(Ss[:, :, :], d['state_dn_S'].ap()[l, :, hh].rearrange("s k v -> k s v"), W=[Ss])
                        self.cp('act', Ssb[:, :, :], Ss[:, :, :], R=[Ss], W=[Ssb])
                        cur[0] = hh
                    return Ss, Ssb
                self.dn_mixer(G, l, S_tiles, last_group=True)
                P.dma(d['s_dn_S'].ap()[l, :, cur[0]].rearrange("s k v -> k s v"), Ss[:, :, :], R=[Ss], W=[d['s_dn_S']],
                      q='pool')
                self.ffn(G, l)
            P.dma(d['x2_s'].ap(), T['x'][:, 0, :], R=[T['x']], W=[d['x2_s']], q='pool')
            for s4 in range(4):
                P.dma(d['s_win_kv'].ap()[s4 * 4:(s4 + 1) * 4, 0:508, :], d['state_win_kv'].ap()[s4 * 4:(s4 + 1) * 4, 4:512, :],
                      W=[d['s_win_kv']], q='sp')

            def win_cb_s(t, rp):
                for sq in range(16):
                    P.dma(d['s_win_kv'].ap()[sq, 508:512, :], rp[4 * sq:4 * sq + 4, :], R=[rp],
                          W=[d['s_win_kv']], q='pool')
            self.shared_rows(G, d['s_kv_rows'], 0, win_cb_s)
        if nsa and cfg.get('sample', True):
            self.nsa_sample()
        P.emit()
        return self.nc


_CONST_CACHE = {}


def const_inputs():
    if not _CONST_CACHE:
        for pre, NS, TS in (('cp_', 1, 64), ('cs_', 16, 4)):
            for k, v in host_consts(NS, TS).items():
                _CONST_CACHE[pre + k] = v
    return _CONST_CACHE


def make_in_maps(inp, cfg, n_cores=8):
    SEQ = cfg['SEQ']
    cst = const_inputs()
    maps = []
    shared = {k: np.ascontiguousarray(inp[k]) for k in
              ('norm_mix', 'norm_ffn', 'norm_kv', 'norm_final', 'ffn_w_in', 'ffn_w_out', 'dn_w_in', 'dn_conv_w',
               'dn_A_log', 'dn_dt_bias', 'dn_out_norm', 'dn_w_out', 'nsa_w_kv')}
    nsa = cfg.get('nsa', True)
    if nsa:
        shared['nsa_w_in'] = np.ascontiguousarray(inp['nsa_w_in'])
        shared['nsa_w_out'] = np.ascontiguousarray(inp['nsa_w_out'])
        shared['nsa_cmp_pos_w'] = np.ascontiguousarray(inp['nsa_cmp_pos_w']).reshape(2, 32, 256)
        shared['nsa_w_cmp'] = np.ascontiguousarray(inp['nsa_w_cmp'])
        shared['rel_bias'] = np.ascontiguousarray(inp['rel_bias'])
        shared['cache_kv'] = np.ascontiguousarray(inp['cache_kv']).reshape(2560 * 128, 1024)
        nsc = [nsa_host_consts(0), nsa_host_consts(1)]
    for c in range(n_cores):
        b = c // 2
        m = dict(shared)
        m.update(cst)
        m['x_prompt'] = np.ascontiguousarray(inp['x_prompt'][b, :SEQ])
        sl = slice(16 * c, 16 * c + 16)
        m['x_sample'] = np.ascontiguousarray(inp['x_sample'][sl]).reshape(64, D)
        m['state_dn_S'] = np.ascontiguousarray(inp['state_dn_S'][:, sl])
        m['state_dn_conv'] = np.ascontiguousarray(inp['state_dn_conv'][:, sl])
        m['state_win_kv'] = np.ascontiguousarray(inp['state_win_kv'][sl]).reshape(16, 512, 512)
        if nsa:
            m.update(nsc[c % 2])
            m['page_table'] = np.ascontiguousarray(inp['page_table'][sl]).reshape(1, 256).astype(np.int32)
        maps.append(m)
    return maps


def kernel(**inp):
    cfg = dict(SEQ=4096, n_dn=2)
    b = Builder(cfg)
    nc = b.build()
    maps = make_in_maps(inp, cfg)
    res = run_bass_kernel_spmd(nc, maps, core_ids=list(range(8)))
    R = res.results
    f32 = np.float32
    y_prompt = np.zeros((4, 4096, D), f32)
    for c in range(8):
        yp = R[c]['y_p'].reshape(16, 128, D)
        y_prompt[c // 2].reshape(32, 128, D)[c % 2::2] = yp
    y_sample = np.concatenate([R[c]['y_s'].reshape(16, 4, D) for c in range(8)], axis=0).astype(f32)
    p_dn_S = np.stack([R[2 * b]['p_dn_S'] for b in range(4)], axis=1)
    p_dn_conv = np.stack([R[2 * b]['p_dn_conv'] for b in range(4)], axis=1)
    p_kv_rows = np.stack([R[2 * b]['p_kv_rows'] for b in range(4)], axis=0).reshape(4, 4096, 4, 4, 64)
    p_win_kv = np.stack([R[2 * b]['p_win_kv'] for b in range(4)], axis=0).reshape(4, 512, 2, 4, 64)
    s_dn_S = np.concatenate([R[c]['s_dn_S'] for c in range(8)], axis=1)
    s_dn_conv = np.concatenate([R[c]['s_dn_conv'] for c in range(8)], axis=1)
    s_kv_rows = np.concatenate([R[c]['s_kv_rows'].reshape(16, 4, 4, 4, 64) for c in range(8)], axis=0)
    s_win_kv = np.concatenate([R[c]['s_win_kv'].reshape(16, 512, 2, 4, 64) for c in range(8)], axis=0)
    return (y_prompt, y_sample, p_dn_S.astype(f32), p_dn_conv.astype(f32), p_kv_rows.astype(f32),
            p_win_kv.astype(f32), s_dn_S.astype(f32), s_dn_conv.astype(f32), s_kv_rows.astype(f32),
            s_win_kv.astype(f32))


def _bucket_np(d):
    n = np.maximum(d, 0)
    nf = np.maximum(n, 1).astype(np.float32)
    large = 16 + (np.log(nf / np.float32(16)) / np.float32(math.log(64.0)) * np.float32(16)).astype(np.int32)
    large = np.minimum(large, 31)
    return np.where(n < 16, n, large)


def _onehot(d, valid):
    b = np.where(valid, _bucket_np(d), 32)
    oh = np.zeros((33, d.shape[0]), np.float32)
    oh[b, np.arange(d.shape[0])] = 1.0
    return oh


def nsa_host_consts(par):
    c = {}
    t = np.arange(1280)
    d = t - 255 + 128 * par
    c['oh_sel'] = _onehot(d, d >= 0)
    t = np.arange(1024)
    d = t - 255 + 128 * par
    c['oh_win'] = _onehot(d, (d >= 0) & (d < 512))
    r = np.arange(16)[:, None]
    w = np.arange(512)[None, :]
    d = (16 * (247 - w + 8 * par) + r - 31).reshape(-1)
    c['oh_cmp'] = _onehot(d, d >= 0)
    tt = np.arange(4)[:, None]
    cc = np.arange(128)[None, :]
    d = (2017 + tt - 16 * cc).reshape(-1)
    c['oh_cs'] = _onehot(d, d >= 0)
    x = np.arange(2304)
    d = x - 127
    c['oh_ss'] = _onehot(d, d >= 0)
    x = np.arange(768)
    d = x - 127
    c['oh_ws'] = _onehot(d, (d >= 0) & (d < 512))
    blk = np.arange(64)[None, None, :]
    qpos = (128 * (2 * np.arange(16)[:, None, None] + par) + np.arange(128)[None, :, None])
    cur = qpos // 64
    forced = (blk == 0) | (blk == cur) | (blk == cur - 1)
    valid = blk * 64 <= qpos
    c['selmul_p'] = np.ascontiguousarray(np.where(valid & ~forced, 1.0, 0.0).astype(np.float32).transpose(1, 0, 2))
    c['seladd_p'] = np.ascontiguousarray(np.where(forced, 1e4, np.where(valid, 0.0, -1.0)).astype(np.float32).transpose(1, 0, 2))
    blk = np.arange(64)[None, :]
    qpos = 2048 + np.arange(4)[:, None]
    cur = qpos // 64
    exists = blk < 33
    forced = ((blk == 0) | (blk == cur) | (blk == cur - 1)) & exists
    valid = (blk * 64 <= qpos) & exists
    c['selmul_s'] = np.where(valid & ~forced, 1.0, 0.0).astype(np.float32)
    c['seladd_s'] = np.where(forced, 1e4, np.where(valid, 0.0, np.where(exists, -1.0, -2.0))).astype(np.float32)
    k = np.arange(4096)[None, :]
    c['expE'] = (k // 64 == np.arange(64)[:, None]).astype(np.float32)
    sel8 = np.zeros((128, 8), np.float32)
    sel8[np.arange(128), np.arange(128) // 16] = 1.0
    c['sel8'] = sel8
    c['antiI'] = np.ascontiguousarray(np.eye(128, dtype=np.float32)[::-1])
    c['parf'] = np.tile(np.array([[float(par), 1.0 - float(par)]], np.float32), (128, 1))
    c['iota'] = np.arange(128, dtype=np.float32).reshape(128, 1)
    return c


def _nsa_declare(self):
    P, cfg, d = self.P, self.cfg, self.d
    SEQ = cfg['SEQ']

    def inp(name, shape, dt=F32):
        d[name] = P.dram(name, shape, dt, kind="ExternalInput")

    inp('nsa_w_in', [2, D, 1072])
    inp('nsa_w_out', [2, D, D])
    inp('nsa_cmp_pos_w', [2, 32, 256])
    inp('nsa_w_cmp', [2, 4, 64, 64])
    inp('rel_bias', [32, 16])
    inp('cache_kv', [2560 * 128, 1024])
    inp('page_table', [1, 256], I32)
    for nm, shp in (('oh_sel', [33, 1280]), ('oh_win', [33, 1024]), ('oh_cmp', [33, 8192]), ('oh_cs', [33, 512]),
                    ('oh_ss', [33, 2304]), ('oh_ws', [33, 768]), ('selmul_p', [128, 16, 64]), ('seladd_p', [128, 16, 64]),
                    ('selmul_s', [4, 64]), ('seladd_s', [4, 64]), ('expE', [64, 4096]), ('sel8', [128, 8]),
                    ('antiI', [128, 128]), ('parf', [128, 2]), ('iota', [128, 1])):
        inp(nm, shp)
    d['y_p'] = P.dram('y_p', [SEQ // 2, D], F32, kind="ExternalOutput")
    d['y_s'] = P.dram('y_s', [64, D], F32, kind="ExternalOutput")
    d['wc_nsa_in'] = [P.dram(f'wc_nsa_in{j}', [8, 128, 8, 128], BF16) for j in range(2)]
    d['wb_nsa_out'] = [P.dram(f'wb_nsa_out{j}', [D, D], BF16) for j in range(2)]
    for nm, ln in (('t_sel', 1280), ('t_win', 1024), ('t_cmp', 8192), ('t_cs', 512), ('t_ss', 2304), ('t_ws', 768)):
        d[nm] = P.dram(nm, [16, ln], F32)
    d['BTd'] = P.dram('BTd', [4, 15, 128, 512], F32)
    NT = SEQ // 128
    d['pwin_d'] = P.dram('pwin_d', [SEQ, 512], F32)
    d['kselT_p'] = P.dram('kselT_p', [4, 128, SEQ], BF16)
    d['vsel_p'] = P.dram('vsel_p', [NT, 128, 4, 66], BF16)
    d['kwinT_p'] = P.dram('kwinT_p', [4, 128, SEQ], BF16)
    d['vwin_p'] = P.dram('vwin_p', [NT, 128, 4, 66], BF16)
    d['kselT_s'] = P.dram('kselT_s', [16, 4, 128, 17 * 128], BF16)
    d['vsel_s'] = P.dram('vsel_s', [16, 17, 128, 4, 66], BF16)
    d['kwinT_s'] = P.dram('kwinT_s', [16, 4, 128, 5 * 128], BF16)
    d['vwin_s'] = P.dram('vwin_s', [16, 5, 128, 4, 66], BF16)


def _nsa_consts(self):
    P, d, c = self.P, self.d, self.c
    antiI = P.sb([128, 128], F32, "antiI")
    P.dma(antiI[:, :], d['antiI'].ap(), W=[antiI])
    sel8 = P.sb([128, 8], BF16, "sel8")
    P.dma(sel8[:, :], d['sel8'].ap(), W=[sel8], q='pool')
    parf = P.sb([128, 2], F32, "parf")
    P.dma(parf[:, :], d['parf'].ap(), W=[parf])
    tabx = P.sb([33, 16], F32, "tabx")
    r31 = P.sb([32, 16], F32, "r31")
    P.dma(tabx[0:32, :], d['rel_bias'].ap(), W=[tabx])
    P.dma(r31[:, :], d['rel_bias'].ap()[31].partition_broadcast(32), W=[r31])
    self.tt('dve', tabx[0:32, :], tabx[0:32, :], r31[:, :], ALU.subtract, R=[tabx, r31], W=[tabx])
    self.memset('pool', tabx[32:33, :], NEG, W=[tabx])
    c.update(antiI=antiI, sel8=sel8, parf=parf, tabx=tabx)
    wg = P.sb([128, 2, 8, 48], BF16, "wg")
    for j in range(2):
        P.dma(wg[:, j, :, :], d['nsa_w_in'].ap()[j, :, 1024:1072].rearrange("(kc p) n -> p kc n", p=128), W=[wg], q='pool')
    c['wg'] = wg


def _nsa_late_consts(self):
    P, d, c = self.P, self.d, self.c
    if 'wlo' in c:
        return
    wlo = P.sb([128, 512], F32, "wlo")
    whi = P.sb([128, 512], F32, "whi")
    for a in range(8):
        P.dma(wlo[16 * a:16 * a + 16, :].rearrange("r (c n) -> r c n", c=2),
              d['nsa_cmp_pos_w'].ap()[:, 0:16, :].rearrange("c r n -> r c n"), W=[wlo])
        P.dma(whi[16 * a:16 * a + 16, :].rearrange("r (c n) -> r c n", c=2),
              d['nsa_cmp_pos_w'].ap()[:, 16:32, :].rearrange("c r n -> r c n"), W=[whi])
    wck = P.sb([128, 4, 128], BF16, "wck")
    wcv = P.sb([128, 4, 64], BF16, "wcv")
    for half in range(2):
        for dup in range(2):
            P.dma(wck[half * 64:(half + 1) * 64, :, dup * 64:(dup + 1) * 64],
                  d['nsa_w_cmp'].ap()[0].rearrange("k d e -> d k e"), W=[wck], q='pool')
        P.dma(wcv[half * 64:(half + 1) * 64, :, :], d['nsa_w_cmp'].ap()[1].rearrange("k d e -> d k e"), W=[wcv], q='pool')
    expE = P.sb([64, 4096], BF16, "expE")
    P.dma(expE[:, :], d['expE'].ap(), W=[expE], q='pool')
    c.update(wlo=wlo, whi=whi, wck=wck, wcv=wcv, expE=expE)


Builder.nsa_late_consts = _nsa_late_consts


def _nsa_tables(self):
    P, d, c = self.P, self.d, self.c
    for oh, dst, ln in (('oh_sel', 't_sel', 1280), ('oh_win', 't_win', 1024), ('oh_cmp', 't_cmp', 8192),
                        ('oh_cs', 't_cs', 512), ('oh_ss', 't_ss', 2304), ('oh_ws', 't_ws', 768)):
        for off in range(0, ln, 512):
            n = min(512, ln - off)
            ot = self.ring('oht', 2, lambda j: P.sb([33, 512], F32, f"oht{j}"))
            P.dma(ot[:, 0:n], d[oh].ap()[:, off:off + n], W=[ot])
            ps = self.next_pb('lin', [1, 2, 3])
            self.mm(ps[0:16, 0:n], c['tabx'][:, :], ot[:, 0:n], R=[c['tabx'], ot], W=[ps])
            tb = self.ring('tbo', 2, lambda j: P.sb([16, 512], F32, f"tbo{j}"))
            self.cp('act', tb[:, 0:n], ps[0:16, 0:n], R=[ps], W=[tb])
            P.dma(d[dst].ap()[:, off:off + n], tb[:, 0:n], R=[tb], W=[d[dst]], q='pool')
    if self.cfg.get('prompt', True):
        for kvh in range(4):
            for idx in range(15):
                tab, ln, e = ('t_sel', 1280, idx - 1) if idx < 9 else ('t_win', 1024, idx - 10)
                tr_ = self.ring('trv', 2, lambda j: P.sb([128, 4, 128], F32, f"trv{j}"))
                src = bass.AP(d[tab].h, 4 * kvh * ln + 128 * (e + 1), [[1, 128], [ln, 4], [1, 128]])
                P.dma(tr_[:, :, :], src, R=[d[tab]], W=[tr_])
                ps = self.next_pb('lin', [1, 2, 3])
                self.mm(ps[:, :], c['antiI'][:, :], tr_[:, :, :].rearrange("p g q -> p (g q)"), R=[c['antiI'], tr_], W=[ps])
                fl = self.ring('flp', 2, lambda j: P.sb([128, 512], F32, f"flp{j}"))
                self.cp('act', fl[:, :], ps[:, :], R=[ps], W=[fl])
                P.dma(d['BTd'].ap()[kvh, idx], fl[:, :], R=[fl], W=[d['BTd']], q='pool')


def _ctx_rows(self, rows, P_idx, lohi, kT_dst, v_dst, kw_dst, vw_dst, has_cmpsel=True, has_win=True, win_rows=None):
    P, c = self.P, self.c
    if has_cmpsel:
        if lohi is not None:
            alo = self.ring('alo', 2, lambda j: P.sb([128, 512], BF16, f"alo{j}"))
            ahi = self.ring('ahi', 2, lambda j: P.sb([128, 512], BF16, f"ahi{j}"))
            self.tt('pool', alo[:, :], rows[:, 0:512], c['wlo'][:, :], ALU.mult, R=[rows, c['wlo']], W=[alo])
            self.tt('dve', ahi[:, :], rows[:, 0:512], c['whi'][:, :], ALU.mult, R=[rows, c['whi']], W=[ahi])
            ps = self.next_pb('ctxp', [0])
            for lh, a in enumerate((alo, ahi)):
                for ch in range(4):
                    self.mm(ps[:, (lh * 4 + ch) * 8:(lh * 4 + ch + 1) * 8], a[:, ch * 128:(ch + 1) * 128], c['sel8'][:, :],
                            R=[a, c['sel8']], W=[ps])
            self.cp('act', lohi[:, :, :, 8 * P_idx:8 * P_idx + 8], ps[:, 0:64].rearrange("p (l c m) -> p l c m", l=2, c=4),
                    R=[ps], W=[lohi])
        for (col0, kdst, vdst) in ((512, kT_dst, v_dst),):
            _kv_tile(self, rows, col0, kdst, vdst)
    if has_win:
        wt, wc0 = win_rows
        _kv_tile(self, wt, wc0, kw_dst, vw_dst)


def _kv_tile(self, rows, col0, kdst, vdst):
    P, c = self.P, self.c
    kd = self.ring('kd', 2, lambda j: P.sb([128, 4, 2, 64], BF16, f"kd{j}"))
    self.cp('dve', kd[:, :, :, :], rows[:, col0:col0 + 256].rearrange("p (k d) -> p k d", k=4).unsqueeze(2).to_broadcast([128, 4, 2, 64]),
            R=[rows], W=[kd])
    pt = self.pt[1]
    for k in range(4):
        self.tr(pt[:, k * 128:(k + 1) * 128], kd[:, k, :, :].rearrange("p a d -> p (a d)"), c['identb'][:, :],
                R=[kd, c['identb']], W=[pt])
    ks = self.ring('ks', 2, lambda j: P.sb([128, 4, 128], BF16, f"ks{j}"))
    self.cp('act', ks[:, :, :].rearrange("p k n -> p (k n)"), pt[:, 0:512], R=[pt], W=[ks])
    kap, ktile = kdst
    P.dma(kap, ks[:, :, :], R=[ks], W=[ktile], q='sp')
    va = self.ring('va', 2, lambda j: P.sb([128, 4, 66], BF16, f"va{j}"))
    self.memset('dve', va[:, :, 64:65], 1.0, W=[va])
    self.memset('dve', va[:, :, 65:66], 0.0, W=[va])
    self.cp('dve', va[:, :, 0:64], rows[:, col0 + 256:col0 + 512].rearrange("p (k d) -> p k d", k=4), R=[rows], W=[va])
    vap, vtile = vdst
    P.dma(vap, va[:, :, :], R=[va], W=[vtile], q='sp')


def _cmp_finish(self, lohi, NCB, kcd_ap, vc_ap_fn, kcd_t, vc_t, ncw=256, njh=2):
    P, c = self.P, self.c
    bl = self.ring('blk', 1, lambda j: P.sb([128, 4, 256], BF16, "blk"))
    self.memset('pool', bl[:, :, :], 0.0, W=[bl])
    self.tt('dve', bl[:, :, 0:NCB], lohi[:, 0, :, 0:NCB], lohi[:, 1, :, 1:NCB + 1], ALU.add, R=[lohi], W=[bl])
    for kvh in range(4):
        hs = slice((kvh % 2) * 64, (kvh % 2) * 64 + 64)
        ps = self.next_pb('lin', [1, 2, 3])
        self.mm(ps[:, 0:256], c['wck'][hs, kvh, :], bl[hs, kvh // 2, :], R=[c['wck'], bl], W=[ps])
        self.cp('act', kcd_ap(kvh), ps[:, 0:ncw], R=[ps], W=[kcd_t])
        for jh in range(njh):
            ps2 = self.next_pb('lin', [1, 2, 3])
            self.mm(ps2[:, 0:64], bl[hs, 2 + kvh // 2, jh * 128:(jh + 1) * 128], c['wcv'][hs, kvh, :], R=[bl, c['wcv']], W=[ps2])
            self.cp('act', vc_ap_fn(jh, kvh), ps2[:, 0:64], R=[ps2], W=[vc_t])


Builder.nsa_declare = _nsa_declare
Builder.nsa_consts = _nsa_consts
Builder.nsa_tables = _nsa_tables
Builder.ctx_rows = _ctx_rows
Builder.cmp_finish = _cmp_finish


def _attend_gen(self, A):
    P, c = self.P, self.c
    NQ, NC, kvh = A['NQ'], A['NC'], A['kvh']
    pb = self.pb
    qT, qTt = A['qT']
    qbd, qcols = A['qbd'], A['qcols']
    gate = A['gate']
    N4 = 4 * NQ

    def sbt(key, shape, dt=F32, nbuf=2):
        k = f"at{NQ}{key}"
        return self.ring(k, nbuf, lambda j: P.sb(shape, dt, k + str(j)))

    if A.get('stop', 99) == 0:
        raise _Stop()
    kc_ap, kc_t = A['kcmp']
    for g in range(4):
        ps = pb[g % 2]
        self.mm(ps[0:NQ, (g // 2) * 256:(g // 2) * 256 + NC], qT(g), kc_ap(g), R=[qTt, kc_t], W=[ps])
    yield 0
    if A.get('stop', 99) == 10:
        raise _Stop()
    bc_ap, bc_t = A['bias_c']
    sc = sbt('sc', [NQ, 4, 256], nbuf=1)
    for g in range(4):
        ps = pb[g % 2]
        self.tt('dve', sc[:, g, 0:NC], ps[0:NQ, (g // 2) * 256:(g // 2) * 256 + NC], bc_ap[:, g, 0:NC], ALU.add,
                R=[ps, bc_t], W=[sc])
    yield 0
    if A.get('stop', 99) == 11:
        raise _Stop()
    ssum = sbt('ssum', [NQ, 8])
    self.memset('pool', ssum[:, :], 0.0, W=[ssum])
    for g in range(4):
        self.actf(sc[:, g, 0:NC], sc[:, g, 0:NC], AF.Exp, R=[sc], W=[sc, ssum], accum_out=ssum[:, g:g + 1])
    yield 0
    if A.get('stop', 99) == 12:
        raise _Stop()
    self.ts('dve', ssum[:, 4:8], ssum[:, 0:4], 1e-30, None, ALU.max, R=[ssum], W=[ssum])
    P.add('dve', lambda e: e.reciprocal(out=ssum[:, 4:8], in_=ssum[:, 4:8]), R=[ssum], W=[ssum])
    if A.get('stop', 99) == 13:
        raise _Stop()
    pc = sbt('pc', [NQ, 4, 256], nbuf=1)
    if not self.rr.get(f'pcinit{NQ}'):
        self.memset('pool', pc[:, :, :], 0.0, W=[pc])
    self.tt('dve', pc[:, :, 0:NC], sc[:, :, 0:NC], ssum[:, 4:8].unsqueeze(2).to_broadcast([NQ, 4, NC]), ALU.mult,
            R=[sc, ssum], W=[pc])
    yield 0
    if A.get('stop', 99) == 1:
        raise _Stop()
    imp = sbt('imp', [NQ, 264], nbuf=1)
    if not self.rr.get(f'pcinit{NQ}'):
        self.memset('pool', imp[:, :], 0.0, W=[imp])
        self.rr[f'pcinit{NQ}'] = True
    P.add('dve', lambda e: e.tensor_reduce(out=imp[:, 1:257], in_=pc[:, :, :].rearrange("p g n -> p n g"),
                                           axis=mybir.AxisListType.X, op=ALU.add), R=[pc], W=[imp])
    yield 0
    cov = sbt('cov', [NQ, 256], nbuf=1)
    self.tt('dve', cov[:, :], imp[:, 1:257], imp[:, 0:256], ALU.add, R=[imp], W=[cov])
    psl = sbt('psl', [NQ, 64], nbuf=1)
    P.add('dve', lambda e: e.tensor_reduce(out=psl[:, :], in_=cov[:, :].rearrange("p (b r) -> p b r", r=4),
                                           axis=mybir.AxisListType.X, op=ALU.add), R=[cov], W=[psl])
    yield 0
    smul, sadd, s_t = A['selc']
    self.tt('dve', psl[:, :], psl[:, :], smul, ALU.mult, R=[psl, s_t], W=[psl])
    self.tt('dve', psl[:, :], psl[:, :], sadd, ALU.add, R=[psl, s_t], W=[psl])
    yield 0
    m16 = sbt('m16', [NQ, 16], nbuf=1)
    ps2 = sbt('psl2', [NQ, 64], nbuf=1)
    P.add('dve', lambda e: e.max(out=m16[:, 0:8], in_=psl[:, :]), R=[psl], W=[m16])
    P.add('dve', lambda e: e.match_replace(out=ps2[:, :], in_to_replace=m16[:, 0:8], in_values=psl[:, :], imm_value=-5.0),
          R=[psl, m16], W=[ps2])
    P.add('dve', lambda e: e.max(out=m16[:, 8:16], in_=ps2[:, :]), R=[ps2], W=[m16])
    yield 0
    nsel = sbt('nsel', [NQ, 64], nbuf=1)
    self.ts('dve', nsel[:, :], psl[:, :], m16[:, 15:16], None, ALU.is_ge, R=[psl, m16], W=[nsel])
    self.ts('dve', nsel[:, :], nsel[:, :], -NEG, NEG, ALU.mult, ALU.add, R=[nsel], W=[nsel])
    self.tr(pb[0][0:64, 0:NQ], nsel[:, :], c['identf'][0:NQ, 0:NQ], R=[nsel, c['identf']], W=[pb[0]])
    nsT = sbt('nsT', [64, 4, NQ], BF16)
    self.cp('act', nsT[:, :, :], pb[0][0:64, 0:NQ].unsqueeze(1).to_broadcast([64, 4, NQ]), R=[pb[0]], W=[nsT])
    yield 0
    if A.get('stop', 99) == 2:
        raise _Stop()
    pcb = sbt('pcb', [NQ, 4, 256], BF16, 1)
    self.cp('pool', pcb[:, :, :], pc[:, :, :], R=[pc], W=[pcb])
    pt = self.pt[0]
    NH = A['NH']
    for g in range(4):
        for hf in range(NH):
            self.tr(pt[:, (g * NH + hf) * NQ:(g * NH + hf + 1) * NQ], pcb[:, g, hf * 128:(hf + 1) * 128],
                    c['identb'][0:NQ, 0:NQ], R=[pcb, c['identb']], W=[pt])
    yield 0
    pcT = sbt('pcT', [128, 4 * NH, NQ], BF16, 1)
    self.cp('act', pcT[:, :, :].rearrange("p a q -> p (a q)"), pt[:, 0:4 * NH * NQ], R=[pt], W=[pcT])
    vc_ap, vc_t = A['vcmp']
    for g in range(4):
        for hf in range(NH):
            self.mm(pb[1][0:NQ, g * 64:(g + 1) * 64], pcT[:, g * NH + hf, :], vc_ap(hf), start=(hf == 0), stop=(hf == NH - 1),
                    R=[pcT, vc_t], W=[pb[1]])
    yield 0
    oacc = sbt('oacc', [NQ, 4, 64])
    g_ap, g_t = gate
    self.tt('dve', oacc[:, :, :], pb[1][0:NQ, 0:256].rearrange("p (g d) -> p g d", g=4),
            g_ap(0).unsqueeze(2).to_broadcast([NQ, 4, 64]), ALU.mult, R=[pb[1], g_t], W=[oacc])

    yield 'CMP_DONE'
    if A.get('stop', 99) == 3:
        raise _Stop()
    for br, (tiles, ksrc, vsrc, po) in enumerate((A['sel'], A['win'])):
        if A.get('stop', 99) == 4 and br == 1:
            raise _Stop()
        nt = len(tiles)
        pend = None
        for c0 in range(0, nt, 4):
            n = min(4, nt - c0)
            kt0 = tiles[c0][0]
            kc = sbt('kc', [128, 512], BF16, 3)
            kap, ktile = ksrc(kt0, n)
            P.dma(kc[:, 0:n * 128], kap, R=[ktile], W=[kc])
            vcx = sbt('vcx', [128, 4, 66], BF16, 3)
            vap, vtile = vsrc(kt0, n)
            P.dma(vcx[:, 0:n, :], vap, R=[vtile], W=[vcx])
            for t in range(n):
                kt, bias = tiles[c0 + t]
                ps = self.next_pb('sc', [2, 3])
                for pr in range(2):
                    self.mm(ps[:, pr * 2 * NQ:(pr + 1) * 2 * NQ], kc[:, t * 128:(t + 1) * 128], qbd[:, pr, :, qcols],
                            start=(pr == 0), stop=(br == 1 and pr == 1), R=[kc, A['qbd_t']], W=[ps])
                if br == 0:
                    self.mm(ps[:, 0:N4], c['expE'][:, kt * 128:(kt + 1) * 128], nsT[:, :, :].rearrange("p g q -> p (g q)"),
                            start=False, stop=True, R=[c['expE'], nsT], W=[ps])
                PT = sbt('PT', [128, N4], BF16, 3)
                if bias is not None:
                    b_ap, b_t = bias
                    sb_ = sbt('sbias', [128, N4], F32, 2)
                    self.tt('dve', sb_[:, :], ps[:, 0:N4], b_ap, ALU.add, R=[ps, b_t], W=[sb_])
                    self.actf(PT[:, :], sb_[:, :], AF.Exp, R=[sb_], W=[PT])
                else:
                    self.actf(PT[:, :], ps[:, 0:N4], AF.Exp, R=[ps], W=[PT])
                if pend is not None:
                    pend()
                def pv(PT=PT, vcx=vcx, t=t, first=(c0 + t == 0), last=(c0 + t == nt - 1), po=po):
                    for g in range(4):
                        self.mm(po[0:NQ, g * 66:(g + 1) * 66], PT[:, g * NQ:(g + 1) * NQ], vcx[:, t, :],
                                start=(first and g == 0), stop=(last and g == 3), R=[PT, vcx], W=[po])
                pend = pv
                yield 1
        if pend is not None:
            pend()
            pend = None
        pov = po[0:NQ, 0:264].rearrange("p (g e) -> p g e", g=4)
        rs = sbt('rs', [NQ, 4])
        self.ts('dve', rs[:, :], pov[:, :, 64], 1e-30, None, ALU.max, R=[po], W=[rs])
        P.add('dve', lambda e, rs=rs: e.reciprocal(out=rs[:, :], in_=rs[:, :]), R=[rs], W=[rs])
        self.tt('dve', rs[:, :], rs[:, :], g_ap(br + 1), ALU.mult, R=[rs, g_t], W=[rs])
        tmpo = sbt('tmpo', [NQ, 4, 64])
        self.tt('dve', tmpo[:, :, :], pov[:, :, 0:64], rs[:, :].unsqueeze(2).to_broadcast([NQ, 4, 64]), ALU.mult,
                R=[po, rs], W=[tmpo])
        self.tt('pool', oacc[:, :, :], oacc[:, :, :], tmpo[:, :, :], ALU.add, R=[oacc, tmpo], W=[oacc])
    o_ap, o_t = A['out']
    self.cp('act', o_ap, oacc[:, :, :], R=[oacc], W=[o_t])


def _attend(self, A):
    for _ in self.attend_gen(A):
        pass


def _attend_pipe(self, thunks):
    prev = None
    for th in thunks:
        g = self.attend_gen(th())
        cmp_done = False
        while not cmp_done or prev is not None:
            if not cmp_done:
                if next(g) == 'CMP_DONE':
                    cmp_done = True
            if prev is not None:
                try:
                    next(prev)
                except StopIteration:
                    prev = None
        prev = g
    if prev is not None:
        for _ in prev:
            pass


Builder.attend_gen = _attend_gen
Builder.attend_pipe = _attend_pipe
Builder.attend = _attend


def _nsa_qproj(self, G, j, qT_all, NT_):
    P, d, T = self.P, self.d, G.T
    GT = G.GT
    for ci in range(8):
        wt = self.load_wchunk(d['wc_nsa_in'][j].ap()[ci])
        ps = self.next_pb('lin', [1, 2, 3])
        for kc in range(8):
            self.mm(ps[:, 0:GT], wt[:, kc, :], T['xnT'][:, kc, :], start=(kc == 0), stop=(kc == 7), R=[wt, T['xnT']], W=[ps])
        self.actf(qT_all[:, ci, :], ps[:, 0:GT], AF.Copy, R=[ps], W=[qT_all], scale=0.125)


def _final_norm(self, G, dst, row0, ytile=None):
    P, d, T = self.P, self.d, G.T
    TP = G.TP
    wrow = self.ring('wrow', 2, lambda j: P.sb([128, D], F32, f"wrow{j}"))
    P.dma(wrow[:, :], d['norm_final'].ap().partition_broadcast(128), W=[wrow])
    x = T['x']
    for t in range(G.NT):
        junk = self.ring('junk', 1, lambda j: P.sb([128, D], BF16, f"junk{j}"))
        st = self.ring('nst', 4, lambda j: P.sb([128, 2], F32, f"nst{j}"))
        self.memset('pool', st[:, :], 0.0, W=[st])
        self.actf(junk[0:TP, :], x[:, t, :], AF.Square, R=[x], W=[junk, st], accum_out=st[0:TP, 0:1])
        self.ts('dve', st[0:TP, 1:2], st[0:TP, 0:1], 1.0 / D, 1e-6, ALU.mult, ALU.add, R=[st], W=[st])
        self.actf(st[0:TP, 1:2], st[0:TP, 1:2], AF.Sqrt, R=[st], W=[st])
        P.add('dve', lambda e, st=st: e.reciprocal(out=st[0:TP, 1:2], in_=st[0:TP, 1:2]), R=[st], W=[st])
        if ytile is None:
            yt = self.ring(G.name + 'yt', 2, lambda j: P.sb([TP, D], F32, G.name + f"yt{j}"))
            ya = yt[:, :]
        else:
            yt = ytile
            ya = ytile[:, t % 2, :]
        self.stt('dve', ya, x[:, t, :], st[0:TP, 1:2], wrow[0:TP, :], ALU.mult, ALU.mult, R=[x, st, wrow], W=[yt])
        P.dma(dst.ap()[row0 + t * TP:row0 + (t + 1) * TP, :], ya, R=[yt], W=[dst], q='pool')


def _nsa_prompt(self):
    P, d, c, cfg = self.P, self.d, self.c, self.cfg
    SEQ = cfg['SEQ']
    NTT = SEQ // 128
    NQT = NTT // 2
    NCB = NTT * 8 - 1
    self.nsa_late_consts()
    kcd = P.sb([128, 4, 256], BF16, "kcd")
    vcm = P.sb([128, 2, 4, 64], BF16, "vcm")
    self.memset('pool', vcm[:, :, :, :], 0.0, W=[vcm])
    with self.phase():
        lohi = P.sb([128, 2, 4, 264], F32, "lohi")
        self.memset('pool', lohi[:, :, :, :], 0.0, W=[lohi])
        for Pi in range(NTT):
            rows = self.ring('crow', 2, lambda j: P.sb([128, 1536], F32, f"crow{j}"))
            P.dma(rows[:, 0:1024], d['p_kv_rows'].ap()[Pi * 128:(Pi + 1) * 128, :], R=[d['p_kv_rows']], W=[rows])
            P.dma(rows[:, 1024:1536], d['pwin_d'].ap()[Pi * 128:(Pi + 1) * 128, :], R=[d['pwin_d']], W=[rows])
            cs = slice(Pi * 128, (Pi + 1) * 128)
            self.ctx_rows(rows, Pi, lohi,
                          (d['kselT_p'].ap()[:, :, cs].rearrange("k p n -> p k n"), d['kselT_p']),
                          (d['vsel_p'].ap()[Pi], d['vsel_p']),
                          (d['kwinT_p'].ap()[:, :, cs].rearrange("k p n -> p k n"), d['kwinT_p']),
                          (d['vwin_p'].ap()[Pi], d['vwin_p']), win_rows=(rows, 1024))
        self.cmp_finish(lohi, NCB, lambda kvh: kcd[:, kvh, :], lambda jh, kvh: vcm[:, jh, kvh, :], kcd, vcm)
    if cfg.get('nsa_stop', 99) == 2:
        raise _Stop()
    with self.phase():
        G = Ctx('n', 128, 4, 1, 1, 5)
        T = {}
        T['x'] = P.sb([128, 4, D], F32, "nx")
        T['xnT'] = P.sb([128, 8, 512], BF16, "nxnT")
        big = P.sb([128, 24, 512], BF16, "nbig")
        T['actT'] = View(big, 0, 22)
        qT_all = View(big, 0, 8)
        T['oT'] = P.sb([128, 8, 512], BF16, "noT")
        G.T = T
        o_tok = P.sb([128, 4, D], BF16, "o_tok")
        gates = P.sb([128, 4, 48], F32, "gates")
        qbd = P.sb([128, 2, 2, 512], BF16, "qbd")
        self.memset('pool', qbd[:, :, :, :], 0.0, W=[qbd])
        BT = P.sb([128, 15, 512], F32, "BT")
        for grp in range(NQT // 4):
            for tl in range(4):
                i = grp * 4 + tl
                xe = self.ring('xeo', 1, lambda j: P.sb([128, 2, D], F32, f"xeo{j}"))
                P.dma(xe[:, :, :], d['x2_p'].ap()[2 * i * 128:(2 * i + 2) * 128, :].rearrange("(e p) n -> p e n", p=128),
                      R=[d['x2_p']], W=[xe])
                self.ts('dve', T['x'][:, tl, :], xe[:, 0, :], c['parf'][:, 1:2], None, ALU.mult, R=[xe, c['parf']], W=[T['x']])
                self.stt('dve', T['x'][:, tl, :], xe[:, 1, :], c['parf'][:, 0:1], T['x'][:, tl, :], ALU.mult, ALU.add,
                         R=[xe, c['parf'], T['x']], W=[T['x']])
            for j in range(2):
                self.norm_T(G, d['norm_mix'].ap()[2 + j])
                _nsa_qproj(self, G, j, qT_all, 4)
                pg = self.pb[0]
                for tl in range(4):
                    for kc in range(8):
                        self.mm(pg[:, tl * 48:(tl + 1) * 48], T['xnT'][:, kc, tl * 128:(tl + 1) * 128], c['wg'][:, j, kc, :],
                                start=(kc == 0), stop=(kc == 7), R=[T['xnT'], c['wg']], W=[pg])
                self.actf(gates[:, :, :].rearrange("p t n -> p (t n)"), pg[:, 0:192], AF.Sigmoid, R=[pg], W=[gates])
                if cfg.get('nsa_stop', 99) == 3:
                    raise _Stop()
                for kvh in range(4):
                    for b3 in range(5):
                        P.dma(BT[:, 3 * b3:3 * b3 + 3, :], d['BTd'].ap()[kvh, 3 * b3:3 * b3 + 3].rearrange("i p n -> p i n"),
                              R=[d['BTd']], W=[BT])
                    for pr in range(2):
                        self.cp('pool', qbd[0:64, pr, 0, :], qT_all[0:64, 2 * kvh + pr, :], R=[qT_all], W=[qbd])
                        self.cp('pool', qbd[64:128, pr, 1, :], qT_all[64:128, 2 * kvh + pr, :], R=[qT_all], W=[qbd])
                    thunks = []
                    for tl in range(4):
                        def mk(tl=tl, kvh=kvh):
                            i = grp * 4 + tl
                            qc = slice(tl * 128, (tl + 1) * 128)
                            bc = self.ring('bcp', 2, lambda jj: P.sb([128, 4, 256], F32, f"bcp{jj}"))
                            smc = self.ring('smc', 2, lambda jj: P.sb([128, 2, 64], F32, f"smc{jj}"))
                            P.dma(smc[:, 0, :], d['selmul_p'].ap()[:, i, :], W=[smc])
                            P.dma(smc[:, 1, :], d['seladd_p'].ap()[:, i, :], W=[smc])
                            for a in range(8):
                                src = bass.AP(d['t_cmp'].h, 4 * kvh * 8192 + 247 - 16 * i - a, [[512, 16], [8192, 4], [1, 256]])
                                P.dma(bc[16 * a:16 * a + 16, :, :], src, R=[d['t_cmp']], W=[bc])
                            nkt = 2 * i + 2
                            sel_tiles = []
                            for kt in range(nkt):
                                e = 2 * i - kt
                                sel_tiles.append((kt, (BT[:, e + 1, :], BT) if e <= 7 else None))
                            win_tiles = []
                            for e in range(4, -2, -1):
                                kt = 2 * i - e
                                if kt >= 0:
                                    win_tiles.append((kt, (BT[:, 10 + e, :], BT)))
                            A = dict(
                                NQ=128, NC=NCB + 1, NH=2, kvh=kvh,
                                qT=(lambda g, qc=qc, kvh=kvh: qT_all[(g % 2) * 64:(g % 2) * 64 + 64, 2 * kvh + g // 2, qc], qT_all),
                                qbd=qbd, qbd_t=qbd, qcols=qc,
                                gate=(lambda br, tl=tl, kvh=kvh: gates[:, tl, :].rearrange("p (h r) -> p h r", r=3)[:, 4 * kvh:4 * kvh + 4, br], gates),
                                kcmp=(lambda g, kvh=kvh: kcd[(g % 2) * 64:(g % 2) * 64 + 64, kvh, 0:NCB + 1], kcd),
                                vcmp=(lambda hf, kvh=kvh: vcm[:, hf, kvh, :], vcm),
                                bias_c=(bc[:, :, :], bc),
                                selc=(smc[:, 0, :], smc[:, 1, :], smc),
                                sel=(sel_tiles,
                                     lambda kt0, n, kvh=kvh: (d['kselT_p'].ap()[kvh, :, kt0 * 128:(kt0 + n) * 128], d['kselT_p']),
                                     lambda kt0, n, kvh=kvh: (d['vsel_p'].ap()[kt0:kt0 + n, :, kvh, :].rearrange("t p e -> p t e"), d['vsel_p']),
                                     self.pb[4]),
                                win=(win_tiles,
                                     lambda kt0, n, kvh=kvh: (d['kwinT_p'].ap()[kvh, :, kt0 * 128:(kt0 + n) * 128], d['kwinT_p']),
                                     lambda kt0, n, kvh=kvh: (d['vwin_p'].ap()[kt0:kt0 + n, :, kvh, :].rearrange("t p e -> p t e"), d['vwin_p']),
                                     self.pb[5]),
                                out=(o_tok[:, tl, kvh * 256:(kvh + 1) * 256].rearrange("p (g e) -> p g e", g=4), o_tok),
                            )

                            return A
                        thunks.append(mk)
                    self.attend_pipe(thunks)
                for tl in range(4):
                    pt = self.pt[0]
                    for cc in range(8):
                        self.tr(pt[:, cc * 128:(cc + 1) * 128], o_tok[:, tl, cc * 128:(cc + 1) * 128], c['identb'][:, :],
                                R=[o_tok, c['identb']], W=[pt])
                    self.cp('act', T['oT'][:, :, tl * 128:(tl + 1) * 128], pt[:, :].rearrange("p (c n) -> p c n", c=8),
                            R=[pt], W=[T['oT']])
                self.tok_linear_add(G, T['oT'], 8, d['wb_nsa_out'][j])
                self.ffn(G, 2 + j)
            _final_norm(self, G, d['y_p'], grp * 512, ytile=self.rr['xeo'][0][0])


Builder.nsa_prompt = _nsa_prompt


def _nsa_sample(self):
    P, d, c, cfg = self.P, self.d, self.c, self.cfg
    NSQ = 16
    self.nsa_late_consts()
    kcd = P.sb([128, NSQ, 4, 128], BF16, "kcds")
    vcm = P.sb([128, NSQ, 4, 64], BF16, "vcms")
    self.memset('pool', vcm[:, :, :, :], 0.0, W=[vcm])
    with self.phase():
        ptb = P.sb([128, 256], I32, "ptb")
        P.dma(ptb[:, :], d['page_table'].ap()[0].partition_broadcast(128), W=[ptb])
        ptf = P.sb([128, 256], F32, "ptf")
        self.cp('dve', ptf[:, :], ptb[:, :], R=[ptb], W=[ptf])
        iot = P.sb([128, 1], F32, "iot")
        P.dma(iot[:, :], d['iota'].ap(), W=[iot])
        self.stt('dve', ptf[:, :], ptf[:, :], 128.0, iot[:, 0:1].to_broadcast([128, 256]), ALU.mult, ALU.add,
                 R=[ptf, iot], W=[ptf])
        idx = P.sb([128, 256], I32, "idxall")
        self.cp('dve', idx[:, :], ptf[:, :], R=[ptf], W=[idx])
        for s in range(NSQ):
            lohi = self.ring('lohis', 2, lambda j: P.sb([128, 2, 4, 136], F32, f"lohis{j}"))
            self.memset('pool', lohi[:, :, :, :], 0.0, W=[lohi])
            for Pi in range(17):
                rows = self.ring('crow', 4, lambda j: P.sb([128, 1024], F32, f"crows{j}"))
                if Pi < 16:
                    k = s * 16 + Pi
                    P.add('pool', lambda e, rows=rows, k=k: e.indirect_dma_start(
                        out=rows[:, :], out_offset=None, in_=d['cache_kv'].ap(),
                        in_offset=bass.IndirectOffsetOnAxis(ap=idx[:, k:k + 1], axis=0)),
                        R=[idx, d['cache_kv']], W=[rows], dma=True)
                else:
                    self.memset('pool', rows[:, :], 0.0, W=[rows])
                    P.dma(rows[0:4, :], d['s_kv_rows'].ap()[4 * s:4 * s + 4, :], R=[d['s_kv_rows']], W=[rows])
                cs = slice(Pi * 128, (Pi + 1) * 128)
                self.ctx_rows(rows, Pi, lohi if Pi < 16 else None,
                              (d['kselT_s'].ap()[s][:, :, cs].rearrange("k p n -> p k n"), d['kselT_s']),
                              (d['vsel_s'].ap()[s, Pi], d['vsel_s']), None, None, has_win=False)
            for W_ in range(5):
                wr = self.ring('wrow_s', 2, lambda j: P.sb([128, 512], F32, f"wrows{j}"))
                if W_ < 4:
                    P.dma(wr[:, :], d['state_win_kv'].ap()[s, W_ * 128:(W_ + 1) * 128, :], W=[wr])
                else:
                    self.memset('pool', wr[:, :], 0.0, W=[wr])
                    P.dma(wr[0:4, :], d['s_win_kv'].ap()[s, 508:512, :], R=[d['s_win_kv']], W=[wr])
                cs = slice(W_ * 128, (W_ + 1) * 128)
                self.ctx_rows(None, 0, None, None, None,
                              (d['kwinT_s'].ap()[s][:, :, cs].rearrange("k p n -> p k n"), d['kwinT_s']),
                              (d['vwin_s'].ap()[s, W_], d['vwin_s']), has_cmpsel=False, win_rows=(wr, 0))
            self.cmp_finish(lohi, 127, lambda kvh, s=s: kcd[:, s, kvh, :], lambda jh, kvh, s=s: vcm[:, s, kvh, :], kcd, vcm,
                            ncw=128, njh=1)
    with self.phase():
        G = Ctx('m', 64, 1, 16, 16, 1)
        T = {}
        T['x'] = P.sb([64, 1, D], F32, "mx")
        T['xnT'] = P.sb([128, 8, 64], BF16, "mxnT")
        big = P.sb([128, 24, 64], BF16, "mbig")
        T['actT'] = View(big, 0, 22)
        qT_all = View(big, 0, 8)
        T['oT'] = P.sb([128, 8, 64], BF16, "moT")
        G.T = T
        P.dma(T['x'][:, 0, :], d['x2_s'].ap(), R=[d['x2_s']], W=[T['x']])
        qbd = P.sb([128, 4, 2, 2, 64], BF16, "qbds")
        self.memset('pool', qbd[:, :, :, :, :], 0.0, W=[qbd])
        smul = P.sb([4, 64], F32, "smuls")
        sadd = P.sb([4, 64], F32, "sadds")
        P.dma(smul[:, :], d['selmul_s'].ap(), W=[smul])
        P.dma(sadd[:, :], d['seladd_s'].ap(), W=[sadd])
        bcs = P.sb([4, 16, 128], F32, "bcs")
        P.dma(bcs[:, :, :], bass.AP(d['t_cs'].h, 0, [[128, 4], [512, 16], [1, 128]]), R=[d['t_cs']], W=[bcs])
        trv = P.sb([128, 22, 16, 4], F32, "trvs")
        for Pi in range(17):
            P.dma(trv[:, Pi, :, :], bass.AP(d['t_ss'].h, 2048 - 128 * Pi, [[1, 128], [2304, 16], [1, 4]]), R=[d['t_ss']], W=[trv])
        for W_ in range(5):
            P.dma(trv[:, 17 + W_, :, :], bass.AP(d['t_ws'].h, 512 - 128 * W_, [[1, 128], [768, 16], [1, 4]]), R=[d['t_ws']], W=[trv])
        BTs = P.sb([128, 22, 16, 4], F32, "BTs")
        tf = trv[:, :, :, :].rearrange("p a h t -> p (a h t)")
        bf_ = BTs[:, :, :, :].rearrange("p a h t -> p (a h t)")
        for off in range(0, 22 * 64, 512):
            n = min(512, 22 * 64 - off)
            ps = self.next_pb('lin', [1, 2, 3])
            self.mm(ps[:, 0:n], c['antiI'][:, :], tf[:, off:off + n], R=[c['antiI'], trv], W=[ps])
            self.cp('act', bf_[:, off:off + n], ps[:, 0:n], R=[ps], W=[BTs])
        for j in range(2):
            self.norm_T(G, d['norm_mix'].ap()[2 + j])
            _nsa_qproj(self, G, j, qT_all, 1)
            for kvh in range(4):
                for pr in range(2):
                    self.cp('pool', qbd[0:64, kvh, pr, 0, :], qT_all[0:64, 2 * kvh + pr, :], R=[qT_all], W=[qbd])
                    self.cp('pool', qbd[64:128, kvh, pr, 1, :], qT_all[64:128, 2 * kvh + pr, :], R=[qT_all], W=[qbd])
            gs_all = self.ring('gs_all', 1, lambda jj: P.sb([4, NSQ, 48], F32, "gs_all"))
            for half in range(2):
                pg = self.pb[0]
                for s8 in range(8):
                    s = half * 8 + s8
                    for kc in range(8):
                        self.mm(pg[0:4, s8 * 48:(s8 + 1) * 48], T['xnT'][:, kc, 4 * s:4 * s + 4], c['wg'][:, j, kc, :],
                                start=(kc == 0), stop=(kc == 7), R=[T['xnT'], c['wg']], W=[pg])
                self.actf(gs_all[:, half * 8:(half + 1) * 8, :].rearrange("p s n -> p (s n)"), pg[0:4, 0:384], AF.Sigmoid,
                          R=[pg], W=[gs_all])
            o_all = self.ring('o_all', 1, lambda jj: P.sb([4, NSQ, D], BF16, "o_all"))
            thunks = []
            for s in range(NSQ):
                for kvh in range(4):
                    def mk(s=s, kvh=kvh):
                        qc = slice(4 * s, 4 * s + 4)
                        sel_tiles = [(Pi, (BTs[:, Pi, 4 * kvh:4 * kvh + 4, :].rearrange("p h t -> p (h t)"), BTs)) for Pi in range(17)]
                        win_tiles = [(W_, (BTs[:, 17 + W_, 4 * kvh:4 * kvh + 4, :].rearrange("p h t -> p (h t)"), BTs)) for W_ in range(5)]
                        return dict(
                            NQ=4, NC=128, NH=1, kvh=kvh,
                            qT=(lambda g: qT_all[(g % 2) * 64:(g % 2) * 64 + 64, 2 * kvh + g // 2, qc], qT_all),
                            qbd=qbd.h[:, kvh], qbd_t=qbd, qcols=qc,
                            gate=(lambda br: gs_all[:, s, :].rearrange("p (h r) -> p h r", r=3)[:, 4 * kvh:4 * kvh + 4, br], gs_all),
                            kcmp=(lambda g: kcd[(g % 2) * 64:(g % 2) * 64 + 64, s, kvh, :], kcd),
                            vcmp=(lambda hf: vcm[:, s, kvh, :], vcm),
                            bias_c=(bcs[:, 4 * kvh:4 * kvh + 4, :], bcs),
                            selc=(smul[:, :], sadd[:, :], smul),
                            sel=(sel_tiles,
                                 lambda kt0, n: (d['kselT_s'].ap()[s, kvh, :, kt0 * 128:(kt0 + n) * 128], d['kselT_s']),
                                 lambda kt0, n: (d['vsel_s'].ap()[s, kt0:kt0 + n, :, kvh, :].rearrange("t p e -> p t e"), d['vsel_s']),
                                 self.pb[4]),
                            win=(win_tiles,
                                 lambda kt0, n: (d['kwinT_s'].ap()[s, kvh, :, kt0 * 128:(kt0 + n) * 128], d['kwinT_s']),
                                 lambda kt0, n: (d['vwin_s'].ap()[s, kt0:kt0 + n, :, kvh, :].rearrange("t p e -> p t e"), d['vwin_s']),
                                 self.pb[5]),
                            out=(o_all[:, s, kvh * 256:(kvh + 1) * 256].rearrange("p (g e) -> p g e", g=4), o_all),
                        )
                    thunks.append(mk)
            self.attend_pipe(thunks)
            for s in range(NSQ):
                qc = slice(4 * s, 4 * s + 4)
                pt = self.pt[0]
                for cc in range(8):
                    self.tr(pt[:, cc * 4:(cc + 1) * 4], o_all[:, s, cc * 128:(cc + 1) * 128], c['identb'][0:4, 0:4],
                            R=[o_all, c['identb']], W=[pt])
                self.cp('act', T['oT'][:, :, qc], pt[:, 0:32].rearrange("p (c n) -> p c n", c=8), R=[pt], W=[T['oT']])
            self.tok_linear_add(G, T['oT'], 8, d['wb_nsa_out'][j])
            self.ffn(G, 2 + j)
        _final_norm(self, G, d['y_s'], 0)


Builder.nsa_sample = _nsa_sample
```

```python
import math
import numpy as np
import concourse.bass as bass
import concourse.mybir as mybir
from concourse.bass_utils import run_bass_kernel_spmd

F32 = mybir.dt.float32
BF16 = mybir.dt.bfloat16
I32 = mybir.dt.int32
AF = mybir.ActivationFunctionType
ALU = mybir.AluOpType

EPOCH = 16000
EMBED_WAIT = True
NSLOT = 8
ENGS = ('pe', 'act', 'dve', 'pool', 'sp')
SAME_ENGINE_SYNC = {'pe': False, 'act': True, 'dve': True, 'pool': True, 'sp': True}

D = 1024
H = 8
DFF = 2816
NEG = -30000.0


class Buf:
    __slots__ = ('name', 'lw', 'rd')

    def __init__(self, name):
        self.name = name
        self.lw = None
        self.rd = []


class Tile:
    def __init__(self, h, name):
        self.h = h
        self.buf = Buf(name)

    def __getitem__(self, idx):
        return self.h[idx]

    def ap(self):
        return self.h.ap()


class View:
    def __init__(self, base, off, n):
        self.base, self.buf, self.off, self.n = base, base.buf, off, n

    def __getitem__(self, idx):
        idx = list(idx)
        a = idx[1]
        if isinstance(a, slice):
            st = (a.start or 0) + self.off
            en = (a.stop if a.stop is not None else self.n) + self.off
            idx[1] = slice(st, en)
        else:
            idx[1] = a + self.off
        return self.base.h[tuple(idx)]


class Prog:
    def __init__(self, nc):
        self.nc = nc
        self.ops = {e: [] for e in ENGS}
        self.cnt = {e: 0 for e in ENGS}
        self.dcnt = {e: 0 for e in ENGS}
        self.known = {e: {} for e in ENGS}
        self.kev = {e: [] for e in ENGS}
        self.kptr = {}
        self.nt = 0
        self.stack = None

    def sb(self, shape, dtype=F32, name=None):
        self.nt += 1
        name = name or "t"
        if self.stack is not None:
            h = self.stack.enter_context(self.nc.sbuf_tensor(f"{name}_{self.nt}", list(shape), dtype))
        else:
            h = self.nc.alloc_sbuf_tensor(f"{name}_{self.nt}", list(shape), dtype)
        return Tile(h, name)

    def barrier(self):
        evs = []
        for f in ENGS:
            if self.cnt[f] > 0:
                evs.append(('c', f, self.cnt[f]))
            n = self.dcnt[f]
            for slot in range(min(n, NSLOT)):
                evs.append(('d', f, slot, (n - 1 - slot) // NSLOT + 1))
        for e in ENGS:
            waits = []
            for ev in evs:
                if ev[0] == 'c' and ev[1] == e:
                    continue
                self._need(e, ev, waits)
            self.cnt[e] += 1
            self.ops[e].append((waits, (lambda en: en.nop()), ('c', e, self.cnt[e])))

    def ps(self, shape, dtype=F32, name=None):
        self.nt += 1
        name = name or "p"
        h = self.nc.alloc_psum_tensor(f"{name}_{self.nt}", list(shape), dtype)
        return Tile(h, name)

    def dram(self, name, shape, dtype=F32, kind="Internal"):
        h = self.nc.dram_tensor(name, list(shape), dtype, kind=kind)
        return Tile(h, name)

    def _learn(self, eng, key, val):
        if self.known[eng].get(key, 0) >= val:
            return False
        self.known[eng][key] = val
        self.kev[eng].append((self.cnt[eng] + 1, key, val))
        return True

    def _absorb(self, eng, f, seq):
        evs = self.kev[f]
        i = self.kptr.get((eng, f), 0)
        n = len(evs)
        while i < n and evs[i][0] <= seq:
            _, key, val = evs[i]
            if not (key[0] == 'c' and key[1] == eng):
                self._learn(eng, key, val)
            i += 1
        self.kptr[(eng, f)] = i

    def _need(self, eng, ev, waits):
        if ev is None:
            return
        if ev[0] == 'c':
            _, f, seq = ev
            if f == eng and not SAME_ENGINE_SYNC[eng]:
                return
            if self._learn(eng, ('c', f), seq):
                waits.append(ev)
                if f != eng:
                    self._absorb(eng, f, seq)
        else:
            _, q, slot, k = ev
            if self._learn(eng, ('d', q, slot), k):
                waits.append(ev)

    def add(self, eng, emit, R=(), W=(), dma=False):
        waits = []
        for t in R:
            self._need(eng, t.buf.lw, waits)
        for t in W:
            b = t.buf
            self._need(eng, b.lw, waits)
            for ev in b.rd:
                self._need(eng, ev, waits)
        if dma:
            i = self.dcnt[eng]
            self.dcnt[eng] += 1
            slot, k = i % NSLOT, i // NSLOT + 1
            if k > 1:
                self._need(eng, ('d', eng, slot, k - 1), waits)
            ev = ('d', eng, slot, k)
        else:
            self.cnt[eng] += 1
            ev = ('c', eng, self.cnt[eng])
        for t in R:
            t.buf.rd.append(ev)
        for t in W:
            t.buf.lw = ev
            t.buf.rd = []
        self.ops[eng].append((waits, emit, ev))
        return ev

    def dma(self, out_ap, in_ap, R=(), W=(), q='sp', **kw):
        return self.add(q, lambda e: e.dma_start(out=out_ap, in_=in_ap, **kw), R, W, dma=True)

    def emit(self):
        nc = self.nc
        csem = {}
        for e in ENGS:
            n = (self.cnt[e] + EPOCH - 1) // EPOCH
            csem[e] = [nc.alloc_semaphore(f"c_{e}_{j}") for j in range(n)]
        dsem = {}
        for e in ENGS:
            n = min(self.dcnt[e], NSLOT)
            dsem[e] = [nc.alloc_semaphore(f"d_{e}_{j}") for j in range(n)]

        def semval(ev):
            if ev[0] == 'c':
                _, f, seq = ev
                return csem[f][(seq - 1) // EPOCH], (seq - 1) % EPOCH + 1
            _, q, slot, k = ev
            return dsem[q][slot], 16 * k

        def run(eng, e):
            for waits, emit, ev in self.ops[eng]:
                emb = waits[-1] if (waits and EMBED_WAIT) else None
                for w in (waits[:-1] if emb is not None else waits):
                    s, v = semval(w)
                    e.wait_ge(s, v)
                ins = emit(e)
                if emb is not None:
                    s, v = semval(emb)
                    ins._wait_ge(s, v)
                s, v = semval(ev)
                ins.then_inc(s, 16 if ev[0] == 'd' else 1)
            n = self.dcnt[eng]
            for slot in range(min(n, NSLOT)):
                k = (n - 1 - slot) // NSLOT + 1
                if self.known[eng].get(('d', eng, slot), 0) < k:
                    e.wait_ge(dsem[eng][slot], 16 * k)

        with nc.Block() as block:
            @block.tensor
            def _(e):
                run('pe', e)

            @block.scalar
            def _(e):
                run('act', e)

            @block.vector
            def _(e):
                run('dve', e)

            @block.gpsimd
            def _(e):
                run('pool', e)

            @block.sync
            def _(e):
                run('sp', e)
        return nc


class Ctx:
    def __init__(self, name, TP, NT, NSEQ, NS, nlev):
        self.name = name
        self.TP = TP
        self.NT = NT
        self.GT = TP * NT
        self.NSEQ = NSEQ
        self.TS = self.GT // NSEQ
        self.NCH = self.GT // 64
        self.NS = NS
        self.nlev = nlev


def host_consts(NS, TS):
    seg = np.arange(64) // TS if NS > 1 else np.zeros(64, np.int64)
    j = np.arange(64)[:, None]
    i = np.arange(64)[None, :]
    same = (seg[:, None] == seg[None, :])
    c = {}
    c['ucs'] = ((j <= i) & same).astype(np.float32)
    c['maskT'] = np.where((j <= i) & same, 0.0, NEG).astype(np.float32)
    c['noff'] = -np.where((j != i), 1.0, 0.0).astype(np.float32)
    c['same'] = same.astype(np.float32)
    si = np.zeros((64, NS), np.float32)
    si[np.arange(64), seg] = 1.0
    c['seqind'] = si
    cm = np.zeros((128, NS, 64), np.float32)
    cm[:, seg, np.arange(64)] = 1.0
    c['colmask'] = cm.reshape(128, NS * 64)
    return c


class _Stop(Exception):
    pass


class Builder:
    def __init__(self, cfg):
        self.cfg = cfg
        nc = bass.Bass("TRN2", target_bir_lowering=False)
        self.nc = nc
        self.P = Prog(nc)
        self.rr = {}

    def mm(self, out, lhsT, rhs, start=True, stop=True, R=(), W=()):
        self.P.add('pe', lambda e: e.matmul(out, lhsT=lhsT, rhs=rhs, start=start, stop=stop), R, W)

    def tr(self, out, in_, ident, R=(), W=()):
        self.P.add('pe', lambda e: e.transpose(out=out, in_=in_, identity=ident), R, W)

    def actf(self, out, in_, func, R=(), W=(), bias=None, scale=None, accum_out=None):
        kw = {}
        if bias is not None:
            kw['bias'] = bias
        if scale is not None:
            kw['scale'] = scale
        if accum_out is not None:
            kw['accum_out'] = accum_out
        self.P.add('act', lambda e: e.activation(out=out, in_=in_, func=func, **kw), R, W)

    def cp(self, eng, out, in_, R=(), W=()):
        if eng == 'act':
            self.P.add('act', lambda e: e.copy(out=out, in_=in_), R, W)
        else:
            self.P.add(eng, lambda e: e.tensor_copy(out=out, in_=in_), R, W)

    def tt(self, eng, out, in0, in1, op, R=(), W=()):
        self.P.add(eng, lambda e: e.tensor_tensor(out=out, in0=in0, in1=in1, op=op), R, W)

    def ts(self, eng, out, in0, s1, s2, op0, op1=None, R=(), W=()):
        if op1 is None:
            self.P.add(eng, lambda e: e.tensor_scalar(out=out, in0=in0, scalar1=s1, scalar2=None, op0=op0), R, W)
        else:
            self.P.add(eng, lambda e: e.tensor_scalar(out=out, in0=in0, scalar1=s1, scalar2=s2, op0=op0, op1=op1), R, W)

    def stt(self, eng, out, in0, scalar, in1, op0, op1, R=(), W=()):
        self.P.add(eng, lambda e: e.scalar_tensor_tensor(out=out, in0=in0, scalar=scalar, in1=in1, op0=op0, op1=op1), R, W)

    def memset(self, eng, ap, val, W=()):
        self.P.add(eng, lambda e: e.memset(ap, val), (), W)

    def ring(self, key, n, make):
        if key not in self.rr:
            self.rr[key] = [[make(i) for i in range(n)], 0]
        r = self.rr[key]
        t = r[0][r[1] % n]
        r[1] += 1
        return t

    def declare_io(self):
        P, cfg = self.P, self.cfg
        d = {}

        def inp(name, shape, dt=F32):
            d[name] = P.dram(name, shape, dt, kind="ExternalInput")

        def out(name, shape, dt=F32):
            d[name] = P.dram(name, shape, dt, kind="ExternalOutput")

        SEQ = cfg['SEQ']
        inp('x_prompt', [SEQ, D])
        inp('x_sample', [64, D])
        inp('state_dn_S', [2, 16, H, 128, 128])
        inp('state_dn_conv', [2, 16, 3, 3072])
        inp('state_win_kv', [16, 512, 512])
        inp('norm_mix', [4, D])
        inp('norm_ffn', [4, D])
        inp('norm_kv', [D])
        inp('norm_final', [D])
        inp('ffn_w_in', [4, D, 2 * DFF])
        inp('ffn_w_out', [4, DFF, D])
        inp('dn_w_in', [2, D, 4112])
        inp('dn_conv_w', [2, 4, 3072])
        inp('dn_A_log', [2, H])
        inp('dn_dt_bias', [2, H])
        inp('dn_out_norm', [2, 128])
        inp('dn_w_out', [2, D, D])
        inp('nsa_w_kv', [D, 1536])
        for pre, NS in (('cp_', 1), ('cs_', 16)):
            inp(pre + 'ucs', [64, 64])
            inp(pre + 'maskT', [64, 64])
            inp(pre + 'noff', [64, 64])
            inp(pre + 'same', [64, 64])
            inp(pre + 'seqind', [64, NS])
            inp(pre + 'colmask', [128, NS * 64])
        out('p_dn_S', [2, H, 128, 128])
        out('p_dn_conv', [2, 3, 3072])
        out('p_kv_rows', [SEQ, 1024])
        out('p_win_kv', [512, 512])
        out('s_dn_S', [2, 16, H, 128, 128])
        out('s_dn_conv', [2, 16, 3, 3072])
        out('s_kv_rows', [64, 1024])
        out('s_win_kv', [16, 512, 512])
        out('x2_p', [SEQ, D])
        out('x2_s', [64, D])
        d['wc_dn_in'] = [P.dram(f'wc_dn_in{l}', [32, 128, 8, 128], BF16) for l in range(2)]
        d['wc_ffn_in'] = [P.dram(f'wc_ffn_in{l}', [44, 128, 8, 128], BF16) for l in range(4)]
        d['wb_ffn_out'] = [P.dram(f'wb_ffn_out{l}', [DFF, D], BF16) for l in range(4)]
        d['wb_dn_out'] = [P.dram(f'wb_dn_out{l}', [D, D], BF16) for l in range(2)]
        d['wb_kv'] = P.dram('wb_kv', [D, 1536], BF16)
        self.d = d

    def consts(self):
        P, d = self.P, self.d
        c = {}
        identf = P.sb([128, 128], F32, "identf")
        self.memset('pool', identf[:, :], 1.0, W=[identf])
        P.add('pool', lambda e: e.affine_select(out=identf[:, :], in_=identf[:, :], pattern=[[-1, 128]],
                                                compare_op=ALU.is_equal, fill=0.0, base=0, channel_multiplier=1),
              R=[identf], W=[identf])
        identb = P.sb([128, 128], BF16, "identb")
        self.cp('dve', identb[:, :], identf[:, :], R=[identf], W=[identb])
        onesb = P.sb([128, 128], BF16, "onesb")
        self.memset('pool', onesb[:, :], 1.0, W=[onesb])
        onesf = P.sb([64, 128], F32, "onesf")
        self.memset('pool', onesf[:, :], 1.0, W=[onesf])
        c.update(identf=identf, identb=identb, onesb=onesb, onesf=onesf)
        for pre, NS in (('cp_', 1), ('cs_', 16)):
            for nm, shp in (('ucs', [64, 64]), ('maskT', [64, 64]), ('noff', [64, 64]), ('same', [64, 64]),
                            ('seqind', [64, NS])):
                t = P.sb(shp, F32, pre + nm)
                P.dma(t[:, :], d[pre + nm].ap(), W=[t])
                c[pre + nm] = t
            if NS > 1:
                t = P.sb([128, NS * 64], BF16, pre + 'colmask')
                P.dma(t[:, :], d[pre + 'colmask'].ap(), W=[t], q='pool')
                c[pre + 'colmask'] = t
        cw = P.sb([128, 2, 24, 4], F32, "cw")
        for l in range(2):
            for i in range(4):
                P.dma(cw[:, l, :, i], d['dn_conv_w'].ap()[l, i].rearrange("(c p) -> p c", p=128), W=[cw],
                      allow_slow_non_contiguous=True)
        c['cw'] = cw
        negA = P.sb([128, 2, H], F32, "negA")
        dtb = P.sb([128, 2, H], F32, "dtb")
        P.dma(negA[:, :, :], d['dn_A_log'].ap().rearrange("l h -> (l h)").partition_broadcast(128).rearrange("p (l h) -> p l h", l=2), W=[negA])
        P.dma(dtb[:, :, :], d['dn_dt_bias'].ap().rearrange("l h -> (l h)").partition_broadcast(128).rearrange("p (l h) -> p l h", l=2), W=[dtb])
        self.actf(negA[:, :, :], negA[:, :, :], AF.Exp, R=[negA], W=[negA])
        self.ts('dve', negA[:, :, :], negA[:, :, :], -1.0, None, ALU.mult, R=[negA], W=[negA])
        c.update(negA=negA, dtb=dtb)
        onw = P.sb([128, 2], F32, "onw")
        P.dma(onw[:, :], d['dn_out_norm'].ap().rearrange("l p -> p l"), W=[onw], allow_slow_non_contiguous=True)
        c['onw'] = onw
        wab = P.sb([128, 2, 8, 16], BF16, "wab")
        for l in range(2):
            P.dma(wab[:, l, :, :], d['dn_w_in'].ap()[l, :, 4096:4112].rearrange("(kc p) n -> p kc n", p=128), W=[wab], q='pool')
        c['wab'] = wab
        self.c = c
        self.pb = [P.ps([128, 512], F32, f"pb{i}") for i in range(6)]
        self.pt = [P.ps([128, 1024], BF16, f"pt{i}") for i in range(2)]

    def convert_weights(self):
        P, d = self.P, self.d
        k = [0]
        SW = 2048

        def stage():
            i = k[0]
            k[0] += 1
            f = self.ring('cvf', 4, lambda j: P.sb([128, SW], F32, f"cvf{j}"))
            b = self.ring('cvb', 4, lambda j: P.sb([128, SW], BF16, f"cvb{j}"))
            return f, b, ('act', 'dve', 'pool')[i % 3]

        def conv_chunked(W_ap, dst, nch):
            for g in range(nch // 2):
                f, b, e = stage()
                P.dma(f[:, :].rearrange("p (k n) -> p k n", k=8),
                      W_ap[:, g * 256:(g + 1) * 256].rearrange("(kc p) n -> p kc n", p=128), W=[f], q='sp')
                o = b[:, :].rearrange("p (c k n) -> p k c n", c=2, k=8)
                sv = f[:, :].rearrange("p (k c n) -> p k c n", k=8, c=2)
                self.cp(e, o, sv, R=[f], W=[b])
                P.dma(dst.ap()[g * 2:(g + 1) * 2].rearrange("c p k n -> p c (k n)"),
                      b[:, :].rearrange("p (c kn) -> p c kn", c=2), R=[b], W=[dst], q='act')

        def conv_natural(W_ap, dst, K, N):
            per = max(1, SW // N)
            nk = K // 128
            kc = 0
            while kc < nk:
                m = min(per, nk - kc)
                f, b, e = stage()
                P.dma(f[:, 0:m * N].rearrange("p (k n) -> p k n", k=m),
                      W_ap[kc * 128:(kc + m) * 128, :].rearrange("(k p) n -> p k n", p=128), W=[f], q='sp')
                self.cp(e, b[:, 0:m * N], f[:, 0:m * N], R=[f], W=[b])
                P.dma(dst.ap()[kc * 128:(kc + m) * 128, :].rearrange("(k p) n -> p k n", p=128),
                      b[:, 0:m * N].rearrange("p (k n) -> p k n", k=m), R=[b], W=[dst], q='act')
                kc += m

        for l in range(self.cfg['n_dn']):
            conv_chunked(d['dn_w_in'].ap()[l, :, 0:4096], d['wc_dn_in'][l], 32)
            conv_natural(d['dn_w_out'].ap()[l], d['wb_dn_out'][l], D, D)
            conv_chunked(d['ffn_w_in'].ap()[l], d['wc_ffn_in'][l], 44)
            conv_natural(d['ffn_w_out'].ap()[l], d['wb_ffn_out'][l], DFF, D)
        conv_natural(d['nsa_w_kv'].ap(), d['wb_kv'], D, 1536)
        if self.cfg.get('nsa', True):
            for j in range(2):
                conv_chunked(d['nsa_w_in'].ap()[j, :, 0:1024], d['wc_nsa_in'][j], 8)
                conv_natural(d['nsa_w_out'].ap()[j], d['wb_nsa_out'][j], D, D)
                conv_chunked(d['ffn_w_in'].ap()[2 + j], d['wc_ffn_in'][2 + j], 44)
                conv_natural(d['ffn_w_out'].ap()[2 + j], d['wb_ffn_out'][2 + j], DFF, D)

    def alloc_ctx(self, G):
        P = self.P
        n = G.name
        T = {}
        T['x'] = P.sb([G.TP, G.NT, D], F32, n + "x")
        T['xnT'] = P.sb([128, 8, G.GT], BF16, n + "xnT")
        big = P.sb([128, 24, G.GT], BF16, n + "big")
        T['qT'] = View(big, 0, 8)
        T['kT'] = View(big, 8, 8)
        T['vT'] = View(big, 16, 8)
        T['actT'] = View(big, 0, 22)
        T['zs'] = P.sb([128, H, G.GT], BF16, n + "zs")
        T['OT'] = P.sb([128, H, G.GT], F32, n + "OT")
        T['oT'] = P.sb([128, H, G.GT], BF16, n + "oT")
        T['carry'] = [P.sb([128, 24, G.NSEQ, 3], F32, n + f"carry{l}") for l in range(2)]
        T['ab'] = P.sb([64, G.NCH, 16], F32, n + "ab")
        T['g'] = P.sb([64, G.NCH, H], F32, n + "g")
        T['beta'] = P.sb([64, G.NCH, H], F32, n + "beta")
        G.T = T

    def norm_T(self, G, wrow_src):
        P, c, T = self.P, self.c, G.T
        TP = G.TP
        wrow = self.ring('wrow', 2, lambda j: P.sb([128, D], F32, f"wrow{j}"))
        P.dma(wrow[:, :], wrow_src.partition_broadcast(128), W=[wrow])
        x = T['x']
        for t in range(G.NT):
            junk = self.ring('junk', 1, lambda j: P.sb([128, D], BF16, f"junk{j}"))
            st = self.ring('nst', 4, lambda j: P.sb([128, 2], F32, f"nst{j}"))
            self.memset('pool', st[:, :], 0.0, W=[st])
            self.actf(junk[0:TP, :], x[:, t, :], AF.Square, R=[x], W=[junk, st], accum_out=st[0:TP, 0:1])
            self.ts('dve', st[0:TP, 1:2], st[0:TP, 0:1], 1.0 / D, 1e-6, ALU.mult, ALU.add, R=[st], W=[st])
            self.actf(st[0:TP, 1:2], st[0:TP, 1:2], AF.Sqrt, R=[st], W=[st])
            P.add('dve', lambda e, st=st: e.reciprocal(out=st[0:TP, 1:2], in_=st[0:TP, 1:2]), R=[st], W=[st])
            xn = self.ring('xn', 2, lambda j: P.sb([128, D], BF16, f"xn{j}"))
            self.stt('dve', xn[0:TP, :], x[:, t, :], st[0:TP, 1:2], wrow[0:TP, :], ALU.mult, ALU.mult,
                     R=[x, st, wrow], W=[xn])
            pt = self.pt[0]
            for cc in range(8):
                self.tr(pt[:, cc * 128:cc * 128 + TP], xn[0:TP, cc * 128:(cc + 1) * 128], c['identb'][0:TP, 0:TP],
                        R=[xn, c['identb']], W=[pt])
            self.cp('act', T['xnT'][:, :, t * TP:(t + 1) * TP],
                    pt[:, :].rearrange("p (c n) -> p c n", c=8)[:, :, 0:TP], R=[pt], W=[T['xnT']])

    def load_wchunk(self, src_ap):
        P = self.P
        wt = self.ring('wch', 4, lambda j: P.sb([128, 8, 128], BF16, f"wch{j}"))
        P.dma(wt[:, :, :], src_ap, W=[wt])
        return wt

    def next_pb(self, key, banks):
        r = self.rr.setdefault('pb_' + key, [0])
        b = banks[r[0] % len(banks)]
        r[0] += 1
        return self.pb[b]

    def dn_mixer(self, G, l, S_tiles, last_group):
        P, c, T, d = self.P, self.c, G.T, self.d
        GT, TP, NT, NSEQ, TS, NCH, NS = G.GT, G.TP, G.NT, G.NSEQ, G.TS, G.NCH, G.NS
        pre = 'cp_' if NS == 1 else 'cs_'
        self.norm_T(G, d['norm_mix'].ap()[l])
        xnT = T['xnT']
        pbs = self.pb[0]
        for n in range(NCH):
            for kc in range(8):
                self.mm(pbs[0:64, n * 16:(n + 1) * 16], xnT[:, kc, n * 64:(n + 1) * 64], c['wab'][:, l, kc, :],
                        start=(kc == 0), stop=(kc == 7), R=[xnT, c['wab']], W=[pbs])
        ab = T['ab']
        self.cp('act', ab[:, :, :], pbs[0:64, 0:NCH * 16].rearrange("p (n k) -> p n k", k=16), R=[pbs], W=[ab])
        gt = self.ring('gtmp', 2, lambda j: P.sb([64, 8, H], F32, f"gtmp{j}"))
        g2 = self.ring('gtmp', 2, lambda j: None)
        av = gt[:, 0:NCH, :]
        a2 = g2[:, 0:NCH, :]
        self.tt('dve', av, ab[:, :, 0:8], c['dtb'][0:64, l, :].unsqueeze(1).to_broadcast([64, NCH, H]), ALU.add,
                R=[ab, c['dtb']], W=[gt])
        self.actf(a2, av, AF.Abs, R=[gt], W=[g2])
        self.actf(a2, a2, AF.Exp, R=[g2], W=[g2], scale=-1.0)
        self.actf(a2, a2, AF.Ln, R=[g2], W=[g2], bias=1.0)
        self.stt('dve', a2, av, 0.0, a2, ALU.max, ALU.add, R=[gt, g2], W=[g2])
        self.tt('dve', T['g'][:, :, :], a2, c['negA'][0:64, l, :].unsqueeze(1).to_broadcast([64, NCH, H]), ALU.mult,
                R=[g2, c['negA']], W=[T['g']])
        self.actf(T['beta'][:, :, :], ab[:, :, 8:16], AF.Sigmoid, R=[ab], W=[T['beta']])

        carry = T['carry'][l]

        def chunk_gen(ci):
            wt = self.load_wchunk(d['wc_dn_in'][l].ap()[ci])
            ps = self.next_pb('lin', [1, 2, 3])
            for kc in range(8):
                self.mm(ps[:, 0:GT], wt[:, kc, :], xnT[:, kc, :], start=(kc == 0), stop=(kc == 7), R=[wt, xnT], W=[ps])
            hh = ci % 8
            if ci >= 24:
                self.actf(T['zs'][:, hh, :], ps[:, 0:GT], AF.Silu, R=[ps], W=[T['zs']])
                return
            xp = self.ring(G.name + 'xp', 2, lambda j: P.sb([128, NSEQ, TS + 3], F32, G.name + f"xp{j}"))
            self.cp('act', xp[:, :, 3:TS + 3], ps[:, 0:GT].rearrange("p (s t) -> p s t", s=NSEQ), R=[ps], W=[xp])
            self.cp('pool', xp[:, :, 0:3], carry[:, ci, :, :], R=[carry], W=[xp])
            self.cp('pool', carry[:, ci, :, :], xp[:, :, TS:TS + 3], R=[xp], W=[carry])
            acc = self.ring(G.name + 'acc', 2, lambda j: P.sb([128, NSEQ, TS], F32, G.name + f"acc{j}"))
            cwl = c['cw']
            self.ts('dve', acc[:, :, :], xp[:, :, 0:TS], cwl[:, l, ci, 0:1], None, ALU.mult, R=[xp, cwl], W=[acc])
            for i in range(1, 4):
                self.stt('dve', acc[:, :, :], xp[:, :, i:TS + i], cwl[:, l, ci, i:i + 1], acc[:, :, :], ALU.mult, ALU.add,
                         R=[xp, cwl, acc], W=[acc])
            accf = acc[:, :, :].rearrange("p s t -> p (s t)")
            if ci >= 16:
                self.actf(T['vT'][:, hh, :], accf, AF.Silu, R=[acc], W=[T['vT']])
                return
            sl = self.ring(G.name + 'sl', 2, lambda j: P.sb([128, GT], F32, G.name + f"sl{j}"))
            self.actf(sl[:, :], accf, AF.Silu, R=[acc], W=[sl])
            sq = self.ring(G.name + 'sq', 2, lambda j: P.sb([128, GT], BF16, G.name + f"sq{j}"))
            self.actf(sq[:, :], sl[:, :], AF.Square, R=[sl], W=[sq])
            ps2 = self.next_pb('nrm', [4, 5])
            self.mm(ps2[:, 0:GT], c['onesb'][:, :], sq[:, :], R=[c['onesb'], sq], W=[ps2])
            yield 0
            rn = self.ring(G.name + 'rn', 2, lambda j: P.sb([128, GT], F32, G.name + f"rn{j}"))
            self.ts('dve', rn[:, :], ps2[:, 0:GT], 1e-6, None, ALU.add, R=[ps2], W=[rn])
            self.actf(rn[:, :], rn[:, :], AF.Sqrt, R=[rn], W=[rn])
            P.add('dve', lambda e, rn=rn: e.reciprocal(out=rn[:, :], in_=rn[:, :]), R=[rn], W=[rn])
            dst = T['qT'] if ci < 8 else T['kT']
            scl = (128 ** -0.5) if ci < 8 else 1.0
            self.stt('dve', dst[:, hh, :], sl[:, :], scl, rn[:, :], ALU.mult, ALU.mult, R=[sl, rn], W=[dst])

        prevg = None
        for ci in range(32):
            gch = chunk_gen(ci)
            try:
                next(gch)
            except StopIteration:
                gch = None
            if prevg is not None:
                for _ in prevg:
                    pass
            prevg = gch
        if prevg is not None:
            for _ in prevg:
                pass

        for n in range(NCH):
            self.dn_chunk(G, l, n, S_tiles, pre)

        OT, oT, zs = T['OT'], T['oT'], T['zs']
        for hh in range(H):
            sq = self.ring(G.name + 'sq', 2, lambda j: None)
            self.actf(sq[:, :], OT[:, hh, :], AF.Square, R=[OT], W=[sq])
            ps2 = self.next_pb('nrm', [4, 5])
            self.mm(ps2[:, 0:GT], c['onesb'][:, :], sq[:, :], R=[c['onesb'], sq], W=[ps2])
            rn = self.ring(G.name + 'rn', 2, lambda j: None)
            self.ts('dve', rn[:, :], ps2[:, 0:GT], 1.0 / 128, 1e-6, ALU.mult, ALU.add, R=[ps2], W=[rn])
            self.actf(rn[:, :], rn[:, :], AF.Sqrt, R=[rn], W=[rn])
            P.add('dve', lambda e, rn=rn: e.reciprocal(out=rn[:, :], in_=rn[:, :]), R=[rn], W=[rn])
            self.stt('dve', rn[:, :], OT[:, hh, :], c['onw'][:, l:l + 1], rn[:, :], ALU.mult, ALU.mult,
                     R=[OT, c['onw'], rn], W=[rn])
            self.tt('dve', oT[:, hh, :], rn[:, :], zs[:, hh, :], ALU.mult, R=[rn, zs], W=[oT])

        self.tok_linear_add(G, oT, 8, d['wb_dn_out'][l])

        if last_group:
            self.conv_state_out(G, l)

    def tok_linear_add(self, G, aT, nk, wsrc):
        P, T = self.P, G.T
        TP, NT = G.TP, G.NT
        x = T['x']
        for half in range(2):
            banks = [self.pb[1 + t] for t in range(NT)]
            for kc in range(nk):
                wt = self.ring('wrh', 4, lambda j: P.sb([128, 512], BF16, f"wrh{j}"))
                P.dma(wt[:, :], wsrc.ap()[kc * 128:(kc + 1) * 128, half * 512:(half + 1) * 512], R=[wsrc], W=[wt])
                for t in range(NT):
                    self.mm(banks[t][0:TP, :], aT[:, kc, t * TP:(t + 1) * TP], wt[:, :], start=(kc == 0),
                            stop=(kc == nk - 1), R=[aT, wt], W=[banks[t]])
            for t in range(NT):
                self.tt('dve', x[:, t, half * 512:(half + 1) * 512], x[:, t, half * 512:(half + 1) * 512],
                        banks[t][0:TP, :], ALU.add, R=[x, banks[t]], W=[x])

    def conv_state_out(self, G, l):
        P, c, T, d = self.P, self.c, G.T, self.d
        NSEQ = G.NSEQ
        R3 = NSEQ * 3
        carry = T['carry'][l]
        for g4 in range(6):
            co = self.ring(G.name + 'co', 2, lambda j: P.sb([R3, 4, 128], F32, G.name + f"co{j}"))
            ps = self.next_pb('lin', [1, 2, 3])
            for j in range(4):
                ci = g4 * 4 + j
                self.tr(ps[0:R3, j * 128:(j + 1) * 128], carry[:, ci, :, :].rearrange("p s r -> p (s r)"),
                        c['identf'][:, :], R=[carry, c['identf']], W=[ps])
            self.cp('act', co[:, :, :], ps[0:R3, :].rearrange("p (j n) -> p j n", j=4), R=[ps], W=[co])
            if G.NS == 1:
                dst = d['p_dn_conv']
                P.dma(dst.ap()[l][:, g4 * 512:(g4 + 1) * 512].rearrange("r (c p) -> r c p", p=128), co[:, :, :],
                      R=[co], W=[dst], q='pool')
            else:
                dst = d['s_dn_conv']
                P.dma(dst.ap()[l][:, :, g4 * 512:(g4 + 1) * 512].rearrange("s r (c p) -> (s r) c p", p=128), co[:, :, :],
                      R=[co], W=[dst], q='pool')

    def conv_state_in(self, G, l):
        P, c, T, d = self.P, self.c, G.T, self.d
        R3 = G.NSEQ * 3
        carry = T['carry'][l]
        ci_t = self.ring(G.name + 'cin', 1, lambda j: P.sb([R3, 3072], F32, G.name + "cin"))
        P.dma(ci_t[:, :], d['state_dn_conv'].ap()[l].rearrange("s r n -> (s r) n"), W=[ci_t])
        for g4 in range(6):
            ps = self.next_pb('lin', [1, 2, 3])
            for j in range(4):
                ci = g4 * 4 + j
                self.tr(ps[:, j * R3:(j + 1) * R3], ci_t[:, ci * 128:(ci + 1) * 128], c['identf'][0:R3, 0:R3],
                        R=[ci_t, c['identf']], W=[ps])
            self.cp('act', carry[:, g4 * 4:(g4 + 1) * 4, :, :].rearrange("p c s r -> p c (s r)"),
                    ps[:, 0:4 * R3].rearrange("p (j n) -> p j n", j=4), R=[ps], W=[carry])

    def dn_chunk(self, G, l, n, S_tiles, pre):
        P, c, T = self.P, self.c, G.T
        NS = G.NS
        cs = slice(n * 64, (n + 1) * 64)
        qT, kT, vT = T['qT'], T['kT'], T['vT']
        ucs, maskT, noff, same, seqind = (c[pre + k] for k in ('ucs', 'maskT', 'noff', 'same', 'seqind'))
        gtok = T['g']
        beta = T['beta']
        pb = self.pb
        nm = G.name

        def sbt(key, shape, dt=F32, nbuf=2):
            pfx = nm if NS > 1 and key in ('SG', 'gl') else 'ck'
            return self.ring(pfx + key, nbuf, lambda j: P.sb(shape, dt, pfx + key + str(j)))

        self.mm(pb[0][0:64, 0:8], ucs[:, :], gtok[:, n, :], R=[ucs, gtok], W=[pb[0]])
        self.mm(pb[0][0:64, 8:16], same[:, :], gtok[:, n, :], R=[same, gtok], W=[pb[0]])
        Gt = sbt('Gt', [64, 16])
        self.cp('act', Gt[:, :], pb[0][0:64, 0:16], R=[pb[0]], W=[Gt])
        eGd = sbt('eGd', [64, 16])
        self.tt('dve', eGd[:, 8:16], Gt[:, 8:16], Gt[:, 0:8], ALU.subtract, R=[Gt], W=[eGd])
        self.cp('dve', eGd[:, 0:8], Gt[:, 0:8], R=[Gt], W=[eGd])
        self.actf(eGd[:, :], eGd[:, :], AF.Exp, R=[eGd], W=[eGd])
        SG = sbt('SG', [64, H, NS])
        self.tt('dve', SG[:, :, :], gtok[:, n, :].unsqueeze(2).to_broadcast([64, H, NS]),
                seqind[:, :].unsqueeze(1).to_broadcast([64, H, NS]), ALU.mult, R=[gtok, seqind], W=[SG])
        self.mm(pb[0][:, 16:16 + H * NS], c['onesf'][:, :], SG[:, :, :].rearrange("p h s -> p (h s)"),
                R=[c['onesf'], SG], W=[pb[0]])
        gl = sbt('gl', [128, H, NS])
        self.actf(gl[:, :, :].rearrange("p h s -> p (h s)"), pb[0][:, 16:16 + H * NS], AF.Exp, R=[pb[0]], W=[gl])
        UG = sbt('UG', nbuf=1, shape=[64, H, 64])
        self.tt('dve', UG[:, :, :], ucs[:, :].unsqueeze(1).to_broadcast([64, H, 64]),
                gtok[:, n, :].unsqueeze(2).to_broadcast([64, H, 64]), ALU.mult, R=[ucs, gtok], W=[UG])
        for hh in range(H):
            self.mm(pb[1][:, hh * 64:(hh + 1) * 64], c['onesf'][:, :], UG[:, hh, :], R=[c['onesf'], UG], W=[pb[1]])
        eGrow = sbt('eGrow', nbuf=1, shape=[128, H, 64])
        self.actf(eGrow[:, :, :].rearrange("p h i -> p (h i)"), pb[1][:, :], AF.Exp, R=[pb[1]], W=[eGrow])
        qgT = sbt('qgT', [128, H, 64], BF16)
        self.tt('dve', qgT[:, :, :], qT[:, :, cs], eGrow[:, :, :], ALU.mult, R=[qT, eGrow], W=[qgT])
        tmp = sbt('dtmp', nbuf=1, shape=[64, H, 64])
        self.tt('dve', tmp[:, :, :], pb[1][0:64, :].rearrange("p (h i) -> p h i", h=H),
                Gt[:, 0:8].unsqueeze(2).to_broadcast([64, H, 64]), ALU.subtract, R=[pb[1], Gt], W=[tmp])
        self.tt('pool', tmp[:, :, :], tmp[:, :, :], maskT[:, :].unsqueeze(1).to_broadcast([64, H, 64]), ALU.add,
                R=[tmp, maskT], W=[tmp])
        DT = sbt('DT', nbuf=1, shape=[64, H, 64])
        self.actf(DT[:, :, :], tmp[:, :, :], AF.Exp, R=[tmp], W=[DT])
        for hh in range(H):
            self.mm(pb[2][0:64, hh * 64:(hh + 1) * 64], kT[:, hh, cs], kT[:, hh, cs], R=[kT], W=[pb[2]])
        for hh in range(H):
            self.mm(pb[3][0:64, hh * 64:(hh + 1) * 64], kT[:, hh, cs], qT[:, hh, cs], R=[kT, qT], W=[pb[3]])
        aqkT = sbt('aqkT', [64, H, 64], BF16)
        self.tt('dve', aqkT[:, :, :], DT[:, :, :], pb[3][0:64, :].rearrange("p (h i) -> p h i", h=H), ALU.mult,
                R=[DT, pb[3]], W=[aqkT])
        nbo = sbt('nbo', nbuf=1, shape=[64, H, 64])
        self.tt('pool', nbo[:, :, :], beta[:, n, :].unsqueeze(2).to_broadcast([64, H, 64]),
                noff[:, :].unsqueeze(1).to_broadcast([64, H, 64]), ALU.mult, R=[beta, noff], W=[nbo])
        X = sbt('X', [64, H, 64])
        self.tt('dve', X[:, :, :], pb[2][0:64, :].rearrange("p (h i) -> p h i", h=H), nbo[:, :, :], ALU.mult,
                R=[pb[2], nbo], W=[X])
        self.tt('dve', X[:, :, :], X[:, :, :], DT[:, :, :], ALU.mult, R=[X, DT], W=[X])
        Xb = sbt('Xb', [64, H, 64], BF16)
        self.cp('pool', Xb[:, :, :], X[:, :, :], R=[X], W=[Xb])
        ptz = self.pt[0]
        for hh in range(H):
            self.tr(ptz[0:64, hh * 64:(hh + 1) * 64], Xb[:, hh, :], c['identb'][0:64, 0:64], R=[Xb, c['identb']], W=[ptz])
        Z = sbt('Zb', [64, H, 64], BF16)
        self.cp('act', Z[:, :, :].rearrange("p h i -> p (h i)"), ptz[0:64, 0:512], R=[ptz], W=[Z])
        Pm = sbt('Pm', [64, H, 64])
        self.tt('pool', Pm[:, :, :], X[:, :, :], c['identf'][0:64, 0:64].unsqueeze(1).to_broadcast([64, H, 64]), ALU.add,
                R=[X, c['identf']], W=[Pm])
        Pb = sbt('Pb', [64, H, 64], BF16)
        self.cp('act', Pb[:, :, :], Pm[:, :, :], R=[Pm], W=[Pb])
        Y = Xb
        for lv in range(G.nlev):
            last = (lv == G.nlev - 1)
            if not last:
                for hh in range(H):
                    self.mm(pb[5][0:64, hh * 64:(hh + 1) * 64], Z[:, hh, :], Y[:, hh, :], R=[Z, Y], W=[pb[5]])
            for hh in range(H):
                self.mm(pb[4][0:64, hh * 64:(hh + 1) * 64], Y[:, hh, :], Z[:, hh, :], R=[Z, Y], W=[pb[4]])
            Zn = sbt('Zb', [64, H, 64], BF16)
            self.cp('act', Zn[:, :, :].rearrange("p h i -> p (h i)"), pb[4][0:64, :], R=[pb[4]], W=[Zn])
            if not last:
                Yn = sbt('Xb', [64, H, 64], BF16)
                self.cp('dve', Yn[:, :, :].rearrange("p h i -> p (h i)"), pb[5][0:64, :], R=[pb[5]], W=[Yn])
                Y = Yn
            Z = Zn
            for hh in range(H):
                self.mm(pb[2][0:64, hh * 64:(hh + 1) * 64], Z[:, hh, :], Pb[:, hh, :], R=[Z, Pb], W=[pb[2]])
            Pn = sbt('Pm', [64, H, 64])
            self.tt('dve', Pn[:, :, :].rearrange("p h i -> p (h i)"), Pm[:, :, :].rearrange("p h i -> p (h i)"),
                    pb[2][0:64, :], ALU.add, R=[Pm, pb[2]], W=[Pn])
            Pm = Pn
            Pb = sbt('Pb', [64, H, 64], BF16)
            self.cp('pool', Pb[:, :, :], Pm[:, :, :], R=[Pm], W=[Pb])
        vtok = sbt('vtok', [64, H, 128], BF16, 1)
        ktok = sbt('ktok', [64, H, 128], BF16, 1)
        for src, dst, pt in ((vT, vtok, self.pt[0]), (kT, ktok, self.pt[1])):
            for hh in range(H):
                self.tr(pt[0:64, hh * 128:(hh + 1) * 128], src[:, hh, cs], c['identb'][:, :], R=[src, c['identb']], W=[pt])
            self.cp('act', dst[:, :, :].rearrange("p h d -> p (h d)"), pt[0:64, :], R=[pt], W=[dst])
        kdec = sbt('kdec', [64, H, 128], BF16)
        self.tt('pool', kdec[:, :, :], ktok[:, :, :], eGd[:, 8:16].unsqueeze(2).to_broadcast([64, H, 128]), ALU.mult,
                R=[ktok, eGd], W=[kdec])

        OT = T['OT']
        if NS == 1 and getattr(G, 'Sall', None) is not None:
            Sf8, Sb8 = G.Sall[l]
            pk = (pb[0], pb[1])
            for hh in range(H):
                self.mm(pk[hh // 4][0:64, (hh % 4) * 128:(hh % 4 + 1) * 128], kT[:, hh, cs], Sb8[:, hh, :], R=[kT, Sb8],
                        W=[pk[hh // 4]])
            r = sbt('rB', [64, H, 128], BF16, 1)
            rf = sbt('rBf', [64, H, 128], F32, 1)
            for b2 in range(2):
                self.tt('dve', rf[:, 4 * b2:4 * b2 + 4, :], pk[b2][0:64, :].rearrange("p (h d) -> p h d", h=4),
                        eGd[:, 4 * b2:4 * b2 + 4].unsqueeze(2).to_broadcast([64, 4, 128]), ALU.mult, R=[pk[b2], eGd], W=[rf])
            self.tt('pool', r[:, :, :], vtok[:, :, :], rf[:, :, :], ALU.subtract, R=[vtok, rf], W=[r])
            pu = (pb[2], pb[3])
            for hh in range(H):
                self.mm(pu[hh // 4][0:64, (hh % 4) * 128:(hh % 4 + 1) * 128], Pb[:, hh, :], r[:, hh, :], R=[Pb, r],
                        W=[pu[hh // 4]])
            U = sbt('UB', [64, H, 128], BF16, 1)
            for b2 in range(2):
                self.tt('dve', U[:, 4 * b2:4 * b2 + 4, :], pu[b2][0:64, :].rearrange("p (h d) -> p h d", h=4),
                        beta[:, n, 4 * b2:4 * b2 + 4].unsqueeze(2).to_broadcast([64, 4, 128]), ALU.mult, R=[pu[b2], beta], W=[U])
            po = pb[4]
            for hh in range(H):
                self.mm(po[:, hh * 64:(hh + 1) * 64], Sb8[:, hh, :], qgT[:, hh, :], start=True, stop=False, R=[Sb8, qgT], W=[po])
                self.mm(po[:, hh * 64:(hh + 1) * 64], U[:, hh, :], aqkT[:, hh, :], start=False, stop=True, R=[U, aqkT], W=[po])
            self.cp('act', OT[:, :, cs], po[:, :].rearrange("p (h i) -> p h i", h=H), R=[po], W=[OT])
            psn = (pb[5], pb[0])
            for hh in range(H):
                self.mm(psn[hh // 4][:, (hh % 4) * 128:(hh % 4 + 1) * 128], kdec[:, hh, :], U[:, hh, :], R=[kdec, U],
                        W=[psn[hh // 4]])
            for b2 in range(2):
                hs4 = slice(4 * b2, 4 * b2 + 4)
                self.tt('dve' if b2 == 0 else 'pool', Sf8[:, hs4, :], Sf8[:, hs4, :], gl[:, hs4, :].to_broadcast([128, 4, 128]), ALU.mult,
                        R=[Sf8, gl], W=[Sf8])
            for b2 in range(2):
                hs4 = slice(4 * b2, 4 * b2 + 4)
                self.tt('dve', Sf8[:, hs4, :], Sf8[:, hs4, :], psn[b2][:, :].rearrange("p (h d) -> p h d", h=4), ALU.add,
                        R=[Sf8, psn[b2]], W=[Sf8])
            self.cp('act', Sb8[:, :, :], Sf8[:, :, :], R=[Sf8], W=[Sb8])
            return
        for hh in range(H):
            Sf, Sb = S_tiles(hh)
            if NS > 1:
                cm = c[pre + 'colmask']
                kTm = sbt('kTm', [128, NS, 64], BF16)
                self.tt('pool', kTm[:, :, :], kT[:, hh, cs].unsqueeze(1).to_broadcast([128, NS, 64]),
                        cm[:, :].rearrange("p (s i) -> p s i", s=NS), ALU.mult, R=[kT, cm], W=[kTm])
                qgm = sbt('qgm', [128, NS, 64], BF16)
                self.tt('pool', qgm[:, :, :], qgT[:, hh, :].unsqueeze(1).to_broadcast([128, NS, 64]),
                        cm[:, :].rearrange("p (s i) -> p s i", s=NS), ALU.mult, R=[qgT, cm], W=[qgm])
                kdm = sbt('kdm', [64, NS, 128], BF16)
                self.tt('pool', kdm[:, :, :], kdec[:, hh, :].unsqueeze(1).to_broadcast([64, NS, 128]),
                        seqind[:, :].unsqueeze(2).to_broadcast([64, NS, 128]), ALU.mult, R=[kdec, seqind], W=[kdm])
            pks = pb[0]
            for s in range(NS):
                lhs = kTm[:, s, :] if NS > 1 else kT[:, hh, cs]
                self.mm(pks[0:64, 0:128], lhs, Sb[:, s, :], start=(s == 0), stop=(s == NS - 1),
                        R=[kTm if NS > 1 else kT, Sb], W=[pks])
            r = sbt('r', [64, 128], BF16)
            rf1 = sbt('rf1', [64, 128])
            self.ts('dve', rf1[:, :], pks[0:64, 0:128], eGd[:, hh:hh + 1], None, ALU.mult, R=[pks, eGd], W=[rf1])
            self.tt('dve', r[:, :], vtok[:, hh, :], rf1[:, :], ALU.subtract, R=[vtok, rf1], W=[r])
            pu = pb[1]
            self.mm(pu[0:64, 0:128], Pb[:, hh, :], r[:, :], R=[Pb, r], W=[pu])
            U = sbt('U', [64, 128], BF16)
            self.ts('dve', U[:, :], pu[0:64, 0:128], beta[:, n, hh:hh + 1], None, ALU.mult, R=[pu, beta], W=[U])
            po = pb[3]
            for s in range(NS):
                rhs = qgm[:, s, :] if NS > 1 else qgT[:, hh, :]
                self.mm(po[:, 0:64], Sb[:, s, :], rhs, start=(s == 0), stop=False, R=[Sb, qgm if NS > 1 else qgT], W=[po])
            self.mm(po[:, 0:64], U[:, :], aqkT[:, hh, :], start=False, stop=True, R=[U, aqkT], W=[po])
            self.cp('act', OT[:, hh, cs], po[:, 0:64], R=[po], W=[OT])
            for s0 in range(0, NS, 4):
                psn = pb[5] if (s0 // 4) % 2 == 0 else pb[4]
                ns = min(4, NS - s0)
                for s in range(s0, s0 + ns):
                    lhs = kdm[:, s, :] if NS > 1 else kdec[:, hh, :]
                    self.mm(psn[:, (s - s0) * 128:(s - s0 + 1) * 128], lhs, U[:, :], R=[kdm if NS > 1 else kdec, U], W=[psn])
                self.tt('dve', Sf[:, s0:s0 + ns, :], Sf[:, s0:s0 + ns, :],
                        gl[:, hh, s0:s0 + ns].unsqueeze(2).to_broadcast([128, ns, 128]), ALU.mult, R=[Sf, gl], W=[Sf])
                self.tt('dve', Sf[:, s0:s0 + ns, :], Sf[:, s0:s0 + ns, :],
                        psn[:, 0:ns * 128].rearrange("p (s d) -> p s d", s=ns), ALU.add, R=[Sf, psn], W=[Sf])
                self.cp('act', Sb[:, s0:s0 + ns, :], Sf[:, s0:s0 + ns, :], R=[Sf], W=[Sb])

    def ffn(self, G, l):
        P, c, T, d = self.P, self.c, G.T, self.d
        GT = G.GT
        self.norm_T(G, d['norm_ffn'].ap()[l])
        xnT, actT = T['xnT'], T['actT']
        for ci in range(22):
            wg = self.load_wchunk(d['wc_ffn_in'][l].ap()[ci])
            wu = self.load_wchunk(d['wc_ffn_in'][l].ap()[22 + ci])
            pg = self.next_pb('ffg', [1, 2])
            pu = self.next_pb('ffu', [3, 4])
            for kc in range(8):
                self.mm(pg[:, 0:GT], wg[:, kc, :], xnT[:, kc, :], start=(kc == 0), stop=(kc == 7), R=[wg, xnT], W=[pg])
            for kc in range(8):
                self.mm(pu[:, 0:GT], wu[:, kc, :], xnT[:, kc, :], start=(kc == 0), stop=(kc == 7), R=[wu, xnT], W=[pu])
            sg = self.ring(G.name + 'sl', 2, lambda j: P.sb([128, GT], F32, G.name + f"sl{j}"))
            self.actf(sg[:, :], pg[:, 0:GT], AF.Silu, R=[pg], W=[sg])
            self.tt('dve', actT[:, ci, :], sg[:, :], pu[:, 0:GT], ALU.mult, R=[sg, pu], W=[actT])
        self.tok_linear_add(G, actT, 22, d['wb_ffn_out'][l])

    def shared_rows(self, G, dst_kv, row0, win_cb):
        P, c, T, d = self.P, self.c, G.T, self.d
        TP, NT = G.TP, G.NT
        self.norm_T(G, d['norm_kv'].ap())
        xnT = T['xnT']
        for third in range(3):
            wt = self.ring('kvw', 1, lambda j: P.sb([128, 8, 512], BF16, f"kvw{j}"))
            P.dma(wt[:, :, :], d['wb_kv'].ap()[:, third * 512:(third + 1) * 512].rearrange("(kc p) n -> p kc n", p=128),
                  R=[d['wb_kv']], W=[wt])
            for t in range(NT):
                ps = self.next_pb('lin', [1, 2, 3])
                for kc in range(8):
                    self.mm(ps[0:TP, :], xnT[:, kc, t * TP:(t + 1) * TP], wt[:, kc, :], start=(kc == 0), stop=(kc == 7),
                            R=[xnT, wt], W=[ps])
                rp = self.ring(G.name + 'rowp', 2, lambda j: P.sb([TP, 512], F32, G.name + f"rowp{j}"))
                self.cp('act', rp[:, :], ps[0:TP, :], R=[ps], W=[rp])
                if third < 2:
                    P.dma(dst_kv.ap()[row0 + t * TP: row0 + (t + 1) * TP, third * 512:(third + 1) * 512], rp[:, :],
                          R=[rp], W=[dst_kv], q='pool')
                else:
                    win_cb(t, rp)

    def phase(self):
        from contextlib import ExitStack
        b = self

        class _Ph:
            def __enter__(self_):
                self_.st = ExitStack()
                b.P.stack = self_.st
                b.rr = {}
                return self_

            def __exit__(self_, *a):
                b.P.barrier()
                b.P.stack = None
                b.rr = {}
                self_.st.close()
                return False
        return _Ph()

    def build(self):
        P, cfg = self.P, self.cfg
        self.declare_io()
        d = self.d
        nsa = cfg.get('nsa', True)
        if nsa:
            self.nsa_declare()
        self.consts()
        if nsa:
            self.nsa_consts()
        with self.phase():
            self.convert_weights()
        if nsa:
            with self.phase():
                self.nsa_tables()
        n_dn = cfg['n_dn']
        if cfg.get('prompt', True):
          with self.phase():
            G = Ctx('p', 128, 4, 1, 1, 5)
            self.alloc_ctx(G)
            T = G.T
            Sp = [(P.sb([128, H, 128], F32, f"Sp{l}"), P.sb([128, H, 128], BF16, f"Sbp{l}")) for l in range(2)]
            G.Sall = Sp
            for l in range(2):
                self.memset('pool', Sp[l][0][:, :, :], 0.0, W=[Sp[l][0]])
                self.memset('pool', Sp[l][1][:, :, :], 0.0, W=[Sp[l][1]])
                self.memset('pool', T['carry'][l][:, :, :, :], 0.0, W=[T['carry'][l]])
            ngrp = cfg['SEQ'] // G.GT
            for g in range(ngrp):
                P.dma(T['x'][:, :, :], d['x_prompt'].ap()[g * 512:(g + 1) * 512, :].rearrange("(t p) n -> p t n", p=128),
                      W=[T['x']])
                for l in range(n_dn):
                    self.dn_mixer(G, l, None, last_group=(g == ngrp - 1))
                    self.ffn(G, l)
                P.dma(d['x2_p'].ap()[g * 512:(g + 1) * 512, :].rearrange("(t p) n -> p t n", p=128), T['x'][:, :, :],
                      R=[T['x']], W=[d['x2_p']], q='pool')

                def win_cb(t, rp, g=g):
                    r0 = g * 512 + t * 128 - (cfg['SEQ'] - 512)
                    if nsa:
                        P.dma(d['pwin_d'].ap()[g * 512 + t * 128:g * 512 + (t + 1) * 128, :], rp[:, :], R=[rp],
                              W=[d['pwin_d']], q='pool')
                    if r0 >= 0:
                        P.dma(d['p_win_kv'].ap()[r0:r0 + 128, :], rp[:, :], R=[rp], W=[d['p_win_kv']], q='pool')
                self.shared_rows(G, d['p_kv_rows'], g * 512, win_cb)
            for l in range(2):
                P.dma(d['p_dn_S'].ap()[l].rearrange("h k v -> k h v"), Sp[l][0][:, :, :], R=[Sp[l][0]], W=[d['p_dn_S']], q='pool')
        if nsa and cfg.get('prompt', True) and cfg.get('nsa_stop', 99) > 1:
            try:
                self.nsa_prompt()
            except _Stop:
                pass
        if cfg.get('sample', True):
          with self.phase():
            G = Ctx('s', 64, 1, 16, 16, 1)
            self.alloc_ctx(G)
            T = G.T
            P.dma(T['x'][:, 0, :], d['x_sample'].ap(), W=[T['x']])
            Ss = P.sb([128, 16, 128], F32, "Ss")
            Ssb = P.sb([128, 16, 128], BF16, "Ssb")
            for l in range(n_dn):
                self.conv_state_in(G, l)
                cur = [None]

                def S_tiles(hh, l=l, cur=cur):
                    if cur[0] != hh:
                        if cur[0] is not None:
                            P.dma(d['s_dn_S'].ap()[l, :, cur[0]].rearrange("s k v -> k s v"), Ss[:, :, :], R=[Ss],
                                  W=[d['s_dn_S']], q='pool')
                        P.dma(Ss[:, :, :], d['state_dn_S'].ap()[l, :, hh].rearrange("s k v -> k s v"), W=[Ss])
                        self.cp('act', Ssb[:, :, :], Ss[:, :, :], R=[Ss], W=[Ssb])
                        cur[0] = hh
                    return Ss, Ssb
                self.dn_mixer(G, l, S_tiles, last_group=True)
                P.dma(d['s_dn_S'].ap()[l, :, cur[0]].rearrange("s k v -> k s v"), Ss[:, :, :], R=[Ss], W=[d['s_dn_S']],
                      q='pool')
                self.ffn(G, l)
            P.dma(d['x2_s'].ap(), T['x'][:, 0, :], R=[T['x']], W=[d['x2_s']], q='pool')
            for s4 in range(4):
                P.dma(d['s_win_kv'].ap()[s4 * 4:(s4 + 1) * 4, 0:508, :], d['state_win_kv'].ap()[s4 * 4:(s4 + 1) * 4, 4:512, :],
                      W=[d['s_win_kv']], q='sp')

            def win_cb_s(t, rp):
                for sq in range(16):
                    P.dma(d['s_win_kv'].ap()[sq, 508:512, :], rp[4 * sq:4 * sq + 4, :], R=[rp],
                          W=[d['s_win_kv']], q='pool')
            self.shared_rows(G, d['s_kv_rows'], 0, win_cb_s)
        if nsa and cfg.get('sample', True):
            self.nsa_sample()
        P.emit()
        return self.nc


_CONST_CACHE = {}


def const_inputs():
    if not _CONST_CACHE:
        for pre, NS, TS in (('cp_', 1, 64), ('cs_', 16, 4)):
            for k, v in host_consts(NS, TS).items():
                _CONST_CACHE[pre + k] = v
    return _CONST_CACHE


def make_in_maps(inp, cfg, n_cores=8):
    SEQ = cfg['SEQ']
    cst = const_inputs()
    maps = []
    shared = {k: np.ascontiguousarray(inp[k]) for k in
              ('norm_mix', 'norm_ffn', 'norm_kv', 'norm_final', 'ffn_w_in', 'ffn_w_out', 'dn_w_in', 'dn_conv_w',
               'dn_A_log', 'dn_dt_bias', 'dn_out_norm', 'dn_w_out', 'nsa_w_kv')}
    nsa = cfg.get('nsa', True)
    if nsa:
        shared['nsa_w_in'] = np.ascontiguousarray(inp['nsa_w_in'])
        shared['nsa_w_out'] = np.ascontiguousarray(inp['nsa_w_out'])
        shared['nsa_cmp_pos_w'] = np.ascontiguousarray(inp['nsa_cmp_pos_w']).reshape(2, 32, 256)
        shared['nsa_w_cmp'] = np.ascontiguousarray(inp['nsa_w_cmp'])
        shared['rel_bias'] = np.ascontiguousarray(inp['rel_bias'])
        shared['cache_kv'] = np.ascontiguousarray(inp['cache_kv']).reshape(2560 * 128, 1024)
        nsc = [nsa_host_consts(0), nsa_host_consts(1)]
    for c in range(n_cores):
        b = c // 2
        m = dict(shared)
        m.update(cst)
        m['x_prompt'] = np.ascontiguousarray(inp['x_prompt'][b, :SEQ])
        sl = slice(16 * c, 16 * c + 16)
        m['x_sample'] = np.ascontiguousarray(inp['x_sample'][sl]).reshape(64, D)
        m['state_dn_S'] = np.ascontiguousarray(inp['state_dn_S'][:, sl])
        m['state_dn_conv'] = np.ascontiguousarray(inp['state_dn_conv'][:, sl])
        m['state_win_kv'] = np.ascontiguousarray(inp['state_win_kv'][sl]).reshape(16, 512, 512)
        if nsa:
            m.update(nsc[c % 2])
            m['page_table'] = np.ascontiguousarray(inp['page_table'][sl]).reshape(1, 256).astype(np.int32)
        maps.append(m)
    return maps


def kernel(**inp):
    cfg = dict(SEQ=4096, n_dn=2)
    b = Builder(cfg)
    nc = b.build()
    maps = make_in_maps(inp, cfg)
    res = run_bass_kernel_spmd(nc, maps, core_ids=list(range(8)))
    R = res.results
    f32 = np.float32
    y_prompt = np.zeros((4, 4096, D), f32)
    for c in range(8):
        yp = R[c]['y_p'].reshape(16, 128, D)
        y_prompt[c // 2].reshape(32, 128, D)[c % 2::2] = yp
    y_sample = np.concatenate([R[c]['y_s'].reshape(16, 4, D) for c in range(8)], axis=0).astype(f32)
    p_dn_S = np.stack([R[2 * b]['p_dn_S'] for b in range(4)], axis=1)
    p_dn_conv = np.stack([R[2 * b]['p_dn_conv'] for b in range(4)], axis=1)
    p_kv_rows = np.stack([R[2 * b]['p_kv_rows'] for b in range(4)], axis=0).reshape(4, 4096, 4, 4, 64)
    p_win_kv = np.stack([R[2 * b]['p_win_kv'] for b in range(4)], axis=0).reshape(4, 512, 2, 4, 64)
    s_dn_S = np.concatenate([R[c]['s_dn_S'] for c in range(8)], axis=1)
    s_dn_conv = np.concatenate([R[c]['s_dn_conv'] for c in range(8)], axis=1)
    s_kv_rows = np.concatenate([R[c]['s_kv_rows'].reshape(16, 4, 4, 4, 64) for c in range(8)], axis=0)
    s_win_kv = np.concatenate([R[c]['s_win_kv'].reshape(16, 512, 2, 4, 64) for c in range(8)], axis=0)
    return (y_prompt, y_sample, p_dn_S.astype(f32), p_dn_conv.astype(f32), p_kv_rows.astype(f32),
            p_win_kv.astype(f32), s_dn_S.astype(f32), s_dn_conv.astype(f32), s_kv_rows.astype(f32),
            s_win_kv.astype(f32))


def _bucket_np(d):
    n = np.maximum(d, 0)
    nf = np.maximum(n, 1).astype(np.float32)
    large = 16 + (np.log(nf / np.float32(16)) / np.float32(math.log(64.0)) * np.float32(16)).astype(np.int32)
    large = np.minimum(large, 31)
    return np.where(n < 16, n, large)


def _onehot(d, valid):
    b = np.where(valid, _bucket_np(d), 32)
    oh = np.zeros((33, d.shape[0]), np.float32)
    oh[b, np.arange(d.shape[0])] = 1.0
    return oh


def nsa_host_consts(par):
    c = {}
    t = np.arange(1280)
    d = t - 255 + 128 * par
    c['oh_sel'] = _onehot(d, d >= 0)
    t = np.arange(1024)
    d = t - 255 + 128 * par
    c['oh_win'] = _onehot(d, (d >= 0) & (d < 512))
    r = np.arange(16)[:, None]
    w = np.arange(512)[None, :]
    d = (16 * (247 - w + 8 * par) + r - 31).reshape(-1)
    c['oh_cmp'] = _onehot(d, d >= 0)
    tt = np.arange(4)[:, None]
    cc = np.arange(128)[None, :]
    d = (2017 + tt - 16 * cc).reshape(-1)
    c['oh_cs'] = _onehot(d, d >= 0)
    x = np.arange(2304)
    d = x - 127
    c['oh_ss'] = _onehot(d, d >= 0)
    x = np.arange(768)
    d = x - 127
    c['oh_ws'] = _onehot(d, (d >= 0) & (d < 512))
    blk = np.arange(64)[None, None, :]
    qpos = (128 * (2 * np.arange(16)[:, None, None] + par) + np.arange(128)[None, :, None])
    cur = qpos // 64
    forced = (blk == 0) | (blk == cur) | (blk == cur - 1)
    valid = blk * 64 <= qpos
    c['selmul_p'] = np.ascontiguousarray(np.where(valid & ~forced, 1.0, 0.0).astype(np.float32).transpose(1, 0, 2))
    c['seladd_p'] = np.ascontiguousarray(np.where(forced, 1e4, np.where(valid, 0.0, -1.0)).astype(np.float32).transpose(1, 0, 2))
    blk = np.arange(64)[None, :]
    qpos = 2048 + np.arange(4)[:, None]
    cur = qpos // 64
    exists = blk < 33
    forced = ((blk == 0) | (blk == cur) | (blk == cur - 1)) & exists
    valid = (blk * 64 <= qpos) & exists
    c['selmul_s'] = np.where(valid & ~forced, 1.0, 0.0).astype(np.float32)
    c['seladd_s'] = np.where(forced, 1e4, np.where(valid, 0.0, np.where(exists, -1.0, -2.0))).astype(np.float32)
    k = np.arange(4096)[None, :]
    c['expE'] = (k // 64 == np.arange(64)[:, None]).astype(np.float32)
    sel8 = np.zeros((128, 8), np.float32)
    sel8[np.arange(128), np.arange(128) // 16] = 1.0
    c['sel8'] = sel8
    c['antiI'] = np.ascontiguousarray(np.eye(128, dtype=np.float32)[::-1])
    c['parf'] = np.tile(np.array([[float(par), 1.0 - float(par)]], np.float32), (128, 1))
    c['iota'] = np.arange(128, dtype=np.float32).reshape(128, 1)
    return c


def _nsa_declare(self):
    P, cfg, d = self.P, self.cfg, self.d
    SEQ = cfg['SEQ']

    def inp(name, shape, dt=F32):
        d[name] = P.dram(name, shape, dt, kind="ExternalInput")

    inp('nsa_w_in', [2, D, 1072])
    inp('nsa_w_out', [2, D, D])
    inp('nsa_cmp_pos_w', [2, 32, 256])
    inp('nsa_w_cmp', [2, 4, 64, 64])
    inp('rel_bias', [32, 16])
    inp('cache_kv', [2560 * 128, 1024])
    inp('page_table', [1, 256], I32)
    for nm, shp in (('oh_sel', [33, 1280]), ('oh_win', [33, 1024]), ('oh_cmp', [33, 8192]), ('oh_cs', [33, 512]),
                    ('oh_ss', [33, 2304]), ('oh_ws', [33, 768]), ('selmul_p', [128, 16, 64]), ('seladd_p', [128, 16, 64]),
                    ('selmul_s', [4, 64]), ('seladd_s', [4, 64]), ('expE', [64, 4096]), ('sel8', [128, 8]),
                    ('antiI', [128, 128]), ('parf', [128, 2]), ('iota', [128, 1])):
        inp(nm, shp)
    d['y_p'] = P.dram('y_p', [SEQ // 2, D], F32, kind="ExternalOutput")
    d['y_s'] = P.dram('y_s', [64, D], F32, kind="ExternalOutput")
    d['wc_nsa_in'] = [P.dram(f'wc_nsa_in{j}', [8, 128, 8, 128], BF16) for j in range(2)]
    d['wb_nsa_out'] = [P.dram(f'wb_nsa_out{j}', [D, D], BF16) for j in range(2)]
    for nm, ln in (('t_sel', 1280), ('t_win', 1024), ('t_cmp', 8192), ('t_cs', 512), ('t_ss', 2304), ('t_ws', 768)):
        d[nm] = P.dram(nm, [16, ln], F32)
    d['BTd'] = P.dram('BTd', [4, 15, 128, 512], F32)
    NT = SEQ // 128
    d['pwin_d'] = P.dram('pwin_d', [SEQ, 512], F32)
    d['kselT_p'] = P.dram('kselT_p', [4, 128, SEQ], BF16)
    d['vsel_p'] = P.dram('vsel_p', [NT, 128, 4, 66], BF16)
    d['kwinT_p'] = P.dram('kwinT_p', [4, 128, SEQ], BF16)
    d['vwin_p'] = P.dram('vwin_p', [NT, 128, 4, 66], BF16)
    d['kselT_s'] = P.dram('kselT_s', [16, 4, 128, 17 * 128], BF16)
    d['vsel_s'] = P.dram('vsel_s', [16, 17, 128, 4, 66], BF16)
    d['kwinT_s'] = P.dram('kwinT_s', [16, 4, 128, 5 * 128], BF16)
    d['vwin_s'] = P.dram('vwin_s', [16, 5, 128, 4, 66], BF16)


def _nsa_consts(self):
    P, d, c = self.P, self.d, self.c
    antiI = P.sb([128, 128], F32, "antiI")
    P.dma(antiI[:, :], d['antiI'].ap(), W=[antiI])
    sel8 = P.sb([128, 8], BF16, "sel8")
    P.dma(sel8[:, :], d['sel8'].ap(), W=[sel8], q='pool')
    parf = P.sb([128, 2], F32, "parf")
    P.dma(parf[:, :], d['parf'].ap(), W=[parf])
    tabx = P.sb([33, 16], F32, "tabx")
    r31 = P.sb([32, 16], F32, "r31")
    P.dma(tabx[0:32, :], d['rel_bias'].ap(), W=[tabx])
    P.dma(r31[:, :], d['rel_bias'].ap()[31].partition_broadcast(32), W=[r31])
    self.tt('dve', tabx[0:32, :], tabx[0:32, :], r31[:, :], ALU.subtract, R=[tabx, r31], W=[tabx])
    self.memset('pool', tabx[32:33, :], NEG, W=[tabx])
    c.update(antiI=antiI, sel8=sel8, parf=parf, tabx=tabx)
    wg = P.sb([128, 2, 8, 48], BF16, "wg")
    for j in range(2):
        P.dma(wg[:, j, :, :], d['nsa_w_in'].ap()[j, :, 1024:1072].rearrange("(kc p) n -> p kc n", p=128), W=[wg], q='pool')
    c['wg'] = wg


def _nsa_late_consts(self):
    P, d, c = self.P, self.d, self.c
    if 'wlo' in c:
        return
    wlo = P.sb([128, 512], F32, "wlo")
    whi = P.sb([128, 512], F32, "whi")
    for a in range(8):
        P.dma(wlo[16 * a:16 * a + 16, :].rearrange("r (c n) -> r c n", c=2),
              d['nsa_cmp_pos_w'].ap()[:, 0:16, :].rearrange("c r n -> r c n"), W=[wlo])
        P.dma(whi[16 * a:16 * a + 16, :].rearrange("r (c n) -> r c n", c=2),
              d['nsa_cmp_pos_w'].ap()[:, 16:32, :].rearrange("c r n -> r c n"), W=[whi])
    wck = P.sb([128, 4, 128], BF16, "wck")
    wcv = P.sb([128, 4, 64], BF16, "wcv")
    for half in range(2):
        for dup in range(2):
            P.dma(wck[half * 64:(half + 1) * 64, :, dup * 64:(dup + 1) * 64],
                  d['nsa_w_cmp'].ap()[0].rearrange("k d e -> d k e"), W=[wck], q='pool')
        P.dma(wcv[half * 64:(half + 1) * 64, :, :], d['nsa_w_cmp'].ap()[1].rearrange("k d e -> d k e"), W=[wcv], q='pool')
    expE = P.sb([64, 4096], BF16, "expE")
    P.dma(expE[:, :], d['expE'].ap(), W=[expE], q='pool')
    c.update(wlo=wlo, whi=whi, wck=wck, wcv=wcv, expE=expE)


Builder.nsa_late_consts = _nsa_late_consts


def _nsa_tables(self):
    P, d, c = self.P, self.d, self.c
    for oh, dst, ln in (('oh_sel', 't_sel', 1280), ('oh_win', 't_win', 1024), ('oh_cmp', 't_cmp', 8192),
                        ('oh_cs', 't_cs', 512), ('oh_ss', 't_ss', 2304), ('oh_ws', 't_ws', 768)):
        for off in range(0, ln, 512):
            n = min(512, ln - off)
            ot = self.ring('oht', 2, lambda j: P.sb([33, 512], F32, f"oht{j}"))
            P.dma(ot[:, 0:n], d[oh].ap()[:, off:off + n], W=[ot])
            ps = self.next_pb('lin', [1, 2, 3])
            self.mm(ps[0:16, 0:n], c['tabx'][:, :], ot[:, 0:n], R=[c['tabx'], ot], W=[ps])
            tb = self.ring('tbo', 2, lambda j: P.sb([16, 512], F32, f"tbo{j}"))
            self.cp('act', tb[:, 0:n], ps[0:16, 0:n], R=[ps], W=[tb])
            P.dma(d[dst].ap()[:, off:off + n], tb[:, 0:n], R=[tb], W=[d[dst]], q='pool')
    if self.cfg.get('prompt', True):
        for kvh in range(4):
            for idx in range(15):
                tab, ln, e = ('t_sel', 1280, idx - 1) if idx < 9 else ('t_win', 1024, idx - 10)
                tr_ = self.ring('trv', 2, lambda j: P.sb([128, 4, 128], F32, f"trv{j}"))
                src = bass.AP(d[tab].h, 4 * kvh * ln + 128 * (e + 1), [[1, 128], [ln, 4], [1, 128]])
                P.dma(tr_[:, :, :], src, R=[d[tab]], W=[tr_])
                ps = self.next_pb('lin', [1, 2, 3])
                self.mm(ps[:, :], c['antiI'][:, :], tr_[:, :, :].rearrange("p g q -> p (g q)"), R=[c['antiI'], tr_], W=[ps])
                fl = self.ring('flp', 2, lambda j: P.sb([128, 512], F32, f"flp{j}"))
                self.cp('act', fl[:, :], ps[:, :], R=[ps], W=[fl])
                P.dma(d['BTd'].ap()[kvh, idx], fl[:, :], R=[fl], W=[d['BTd']], q='pool')


def _ctx_rows(self, rows, P_idx, lohi, kT_dst, v_dst, kw_dst, vw_dst, has_cmpsel=True, has_win=True, win_rows=None):
    P, c = self.P, self.c
    if has_cmpsel:
        if lohi is not None:
            alo = self.ring('alo', 2, lambda j: P.sb([128, 512], BF16, f"alo{j}"))
            ahi = self.ring('ahi', 2, lambda j: P.sb([128, 512], BF16, f"ahi{j}"))
            self.tt('pool', alo[:, :], rows[:, 0:512], c['wlo'][:, :], ALU.mult, R=[rows, c['wlo']], W=[alo])
            self.tt('dve', ahi[:, :], rows[:, 0:512], c['whi'][:, :], ALU.mult, R=[rows, c['whi']], W=[ahi])
            ps = self.next_pb('ctxp', [0])
            for lh, a in enumerate((alo, ahi)):
                for ch in range(4):
                    self.mm(ps[:, (lh * 4 + ch) * 8:(lh * 4 + ch + 1) * 8], a[:, ch * 128:(ch + 1) * 128], c['sel8'][:, :],
                            R=[a, c['sel8']], W=[ps])
            self.cp('act', lohi[:, :, :, 8 * P_idx:8 * P_idx + 8], ps[:, 0:64].rearrange("p (l c m) -> p l c m", l=2, c=4),
                    R=[ps], W=[lohi])
        for (col0, kdst, vdst) in ((512, kT_dst, v_dst),):
            _kv_tile(self, rows, col0, kdst, vdst)
    if has_win:
        wt, wc0 = win_rows
        _kv_tile(self, wt, wc0, kw_dst, vw_dst)


def _kv_tile(self, rows, col0, kdst, vdst):
    P, c = self.P, self.c
    kd = self.ring('kd', 2, lambda j: P.sb([128, 4, 2, 64], BF16, f"kd{j}"))
    self.cp('dve', kd[:, :, :, :], rows[:, col0:col0 + 256].rearrange("p (k d) -> p k d", k=4).unsqueeze(2).to_broadcast([128, 4, 2, 64]),
            R=[rows], W=[kd])
    pt = self.pt[1]
    for k in range(4):
        self.tr(pt[:, k * 128:(k + 1) * 128], kd[:, k, :, :].rearrange("p a d -> p (a d)"), c['identb'][:, :],
                R=[kd, c['identb']], W=[pt])
    ks = self.ring('ks', 2, lambda j: P.sb([128, 4, 128], BF16, f"ks{j}"))
    self.cp('act', ks[:, :, :].rearrange("p k n -> p (k n)"), pt[:, 0:512], R=[pt], W=[ks])
    kap, ktile = kdst
    P.dma(kap, ks[:, :, :], R=[ks], W=[ktile], q='sp')
    va = self.ring('va', 2, lambda j: P.sb([128, 4, 66], BF16, f"va{j}"))
    self.memset('dve', va[:, :, 64:65], 1.0, W=[va])
    self.memset('dve', va[:, :, 65:66], 0.0, W=[va])
    self.cp('dve', va[:, :, 0:64], rows[:, col0 + 256:col0 + 512].rearrange("p (k d) -> p k d", k=4), R=[rows], W=[va])
    vap, vtile = vdst
    P.dma(vap, va[:, :, :], R=[va], W=[vtile], q='sp')


def _cmp_finish(self, lohi, NCB, kcd_ap, vc_ap_fn, kcd_t, vc_t, ncw=256, njh=2):
    P, c = self.P, self.c
    bl = self.ring('blk', 1, lambda j: P.sb([128, 4, 256], BF16, "blk"))
    self.memset('pool', bl[:, :, :], 0.0, W=[bl])
    self.tt('dve', bl[:, :, 0:NCB], lohi[:, 0, :, 0:NCB], lohi[:, 1, :, 1:NCB + 1], ALU.add, R=[lohi], W=[bl])
    for kvh in range(4):
        hs = slice((kvh % 2) * 64, (kvh % 2) * 64 + 64)
        ps = self.next_pb('lin', [1, 2, 3])
        self.mm(ps[:, 0:256], c['wck'][hs, kvh, :], bl[hs, kvh // 2, :], R=[c['wck'], bl], W=[ps])
        self.cp('act', kcd_ap(kvh), ps[:, 0:ncw], R=[ps], W=[kcd_t])
        for jh in range(njh):
            ps2 = self.next_pb('lin', [1, 2, 3])
            self.mm(ps2[:, 0:64], bl[hs, 2 + kvh // 2, jh * 128:(jh + 1) * 128], c['wcv'][hs, kvh, :], R=[bl, c['wcv']], W=[ps2])
            self.cp('act', vc_ap_fn(jh, kvh), ps2[:, 0:64], R=[ps2], W=[vc_t])


Builder.nsa_declare = _nsa_declare
Builder.nsa_consts = _nsa_consts
Builder.nsa_tables = _nsa_tables
Builder.ctx_rows = _ctx_rows
Builder.cmp_finish = _cmp_finish


def _attend_gen(self, A):
    P, c = self.P, self.c
    NQ, NC, kvh = A['NQ'], A['NC'], A['kvh']
    pb = self.pb
    qT, qTt = A['qT']
    qbd, qcols = A['qbd'], A['qcols']
    gate = A['gate']
    N4 = 4 * NQ

    def sbt(key, shape, dt=F32, nbuf=2):
        k = f"at{NQ}{key}"
        return self.ring(k, nbuf, lambda j: P.sb(shape, dt, k + str(j)))

    if A.get('stop', 99) == 0:
        raise _Stop()
    kc_ap, kc_t = A['kcmp']
    for g in range(4):
        ps = pb[g % 2]
        self.mm(ps[0:NQ, (g // 2) * 256:(g // 2) * 256 + NC], qT(g), kc_ap(g), R=[qTt, kc_t], W=[ps])
    yield 0
    if A.get('stop', 99) == 10:
        raise _Stop()
    bc_ap, bc_t = A['bias_c']
    sc = sbt('sc', [NQ, 4, 256], nbuf=1)
    for g in range(4):
        ps = pb[g % 2]
        self.tt('dve', sc[:, g, 0:NC], ps[0:NQ, (g // 2) * 256:(g // 2) * 256 + NC], bc_ap[:, g, 0:NC], ALU.add,
                R=[ps, bc_t], W=[sc])
    yield 0
    if A.get('stop', 99) == 11:
        raise _Stop()
    ssum = sbt('ssum', [NQ, 8])
    self.memset('pool', ssum[:, :], 0.0, W=[ssum])
    for g in range(4):
        self.actf(sc[:, g, 0:NC], sc[:, g, 0:NC], AF.Exp, R=[sc], W=[sc, ssum], accum_out=ssum[:, g:g + 1])
    yield 0
    if A.get('stop', 99) == 12:
        raise _Stop()
    self.ts('dve', ssum[:, 4:8], ssum[:, 0:4], 1e-30, None, ALU.max, R=[ssum], W=[ssum])
    P.add('dve', lambda e: e.reciprocal(out=ssum[:, 4:8], in_=ssum[:, 4:8]), R=[ssum], W=[ssum])
    if A.get('stop', 99) == 13:
        raise _Stop()
    pc = sbt('pc', [NQ, 4, 256], nbuf=1)
    if not A.get('pc_init'):
        pass
    self.memset('pool', pc[:, :, :], 0.0, W=[pc])
    self.tt('dve', pc[:, :, 0:NC], sc[:, :, 0:NC], ssum[:, 4:8].unsqueeze(2).to_broadcast([NQ, 4, NC]), ALU.mult,
            R=[sc, ssum], W=[pc])
    yield 0
    if A.get('stop', 99) == 1:
        raise _Stop()
    imp = sbt('imp', [NQ, 264], nbuf=1)
    self.memset('pool', imp[:, :], 0.0, W=[imp])
    P.add('dve', lambda e: e.tensor_reduce(out=imp[:, 1:257], in_=pc[:, :, :].rearrange("p g n -> p n g"),
                                           axis=mybir.AxisListType.X, op=ALU.add), R=[pc], W=[imp])
    yield 0
    cov = sbt('cov', [NQ, 256], nbuf=1)
    self.tt('dve', cov[:, :], imp[:, 1:257], imp[:, 0:256], ALU.add, R=[imp], W=[cov])
    psl = sbt('psl', [NQ, 64], nbuf=1)
    P.add('dve', lambda e: e.tensor_reduce(out=psl[:, :], in_=cov[:, :].rearrange("p (b r) -> p b r", r=4),
                                           axis=mybir.AxisListType.X, op=ALU.add), R=[cov], W=[psl])
    yield 0
    smul, sadd, s_t = A['selc']
    self.tt('dve', psl[:, :], psl[:, :], smul, ALU.mult, R=[psl, s_t], W=[psl])
    self.tt('dve', psl[:, :], psl[:, :], sadd, ALU.add, R=[psl, s_t], W=[psl])
    yield 0
    m16 = sbt('m16', [NQ, 16], nbuf=1)
    ps2 = sbt('psl2', [NQ, 64], nbuf=1)
    P.add('dve', lambda e: e.max(out=m16[:, 0:8], in_=psl[:, :]), R=[psl], W=[m16])
    P.add('dve', lambda e: e.match_replace(out=ps2[:, :], in_to_replace=m16[:, 0:8], in_values=psl[:, :], imm_value=-5.0),
          R=[psl, m16], W=[ps2])
    P.add('dve', lambda e: e.max(out=m16[:, 8:16], in_=ps2[:, :]), R=[ps2], W=[m16])
    yield 0
    nsel = sbt('nsel', [NQ, 64], nbuf=1)
    self.ts('dve', nsel[:, :], psl[:, :], m16[:, 15:16], None, ALU.is_ge, R=[psl, m16], W=[nsel])
    self.ts('dve', nsel[:, :], nsel[:, :], -NEG, NEG, ALU.mult, ALU.add, R=[nsel], W=[nsel])
    self.tr(pb[0][0:64, 0:NQ], nsel[:, :], c['identf'][0:NQ, 0:NQ], R=[nsel, c['identf']], W=[pb[0]])
    nsT = sbt('nsT', [64, 4, NQ], BF16)
    self.cp('act', nsT[:, :, :], pb[0][0:64, 0:NQ].unsqueeze(1).to_broadcast([64, 4, NQ]), R=[pb[0]], W=[nsT])
    yield 0
    if A.get('stop', 99) == 2:
        raise _Stop()
    pcb = sbt('pcb', [NQ, 4, 256], BF16, 1)
    self.cp('pool', pcb[:, :, :], pc[:, :, :], R=[pc], W=[pcb])
    pt = self.pt[0]
    NH = A['NH']
    for g in range(4):
        for hf in range(NH):
            self.tr(pt[:, (g * NH + hf) * NQ:(g * NH + hf + 1) * NQ], pcb[:, g, hf * 128:(hf + 1) * 128],
                    c['identb'][0:NQ, 0:NQ], R=[pcb, c['identb']], W=[pt])
    yield 0
    pcT = sbt('pcT', [128, 4 * NH, NQ], BF16, 1)
    self.cp('act', pcT[:, :, :].rearrange("p a q -> p (a q)"), pt[:, 0:4 * NH * NQ], R=[pt], W=[pcT])
    vc_ap, vc_t = A['vcmp']
    for g in range(4):
        for hf in range(NH):
            self.mm(pb[1][0:NQ, g * 64:(g + 1) * 64], pcT[:, g * NH + hf, :], vc_ap(hf), start=(hf == 0), stop=(hf == NH - 1),
                    R=[pcT, vc_t], W=[pb[1]])
    yield 0
    oacc = sbt('oacc', [NQ, 4, 64])
    g_ap, g_t = gate
    self.tt('dve', oacc[:, :, :], pb[1][0:NQ, 0:256].rearrange("p (g d) -> p g d", g=4),
            g_ap(0).unsqueeze(2).to_broadcast([NQ, 4, 64]), ALU.mult, R=[pb[1], g_t], W=[oacc])

    yield 'CMP_DONE'
    if A.get('stop', 99) == 3:
        raise _Stop()
    for br, (tiles, ksrc, vsrc, po) in enumerate((A['sel'], A['win'])):
        if A.get('stop', 99) == 4 and br == 1:
            raise _Stop()
        nt = len(tiles)
        pend = None
        for c0 in range(0, nt, 4):
            n = min(4, nt - c0)
            kt0 = tiles[c0][0]
            kc = sbt('kc', [128, 512], BF16, 3)
            kap, ktile = ksrc(kt0, n)
            P.dma(kc[:, 0:n * 128], kap, R=[ktile], W=[kc])
            vcx = sbt('vcx', [128, 4, 66], BF16, 3)
            vap, vtile = vsrc(kt0, n)
            P.dma(vcx[:, 0:n, :], vap, R=[vtile], W=[vcx])
            for t in range(n):
                kt, bias = tiles[c0 + t]
                ps = self.next_pb('sc', [2, 3])
                for pr in range(2):
                    self.mm(ps[:, pr * 2 * NQ:(pr + 1) * 2 * NQ], kc[:, t * 128:(t + 1) * 128], qbd[:, pr, :, qcols],
                            start=(pr == 0), stop=(br == 1 and pr == 1), R=[kc, A['qbd_t']], W=[ps])
                if br == 0:
                    self.mm(ps[:, 0:N4], c['expE'][:, kt * 128:(kt + 1) * 128], nsT[:, :, :].rearrange("p g q -> p (g q)"),
                            start=False, stop=True, R=[c['expE'], nsT], W=[ps])
                PT = sbt('PT', [128, N4], BF16, 3)
                if bias is not None:
                    b_ap, b_t = bias
                    sb_ = sbt('sbias', [128, N4], F32, 2)
                    self.tt('dve', sb_[:, :], ps[:, 0:N4], b_ap, ALU.add, R=[ps, b_t], W=[sb_])
                    self.actf(PT[:, :], sb_[:, :], AF.Exp, R=[sb_], W=[PT])
                else:
                    self.actf(PT[:, :], ps[:, 0:N4], AF.Exp, R=[ps], W=[PT])
                if pend is not None:
                    pend()
                def pv(PT=PT, vcx=vcx, t=t, first=(c0 + t == 0), last=(c0 + t == nt - 1), po=po):
                    for g in range(4):
                        self.mm(po[0:NQ, g * 66:(g + 1) * 66], PT[:, g * NQ:(g + 1) * NQ], vcx[:, t, :],
                                start=(first and g == 0), stop=(last and g == 3), R=[PT, vcx], W=[po])
                pend = pv
                yield 1
        if pend is not None:
            pend()
            pend = None
        pov = po[0:NQ, 0:264].rearrange("p (g e) -> p g e", g=4)
        rs = sbt('rs', [NQ, 4])
        self.ts('dve', rs[:, :], pov[:, :, 64], 1e-30, None, ALU.max, R=[po], W=[rs])
        P.add('dve', lambda e, rs=rs: e.reciprocal(out=rs[:, :], in_=rs[:, :]), R=[rs], W=[rs])
        self.tt('dve', rs[:, :], rs[:, :], g_ap(br + 1), ALU.mult, R=[rs, g_t], W=[rs])
        tmpo = sbt('tmpo', [NQ, 4, 64])
        self.tt('dve', tmpo[:, :, :], pov[:, :, 0:64], rs[:, :].unsqueeze(2).to_broadcast([NQ, 4, 64]), ALU.mult,
                R=[po, rs], W=[tmpo])
        self.tt('pool', oacc[:, :, :], oacc[:, :, :], tmpo[:, :, :], ALU.add, R=[oacc, tmpo], W=[oacc])
    o_ap, o_t = A['out']
    self.cp('act', o_ap, oacc[:, :, :], R=[oacc], W=[o_t])


def _attend(self, A):
    for _ in self.attend_gen(A):
        pass


def _attend_pipe(self, thunks):
    prev = None
    for th in thunks:
        g = self.attend_gen(th())
        cmp_done = False
        while not cmp_done or prev is not None:
            if not cmp_done:
                if next(g) == 'CMP_DONE':
                    cmp_done = True
            if prev is not None:
                try:
                    next(prev)
                except StopIteration:
                    prev = None
        prev = g
    if prev is not None:
        for _ in prev:
            pass


Builder.attend_gen = _attend_gen
Builder.attend_pipe = _attend_pipe
Builder.attend = _attend


def _nsa_qproj(self, G, j, qT_all, NT_):
    P, d, T = self.P, self.d, G.T
    GT = G.GT
    for ci in range(8):
        wt = self.load_wchunk(d['wc_nsa_in'][j].ap()[ci])
        ps = self.next_pb('lin', [1, 2, 3])
        for kc in range(8):
            self.mm(ps[:, 0:GT], wt[:, kc, :], T['xnT'][:, kc, :], start=(kc == 0), stop=(kc == 7), R=[wt, T['xnT']], W=[ps])
        self.actf(qT_all[:, ci, :], ps[:, 0:GT], AF.Copy, R=[ps], W=[qT_all], scale=0.125)


def _final_norm(self, G, dst, row0, ytile=None):
    P, d, T = self.P, self.d, G.T
    TP = G.TP
    wrow = self.ring('wrow', 2, lambda j: P.sb([128, D], F32, f"wrow{j}"))
    P.dma(wrow[:, :], d['norm_final'].ap().partition_broadcast(128), W=[wrow])
    x = T['x']
    for t in range(G.NT):
        junk = self.ring('junk', 1, lambda j: P.sb([128, D], BF16, f"junk{j}"))
        st = self.ring('nst', 4, lambda j: P.sb([128, 2], F32, f"nst{j}"))
        self.memset('pool', st[:, :], 0.0, W=[st])
        self.actf(junk[0:TP, :], x[:, t, :], AF.Square, R=[x], W=[junk, st], accum_out=st[0:TP, 0:1])
        self.ts('dve', st[0:TP, 1:2], st[0:TP, 0:1], 1.0 / D, 1e-6, ALU.mult, ALU.add, R=[st], W=[st])
        self.actf(st[0:TP, 1:2], st[0:TP, 1:2], AF.Sqrt, R=[st], W=[st])
        P.add('dve', lambda e, st=st: e.reciprocal(out=st[0:TP, 1:2], in_=st[0:TP, 1:2]), R=[st], W=[st])
        if ytile is None:
            yt = self.ring(G.name + 'yt', 2, lambda j: P.sb([TP, D], F32, G.name + f"yt{j}"))
            ya = yt[:, :]
        else:
            yt = ytile
            ya = ytile[:, t % 2, :]
        self.stt('dve', ya, x[:, t, :], st[0:TP, 1:2], wrow[0:TP, :], ALU.mult, ALU.mult, R=[x, st, wrow], W=[yt])
        P.dma(dst.ap()[row0 + t * TP:row0 + (t + 1) * TP, :], ya, R=[yt], W=[dst], q='pool')


def _nsa_prompt(self):
    P, d, c, cfg = self.P, self.d, self.c, self.cfg
    SEQ = cfg['SEQ']
    NTT = SEQ // 128
    NQT = NTT // 2
    NCB = NTT * 8 - 1
    self.nsa_late_consts()
    kcd = P.sb([128, 4, 256], BF16, "kcd")
    vcm = P.sb([128, 2, 4, 64], BF16, "vcm")
    self.memset('pool', vcm[:, :, :, :], 0.0, W=[vcm])
    with self.phase():
        lohi = P.sb([128, 2, 4, 264], F32, "lohi")
        self.memset('pool', lohi[:, :, :, :], 0.0, W=[lohi])
        for Pi in range(NTT):
            rows = self.ring('crow', 2, lambda j: P.sb([128, 1536], F32, f"crow{j}"))
            P.dma(rows[:, 0:1024], d['p_kv_rows'].ap()[Pi * 128:(Pi + 1) * 128, :], R=[d['p_kv_rows']], W=[rows])
            P.dma(rows[:, 1024:1536], d['pwin_d'].ap()[Pi * 128:(Pi + 1) * 128, :], R=[d['pwin_d']], W=[rows])
            cs = slice(Pi * 128, (Pi + 1) * 128)
            self.ctx_rows(rows, Pi, lohi,
                          (d['kselT_p'].ap()[:, :, cs].rearrange("k p n -> p k n"), d['kselT_p']),
                          (d['vsel_p'].ap()[Pi], d['vsel_p']),
                          (d['kwinT_p'].ap()[:, :, cs].rearrange("k p n -> p k n"), d['kwinT_p']),
                          (d['vwin_p'].ap()[Pi], d['vwin_p']), win_rows=(rows, 1024))
        self.cmp_finish(lohi, NCB, lambda kvh: kcd[:, kvh, :], lambda jh, kvh: vcm[:, jh, kvh, :], kcd, vcm)
    if cfg.get('nsa_stop', 99) == 2:
        raise _Stop()
    with self.phase():
        G = Ctx('n', 128, 4, 1, 1, 5)
        T = {}
        T['x'] = P.sb([128, 4, D], F32, "nx")
        T['xnT'] = P.sb([128, 8, 512], BF16, "nxnT")
        big = P.sb([128, 24, 512], BF16, "nbig")
        T['actT'] = View(big, 0, 22)
        qT_all = View(big, 0, 8)
        T['oT'] = P.sb([128, 8, 512], BF16, "noT")
        G.T = T
        o_tok = P.sb([128, 4, D], BF16, "o_tok")
        gates = P.sb([128, 4, 48], F32, "gates")
        qbd = P.sb([128, 2, 2, 512], BF16, "qbd")
        self.memset('pool', qbd[:, :, :, :], 0.0, W=[qbd])
        BT = P.sb([128, 15, 512], F32, "BT")
        for grp in range(NQT // 4):
            for tl in range(4):
                i = grp * 4 + tl
                xe = self.ring('xeo', 1, lambda j: P.sb([128, 2, D], F32, f"xeo{j}"))
                P.dma(xe[:, :, :], d['x2_p'].ap()[2 * i * 128:(2 * i + 2) * 128, :].rearrange("(e p) n -> p e n", p=128),
                      R=[d['x2_p']], W=[xe])
                self.ts('dve', T['x'][:, tl, :], xe[:, 0, :], c['parf'][:, 1:2], None, ALU.mult, R=[xe, c['parf']], W=[T['x']])
                self.stt('dve', T['x'][:, tl, :], xe[:, 1, :], c['parf'][:, 0:1], T['x'][:, tl, :], ALU.mult, ALU.add,
                         R=[xe, c['parf'], T['x']], W=[T['x']])
            for j in range(2):
                self.norm_T(G, d['norm_mix'].ap()[2 + j])
                _nsa_qproj(self, G, j, qT_all, 4)
                pg = self.pb[0]
                for tl in range(4):
                    for kc in range(8):
                        self.mm(pg[:, tl * 48:(tl + 1) * 48], T['xnT'][:, kc, tl * 128:(tl + 1) * 128], c['wg'][:, j, kc, :],
                                start=(kc == 0), stop=(kc == 7), R=[T['xnT'], c['wg']], W=[pg])
                self.actf(gates[:, :, :].rearrange("p t n -> p (t n)"), pg[:, 0:192], AF.Sigmoid, R=[pg], W=[gates])
                if cfg.get('nsa_stop', 99) == 3:
                    raise _Stop()
                for kvh in range(4):
                    for b3 in range(5):
                        P.dma(BT[:, 3 * b3:3 * b3 + 3, :], d['BTd'].ap()[kvh, 3 * b3:3 * b3 + 3].rearrange("i p n -> p i n"),
                              R=[d['BTd']], W=[BT])
                    for pr in range(2):
                        self.cp('pool', qbd[0:64, pr, 0, :], qT_all[0:64, 2 * kvh + pr, :], R=[qT_all], W=[qbd])
                        self.cp('pool', qbd[64:128, pr, 1, :], qT_all[64:128, 2 * kvh + pr, :], R=[qT_all], W=[qbd])
                    thunks = []
                    for tl in range(4):
                        def mk(tl=tl, kvh=kvh):
                            i = grp * 4 + tl
                            qc = slice(tl * 128, (tl + 1) * 128)
                            bc = self.ring('bcp', 2, lambda jj: P.sb([128, 4, 256], F32, f"bcp{jj}"))
                            smc = self.ring('smc', 2, lambda jj: P.sb([128, 2, 64], F32, f"smc{jj}"))
                            P.dma(smc[:, 0, :], d['selmul_p'].ap()[:, i, :], W=[smc])
                            P.dma(smc[:, 1, :], d['seladd_p'].ap()[:, i, :], W=[smc])
                            for a in range(8):
                                src = bass.AP(d['t_cmp'].h, 4 * kvh * 8192 + 247 - 16 * i - a, [[512, 16], [8192, 4], [1, 256]])
                                P.dma(bc[16 * a:16 * a + 16, :, :], src, R=[d['t_cmp']], W=[bc])
                            nkt = 2 * i + 2
                            sel_tiles = []
                            for kt in range(nkt):
                                e = 2 * i - kt
                                sel_tiles.append((kt, (BT[:, e + 1, :], BT) if e <= 7 else None))
                            win_tiles = []
                            for e in range(4, -2, -1):
                                kt = 2 * i - e
                                if kt >= 0:
                                    win_tiles.append((kt, (BT[:, 10 + e, :], BT)))
                            A = dict(
                                NQ=128, NC=NCB + 1, NH=2, kvh=kvh,
                                qT=(lambda g, qc=qc, kvh=kvh: qT_all[(g % 2) * 64:(g % 2) * 64 + 64, 2 * kvh + g // 2, qc], qT_all),
                                qbd=qbd, qbd_t=qbd, qcols=qc,
                                gate=(lambda br, tl=tl, kvh=kvh: gates[:, tl, :].rearrange("p (h r) -> p h r", r=3)[:, 4 * kvh:4 * kvh + 4, br], gates),
                                kcmp=(lambda g, kvh=kvh: kcd[(g % 2) * 64:(g % 2) * 64 + 64, kvh, 0:NCB + 1], kcd),
                                vcmp=(lambda hf, kvh=kvh: vcm[:, hf, kvh, :], vcm),
                                bias_c=(bc[:, :, :], bc),
                                selc=(smc[:, 0, :], smc[:, 1, :], smc),
                                sel=(sel_tiles,
                                     lambda kt0, n, kvh=kvh: (d['kselT_p'].ap()[kvh, :, kt0 * 128:(kt0 + n) * 128], d['kselT_p']),
                                     lambda kt0, n, kvh=kvh: (d['vsel_p'].ap()[kt0:kt0 + n, :, kvh, :].rearrange("t p e -> p t e"), d['vsel_p']),
                                     self.pb[4]),
                                win=(win_tiles,
                                     lambda kt0, n, kvh=kvh: (d['kwinT_p'].ap()[kvh, :, kt0 * 128:(kt0 + n) * 128], d['kwinT_p']),
                                     lambda kt0, n, kvh=kvh: (d['vwin_p'].ap()[kt0:kt0 + n, :, kvh, :].rearrange("t p e -> p t e"), d['vwin_p']),
                                     self.pb[5]),
                                out=(o_tok[:, tl, kvh * 256:(kvh + 1) * 256].rearrange("p (g e) -> p g e", g=4), o_tok),
                            )

                            return A
                        thunks.append(mk)
                    self.attend_pipe(thunks)
                for tl in range(4):
                    pt = self.pt[0]
                    for cc in range(8):
                        self.tr(pt[:, cc * 128:(cc + 1) * 128], o_tok[:, tl, cc * 128:(cc + 1) * 128], c['identb'][:, :],
                                R=[o_tok, c['identb']], W=[pt])
                    self.cp('act', T['oT'][:, :, tl * 128:(tl + 1) * 128], pt[:, :].rearrange("p (c n) -> p c n", c=8),
                            R=[pt], W=[T['oT']])
                self.tok_linear_add(G, T['oT'], 8, d['wb_nsa_out'][j])
                self.ffn(G, 2 + j)
            _final_norm(self, G, d['y_p'], grp * 512, ytile=self.rr['xeo'][0][0])


Builder.nsa_prompt = _nsa_prompt


def _nsa_sample(self):
    P, d, c, cfg = self.P, self.d, self.c, self.cfg
    NSQ = 16
    self.nsa_late_consts()
    kcd = P.sb([128, NSQ, 4, 128], BF16, "kcds")
    vcm = P.sb([128, NSQ, 4, 64], BF16, "vcms")
    self.memset('pool', vcm[:, :, :, :], 0.0, W=[vcm])
    with self.phase():
        ptb = P.sb([128, 256], I32, "ptb")
        P.dma(ptb[:, :], d['page_table'].ap()[0].partition_broadcast(128), W=[ptb])
        ptf = P.sb([128, 256], F32, "ptf")
        self.cp('dve', ptf[:, :], ptb[:, :], R=[ptb], W=[ptf])
        iot = P.sb([128, 1], F32, "iot")
        P.dma(iot[:, :], d['iota'].ap(), W=[iot])
        self.stt('dve', ptf[:, :], ptf[:, :], 128.0, iot[:, 0:1].to_broadcast([128, 256]), ALU.mult, ALU.add,
                 R=[ptf, iot], W=[ptf])
        idx = P.sb([128, 256], I32, "idxall")
        self.cp('dve', idx[:, :], ptf[:, :], R=[ptf], W=[idx])
        for s in range(NSQ):
            lohi = self.ring('lohis', 2, lambda j: P.sb([128, 2, 4, 136], F32, f"lohis{j}"))
            self.memset('pool', lohi[:, :, :, :], 0.0, W=[lohi])
            for Pi in range(17):
                rows = self.ring('crow', 4, lambda j: P.sb([128, 1024], F32, f"crows{j}"))
                if Pi < 16:
                    k = s * 16 + Pi
                    P.add('pool', lambda e, rows=rows, k=k: e.indirect_dma_start(
                        out=rows[:, :], out_offset=None, in_=d['cache_kv'].ap(),
                        in_offset=bass.IndirectOffsetOnAxis(ap=idx[:, k:k + 1], axis=0)),
                        R=[idx, d['cache_kv']], W=[rows], dma=True)
                else:
                    self.memset('pool', rows[:, :], 0.0, W=[rows])
                    P.dma(rows[0:4, :], d['s_kv_rows'].ap()[4 * s:4 * s + 4, :], R=[d['s_kv_rows']], W=[rows])
                cs = slice(Pi * 128, (Pi + 1) * 128)
                self.ctx_rows(rows, Pi, lohi if Pi < 16 else None,
                              (d['kselT_s'].ap()[s][:, :, cs].rearrange("k p n -> p k n"), d['kselT_s']),
                              (d['vsel_s'].ap()[s, Pi], d['vsel_s']), None, None, has_win=False)
            for W_ in range(5):
                wr = self.ring('wrow_s', 2, lambda j: P.sb([128, 512], F32, f"wrows{j}"))
                if W_ < 4:
                    P.dma(wr[:, :], d['state_win_kv'].ap()[s, W_ * 128:(W_ + 1) * 128, :], W=[wr])
                else:
                    self.memset('pool', wr[:, :], 0.0, W=[wr])
                    P.dma(wr[0:4, :], d['s_win_kv'].ap()[s, 508:512, :], R=[d['s_win_kv']], W=[wr])
                cs = slice(W_ * 128, (W_ + 1) * 128)
                self.ctx_rows(None, 0, None, None, None,
                              (d['kwinT_s'].ap()[s][:, :, cs].rearrange("k p n -> p k n"), d['kwinT_s']),
                              (d['vwin_s'].ap()[s, W_], d['vwin_s']), has_cmpsel=False, win_rows=(wr, 0))
            self.cmp_finish(lohi, 127, lambda kvh, s=s: kcd[:, s, kvh, :], lambda jh, kvh, s=s: vcm[:, s, kvh, :], kcd, vcm,
                            ncw=128, njh=1)
    with self.phase():
        G = Ctx('m', 64, 1, 16, 16, 1)
        T = {}
        T['x'] = P.sb([64, 1, D], F32, "mx")
        T['xnT'] = P.sb([128, 8, 64], BF16, "mxnT")
        big = P.sb([128, 24, 64], BF16, "mbig")
        T['actT'] = View(big, 0, 22)
        qT_all = View(big, 0, 8)
        T['oT'] = P.sb([128, 8, 64], BF16, "moT")
        G.T = T
        P.dma(T['x'][:, 0, :], d['x2_s'].ap(), R=[d['x2_s']], W=[T['x']])
        qbd = P.sb([128, 4, 2, 2, 64], BF16, "qbds")
        self.memset('pool', qbd[:, :, :, :, :], 0.0, W=[qbd])
        smul = P.sb([4, 64], F32, "smuls")
        sadd = P.sb([4, 64], F32, "sadds")
        P.dma(smul[:, :], d['selmul_s'].ap(), W=[smul])
        P.dma(sadd[:, :], d['seladd_s'].ap(), W=[sadd])
        bcs = P.sb([4, 16, 128], F32, "bcs")
        P.dma(bcs[:, :, :], bass.AP(d['t_cs'].h, 0, [[128, 4], [512, 16], [1, 128]]), R=[d['t_cs']], W=[bcs])
        trv = P.sb([128, 22, 16, 4], F32, "trvs")
        for Pi in range(17):
            P.dma(trv[:, Pi, :, :], bass.AP(d['t_ss'].h, 2048 - 128 * Pi, [[1, 128], [2304, 16], [1, 4]]), R=[d['t_ss']], W=[trv])
        for W_ in range(5):
            P.dma(trv[:, 17 + W_, :, :], bass.AP(d['t_ws'].h, 512 - 128 * W_, [[1, 128], [768, 16], [1, 4]]), R=[d['t_ws']], W=[trv])
        BTs = P.sb([128, 22, 16, 4], F32, "BTs")
        tf = trv[:, :, :, :].rearrange("p a h t -> p (a h t)")
        bf_ = BTs[:, :, :, :].rearrange("p a h t -> p (a h t)")
        for off in range(0, 22 * 64, 512):
            n = min(512, 22 * 64 - off)
            ps = self.next_pb('lin', [1, 2, 3])
            self.mm(ps[:, 0:n], c['antiI'][:, :], tf[:, off:off + n], R=[c['antiI'], trv], W=[ps])
            self.cp('act', bf_[:, off:off + n], ps[:, 0:n], R=[ps], W=[BTs])
        for j in range(2):
            self.norm_T(G, d['norm_mix'].ap()[2 + j])
            _nsa_qproj(self, G, j, qT_all, 1)
            for kvh in range(4):
                for pr in range(2):
                    self.cp('pool', qbd[0:64, kvh, pr, 0, :], qT_all[0:64, 2 * kvh + pr, :], R=[qT_all], W=[qbd])
                    self.cp('pool', qbd[64:128, kvh, pr, 1, :], qT_all[64:128, 2 * kvh + pr, :], R=[qT_all], W=[qbd])
            gs_all = self.ring('gs_all', 1, lambda jj: P.sb([4, NSQ, 48], F32, "gs_all"))
            for half in range(2):
                pg = self.pb[0]
                for s8 in range(8):
                    s = half * 8 + s8
                    for kc in range(8):
                        self.mm(pg[0:4, s8 * 48:(s8 + 1) * 48], T['xnT'][:, kc, 4 * s:4 * s + 4], c['wg'][:, j, kc, :],
                                start=(kc == 0), stop=(kc == 7), R=[T['xnT'], c['wg']], W=[pg])
                self.actf(gs_all[:, half * 8:(half + 1) * 8, :].rearrange("p s n -> p (s n)"), pg[0:4, 0:384], AF.Sigmoid,
                          R=[pg], W=[gs_all])
            o_all = self.ring('o_all', 1, lambda jj: P.sb([4, NSQ, D], BF16, "o_all"))
            thunks = []
            for s in range(NSQ):
                for kvh in range(4):
                    def mk(s=s, kvh=kvh):
                        qc = slice(4 * s, 4 * s + 4)
                        sel_tiles = [(Pi, (BTs[:, Pi, 4 * kvh:4 * kvh + 4, :].rearrange("p h t -> p (h t)"), BTs)) for Pi in range(17)]
                        win_tiles = [(W_, (BTs[:, 17 + W_, 4 * kvh:4 * kvh + 4, :].rearrange("p h t -> p (h t)"), BTs)) for W_ in range(5)]
                        return dict(
                            NQ=4, NC=128, NH=1, kvh=kvh,
                            qT=(lambda g: qT_all[(g % 2) * 64:(g % 2) * 64 + 64, 2 * kvh + g // 2, qc], qT_all),
                            qbd=qbd.h[:, kvh], qbd_t=qbd, qcols=qc,
                            gate=(lambda br: gs_all[:, s, :].rearrange("p (h r) -> p h r", r=3)[:, 4 * kvh:4 * kvh + 4, br], gs_all),
                            kcmp=(lambda g: kcd[(g % 2) * 64:(g % 2) * 64 + 64, s, kvh, :], kcd),
                            vcmp=(lambda hf: vcm[:, s, kvh, :], vcm),
                            bias_c=(bcs[:, 4 * kvh:4 * kvh + 4, :], bcs),
                            selc=(smul[:, :], sadd[:, :], smul),
                            sel=(sel_tiles,
                                 lambda kt0, n: (d['kselT_s'].ap()[s, kvh, :, kt0 * 128:(kt0 + n) * 128], d['kselT_s']),
                                 lambda kt0, n: (d['vsel_s'].ap()[s, kt0:kt0 + n, :, kvh, :].rearrange("t p e -> p t e"), d['vsel_s']),
                                 self.pb[4]),
                            win=(win_tiles,
                                 lambda kt0, n: (d['kwinT_s'].ap()[s, kvh, :, kt0 * 128:(kt0 + n) * 128], d['kwinT_s']),
                                 lambda kt0, n: (d['vwin_s'].ap()[s, kt0:kt0 + n, :, kvh, :].rearrange("t p e -> p t e"), d['vwin_s']),
                                 self.pb[5]),
                            out=(o_all[:, s, kvh * 256:(kvh + 1) * 256].rearrange("p (g e) -> p g e", g=4), o_all),
                        )
                    thunks.append(mk)
            self.attend_pipe(thunks)
            for s in range(NSQ):
                qc = slice(4 * s, 4 * s + 4)
                pt = self.pt[0]
                for cc in range(8):
                    self.tr(pt[:, cc * 4:(cc + 1) * 4], o_all[:, s, cc * 128:(cc + 1) * 128], c['identb'][0:4, 0:4],
                            R=[o_all, c['identb']], W=[pt])
                self.cp('act', T['oT'][:, :, qc], pt[:, 0:32].rearrange("p (c n) -> p c n", c=8), R=[pt], W=[T['oT']])
            self.tok_linear_add(G, T['oT'], 8, d['wb_nsa_out'][j])
            self.ffn(G, 2 + j)
        _final_norm(self, G, d['y_s'], 0)


Builder.nsa_sample = _nsa_sample
```
